# Optimizing a Trainium2 kernel written in Bass

```python
import math
import jax, jax.numpy as jnp
from jax import lax
import numpy as np

D_MODEL = 1024
BATCH = 8
SEQ = 2048
DEPTH = 2

MIX_WIDTH = D_MODEL
M_HEADS = 4
M_HEAD_DIM = (MIX_WIDTH // 2) // M_HEADS
M_WIDTH = M_HEADS * M_HEAD_DIM
A_HEADS = 8
A_HEAD_DIM = (MIX_WIDTH - M_WIDTH) // A_HEADS
A_WIDTH = A_HEADS * A_HEAD_DIM
D_FF = 4 * D_MODEL
CONV_WIDTH = 5
M_CHUNK = 64
DILATED_PATTERNS = ((128, 1), (512, 4), (2048, 16))
BAND_BLOCK = 64
ROPE_THETA = 10000.0
NORM_EPS = 1e-6
NEG_INF = -1e30
IN_SPLITS = (M_WIDTH, M_WIDTH, M_WIDTH, M_WIDTH, 2 * M_HEADS, 2 * M_HEADS, A_WIDTH, A_WIDTH, A_WIDTH)
IN_WIDTH = 4 * M_WIDTH + 4 * M_HEADS + 3 * A_WIDTH

kernel_name = "hybrid_mlstm_dilated_attn_encoder"


def rms_norm(x, g):
    xf = x.astype(jnp.float32)
    y = xf * lax.rsqrt(jnp.mean(xf * xf, axis=-1, keepdims=True) + NORM_EPS)
    return (y * g.astype(jnp.float32)).astype(x.dtype)


def rope(t):
    S, D = t.shape[1], t.shape[-1]
    half = D // 2
    inv_freq = ROPE_THETA ** (-jnp.arange(half, dtype=jnp.float32) / half)
    ang = jnp.arange(S, dtype=jnp.float32)[:, None] * inv_freq[None, :]
    cos = jnp.cos(ang)[None, :, None, :]
    sin = jnp.sin(ang)[None, :, None, :]
    tf = t.astype(jnp.float32)
    t1, t2 = tf[..., :half], tf[..., half:]
    return jnp.concatenate([t1 * cos - t2 * sin, t1 * sin + t2 * cos], axis=-1)


def centred_depthwise_conv(x, w):
    K, C = w.shape
    return lax.conv_general_dilated(
        x, w[:, None, :], window_strides=(1,), padding=[(K // 2, K // 2)],
        dimension_numbers=("NWC", "WIO", "NWC"), feature_group_count=C)


def mlstm_chunkwise(q, k, v, li, lf):
    B, H, S, D = q.shape
    L = M_CHUNK
    NC = S // L
    q = q.reshape(B, H, NC, L, D)
    k = k.reshape(B, H, NC, L, D)
    v = v.reshape(B, H, NC, L, D)
    li = li.reshape(B, H, NC, L)
    lf = lf.reshape(B, H, NC, L)
    b = jnp.cumsum(lf, axis=-1)
    b_last = b[..., -1]
    g = b_last[..., None] - b + li
    g_max = jnp.max(g, axis=-1)

    def step(carry, inp):
        C, n, m = carry
        k_c, v_c, g_c, gmax_c, bl_c = inp
        m_new = jnp.maximum(bl_c + m, gmax_c)
        decay = jnp.exp(bl_c + m - m_new)
        w = jnp.exp(g_c - m_new[..., None])
        C_new = decay[..., None, None] * C + jnp.einsum("bhs,bhsk,bhsv->bhkv", w, k_c, v_c)
        n_new = decay[..., None] * n + jnp.einsum("bhs,bhsk->bhk", w, k_c)
        return (C_new, n_new, m_new), (C, n, m)

    init = (jnp.zeros((B, H, D, D), jnp.float32), jnp.zeros((B, H, D), jnp.float32),
            jnp.full((B, H), NEG_INF, jnp.float32))
    xs = (jnp.moveaxis(k, 2, 0), jnp.moveaxis(v, 2, 0), jnp.moveaxis(g, 2, 0),
          jnp.moveaxis(g_max, 2, 0), jnp.moveaxis(b_last, 2, 0))
    _, (C_prev, n_prev, m_prev) = lax.scan(step, init, xs)
    C_prev = jnp.moveaxis(C_prev, 0, 2)
    n_prev = jnp.moveaxis(n_prev, 0, 2)
    m_prev = jnp.moveaxis(m_prev, 0, 2)

    a = b + m_prev[..., None]
    lower = jnp.tril(jnp.ones((L, L), dtype=bool))
    dmat = jnp.where(lower, b[..., :, None] - b[..., None, :] + li[..., None, :], NEG_INF)
    m_t = jnp.maximum(a, jnp.max(dmat, axis=-1))
    s = jnp.einsum("bhnqd,bhnkd->bhnqk", q, k) * jnp.exp(dmat - m_t[..., None])
    inter = jnp.exp(a - m_t)
    num = (jnp.einsum("bhnqk,bhnkd->bhnqd", s, v)
           + inter[..., None] * jnp.einsum("bhnqk,bhnkv->bhnqv", q, C_prev))
    den = jnp.sum(s, axis=-1) + inter * jnp.einsum("bhnqk,bhnk->bhnq", q, n_prev)
    h = num / jnp.maximum(jnp.abs(den), jnp.exp(-m_t))[..., None]
    return h.reshape(B, H, S, D)


def strided_band_attention(q, k, v, dilation, half):
    B, H, S, D = q.shape
    L = S // dilation
    qb_len = math.gcd(L, BAND_BLOCK)
    nb = L // qb_len
    span = qb_len + 2 * half

    def to_strided(t):
        return t.reshape(B, H, L, dilation, D).transpose(0, 1, 3, 2, 4)

    qs = to_strided(q).reshape(B, H, dilation, nb, qb_len, D)
    pad = ((0, 0), (0, 0), (0, 0), (half, half), (0, 0))
    kp = jnp.pad(to_strided(k), pad)
    vp = jnp.pad(to_strided(v), pad)
    idx = (jnp.arange(nb) * qb_len)[:, None] + jnp.arange(span)[None, :]
    kb = kp[:, :, :, idx, :]
    vb = vp[:, :, :, idx, :]
    qpos = (jnp.arange(nb) * qb_len)[:, None, None] + jnp.arange(qb_len)[None, :, None]
    kpos = idx[:, None, :] - half
    valid = (jnp.abs(kpos - qpos) <= half) & (kpos >= 0) & (kpos < L)
    s = jnp.einsum("bhrnqd,bhrnkd->bhrnqk", qs, kb) * (1.0 / math.sqrt(D))
    s = jnp.where(valid, s, NEG_INF)
    m = jnp.max(s, axis=-1)
    p = jnp.exp(s - m[..., None])
    l = jnp.sum(p, axis=-1)
    o = jnp.einsum("bhrnqk,bhrnkd->bhrnqd", p, vb) / l[..., None]

    def from_strided(t):
        tail = t.shape[5:]
        t = jnp.moveaxis(t.reshape((B, H, dilation, L) + tail), 2, 3)
        return t.reshape((B, H, S) + tail)

    return from_strided(o), from_strided(m), from_strided(l)


def dilated_attention(q, k, v):
    outs = [strided_band_attention(q, k, v, d, w // (2 * d)) for (w, d) in DILATED_PATTERNS]
    m_all = jnp.max(jnp.stack([m for (_, m, _) in outs]), axis=0)
    weights = [l * jnp.exp(m - m_all) for (_, m, l) in outs]
    num = sum(w[..., None] * o for w, (o, _, _) in zip(weights, outs))
    return num / sum(weights)[..., None]


def hybrid_layer(x, norm1_g, w_in, conv_w, gate_i_b, gate_f_b, head_norm_g, w_out,
                 norm2_g, w_up, w_down):
    B, S, _ = x.shape
    h = rms_norm(x, norm1_g)
    proj = h @ w_in
    mq, mk, mv, mo, mi, mf, aq, ak, av = jnp.split(
        proj, list(np.cumsum(IN_SPLITS)[:-1]), axis=-1)

    qk = jax.nn.silu(centred_depthwise_conv(jnp.concatenate([mq, mk], axis=-1), conv_w))
    mq, mk = qk[..., :M_WIDTH], qk[..., M_WIDTH:]

    def m_heads(t):
        return t.reshape(B, S, M_HEADS, M_HEAD_DIM).transpose(0, 2, 1, 3).astype(jnp.float32)

    q_m = m_heads(mq)
    k_m = m_heads(mk) * (1.0 / math.sqrt(M_HEAD_DIM))
    v_m = m_heads(mv)
    gi = (mi + gate_i_b).astype(jnp.float32).reshape(B, S, 2, M_HEADS).transpose(2, 0, 3, 1)
    gf = jax.nn.log_sigmoid((mf + gate_f_b).astype(jnp.float32)).reshape(
        B, S, 2, M_HEADS).transpose(2, 0, 3, 1)
    h_fwd = mlstm_chunkwise(q_m, k_m, v_m, gi[0], gf[0])
    flip = lambda t: jnp.flip(t, axis=2)
    h_bwd = flip(mlstm_chunkwise(flip(q_m), flip(k_m), flip(v_m), flip(gi[1]), flip(gf[1])))
    hm = h_fwd + h_bwd
    hm = hm * lax.rsqrt(jnp.mean(hm * hm, axis=-1, keepdims=True) + NORM_EPS)
    hm = hm.transpose(0, 2, 1, 3).reshape(B, S, M_WIDTH) * head_norm_g.astype(jnp.float32)
    y_m = (hm * jax.nn.sigmoid(mo.astype(jnp.float32))).astype(x.dtype)

    def a_heads(t, rotate):
        t = t.reshape(B, S, A_HEADS, A_HEAD_DIM)
        t = rope(t) if rotate else t.astype(jnp.float32)
        return t.transpose(0, 2, 1, 3)

    o_a = dilated_attention(a_heads(aq, True), a_heads(ak, True), a_heads(av, False))
    y_a = o_a.transpose(0, 2, 1, 3).reshape(B, S, A_WIDTH).astype(x.dtype)

    x = x + jnp.concatenate([y_m, y_a], axis=-1) @ w_out

    u = jax.nn.relu(rms_norm(x, norm2_g) @ w_up)
    return x + (u * u) @ w_down


def setup_inputs(seed: int = 0) -> dict:
    key = jax.random.key(seed)
    ks = jax.random.split(key, 13)
    f32 = jnp.float32
    nrm = lambda k, shape: jax.random.normal(k, shape, f32)
    x = nrm(ks[0], (BATCH, SEQ, D_MODEL))
    norm1_g = 1.0 + 0.02 * nrm(ks[1], (DEPTH, D_MODEL))
    w_in = nrm(ks[2], (DEPTH, D_MODEL, IN_WIDTH)) * D_MODEL ** -0.5
    conv_w = nrm(ks[3], (DEPTH, CONV_WIDTH, 2 * M_WIDTH)) * CONV_WIDTH ** -0.5
    gate_i_b = 0.1 * nrm(ks[4], (DEPTH, 2 * M_HEADS))
    gate_f_b = (jnp.tile(jnp.linspace(3.0, 6.0, M_HEADS, dtype=f32), 2)[None, :]
                + 0.1 * nrm(ks[5], (DEPTH, 2 * M_HEADS)))
    head_norm_g = 1.0 + 0.02 * nrm(ks[6], (DEPTH, M_WIDTH))
    w_out = nrm(ks[7], (DEPTH, MIX_WIDTH, D_MODEL)) * MIX_WIDTH ** -0.5
    norm2_g = 1.0 + 0.02 * nrm(ks[8], (DEPTH, D_MODEL))
    w_up = nrm(ks[9], (DEPTH, D_MODEL, D_FF)) * D_MODEL ** -0.5
    w_down = nrm(ks[10], (DEPTH, D_FF, D_MODEL)) * D_FF ** -0.5
    final_g = 1.0 + 0.02 * nrm(ks[11], (D_MODEL,))
    return {"x": x, "norm1_g": norm1_g, "w_in": w_in, "conv_w": conv_w,
            "gate_i_b": gate_i_b, "gate_f_b": gate_f_b, "head_norm_g": head_norm_g,
            "w_out": w_out, "norm2_g": norm2_g, "w_up": w_up, "w_down": w_down,
            "final_g": final_g}


def reference(x, norm1_g, w_in, conv_w, gate_i_b, gate_f_b, head_norm_g, w_out,
              norm2_g, w_up, w_down, final_g):
    for layer in range(DEPTH):
        x = hybrid_layer(x, norm1_g[layer], w_in[layer], conv_w[layer], gate_i_b[layer],
                         gate_f_b[layer], head_norm_g[layer], w_out[layer], norm2_g[layer],
                         w_up[layer], w_down[layer])
    return rms_norm(x, final_g)
```

```python
import math
from contextlib import ExitStack

import numpy as np
import concourse.bass as bass
import concourse.mybir as mybir
from concourse.bass_utils import run_bass_kernel_spmd

F32 = mybir.dt.float32
BF16 = mybir.dt.bfloat16
ALU = mybir.AluOpType
AF = mybir.ActivationFunctionType
AX = mybir.AxisListType

P = 128
S = 2048
D = 1024
DFF = 4096
NL = 2
EPS = 1e-6
NFULL = 26
NSMALL = 5
TB = 1024

CFG = {"mlstm": True, "attn": True, "ffn": True, "layers": NL, "debug": False}


def I(name, *args, **kw):
    return (name, args, kw)


class Op:
    __slots__ = ("eng", "fn", "deps", "dma", "slot", "tok", "signal", "waits", "idx", "semkey", "semval")


class WK(tuple):
    def __new__(cls, slot):
        return tuple.__new__(cls, (("wb", slot, 0), ("wb", slot, 1)))


def _flat(keys):
    out = []
    for k in keys:
        if isinstance(k, WK):
            out.extend(k)
        else:
            out.append(k)
    return out


class Sched:
    EPOCH = 30000

    def __init__(self):
        self.ops = []
        self.last_w = {}
        self.readers = {}
        self.stream_len = {"pe": 0, "dve": 0, "act": 0, "pool": 0, "sp": 0}
        self.slot_cnt = {}

    def add(self, eng, fn, r=(), w=(), dma_slot=None):
        op = Op()
        op.eng, op.fn, op.dma, op.slot = eng, fn, dma_slot is not None, dma_slot
        op.signal = False
        op.waits = []
        op.idx = len(self.ops)
        r = _flat(r)
        w = _flat(w)
        deps = set()
        for k in r:
            if k in self.last_w:
                deps.add(self.last_w[k])
        for k in w:
            if k in self.last_w:
                deps.add(self.last_w[k])
            for j in self.readers.get(k, ()):
                deps.add(j)
        op.deps = deps
        if op.dma:
            c = self.slot_cnt.get(dma_slot, 0) + 1
            self.slot_cnt[dma_slot] = c
            op.tok = (("dma", dma_slot), c)
            self.stream_len[eng] += 1
        else:
            self.stream_len[eng] += 1
            op.tok = (("eng", eng), self.stream_len[eng])
        for k in r:
            self.readers.setdefault(k, []).append(op.idx)
        for k in w:
            self.last_w[k] = op.idx
            self.readers[k] = []
        self.ops.append(op)
        return op

    def barrier(self):
        last = {}
        for op in self.ops:
            if op.fn is not None:
                last[op.eng] = op.idx
        for e in list(self.stream_len):
            op = Op()
            op.eng, op.fn, op.dma, op.slot = e, None, False, None
            op.signal = False
            op.waits = []
            op.idx = len(self.ops)
            op.deps = set(v for k, v in last.items() if k != e)
            self.stream_len[e] += 1
            op.tok = (("eng", e), self.stream_len[e])
            self.ops.append(op)

    def analyse(self):
        seen = {e: {} for e in self.stream_len}
        for op in self.ops:
            sn = seen[op.eng]
            for j in sorted(op.deps):
                d = self.ops[j]
                if (not d.dma) and d.eng == "pe" and op.eng == "pe" and not op.dma:
                    continue
                fam, v = d.tok
                if sn.get(fam, 0) >= v:
                    continue
                sn[fam] = v
                d.signal = True
                op.waits.append(j)
        rank = {e: 0 for e in self.stream_len}
        self.nepoch = {e: 1 for e in self.stream_len}
        for op in self.ops:
            if op.dma:
                op.semkey = op.tok[0]
                op.semval = 16 * op.tok[1]
            elif op.signal:
                r = rank[op.eng]
                rank[op.eng] = r + 1
                ep = r // self.EPOCH
                self.nepoch[op.eng] = max(self.nepoch[op.eng], ep + 1)
                op.semkey = ("eng", op.eng, ep)
                op.semval = r - ep * self.EPOCH + 1

    def emit(self, nc, es):
        self.analyse()
        sems = {}

        def getsem(key):
            if key not in sems:
                nm = "s_" + "_".join(str(x) for x in key).replace("(", "").replace(")", "").replace(",", "_").replace(" ", "").replace("'", "")
                sems[key] = es.enter_context(nc.semaphore(nm))
            return sems[key]

        for op in self.ops:
            if op.dma or op.signal:
                getsem(op.semkey)
        block = es.enter_context(nc.Block())
        by_eng = {e: [o for o in self.ops if o.eng == e] for e in self.stream_len}

        def body(eng_name):
            def f(eng):
                for op in by_eng[eng_name]:
                    for j in op.waits:
                        d = self.ops[j]
                        eng.wait_ge(sems[d.semkey], d.semval)
                    if op.fn is None:
                        continue
                    name, args, kw = op.fn
                    ins = getattr(eng, name)(*args, **kw)
                    if op.dma:
                        ins.then_inc(sems[op.semkey], 16)
                    elif op.signal:
                        ins.then_inc(sems[op.semkey], 1)
            return f

        block.tensor(body("pe"))
        block.vector(body("dve"))
        block.scalar(body("act"))
        block.gpsimd(body("pool"))
        block.sync(body("sp"))


def _bform(w, cols):
    out = np.empty((P, 4, 8, P), np.float32)
    for n, ci in enumerate(cols):
        blk = w[:, ci]
        out[:, n] = blk.reshape(8, P, P).transpose(1, 0, 2)
    return out.reshape(P, 4096)


def _aform(w, ci):
    blk = w[:, ci]
    C = blk.shape[1]
    return blk.reshape(8, P, C).transpose(1, 0, 2).reshape(P, 8 * C)


def prep_weights(inp):
    wfull = np.zeros((NL * NFULL, P, 4096), np.float32)
    wsmall = np.zeros((NL * NSMALL, P, 1024), np.float32)
    ar = np.arange(P)
    for l in range(NL):
        w_in = np.asarray(inp["w_in"][l], np.float32)
        w_out = np.asarray(inp["w_out"][l], np.float32)
        w_up = np.asarray(inp["w_up"][l], np.float32)
        w_dn = np.asarray(inp["w_down"][l], np.float32)
        fb = l * NFULL
        sb = l * NSMALL
        for hp in range(2):
            h0, h1 = 2 * hp, 2 * hp + 1
            wfull[fb + 2 * hp] = _bform(w_in, [h0 * P + ar, h1 * P + ar, 512 + h0 * P + ar, 512 + h1 * P + ar])
            ci = np.concatenate([1024 + h0 * P + ar, 1024 + h1 * P + ar, 1536 + h0 * P + ar, 1536 + h1 * P + ar])
            wfull[fb + 2 * hp + 1] = _aform(w_in, ci)
        a = ar // 64
        dd = ar % 64
        sw = a * 64 + (dd + 32) % 64
        for hp in range(4):
            qb, kb = 2064 + hp * P, 2576 + hp * P
            wfull[fb + 4 + hp] = _bform(w_in, [qb + ar, qb + sw, kb + ar, kb + sw])
            wsmall[sb + 1 + hp] = _aform(w_in, 3088 + hp * P + ar)
        wsmall[sb + 0, :, :128] = _aform(w_in, 2048 + np.arange(16))
        for hf in range(2):
            blk = w_out[hf * 512:(hf + 1) * 512, :]
            t = blk.reshape(4, P, 8, P).transpose(1, 2, 0, 3)
            wfull[fb + 8 + hf] = t.reshape(P, 4096)
        for g in range(8):
            wfull[fb + 10 + g] = _bform(w_up, [(4 * g + n) * P + ar for n in range(4)])
        for m in range(8):
            blk = w_dn[:, m * P:(m + 1) * P]
            wfull[fb + 18 + m] = blk.reshape(32, P, P).transpose(1, 0, 2).reshape(P, 4096)
    return wfull, wsmall


def _cst_layout():
    off = {}
    o = 0

    def put(name, n):
        nonlocal o
        off[name] = (o, n)
        o += n
    put("g1", NL * 8)
    put("g2", NL * 8)
    put("gf", 8)
    put("convw", NL * 40)
    put("gb", NL * 16)
    for nm in ("Tf", "Tb", "NEGf", "NEGb", "identF", "onesF", "sel127", "sel0"):
        put(nm, P)
    put("epsc", 4)
    off["_NCS"] = (o, 0)
    put("hng", NL * 512)
    put("band", 512)
    return off, o


CST_OFF, NCST = _cst_layout()
NCS = CST_OFF["_NCS"][0]


def prep_consts(inp):
    c = np.zeros((P, NCST), np.float32)

    def setc(name, arr):
        o, n = CST_OFF[name]
        c[:, o:o + n] = arr.reshape(P, n)
    g1 = np.asarray(inp["norm1_g"], np.float32).reshape(NL, 8, P).transpose(2, 0, 1)
    g2 = np.asarray(inp["norm2_g"], np.float32).reshape(NL, 8, P).transpose(2, 0, 1)
    gf = np.asarray(inp["final_g"], np.float32).reshape(8, P).transpose(1, 0)
    setc("g1", np.ascontiguousarray(g1))
    setc("g2", np.ascontiguousarray(g2))
    setc("gf", np.ascontiguousarray(gf))
    cw = np.asarray(inp["conv_w"], np.float32).reshape(NL, 5, 8, P).transpose(3, 0, 1, 2)
    setc("convw", np.ascontiguousarray(cw))
    gb = np.concatenate([np.asarray(inp["gate_i_b"], np.float32), np.asarray(inp["gate_f_b"], np.float32)], axis=1)
    setc("gb", np.ascontiguousarray(np.broadcast_to(gb.reshape(1, NL * 16), (P, NL * 16))))
    hng = np.asarray(inp["head_norm_g"], np.float32).reshape(1, NL * 512)
    setc("hng", np.ascontiguousarray(np.broadcast_to(hng, (P, NL * 512))))
    k = np.arange(P)[:, None]
    i = np.arange(P)[None, :]
    NEG = -30000.0
    setc("Tf", (k <= i).astype(np.float32))
    setc("Tb", (k >= i).astype(np.float32))
    setc("NEGf", np.where(k <= i, 0.0, NEG).astype(np.float32))
    setc("NEGb", np.where(k >= i, 0.0, NEG).astype(np.float32))
    setc("identF", np.eye(P, dtype=np.float32))
    setc("onesF", np.ones((P, P), np.float32))
    s127 = np.zeros((P, P), np.float32)
    s127[127, :] = 1.0
    s0 = np.zeros((P, P), np.float32)
    s0[0, :] = 1.0
    setc("sel127", s127)
    setc("sel0", s0)
    cc = np.arange(256)[None, :]
    m1 = ((cc >= k) & (cc <= k + 128)).astype(np.float32)
    band = np.concatenate([m1, m1], axis=1)
    setc("band", band)
    ec = np.zeros((P, 4), np.float32)
    ec[:, 0] = EPS
    ec[:, 1] = 1.0
    ec[:, 2] = -0.5 * math.log(128.0)
    setc("epsc", ec)
    return c


def prep_rope():
    half = 32
    inv = (10000.0 ** (-np.arange(half, dtype=np.float32) / half)).astype(np.float32)
    ang = np.arange(S, dtype=np.float32)[:, None] * inv[None, :]
    cos = np.cos(ang).astype(np.float32).T
    sin = np.sin(ang).astype(np.float32).T
    cos64 = np.concatenate([cos, cos], 0)
    sin64 = np.concatenate([-sin, sin], 0)
    r = np.zeros((2, P, S), np.float32)
    r[0] = np.concatenate([cos64, cos64], 0)
    r[1] = np.concatenate([sin64, sin64], 0)
    return r


def build(cfg=CFG):
    nc = bass.Bass("TRN2", target_bir_lowering=False)
    nlay = cfg["layers"]
    xT_d = nc.dram_tensor("xT", [D, S], F32, kind="ExternalInput").ap()
    wf_d = nc.dram_tensor("wfull", [NL * NFULL, P, 4096], F32, kind="ExternalInput").ap()
    ws_d = nc.dram_tensor("wsmall", [NL * NSMALL, P, 1024], F32, kind="ExternalInput").ap()
    cst_d = nc.dram_tensor("cst", [P, NCST], F32, kind="ExternalInput").ap()
    rope_d = nc.dram_tensor("rope", [2, P, S], F32, kind="ExternalInput").ap()
    out_d = nc.dram_tensor("outT", [D, S], F32, kind="ExternalOutput").ap()

    es = ExitStack()
    sc = Sched()

    def sb(name, shape, dt=F32):
        return es.enter_context(nc.sbuf_tensor(name, shape, dt))

    def ps(name, shape, dt=F32):
        return es.enter_context(nc.psum_tensor(name, shape, dt))

    xT = sb("xT_sb", [P, 8, S])
    cst = sb("cst_sb", [P, NCS])
    NWB = 2
    wb = [sb(f"wb{i}", [P, 4096], BF16) for i in range(NWB)]
    wsb = sb("wsb", [P, 1024], BF16)
    NU = 54944
    U = sb("U", [P, NU], BF16)
    hT = U[:, 0:16384].rearrange("p (c t) -> p c t", c=8)
    yT = U[:, 16384:24576].rearrange("p (c t) -> p c t", c=4)
    RBASE = 24576

    class Carver:
        def __init__(self, base):
            self.o = base

        def bf(self, n):
            a = U[:, self.o:self.o + n]
            self.o += n + (n % 2)
            assert self.o <= NU
            return a

        def f32(self, n):
            a = U[:, self.o:self.o + 2 * n].bitcast(F32)
            self.o += 2 * n
            assert self.o <= NU
            return a

    sqb = [sb(f"sqb{i}", [P, 512], BF16) for i in range(2)]
    rst = [sb("rst0", [P, 512])] * 2
    ones_bf = sb("ones_bf", [P, P], BF16)
    ones1_bf = sb("ones1_bf", [P, 64], BF16)
    ident_bf = sb("ident_bf", [P, P], BF16)
    mask2 = sb("mask2", [P, 2, 256], BF16)
    diag = [sb(f"diag{i}", [P, 5, P], BF16) for i in range(2)]
    G = {nm: sb("G_" + nm, [P, 2, 16, 4]) for nm in ("gi", "lf", "bcol", "col", "blast", "w", "decay")}
    graw = sb("graw", [P, 16, 16])
    pbank = [ps(f"pb{i}", [P, 512]) for i in range(8)]

    def C(name, a=0, n=None):
        o, ln = CST_OFF[name]
        n = ln - a if n is None else n
        return cst[:, o + a:o + a + n]

    sc.add("sp", I("dma_start", out=cst[:], in_=cst_d[:, 0:NCS]), w=["cst"], dma_slot="cst")
    xv = xT_d.rearrange("(c p) t -> p c t", p=P)
    for c in range(8):
        sc.add("sp", I("dma_start", out=xT[:, c, :], in_=xv[:, c, :]), w=[("xT", c)], dma_slot=("x", c))
    sc.add("pool", I("memset", ones_bf[:], 1.0 / D), w=["ones_bf"])
    sc.add("pool", I("memset", ones1_bf[:], 1.0), w=["ones1_bf"])
    sc.add("dve", I("tensor_copy", out=ident_bf[:], in_=C("identF")), r=["cst"], w=["ident_bf"])
    bo = CST_OFF["band"][0]
    sc.add("pool", I("dma_start", out=mask2[:].rearrange("p a b -> p (a b)"), in_=cst_d[:, bo:bo + 512]), w=["mask2"], dma_slot="band")

    wstate = {"n": 0, "issued": 0}
    plan = []
    for l_ in range(nlay):
        fb_ = l_ * NFULL
        if cfg["mlstm"]:
            plan += [(fb_ + 0, 4096, 0), (fb_ + 1, 4096, 0), (fb_ + 2, 4096, 0), (fb_ + 3, 4096, 0), (fb_ + 8, 4096, 0)]
        if cfg["attn"]:
            plan += [(fb_ + 4 + hp_, 4096, 0) for hp_ in range(4)] + [(fb_ + 9, 4096, 0)]
        if cfg["ffn"]:
            for half_ in range(2):
                plan += [(fb_ + 10 + half_ * 4 + g_, 4096, 0) for g_ in range(4)]
                plan += [(fb_ + 18 + m_, 2048, half_ * 2048) for m_ in range(8)]

    def _issue(idx):
        gidx, ncols, coloff = plan[idx]
        slot = idx % NWB
        buf = wb[slot]
        for hlf in range(ncols // 2048):
            sc.add("pool", I("dma_start", out=buf[:, hlf * 2048:(hlf + 1) * 2048], in_=wf_d[gidx, :, coloff + hlf * 2048:coloff + (hlf + 1) * 2048]),
                   w=[("wb", slot, hlf)], dma_slot=("wb", slot, hlf))

    def load_w(gidx, ncols=4096, coloff=0):
        n = wstate["n"]
        assert plan[n] == (gidx, ncols, coloff), (n, plan[n], gidx, ncols, coloff)
        wstate["n"] = n + 1
        while wstate["issued"] <= min(n + 1, len(plan) - 1):
            _issue(wstate["issued"])
            wstate["issued"] += 1
        slot = n % NWB
        if ncols // 2048 == 1:
            return wb[slot], ("wb", slot, 0)
        return wb[slot], WK(slot)

    def load_wsmall(gidx, ncols=1024):
        sc.add("pool", I("dma_start", out=wsb[:, 0:ncols], in_=ws_d[gidx, :, 0:ncols]), w=["wsb"], dma_slot="wsb")
        return wsb, "wsb"

    pcount = {"n": 0}

    def next_bank():
        i = pcount["n"] % 8
        pcount["n"] += 1
        return i

    def rmsnorm(gname, goff, t0, t1, dst_fn, dst_keys):
        nb = (t1 - t0) // 512
        for b in range(nb):
            lo = t0 + b * 512
            hi = lo + 512
            bk = next_bank()
            for c in range(8):
                q = sqb[c % 2]
                sc.add("act", I("activation", out=q[:], in_=xT[:, c, lo:hi], func=AF.Square),
                       r=[("xT", c)], w=[("sqb", c % 2)])
                sc.add("pe", I("matmul", pbank[bk][:], lhsT=ones_bf[:], rhs=q[:], start=(c == 0), stop=(c == 7)),
                       r=[("sqb", c % 2), "ones_bf"], w=[("pb", bk)])
            r_ = rst[b % 2]
            sc.add("act", I("activation", out=r_[:], in_=pbank[bk][:], func=AF.Ln, bias=C("epsc", 0, 1), scale=1.0),
                   r=[("pb", bk), "cst"], w=[("rst", 0)])
            sc.add("act", I("activation", out=r_[:], in_=r_[:], func=AF.Exp, scale=-0.5),
                   r=[("rst", 0)], w=[("rst", 0)])
            for c in range(8):
                eng = "dve"
                sc.add(eng, I("scalar_tensor_tensor",
                    out=dst_fn(c, lo, hi), in0=xT[:, c, lo:hi], scalar=C(gname, goff + c, 1), in1=r_[:], op0=ALU.mult, op1=ALU.mult),
                    r=[("xT", c), ("rst", 0), "cst"], w=dst_keys(c))

    def ffn(l):
        uTv = U[:, 16384:16384 + 16 * S].rearrange("p (n t) -> p n t", t=S)
        rtmp = [U[:, 49152 + i * 1024:49152 + (i + 1) * 1024].bitcast(F32) for i in range(2)]
        rmsnorm("g2", l * 8, 0, S, lambda c, lo, hi: hT[:, c, lo:hi], lambda c: [("hT", c)])
        cnt = 0
        for half in range(2):
            for g in range(4):
                w, wk = load_w(l * NFULL + 10 + half * 4 + g)
                wv = w[:].rearrange("p (n k c) -> p n k c", n=4, k=8)
                for n in range(4):
                    ch = 4 * g + n
                    for blk in range(4):
                        lo = blk * 512
                        bk = next_bank()
                        for kc in range(8):
                            sc.add("pe", I("matmul", pbank[bk][:], lhsT=wv[:, n, kc, :], rhs=hT[:, kc, lo:lo + 512], start=(kc == 0), stop=(kc == 7)),
                                   r=[wk, ("hT", kc)], w=[("pb", bk)])
                        rt = rtmp[cnt % 2]
                        sc.add("act", I("activation", out=rt[:], in_=pbank[bk][:], func=AF.Relu), r=[("pb", bk)], w=[("rtmp", cnt % 2)])
                        sc.add("pool", I("tensor_tensor", out=uTv[:, ch, lo:lo + 512], in0=rt[:], in1=rt[:], op=ALU.mult),
                               r=[("rtmp", cnt % 2)], w=[("uT", ch, blk)])
                        cnt += 1
            for m in range(8):
                w, wk = load_w(l * NFULL + 18 + m, ncols=2048, coloff=half * 2048)
                wv = w[:, 0:2048].rearrange("p (n c) -> p n c", n=16)
                for blk in range(4):
                    lo = blk * 512
                    bk = next_bank()
                    for n in range(16):
                        sc.add("pe", I("matmul", pbank[bk][:], lhsT=wv[:, n, :], rhs=uTv[:, n, lo:lo + 512], start=(n == 0), stop=(n == 15)),
                               r=[wk, ("uT", n, blk)], w=[("pb", bk)])
                    sc.add("dve", I("tensor_tensor", out=xT[:, m, lo:lo + 512], in0=xT[:, m, lo:lo + 512], in1=pbank[bk][:], op=ALU.add),
                           r=[("pb", bk), ("xT", m)], w=[("xT", m)])

    def proj_B(wv_n, wk, blk, bk):
        lo = blk * 512
        for kc in range(8):
            sc.add("pe", I("matmul", pbank[bk][:], lhsT=wv_n[:, kc, :], rhs=hT[:, kc, lo:lo + 512], start=(kc == 0), stop=(kc == 7)),
                   r=[wk, ("hT", kc)], w=[("pb", bk)])

    def proj_A(wv, wk, ncol, tok_ap_fn, out_ap, bk):
        for kc in range(8):
            sc.add("pe", I("matmul", out_ap, lhsT=tok_ap_fn(kc), rhs=wv[:, kc, 0:ncol], start=(kc == 0), stop=(kc == 7)),
                   r=[wk, ("hT", kc)], w=[("pb", bk)])

    def pbf(bk):
        return pbank[bk][:].bitcast(BF16)

    def gates(l):
        w, wk = load_wsmall(l * NSMALL + 0, 128)
        wv = w[:, 0:128].rearrange("p (k c) -> p k c", k=8)
        bk = next_bank()
        for t in range(16):
            proj_A(wv, wk, 16, lambda kc, t=t: hT[:, kc, t * P:(t + 1) * P], pbank[bk][:, t * 16:(t + 1) * 16], bk)
        gbv = C("gb", l * 16, 16).unsqueeze(1).to_broadcast([P, 16, 16])
        sc.add("dve", I("tensor_tensor", out=graw[:], in0=pbank[bk][:, 0:256].rearrange("p (t c) -> p t c", t=16), in1=gbv, op=ALU.add),
               r=[("pb", bk), "cst"], w=["graw"])
        gi_src = graw[:, :, 0:8].rearrange("p t (d h) -> p d t h", d=2)
        gf_src = graw[:, :, 8:16].rearrange("p t (d h) -> p d t h", d=2)
        sc.add("dve", I("tensor_scalar", out=G["gi"][:], in0=gi_src, scalar1=-0.5 * math.log(128.0), scalar2=0.0, op0=ALU.add, op1=ALU.add),
               r=["graw"], w=["G_gi"])
        sc.add("act", I("activation", out=G["w"][:], in_=gf_src, func=AF.Exp, scale=-1.0), r=["graw"], w=["G_w"])
        sc.add("act", I("activation", out=G["w"][:], in_=G["w"][:], func=AF.Ln, bias=C("epsc", 1, 1), scale=1.0), r=["G_w", "cst"], w=["G_w"])
        sc.add("dve", I("tensor_scalar", out=G["lf"][:], in0=G["w"][:], scalar1=-1.0, scalar2=0.0, op0=ALU.mult, op1=ALU.add),
               r=["G_w"], w=["G_lf"])
        bk2 = next_bank()
        for d_, Tn in ((0, "Tf"), (1, "Tb")):
            sc.add("pe", I("matmul", pbank[bk2][:, d_ * 64:(d_ + 1) * 64], lhsT=C(Tn), rhs=G["lf"][:, d_].rearrange("p t h -> p (t h)"), start=True, stop=True),
                   r=["G_lf", "cst"], w=[("pb", bk2)])
        sc.add("dve", I("tensor_copy", out=G["bcol"][:].rearrange("p d t h -> p (d t h)"), in_=pbank[bk2][:, 0:128]), r=[("pb", bk2)], w=["G_bcol"])
        sc.add("dve", I("tensor_tensor", out=G["col"][:], in0=G["gi"][:], in1=G["bcol"][:], op=ALU.subtract), r=["G_gi", "G_bcol"], w=["G_col"])
        bk3 = next_bank()
        for d_, Sn in ((0, "sel127"), (1, "sel0")):
            sc.add("pe", I("matmul", pbank[bk3][:, d_ * 64:(d_ + 1) * 64], lhsT=C(Sn), rhs=G["bcol"][:, d_].rearrange("p t h -> p (t h)"), start=True, stop=True),
                   r=["G_bcol", "cst"], w=[("pb", bk3)])
        sc.add("dve", I("tensor_copy", out=G["blast"][:].rearrange("p d t h -> p (d t h)"), in_=pbank[bk3][:, 0:128]), r=[("pb", bk3)], w=["G_blast"])
        sc.add("act", I("activation", out=G["decay"][:], in_=G["blast"][:], func=AF.Exp), r=["G_blast"], w=["G_decay"])
        sc.add("dve", I("tensor_tensor", out=G["w"][:], in0=G["col"][:], in1=G["blast"][:], op=ALU.add), r=["G_col", "G_blast"], w=["G_w"])
        sc.add("act", I("activation", out=G["w"][:], in_=G["w"][:], func=AF.Exp), r=["G_w"], w=["G_w"])

    def mlstm_pair(l, hp):
        cv = Carver(RBASE)
        mqk = cv.bf(4 * S).rearrange("p (n t) -> p n t", n=4)
        Vext = cv.bf(16 * 2 * 129).rearrange("p (t a c) -> p t a c", t=16, a=2)
        ogt = cv.bf(16 * 256).rearrange("p (t c) -> p t c", t=16)
        Cstb = cv.bf(16 * 2 * 129).rearrange("p (t a c) -> p t a c", t=16, a=2)
        tmp_base = cv.o
        pre = [cv.bf(2052), cv.bf(2052)]
        cv.o = tmp_base
        Tlf = [cv.f32(256).rearrange("p (a i) -> p a i", a=2) for _ in range(2)]
        rhs2 = [cv.f32(256).rearrange("p (a i) -> p a i", a=2) for _ in range(2)]
        Dx = [cv.bf(256) for _ in range(2)]
        eb = [cv.bf(256).rearrange("p (a i) -> p a i", a=2) for _ in range(2)]
        qs = [cv.bf(256).rearrange("p (a i) -> p a i", a=2) for _ in range(2)]
        PT = [cv.bf(256).rearrange("p (a i) -> p a i", a=2) for _ in range(2)]
        cv.o = max(cv.o, tmp_base + 2 * 2052)
        wK = cv.bf(256).rearrange("p (a c) -> p a c", a=2)
        Crun = [cv.f32(258).rearrange("p (a c) -> p a c", a=2) for _ in range(2)]
        CfR = [cv.bf(258).rearrange("p (a c) -> p a c", a=2) for _ in range(3)]
        hsum = cv.f32(256).rearrange("p (a c) -> p a c", a=2)
        sq = cv.f32(256).rearrange("p (a c) -> p a c", a=2)
        gog = cv.f32(256)
        halfg = cv.f32(256)
        ybf = cv.bf(256).rearrange("p (a c) -> p a c", a=2)
        rr = cv.f32(8)
        ssq = cv.f32(2)
        K = lambda nm: ("m", nm)

        w, wk = load_w(l * NFULL + 2 * hp)
        wv = w[:].rearrange("p (n k c) -> p n k c", n=4, k=8)
        for i in range(2):
            sc.add("pool", I("memset", pre[i][:, 0:2], 0.0), w=[K(("pre", i))])
            sc.add("pool", I("memset", pre[i][:, 2050:2052], 0.0), w=[K(("pre", i))])
        for n in range(4):
            c8 = (2 * hp + n) if n < 2 else (4 + 2 * hp + n - 2)
            pb_ = pre[n % 2]
            dg = diag[n % 2]
            for tau in range(5):
                sc.add("pool", I("tensor_tensor", out=dg[:, tau, :], in0=C("identF"), in1=C("convw", l * 40 + tau * 8 + c8, 1).to_broadcast([P, P]), op=ALU.mult),
                       r=["ident_bf", "cst"], w=[("diag", n % 2)])
            for blk in range(4):
                bk = next_bank()
                proj_B(wv[:, n], wk, blk, bk)
                sc.add("act", I("activation", out=pb_[:, 2 + blk * 512:2 + (blk + 1) * 512], in_=pbank[bk][:], func=AF.Copy),
                       r=[("pb", bk)], w=[K(("pre", n % 2))])
            for blk in range(4):
                bk = next_bank()
                for tau in range(5):
                    sc.add("pe", I("matmul", pbank[bk][:], lhsT=dg[:, tau, :], rhs=pb_[:, blk * 512 + tau:blk * 512 + tau + 512], start=(tau == 0), stop=(tau == 4)),
                           r=[K(("pre", n % 2)), ("diag", n % 2)], w=[("pb", bk)])
                sc.add("act", I("activation", out=mqk[:, n, blk * 512:(blk + 1) * 512], in_=pbank[bk][:], func=AF.Silu),
                       r=[("pb", bk)], w=[K(("mqk", n))])
        w, wk = load_w(l * NFULL + 2 * hp + 1)
        wv = w[:].rearrange("p (k c) -> p k c", k=8)
        sc.add("pool", I("memset", Vext[:, :, :, 128:129], 1.0), w=[K("Vext")])
        for t in range(16):
            bk = next_bank()
            proj_A(wv, wk, 512, lambda kc, t=t: hT[:, kc, t * P:(t + 1) * P], pbank[bk][:], bk)
            sc.add("act", I("activation", out=Vext[:, t, :, 0:128], in_=pbank[bk][:, 0:256].rearrange("p (a c) -> p a c", a=2), func=AF.Copy),
                   r=[("pb", bk)], w=[K("Vext")])
            sc.add("act", I("activation", out=ogt[:, t, :], in_=pbank[bk][:, 256:512], func=AF.Tanh, scale=0.5),
                   r=[("pb", bk)], w=[K("ogt")])
        ho = CST_OFF["hng"][0] + l * 512 + hp * 256
        sc.add("sp", I("dma_start", out=halfg[:], in_=cst_d[:, ho:ho + 256]), w=[K("halfg")], dma_slot="hng")
        sc.add("pool", I("tensor_scalar", out=halfg[:], in0=halfg[:], scalar1=0.5, scalar2=0.0, op0=ALU.mult, op1=ALU.add),
               r=[K("halfg")], w=[K("halfg")])

        def gsl(nm, d_, t):
            return G[nm][:, d_, t, 2 * hp:2 * hp + 2]

        def state_update(d_, t):
            bkT = next_bank()
            for a in range(2):
                sc.add("pe", I("transpose", pbf(bkT)[:, a * P:(a + 1) * P], mqk[:, 2 + a, t * P:(t + 1) * P], ident_bf[:]),
                       r=[K(("mqk", 2 + a)), "ident_bf"], w=[("pb", bkT)])
            sc.add("dve", I("tensor_tensor", out=wK[:], in0=pbf(bkT)[:, 0:256].rearrange("p (a c) -> p a c", a=2),
                                                     in1=gsl("w", d_, t).unsqueeze(2).to_broadcast([P, 2, P]), op=ALU.mult),
                   r=[("pb", bkT), "G_w"], w=[K("wK")])
            bkD = next_bank()
            for a in range(2):
                sc.add("pe", I("matmul", pbank[bkD][:, a * 129:(a + 1) * 129], lhsT=wK[:, a, :], rhs=Vext[:, t, a, :], start=True, stop=True),
                       r=[K("wK"), K("Vext")], w=[("pb", bkD)])
            for a in range(2):
                sc.add("dve", I("scalar_tensor_tensor", out=Crun[d_][:, a, :], in0=Crun[d_][:, a, :], scalar=G["decay"][:, d_, t, 2 * hp + a:2 * hp + a + 1],
                                                                    in1=pbank[bkD][:, a * 129:(a + 1) * 129], op0=ALU.mult, op1=ALU.add),
                       r=[("pb", bkD), "G_decay", K(("Crun", d_))], w=[K(("Crun", d_))])

        sc.barrier()
        for d_ in range(2):
            sc.add("pool", I("memset", Crun[d_][:], 0.0), w=[K(("Crun", d_))])
        for t in range(15, -1, -1):
            sc.add("act", I("activation", out=Cstb[:, t], in_=Crun[1][:], func=AF.Copy), r=[K(("Crun", 1))], w=[K("Cstb")])
            if t > 0:
                state_update(1, t)

        def state_update_f(t):
            for a in range(2):
                sc.add("pe", I("transpose", pbf(5)[:, a * P:(a + 1) * P], mqk[:, 2 + a, t * P:(t + 1) * P], ident_bf[:]),
                       r=[K(("mqk", 2 + a)), "ident_bf"], w=[("pb", 5)])
            sc.add("dve", I("tensor_tensor", out=wK[:], in0=pbf(5)[:, 0:256].rearrange("p (a c) -> p a c", a=2),
                            in1=gsl("w", 0, t).unsqueeze(2).to_broadcast([P, 2, P]), op=ALU.mult),
                   r=[("pb", 5), "G_w"], w=[K("wK")])
            for a in range(2):
                sc.add("pe", I("matmul", pbank[7][:, a * 129:(a + 1) * 129], lhsT=wK[:, a, :], rhs=Vext[:, t, a, :], start=True, stop=True),
                       r=[K("wK"), K("Vext")], w=[("pb", 7)])
            for a in range(2):
                sc.add("dve", I("scalar_tensor_tensor", out=Crun[0][:, a, :], in0=Crun[0][:, a, :], scalar=G["decay"][:, 0, t, 2 * hp + a:2 * hp + a + 1],
                                in1=pbank[7][:, a * 129:(a + 1) * 129], op0=ALU.mult, op1=ALU.add),
                       r=[("pb", 7), "G_decay", K(("Crun", 0))], w=[K(("Crun", 0))])

        def y_transposes(t):
            for a in range(2):
                sc.add("pe", I("transpose", pbf(6)[:, a * P:(a + 1) * P], ybf[:, a, :], ident_bf[:]),
                       r=[K("ybf"), "ident_bf"], w=[("pb", 6)])
            sc.add("act", I("activation", out=yT[:, 2 * hp:2 * hp + 2, t * P:(t + 1) * P], in_=pbf(6)[:, 0:256].rearrange("p (a c) -> p a c", a=2), func=AF.Copy),
                   r=[("pb", 6)], w=[("yT", 2 * hp), ("yT", 2 * hp + 1)])

        def stA(t):
            tl = slice(t * P, (t + 1) * P)
            for a in range(2):
                sc.add("pe", I("matmul", pbank[0][:, a * P:(a + 1) * P], lhsT=mqk[:, 2 + a, tl], rhs=mqk[:, a, tl], start=True, stop=True),
                       r=[K(("mqk", a)), K(("mqk", 2 + a))], w=[("pb", 0)])
            for d_ in range(2):
                Tn, Nn = ("Tf", "NEGf") if d_ == 0 else ("Tb", "NEGb")
                sc.add("pool", I("tensor_tensor", out=Tlf[d_][:], in0=C(Tn).unsqueeze(1).to_broadcast([P, 2, P]),
                                 in1=gsl("lf", d_, t).unsqueeze(2).to_broadcast([P, 2, P]), op=ALU.mult),
                       r=["cst", "G_lf"], w=[K(("Tlf", d_))])
                sc.add("pool", I("tensor_tensor", out=rhs2[d_][:], in0=C(Nn).unsqueeze(1).to_broadcast([P, 2, P]),
                                 in1=gsl("col", d_, t).unsqueeze(2).to_broadcast([P, 2, P]), op=ALU.add),
                       r=["cst", "G_col"], w=[K(("rhs2", d_))])
            for d_ in range(2):
                bk_ = 3 + d_
                Tl2 = Tlf[d_][:].rearrange("p a i -> p (a i)")
                sc.add("pe", I("matmul", pbank[bk_][:, 0:256], lhsT=C("onesF"), rhs=Tl2, start=True, stop=True),
                       r=["cst", K(("Tlf", d_))], w=[("pb", bk_)])
                sc.add("pe", I("matmul", pbank[bk_][:, 256:512], lhsT=C("onesF"), rhs=Tl2, start=True, stop=False),
                       r=["cst", K(("Tlf", d_))], w=[("pb", bk_)])
                sc.add("pe", I("matmul", pbank[bk_][:, 256:512], lhsT=C("identF"), rhs=rhs2[d_][:].rearrange("p a i -> p (a i)"), start=False, stop=True),
                       r=["cst", K(("rhs2", d_))], w=[("pb", bk_)])

        def stB(t):
            tl = slice(t * P, (t + 1) * P)
            for d_ in range(2):
                bk_ = 3 + d_
                sc.add("act", I("activation", out=Dx[d_][:], in_=pbank[bk_][:, 256:512], func=AF.Exp),
                       r=[("pb", bk_)], w=[K(("Dx", d_))])
                sc.add("act", I("activation", out=eb[d_][:].rearrange("p a i -> p (a i)"), in_=pbank[bk_][:, 0:256], func=AF.Exp),
                       r=[("pb", bk_)], w=[K(("eb", d_))])
                sc.add("dve", I("tensor_tensor", out=PT[d_][:].rearrange("p a i -> p (a i)"), in0=pbank[0][:, 0:256], in1=Dx[d_][:], op=ALU.mult),
                       r=[("pb", 0), K(("Dx", d_))], w=[K(("PT", d_))])
                sc.add("dve", I("tensor_tensor", out=qs[d_][:], in0=mqk[:, 0:2, tl], in1=eb[d_][:], op=ALU.mult),
                       r=[K(("mqk", 0)), K(("mqk", 1)), K(("eb", d_))], w=[K(("qs", d_))])

        def stC(t):
            bkO = 1 + (t % 2)
            for d_ in range(2):
                for a in range(2):
                    cprev = CfR[t % 3][:, a, :] if d_ == 0 else Cstb[:, t, a, :]
                    ck = K(("Cf", t % 3)) if d_ == 0 else K("Cstb")
                    oc = (a * 2 + d_) * P
                    sc.add("pe", I("matmul", pbank[bkO][:, oc:oc + P], lhsT=PT[d_][:, a, :], rhs=Vext[:, t, a, 0:128], start=True, stop=False),
                           r=[K(("PT", d_)), K("Vext")], w=[("pb", bkO)])
                    sc.add("pe", I("matmul", pbank[bkO][:, oc:oc + P], lhsT=qs[d_][:, a, :], rhs=cprev[:, 0:128], start=False, stop=True),
                           r=[K(("qs", d_)), ck], w=[("pb", bkO)])
            for d_ in range(2):
                for a in range(2):
                    cprev = CfR[t % 3][:, a, :] if d_ == 0 else Cstb[:, t, a, :]
                    ck = K(("Cf", t % 3)) if d_ == 0 else K("Cstb")
                    dc = 258 + 4 * (t % 2) + a * 2 + d_
                    sc.add("pe", I("matmul", pbank[7][:, dc:dc + 1], lhsT=PT[d_][:, a, :], rhs=Vext[:, t, a, 128:129], start=True, stop=False),
                           r=[K(("PT", d_)), K("Vext")], w=[("pb", 7)])
                    sc.add("pe", I("matmul", pbank[7][:, dc:dc + 1], lhsT=qs[d_][:, a, :], rhs=cprev[:, 128:129], start=False, stop=True),
                           r=[K(("qs", d_)), ck], w=[("pb", 7)])

        def stD(t):
            bkO = 1 + (t % 2)
            rk_ = K("rr")
            den = pbank[7][:, 258 + 4 * (t % 2):262 + 4 * (t % 2)]
            sc.add("dve", I("tensor_scalar", out=rr[:, 4:8], in0=den, scalar1=-1.0, scalar2=0.0, op0=ALU.mult, op1=ALU.add), r=[("pb", 7)], w=[rk_])
            sc.add("dve", I("tensor_tensor", out=rr[:, 0:4], in0=den, in1=rr[:, 4:8], op=ALU.max), r=[("pb", 7), rk_], w=[rk_])
            sc.add("dve", I("tensor_scalar", out=rr[:, 0:4], in0=rr[:, 0:4], scalar1=1.0, scalar2=0.0, op0=ALU.max, op1=ALU.add), r=[rk_], w=[rk_])
            sc.add("dve", I("reciprocal", out=rr[:, 0:4], in_=rr[:, 0:4]), r=[rk_], w=[rk_])
            sc.add("dve", I("scalar_tensor_tensor", out=gog[:], in0=ogt[:, t, :], scalar=1.0, in1=halfg[:], op0=ALU.add, op1=ALU.mult),
                   r=[K("ogt"), K("halfg")], w=[K("gog")])
            for a in range(2):
                oc = a * 2 * P
                sc.add("dve", I("tensor_scalar", out=hsum[:, a, :], in0=pbank[bkO][:, oc:oc + P], scalar1=rr[:, 2 * a:2 * a + 1], scalar2=0.0, op0=ALU.mult, op1=ALU.add),
                       r=[("pb", bkO), rk_], w=[K(("hsum", a))])
                sc.add("dve", I("scalar_tensor_tensor", out=hsum[:, a, :], in0=pbank[bkO][:, oc + P:oc + 2 * P], scalar=rr[:, 2 * a + 1:2 * a + 2], in1=hsum[:, a, :], op0=ALU.mult, op1=ALU.add),
                       r=[("pb", bkO), rk_, K(("hsum", a))], w=[K(("hsum", a))])
            for a in range(2):
                sc.add("act", I("activation", out=sq[:, a, :], in_=hsum[:, a, :], func=AF.Square, accum_out=ssq[:, a:a + 1]),
                       r=[K(("hsum", a))], w=[K(("sq", a)), K(("ssq", a))])
            hk = [K(("hsum", 0)), K(("hsum", 1))]
            sk = [K(("ssq", 0)), K(("ssq", 1))]
            sc.add("act", I("activation", out=ssq[:], in_=ssq[:], func=AF.Ln, bias=C("epsc", 0, 1), scale=1.0 / 128.0), r=sk + ["cst"], w=sk)
            sc.add("act", I("activation", out=ssq[:], in_=ssq[:], func=AF.Exp, scale=-0.5), r=sk, w=sk)
            sc.add("dve", I("tensor_tensor", out=sq[:], in0=hsum[:], in1=ssq[:].unsqueeze(2).to_broadcast([P, 2, P]), op=ALU.mult),
                   r=hk + sk + [K(("sq", 0)), K(("sq", 1))], w=[K(("sq", 0)), K(("sq", 1))])
            sc.add("pool", I("tensor_tensor", out=ybf[:].rearrange("p a c -> p (a c)"), in0=sq[:].rearrange("p a c -> p (a c)"), in1=gog[:], op=ALU.mult),
                   r=[K(("sq", 0)), K(("sq", 1)), K("gog")], w=[K("ybf")])

        def stE(t):
            state_update_f(t)
            sc.add("act", I("activation", out=CfR[(t + 1) % 3][:], in_=Crun[0][:], func=AF.Copy), r=[K(("Crun", 0))], w=[K(("Cf", (t + 1) % 3))])

        sc.add("pool", I("memset", CfR[0][:], 0.0), w=[K(("Cf", 0))])
        stE(0)
        stE(1)
        stA(0)
        stB(0)
        for t in range(16):
            stC(t)
            if t + 2 <= 14:
                stE(t + 2)
            if t > 1:
                y_transposes(t - 2)
            if t > 0:
                stD(t - 1)
            if t < 15:
                stA(t + 1)
                stB(t + 1)
        y_transposes(14)
        stD(15)
        y_transposes(15)
        sc.barrier()

    def attn_pair(l, hp):
        cv = Carver(RBASE)
        aq = cv.bf(S)
        ak = cv.bf(S)
        Vp = [cv.bf(16 * P).rearrange("p (t c) -> p t c", t=16) for _ in range(3)]
        accn = cv.f32(S)
        accd = cv.f32(S)
        qP = cv.bf(S)
        kP = cv.bf(S)
        PtP = [cv.bf(512).rearrange("p (a c) -> p a c", a=2) for _ in range(3)]
        ropeb = [cv.f32(1024).rearrange("p (k t) -> p k t", k=2) for _ in range(2)]
        t1 = [cv.f32(512)] * 2
        t2 = [cv.f32(512)] * 2
        K = lambda nm: ("a", nm)
        DILS = (1, 4, 16)
        scnt = {"n": 0}
        wsm_, wkv = load_wsmall(l * NSMALL + 1 + hp, 1024)
        wvv = wsm_[:, 0:1024].rearrange("p (k c) -> p k c", k=8)
        w, wk = load_w(l * NFULL + 4 + hp)
        wv = w[:].rearrange("p (n k c) -> p n k c", n=4, k=8)
        for qi, dst in ((0, aq), (1, ak)):
            for blk in range(4):
                lo = blk * 512
                rb = ropeb[blk % 2]
                sc.add("sp", I("dma_start", out=rb[:], in_=rope_d[:, :, lo:lo + 512].rearrange("k p t -> p k t")), w=[K(("rope", blk % 2))], dma_slot=("rope", blk % 2))
                bk0 = next_bank()
                proj_B(wv[:, 2 * qi], wk, blk, bk0)
                bk1 = next_bank()
                proj_B(wv[:, 2 * qi + 1], wk, blk, bk1)
                sc.add("dve", I("tensor_tensor", out=t1[blk % 2][:], in0=pbank[bk0][:], in1=rb[:, 0, :], op=ALU.mult),
                       r=[("pb", bk0), K(("rope", blk % 2))], w=[K(("t1", 0))])
                sc.add("dve", I("tensor_tensor", out=t2[blk % 2][:], in0=pbank[bk1][:], in1=rb[:, 1, :], op=ALU.mult),
                       r=[("pb", bk1), K(("rope", blk % 2))], w=[K(("t2", 0))])
                sc.add("pool", I("tensor_tensor", out=dst[:, lo:lo + 512], in0=t1[blk % 2][:], in1=t2[blk % 2][:], op=ALU.add),
                       r=[K(("t1", 0)), K(("t2", 0))], w=[K(("qk", qi))])
        VT = qP
        if not cfg.get("vt", True):
            for di, d_ in enumerate(DILS):
                nb = 16 // d_
                for g in range(4):
                    bk = next_bank()
                    for tt in range(4):
                        tp = 4 * g + tt
                        r_, lb = tp // nb, tp % nb
                        st = d_ * P * lb + r_

                        def tok(kc, st=st, d_=d_):
                            return hT[:, kc, st:st + d_ * (P - 1) + 1:d_]
                        proj_A(wvv, wkv, P, tok, pbank[bk][:, tt * P:(tt + 1) * P], bk)
                    sc.add("act", I("activation", out=Vp[di][:, 4 * g:4 * g + 4, :], in_=pbank[bk][:].rearrange("p (t c) -> p t c", t=4), func=AF.Copy),
                           r=[("pb", bk)], w=[K(("V", di))])
        for blk in (range(4) if cfg.get("vt", True) else []):
            bk = next_bank()
            lo = blk * 512
            for kc in range(8):
                sc.add("pe", I("matmul", pbank[bk][:], lhsT=wvv[:, kc, :], rhs=hT[:, kc, lo:lo + 512], start=(kc == 0), stop=(kc == 7)),
                       r=[wkv, ("hT", kc)], w=[("pb", bk)])
            sc.add("act", I("activation", out=VT[:, lo:lo + 512], in_=pbank[bk][:], func=AF.Copy), r=[("pb", bk)], w=[K(("qP", blk))])
        for di, d_ in (enumerate(DILS) if cfg.get("vt", True) else []):
            nb = 16 // d_
            if d_ == 1:
                src, skeys = VT, [K(("qP", c_)) for c_ in range(4)]
            else:
                if d_ == 4:
                    sc.add("act", I("activation", out=kP[:].rearrange("p (r l) -> p r l", r=d_), in_=VT[:].rearrange("p (l r) -> p r l", r=d_), func=AF.Copy),
                           r=[K(("qP", c_)) for c_ in range(4)], w=[K(("kP", c_)) for c_ in range(4)])
                else:
                    sc.add("dve", I("tensor_copy", out=kP[:].rearrange("p (r l) -> p r l", r=d_), in_=VT[:].rearrange("p (l r) -> p r l", r=d_)),
                           r=[K(("qP", c_)) for c_ in range(4)], w=[K(("kP", c_)) for c_ in range(4)])
                src, skeys = kP, [K(("kP", c_)) for c_ in range(4)]
            for g in range(4):
                bk = next_bank()
                for tt in range(4):
                    tp = 4 * g + tt
                    sc.add("pe", I("transpose", pbf(bk)[:, tt * P:(tt + 1) * P], src[:, tp * P:(tp + 1) * P], ident_bf[:]),
                           r=skeys + ["ident_bf"], w=[("pb", bk)])
                if g % 2 == 0:
                    sc.add("act", I("activation", out=Vp[di][:, 4 * g:4 * g + 4, :], in_=pbf(bk)[:, 0:512].rearrange("p (t c) -> p t c", t=4), func=AF.Copy),
                           r=[("pb", bk)], w=[K(("V", di))])
                else:
                    sc.add("dve", I("tensor_copy", out=Vp[di][:, 4 * g:4 * g + 4, :], in_=pbf(bk)[:, 0:512].rearrange("p (t c) -> p t c", t=4)),
                           r=[("pb", bk)], w=[K(("V", di))])

        def perm_copy(d_, c):
            if d_ == 4:
                qi_ = aq[:].rearrange("p (l r) -> p r l", r=4)[:, c, :]
                ki_ = ak[:].rearrange("p (l r) -> p r l", r=4)[:, c, :]
                qo_, ko_ = qP[:, c * 512:(c + 1) * 512], kP[:, c * 512:(c + 1) * 512]
            else:
                qi_ = aq[:].rearrange("p (l r) -> p r l", r=16)[:, 4 * c:4 * c + 4, :]
                ki_ = ak[:].rearrange("p (l r) -> p r l", r=16)[:, 4 * c:4 * c + 4, :]
                qo_ = qP[:, c * 512:(c + 1) * 512].rearrange("p (r l) -> p r l", r=4)
                ko_ = kP[:, c * 512:(c + 1) * 512].rearrange("p (r l) -> p r l", r=4)
            sc.add("pool", I("tensor_copy", out=qo_, in_=qi_), r=[K(("qk", 0))], w=[K(("qP", c))])
            sc.add("dve", I("tensor_copy", out=ko_, in_=ki_), r=[K(("qk", 1))], w=[K(("kP", c))])

        kts = []
        blk_last = {}
        for di, d_ in enumerate(DILS):
            nb = 16 // d_
            Ld = nb * P
            for r_ in range(d_):
                for kt in range(nb):
                    lo, hi = max(0, kt * P - 64), min(Ld, kt * P + 192)
                    u = dict(di=di, d=d_, T=r_ * nb + kt, c_lo=r_ * Ld + lo, c_hi=r_ * Ld + hi, m0=lo - (kt * P - 64))
                    for b in range(u["c_lo"] // 512, (u["c_hi"] - 1) // 512 + 1):
                        blk_last[(di, b)] = len(kts)
                    kts.append(u)
        started = set()

        def views(u):
            if u["d"] == 1:
                return aq, ak, [K(("qk", 0))], [K(("qk", 1))]
            cq = range(u["c_lo"] // 512, (u["c_hi"] - 1) // 512 + 1)
            return qP, kP, [K(("qP", c_)) for c_ in cq], [K(("kP", u["T"] // 4))]

        def stageAB(j, u):
            d_ = u["d"]
            if cfg.get("early", True):
                if d_ == 1 and u["T"] in (2, 5, 8, 11):
                    perm_copy(4, (u["T"] - 2) // 3)
                if d_ == 4 and u["T"] % 4 == 0 and u["T"] > 0:
                    perm_copy(16, u["T"] // 4 - 1)
                if d_ == 16 and u["T"] == 0:
                    perm_copy(16, 3)
            elif d_ > 1 and u["T"] == 0:
                for c_ in range(4):
                    perm_copy(d_, c_)
            qv, kv, rq, rk = views(u)
            T, nc_ = u["T"], u["c_hi"] - u["c_lo"]
            pt = PtP[j % 3]
            for a in range(2):
                rows = slice(64 * a, 64 * a + 64)
                bkS = 4 + ((2 * j + a) % 4)
                sc.add("pe", I("matmul", pbank[bkS][:, 0:nc_], lhsT=kv[rows, T * P:(T + 1) * P], rhs=qv[rows, u["c_lo"]:u["c_hi"]], start=True, stop=True),
                       r=rq + rk, w=[("pb", bkS)])
            for a in range(2):
                bkS = 4 + ((2 * j + a) % 4)
                sc.add("act", I("activation", out=pt[:, a, 0:nc_], in_=pbank[bkS][:, 0:nc_], func=AF.Exp, scale=0.125),
                       r=[("pb", bkS)], w=[K(("Pt", j % 3, a))])
            meng = "pool" if (j % 3 == 0 and nc_ == 256 and cfg.get("poolmask", True)) else "dve"
            sc.add(meng, I("tensor_tensor", out=pt[:, :, 0:nc_], in0=pt[:, :, 0:nc_], in1=mask2[:, :, u["m0"]:u["m0"] + nc_], op=ALU.mult),
                   r=[K(("Pt", j % 3, 0)), K(("Pt", j % 3, 1)), "mask2"], w=[K(("Pt", j % 3, 0)), K(("Pt", j % 3, 1))])

        def stageC(j, u):
            di, d_, T = u["di"], u["d"], u["T"]
            pt = PtP[j % 3]
            blks = list(range(u["c_lo"] // 512, (u["c_hi"] - 1) // 512 + 1))
            for b in blks:
                s_lo, s_hi = max(u["c_lo"], 512 * b), min(u["c_hi"], 512 * (b + 1))
                gi_ = di * 4 + b
                bkN, bkD = (0, 1) if gi_ % 2 == 0 else (2, 3)
                for tag in ("N", "D"):
                    for a in range(2):
                        rows = slice(64 * a, 64 * a + 64)
                        mv = pt[:, a, s_lo - u["c_lo"]:s_hi - u["c_lo"]]
                        if tag == "N":
                            bk_, lhs, rkeys = bkN, Vp[di][:, T, rows], [K(("V", di))]
                        else:
                            bk_, lhs, rkeys = bkD, ones1_bf[:, 0:64], ["ones1_bf"]
                        first = (gi_, a, tag) not in started
                        started.add((gi_, a, tag))
                        sc.add("pe", I("matmul", pbank[bk_][rows, s_lo - 512 * b:s_hi - 512 * b], lhsT=lhs, rhs=mv, start=first, stop=True, skip_group_check=True),
                               r=rkeys + [K(("Pt", j % 3, a))], w=[("pb", bk_)])
            for g in blks:
                if blk_last[(di, g)] != j:
                    continue
                gi_ = di * 4 + g
                bkN, bkD = (0, 1) if gi_ % 2 == 0 else (2, 3)
                if d_ == 1:
                    sc.add("act", I("activation", out=accn[:, g * 512:(g + 1) * 512], in_=pbank[bkN][:], func=AF.Copy), r=[("pb", bkN)], w=[K("accn")])
                    sc.add("dve", I("tensor_copy", out=accd[:, g * 512:(g + 1) * 512], in_=pbank[bkD][:]), r=[("pb", bkD)], w=[K("accd")])
                else:
                    if d_ == 4:
                        vn = accn[:].rearrange("p (l r) -> p r l", r=4)[:, g, :]
                        vd = accd[:].rearrange("p (l r) -> p r l", r=4)[:, g, :]
                        pn, pd = pbank[bkN][:], pbank[bkD][:]
                    else:
                        vn = accn[:].rearrange("p (l r) -> p r l", r=16)[:, 4 * g:4 * g + 4, :]
                        vd = accd[:].rearrange("p (l r) -> p r l", r=16)[:, 4 * g:4 * g + 4, :]
                        pn = pbank[bkN][:].rearrange("p (r l) -> p r l", r=4)
                        pd = pbank[bkD][:].rearrange("p (r l) -> p r l", r=4)
                    sc.add("dve", I("tensor_tensor", out=vn, in0=vn, in1=pn, op=ALU.add), r=[("pb", bkN), K("accn")], w=[K("accn")])
                    sc.add("dve", I("tensor_tensor", out=vd, in0=vd, in1=pd, op=ALU.add), r=[("pb", bkD), K("accd")], w=[K("accd")])

        for j in range(len(kts) + 1):
            if j < len(kts):
                stageAB(j, kts[j])
            if j - 1 >= 0:
                stageC(j - 1, kts[j - 1])
        sc.add("dve", I("reciprocal", out=accd[:], in_=accd[:]), r=[K("accd")], w=[K("accd")])
        sc.add("dve", I("tensor_tensor", out=yT[:, hp, :], in0=accn[:], in1=accd[:], op=ALU.mult), r=[K("accn"), K("accd")], w=[("yT", hp)])
        sc.barrier()

    def out_proj(l, hf):
        w, wk = load_w(l * NFULL + 8 + hf)
        wv = w[:].rearrange("p (m k c) -> p m k c", m=8, k=4)
        for m in range(8):
            for blk in range(4):
                lo = blk * 512
                bk = next_bank()
                for kc in range(4):
                    sc.add("pe", I("matmul", pbank[bk][:], lhsT=wv[:, m, kc, :], rhs=yT[:, kc, lo:lo + 512], start=(kc == 0), stop=(kc == 3)),
                           r=[wk, ("yT", kc)], w=[("pb", bk)])
                sc.add("dve", I("tensor_tensor", out=xT[:, m, lo:lo + 512], in0=xT[:, m, lo:lo + 512], in1=pbank[bk][:], op=ALU.add),
                       r=[("pb", bk), ("xT", m)], w=[("xT", m)])

    for l in range(nlay):
        if cfg["mlstm"] or cfg["attn"]:
            rmsnorm("g1", l * 8, 0, S, lambda c, lo, hi: hT[:, c, lo:hi], lambda c: [("hT", c)])
        if cfg["mlstm"]:
            gates(l)
            for hp in range(2):
                mlstm_pair(l, hp)
            out_proj(l, 0)
            sc.barrier()
        if cfg["attn"]:
            for hp in range(4):
                attn_pair(l, hp)
            out_proj(l, 1)
            sc.barrier()
        if cfg["ffn"]:
            ffn(l)
            sc.barrier()

    ov = out_d.rearrange("(c p) t -> p c t", p=P)
    sc.barrier()
    finT = U[:, 0:2048].bitcast(F32).rearrange("p (a t) -> p a t", a=2)
    fcnt = {"n": 0}

    def fin_dst(c, lo, hi):
        return finT[:, c % 2, :]

    for b in range(4):
        lo, hi = b * 512, (b + 1) * 512
        bk = next_bank()
        for c in range(8):
            q = sqb[c % 2]
            sc.add("act", I("activation", out=q[:], in_=xT[:, c, lo:hi], func=AF.Square),
                   r=[("xT", c)], w=[("sqb", c % 2)])
            sc.add("pe", I("matmul", pbank[bk][:], lhsT=ones_bf[:], rhs=q[:], start=(c == 0), stop=(c == 7)),
                   r=[("sqb", c % 2), "ones_bf"], w=[("pb", bk)])
        r_ = rst[b % 2]
        sc.add("act", I("activation", out=r_[:], in_=pbank[bk][:], func=AF.Ln, bias=C("epsc", 0, 1), scale=1.0),
               r=[("pb", bk), "cst"], w=[("rst", 0)])
        sc.add("act", I("activation", out=r_[:], in_=r_[:], func=AF.Exp, scale=-0.5),
               r=[("rst", 0)], w=[("rst", 0)])
        for c in range(8):
            sl = c % 2
            sc.add("dve", I("scalar_tensor_tensor",
                out=finT[:, sl, :], in0=xT[:, c, lo:hi], scalar=C("gf", c, 1), in1=r_[:], op0=ALU.mult, op1=ALU.mult),
                r=[("xT", c), ("rst", 0), "cst"], w=[("fin", sl)])
            sc.add("sp", I("dma_start", out=ov[:, c, lo:hi], in_=finT[:, sl, :]),
                   r=[("fin", sl)], w=[("out", sl)], dma_slot=("out", sl))
    sc.add("sp", None, r=[("out", 0), ("out", 1)])

    sc.emit(nc, es)
    es.close()
    return nc


_PREP_CACHE = {}


def kernel(x, norm1_g, w_in, conv_w, gate_i_b, gate_f_b, head_norm_g, w_out, norm2_g, w_up, w_down, final_g, _cfg=None):
    cfg = dict(CFG)
    if _cfg:
        cfg.update(_cfg)
    inp = dict(x=x, norm1_g=norm1_g, w_in=w_in, conv_w=conv_w, gate_i_b=gate_i_b, gate_f_b=gate_f_b,
               head_norm_g=head_norm_g, w_out=w_out, norm2_g=norm2_g, w_up=w_up, w_down=w_down, final_g=final_g)
    inp = {k: np.asarray(v) for k, v in inp.items()}
    wfull, wsmall = prep_weights(inp)
    cst = prep_consts(inp)
    rope = prep_rope()
    nc = build(cfg)
    xs = np.asarray(inp["x"], np.float32)
    in_maps = []
    for b in range(8):
        in_maps.append({"xT": np.ascontiguousarray(xs[b].T), "wfull": wfull, "wsmall": wsmall, "cst": cst, "rope": rope})
    res = run_bass_kernel_spmd(nc, in_maps, core_ids=list(range(8)))
    out = np.stack([np.ascontiguousarray(r["outT"].T) for r in res.results], axis=0)
    return out.astype(np.float32)
```

```python
import math
from contextlib import ExitStack

import numpy as np
import concourse.bass as bass
import concourse.mybir as mybir
from concourse.bass_utils import run_bass_kernel_spmd

F32 = mybir.dt.float32
BF16 = mybir.dt.bfloat16
ALU = mybir.AluOpType
AF = mybir.ActivationFunctionType
AX = mybir.AxisListType

P = 128
S = 2048
D = 1024
DFF = 4096
NL = 2
EPS = 1e-6
NFULL = 26
NSMALL = 5
TB = 1024

CFG = {"mlstm": True, "attn": True, "ffn": True, "layers": NL, "debug": False}


def I(name, *args, **kw):
    return (name, args, kw)


class Op:
    __slots__ = ("eng", "fn", "deps", "dma", "slot", "tok", "signal", "waits", "idx", "semkey", "semval")


class WK(tuple):
    def __new__(cls, slot):
        return tuple.__new__(cls, (("wb", slot, 0), ("wb", slot, 1)))


def _flat(keys):
    out = []
    for k in keys:
        if isinstance(k, WK):
            out.extend(k)
        else:
            out.append(k)
    return out


class Sched:
    EPOCH = 30000

    def __init__(self):
        self.ops = []
        self.last_w = {}
        self.readers = {}
        self.stream_len = {"pe": 0, "dve": 0, "act": 0, "pool": 0, "sp": 0}
        self.slot_cnt = {}

    def add(self, eng, fn, r=(), w=(), dma_slot=None):
        op = Op()
        op.eng, op.fn, op.dma, op.slot = eng, fn, dma_slot is not None, dma_slot
        op.signal = False
        op.waits = []
        op.idx = len(self.ops)
        r = _flat(r)
        w = _flat(w)
        deps = set()
        for k in r:
            if k in self.last_w:
                deps.add(self.last_w[k])
        for k in w:
            if k in self.last_w:
                deps.add(self.last_w[k])
            for j in self.readers.get(k, ()):
                deps.add(j)
        op.deps = deps
        if op.dma:
            c = self.slot_cnt.get(dma_slot, 0) + 1
            self.slot_cnt[dma_slot] = c
            op.tok = (("dma", dma_slot), c)
            self.stream_len[eng] += 1
        else:
            self.stream_len[eng] += 1
            op.tok = (("eng", eng), self.stream_len[eng])
        for k in r:
            self.readers.setdefault(k, []).append(op.idx)
        for k in w:
            self.last_w[k] = op.idx
            self.readers[k] = []
        self.ops.append(op)
        return op

    def barrier(self):
        last = {}
        for op in self.ops:
            if op.fn is not None:
                last[op.eng] = op.idx
        for e in list(self.stream_len):
            op = Op()
            op.eng, op.fn, op.dma, op.slot = e, None, False, None
            op.signal = False
            op.waits = []
            op.idx = len(self.ops)
            op.deps = set(v for k, v in last.items() if k != e)
            self.stream_len[e] += 1
            op.tok = (("eng", e), self.stream_len[e])
            self.ops.append(op)

    def analyse(self):
        seen = {e: {} for e in self.stream_len}
        for op in self.ops:
            sn = seen[op.eng]
            for j in sorted(op.deps):
                d = self.ops[j]
                if (not d.dma) and d.eng == "pe" and op.eng == "pe" and not op.dma:
                    continue
                fam, v = d.tok
                if sn.get(fam, 0) >= v:
                    continue
                sn[fam] = v
                d.signal = True
                op.waits.append(j)
        rank = {e: 0 for e in self.stream_len}
        self.nepoch = {e: 1 for e in self.stream_len}
        for op in self.ops:
            if op.dma:
                op.semkey = op.tok[0]
                op.semval = 16 * op.tok[1]
            elif op.signal:
                r = rank[op.eng]
                rank[op.eng] = r + 1
                ep = r // self.EPOCH
                self.nepoch[op.eng] = max(self.nepoch[op.eng], ep + 1)
                op.semkey = ("eng", op.eng, ep)
                op.semval = r - ep * self.EPOCH + 1

    def emit(self, nc, es):
        self.analyse()
        sems = {}

        def getsem(key):
            if key not in sems:
                nm = "s_" + "_".join(str(x) for x in key).replace("(", "").replace(")", "").replace(",", "_").replace(" ", "").replace("'", "")
                sems[key] = es.enter_context(nc.semaphore(nm))
            return sems[key]

        for op in self.ops:
            if op.dma or op.signal:
                getsem(op.semkey)
        block = es.enter_context(nc.Block())
        by_eng = {e: [o for o in self.ops if o.eng == e] for e in self.stream_len}

        def body(eng_name):
            def f(eng):
                for op in by_eng[eng_name]:
                    for j in op.waits:
                        d = self.ops[j]
                        eng.wait_ge(sems[d.semkey], d.semval)
                    if op.fn is None:
                        continue
                    name, args, kw = op.fn
                    ins = getattr(eng, name)(*args, **kw)
                    if op.dma:
                        ins.then_inc(sems[op.semkey], 16)
                    elif op.signal:
                        ins.then_inc(sems[op.semkey], 1)
            return f

        block.tensor(body("pe"))
        block.vector(body("dve"))
        block.scalar(body("act"))
        block.gpsimd(body("pool"))
        block.sync(body("sp"))


def _bform(w, cols):
    out = np.empty((P, 4, 8, P), np.float32)
    for n, ci in enumerate(cols):
        blk = w[:, ci]
        out[:, n] = blk.reshape(8, P, P).transpose(1, 0, 2)
    return out.reshape(P, 4096)


def _aform(w, ci):
    blk = w[:, ci]
    C = blk.shape[1]
    return blk.reshape(8, P, C).transpose(1, 0, 2).reshape(P, 8 * C)


def prep_weights(inp):
    wfull = np.zeros((NL * NFULL, P, 4096), np.float32)
    wsmall = np.zeros((NL * NSMALL, P, 1024), np.float32)
    ar = np.arange(P)
    for l in range(NL):
        w_in = np.asarray(inp["w_in"][l], np.float32)
        w_out = np.asarray(inp["w_out"][l], np.float32)
        w_up = np.asarray(inp["w_up"][l], np.float32)
        w_dn = np.asarray(inp["w_down"][l], np.float32)
        fb = l * NFULL
        sb = l * NSMALL
        for hp in range(2):
            h0, h1 = 2 * hp, 2 * hp + 1
            wfull[fb + 2 * hp] = _bform(w_in, [h0 * P + ar, h1 * P + ar, 512 + h0 * P + ar, 512 + h1 * P + ar])
            ci = np.concatenate([1024 + h0 * P + ar, 1024 + h1 * P + ar, 1536 + h0 * P + ar, 1536 + h1 * P + ar])
            wfull[fb + 2 * hp + 1] = _aform(w_in, ci)
        a = ar // 64
        dd = ar % 64
        sw = a * 64 + (dd + 32) % 64
        for hp in range(4):
            qb, kb = 2064 + hp * P, 2576 + hp * P
            wfull[fb + 4 + hp] = _bform(w_in, [qb + ar, qb + sw, kb + ar, kb + sw])
            wsmall[sb + 1 + hp] = _aform(w_in, 3088 + hp * P + ar)
        wsmall[sb + 0, :, :128] = _aform(w_in, 2048 + np.arange(16))
        for hf in range(2):
            blk = w_out[hf * 512:(hf + 1) * 512, :]
            t = blk.reshape(4, P, 8, P).transpose(1, 2, 0, 3)
            wfull[fb + 8 + hf] = t.reshape(P, 4096)
        for g in range(8):
            wfull[fb + 10 + g] = _bform(w_up, [(4 * g + n) * P + ar for n in range(4)])
        for m in range(8):
            blk = w_dn[:, m * P:(m + 1) * P]
            wfull[fb + 18 + m] = blk.reshape(32, P, P).transpose(1, 0, 2).reshape(P, 4096)
    return wfull, wsmall


def _cst_layout():
    off = {}
    o = 0

    def put(name, n):
        nonlocal o
        off[name] = (o, n)
        o += n
    put("g1", NL * 8)
    put("g2", NL * 8)
    put("gf", 8)
    put("convw", NL * 40)
    put("gb", NL * 16)
    for nm in ("Tf", "Tb", "NEGf", "NEGb", "identF", "onesF", "sel127", "sel0"):
        put(nm, P)
    put("epsc", 4)
    off["_NCS"] = (o, 0)
    put("hng", NL * 512)
    put("band", 512)
    return off, o


CST_OFF, NCST = _cst_layout()
NCS = CST_OFF["_NCS"][0]


def prep_consts(inp):
    c = np.zeros((P, NCST), np.float32)

    def setc(name, arr):
        o, n = CST_OFF[name]
        c[:, o:o + n] = arr.reshape(P, n)
    g1 = np.asarray(inp["norm1_g"], np.float32).reshape(NL, 8, P).transpose(2, 0, 1)
    g2 = np.asarray(inp["norm2_g"], np.float32).reshape(NL, 8, P).transpose(2, 0, 1)
    gf = np.asarray(inp["final_g"], np.float32).reshape(8, P).transpose(1, 0)
    setc("g1", np.ascontiguousarray(g1))
    setc("g2", np.ascontiguousarray(g2))
    setc("gf", np.ascontiguousarray(gf))
    cw = np.asarray(inp["conv_w"], np.float32).reshape(NL, 5, 8, P).transpose(3, 0, 1, 2)
    setc("convw", np.ascontiguousarray(cw))
    gb = np.concatenate([np.asarray(inp["gate_i_b"], np.float32), np.asarray(inp["gate_f_b"], np.float32)], axis=1)
    setc("gb", np.ascontiguousarray(np.broadcast_to(gb.reshape(1, NL * 16), (P, NL * 16))))
    hng = np.asarray(inp["head_norm_g"], np.float32).reshape(1, NL * 512)
    setc("hng", np.ascontiguousarray(np.broadcast_to(hng, (P, NL * 512))))
    k = np.arange(P)[:, None]
    i = np.arange(P)[None, :]
    NEG = -30000.0
    setc("Tf", (k <= i).astype(np.float32))
    setc("Tb", (k >= i).astype(np.float32))
    setc("NEGf", np.where(k <= i, 0.0, NEG).astype(np.float32))
    setc("NEGb", np.where(k >= i, 0.0, NEG).astype(np.float32))
    setc("identF", np.eye(P, dtype=np.float32))
    setc("onesF", np.ones((P, P), np.float32))
    s127 = np.zeros((P, P), np.float32)
    s127[127, :] = 1.0
    s0 = np.zeros((P, P), np.float32)
    s0[0, :] = 1.0
    setc("sel127", s127)
    setc("sel0", s0)
    cc = np.arange(256)[None, :]
    m1 = ((cc >= k) & (cc <= k + 128)).astype(np.float32)
    band = np.concatenate([m1, m1], axis=1)
    setc("band", band)
    ec = np.zeros((P, 4), np.float32)
    ec[:, 0] = EPS
    ec[:, 1] = 1.0
    ec[:, 2] = -0.5 * math.log(128.0)
    setc("epsc", ec)
    return c


def prep_rope():
    half = 32
    inv = (10000.0 ** (-np.arange(half, dtype=np.float32) / half)).astype(np.float32)
    ang = np.arange(S, dtype=np.float32)[:, None] * inv[None, :]
    cos = np.cos(ang).astype(np.float32).T
    sin = np.sin(ang).astype(np.float32).T
    cos64 = np.concatenate([cos, cos], 0)
    sin64 = np.concatenate([-sin, sin], 0)
    r = np.zeros((2, P, S), np.float32)
    r[0] = np.concatenate([cos64, cos64], 0)
    r[1] = np.concatenate([sin64, sin64], 0)
    return r


def build(cfg=CFG):
    nc = bass.Bass("TRN2", target_bir_lowering=False)
    nlay = cfg["layers"]
    xT_d = nc.dram_tensor("xT", [D, S], F32, kind="ExternalInput").ap()
    wf_d = nc.dram_tensor("wfull", [NL * NFULL, P, 4096], F32, kind="ExternalInput").ap()
    ws_d = nc.dram_tensor("wsmall", [NL * NSMALL, P, 1024], F32, kind="ExternalInput").ap()
    cst_d = nc.dram_tensor("cst", [P, NCST], F32, kind="ExternalInput").ap()
    rope_d = nc.dram_tensor("rope", [2, P, S], F32, kind="ExternalInput").ap()
    out_d = nc.dram_tensor("outT", [D, S], F32, kind="ExternalOutput").ap()

    es = ExitStack()
    sc = Sched()

    def sb(name, shape, dt=F32):
        return es.enter_context(nc.sbuf_tensor(name, shape, dt))

    def ps(name, shape, dt=F32):
        return es.enter_context(nc.psum_tensor(name, shape, dt))

    xT = sb("xT_sb", [P, 8, S])
    cst = sb("cst_sb", [P, NCS])
    NWB = 2
    wb = [sb(f"wb{i}", [P, 4096], BF16) for i in range(NWB)]
    wsb = sb("wsb", [P, 1024], BF16)
    NU = 54944
    U = sb("U", [P, NU], BF16)
    hT = U[:, 0:16384].rearrange("p (c t) -> p c t", c=8)
    yT = U[:, 16384:24576].rearrange("p (c t) -> p c t", c=4)
    RBASE = 24576

    class Carver:
        def __init__(self, base):
            self.o = base

        def bf(self, n):
            a = U[:, self.o:self.o + n]
            self.o += n + (n % 2)
            assert self.o <= NU
            return a

        def f32(self, n):
            a = U[:, self.o:self.o + 2 * n].bitcast(F32)
            self.o += 2 * n
            assert self.o <= NU
            return a

    sqb = [sb(f"sqb{i}", [P, 512], BF16) for i in range(2)]
    rst = [sb("rst0", [P, 512])] * 2
    ones_bf = sb("ones_bf", [P, P], BF16)
    ones1_bf = sb("ones1_bf", [P, 64], BF16)
    ident_bf = sb("ident_bf", [P, P], BF16)
    mask2 = sb("mask2", [P, 2, 256], BF16)
    diag = [sb(f"diag{i}", [P, 5, P], BF16) for i in range(2)]
    G = {nm: sb("G_" + nm, [P, 2, 16, 4]) for nm in ("gi", "lf", "bcol", "col", "blast", "w", "decay")}
    graw = sb("graw", [P, 16, 16])
    pbank = [ps(f"pb{i}", [P, 512]) for i in range(8)]

    def C(name, a=0, n=None):
        o, ln = CST_OFF[name]
        n = ln - a if n is None else n
        return cst[:, o + a:o + a + n]

    sc.add("sp", I("dma_start", out=cst[:], in_=cst_d[:, 0:NCS]), w=["cst"], dma_slot="cst")
    xv = xT_d.rearrange("(c p) t -> p c t", p=P)
    for c in range(8):
        sc.add("sp", I("dma_start", out=xT[:, c, :], in_=xv[:, c, :]), w=[("xT", c)], dma_slot=("x", c))
    sc.add("pool", I("memset", ones_bf[:], 1.0 / D), w=["ones_bf"])
    sc.add("pool", I("memset", ones1_bf[:], 1.0), w=["ones1_bf"])
    sc.add("dve", I("tensor_copy", out=ident_bf[:], in_=C("identF")), r=["cst"], w=["ident_bf"])
    bo = CST_OFF["band"][0]
    sc.add("pool", I("dma_start", out=mask2[:].rearrange("p a b -> p (a b)"), in_=cst_d[:, bo:bo + 512]), w=["mask2"], dma_slot="band")

    wstate = {"n": 0, "issued": 0}
    plan = []
    for l_ in range(nlay):
        fb_ = l_ * NFULL
        if cfg["mlstm"]:
            plan += [(fb_ + 0, 4096, 0), (fb_ + 1, 4096, 0), (fb_ + 2, 4096, 0), (fb_ + 3, 4096, 0), (fb_ + 8, 4096, 0)]
        if cfg["attn"]:
            plan += [(fb_ + 4 + hp_, 4096, 0) for hp_ in range(4)] + [(fb_ + 9, 4096, 0)]
        if cfg["ffn"]:
            for half_ in range(2):
                plan += [(fb_ + 10 + half_ * 4 + g_, 4096, 0) for g_ in range(4)]
                plan += [(fb_ + 18 + m_, 2048, half_ * 2048) for m_ in range(8)]

    def _issue(idx):
        gidx, ncols, coloff = plan[idx]
        slot = idx % NWB
        buf = wb[slot]
        for hlf in range(ncols // 2048):
            sc.add("pool", I("dma_start", out=buf[:, hlf * 2048:(hlf + 1) * 2048], in_=wf_d[gidx, :, coloff + hlf * 2048:coloff + (hlf + 1) * 2048]),
                   w=[("wb", slot, hlf)], dma_slot=("wb", slot, hlf))

    def load_w(gidx, ncols=4096, coloff=0):
        n = wstate["n"]
        assert plan[n] == (gidx, ncols, coloff), (n, plan[n], gidx, ncols, coloff)
        wstate["n"] = n + 1
        while wstate["issued"] <= min(n + 1, len(plan) - 1):
            _issue(wstate["issued"])
            wstate["issued"] += 1
        slot = n % NWB
        if ncols // 2048 == 1:
            return wb[slot], ("wb", slot, 0)
        return wb[slot], WK(slot)

    def load_wsmall(gidx, ncols=1024):
        sc.add("pool", I("dma_start", out=wsb[:, 0:ncols], in_=ws_d[gidx, :, 0:ncols]), w=["wsb"], dma_slot="wsb")
        return wsb, "wsb"

    pcount = {"n": 0}

    def next_bank():
        i = pcount["n"] % 8
        pcount["n"] += 1
        return i

    def rmsnorm(gname, goff, t0, t1, dst_fn, dst_keys):
        nb = (t1 - t0) // 512
        for b in range(nb):
            lo = t0 + b * 512
            hi = lo + 512
            bk = next_bank()
            for c in range(8):
                q = sqb[c % 2]
                sc.add("act", I("activation", out=q[:], in_=xT[:, c, lo:hi], func=AF.Square),
                       r=[("xT", c)], w=[("sqb", c % 2)])
                sc.add("pe", I("matmul", pbank[bk][:], lhsT=ones_bf[:], rhs=q[:], start=(c == 0), stop=(c == 7)),
                       r=[("sqb", c % 2), "ones_bf"], w=[("pb", bk)])
            r_ = rst[b % 2]
            sc.add("act", I("activation", out=r_[:], in_=pbank[bk][:], func=AF.Ln, bias=C("epsc", 0, 1), scale=1.0),
                   r=[("pb", bk), "cst"], w=[("rst", 0)])
            sc.add("act", I("activation", out=r_[:], in_=r_[:], func=AF.Exp, scale=-0.5),
                   r=[("rst", 0)], w=[("rst", 0)])
            for c in range(8):
                eng = "dve"
                sc.add(eng, I("scalar_tensor_tensor",
                    out=dst_fn(c, lo, hi), in0=xT[:, c, lo:hi], scalar=C(gname, goff + c, 1), in1=r_[:], op0=ALU.mult, op1=ALU.mult),
                    r=[("xT", c), ("rst", 0), "cst"], w=dst_keys(c))

    def ffn(l):
        uTv = U[:, 16384:16384 + 16 * S].rearrange("p (n t) -> p n t", t=S)
        rtmp = [U[:, 49152 + i * 1024:49152 + (i + 1) * 1024].bitcast(F32) for i in range(2)]
        rmsnorm("g2", l * 8, 0, S, lambda c, lo, hi: hT[:, c, lo:hi], lambda c: [("hT", c)])
        cnt = 0
        for half in range(2):
            for g in range(4):
                w, wk = load_w(l * NFULL + 10 + half * 4 + g)
                wv = w[:].rearrange("p (n k c) -> p n k c", n=4, k=8)
                for n in range(4):
                    ch = 4 * g + n
                    for blk in range(4):
                        lo = blk * 512
                        bk = next_bank()
                        for kc in range(8):
                            sc.add("pe", I("matmul", pbank[bk][:], lhsT=wv[:, n, kc, :], rhs=hT[:, kc, lo:lo + 512], start=(kc == 0), stop=(kc == 7)),
                                   r=[wk, ("hT", kc)], w=[("pb", bk)])
                        rt = rtmp[cnt % 2]
                        sc.add("act", I("activation", out=rt[:], in_=pbank[bk][:], func=AF.Relu), r=[("pb", bk)], w=[("rtmp", cnt % 2)])
                        sc.add("pool", I("tensor_tensor", out=uTv[:, ch, lo:lo + 512], in0=rt[:], in1=rt[:], op=ALU.mult),
                               r=[("rtmp", cnt % 2)], w=[("uT", ch, blk)])
                        cnt += 1
            for m in range(8):
                w, wk = load_w(l * NFULL + 18 + m, ncols=2048, coloff=half * 2048)
                wv = w[:, 0:2048].rearrange("p (n c) -> p n c", n=16)
                for blk in range(4):
                    lo = blk * 512
                    bk = next_bank()
                    for n in range(16):
                        sc.add("pe", I("matmul", pbank[bk][:], lhsT=wv[:, n, :], rhs=uTv[:, n, lo:lo + 512], start=(n == 0), stop=(n == 15)),
                               r=[wk, ("uT", n, blk)], w=[("pb", bk)])
                    sc.add("dve", I("tensor_tensor", out=xT[:, m, lo:lo + 512], in0=xT[:, m, lo:lo + 512], in1=pbank[bk][:], op=ALU.add),
                           r=[("pb", bk), ("xT", m)], w=[("xT", m)])

    def proj_B(wv_n, wk, blk, bk):
        lo = blk * 512
        for kc in range(8):
            sc.add("pe", I("matmul", pbank[bk][:], lhsT=wv_n[:, kc, :], rhs=hT[:, kc, lo:lo + 512], start=(kc == 0), stop=(kc == 7)),
                   r=[wk, ("hT", kc)], w=[("pb", bk)])

    def proj_A(wv, wk, ncol, tok_ap_fn, out_ap, bk):
        for kc in range(8):
            sc.add("pe", I("matmul", out_ap, lhsT=tok_ap_fn(kc), rhs=wv[:, kc, 0:ncol], start=(kc == 0), stop=(kc == 7)),
                   r=[wk, ("hT", kc)], w=[("pb", bk)])

    def pbf(bk):
        return pbank[bk][:].bitcast(BF16)

    def gates(l):
        w, wk = load_wsmall(l * NSMALL + 0, 128)
        wv = w[:, 0:128].rearrange("p (k c) -> p k c", k=8)
        bk = next_bank()
        for t in range(16):
            proj_A(wv, wk, 16, lambda kc, t=t: hT[:, kc, t * P:(t + 1) * P], pbank[bk][:, t * 16:(t + 1) * 16], bk)
        gbv = C("gb", l * 16, 16).unsqueeze(1).to_broadcast([P, 16, 16])
        sc.add("dve", I("tensor_tensor", out=graw[:], in0=pbank[bk][:, 0:256].rearrange("p (t c) -> p t c", t=16), in1=gbv, op=ALU.add),
               r=[("pb", bk), "cst"], w=["graw"])
        gi_src = graw[:, :, 0:8].rearrange("p t (d h) -> p d t h", d=2)
        gf_src = graw[:, :, 8:16].rearrange("p t (d h) -> p d t h", d=2)
        sc.add("dve", I("tensor_scalar", out=G["gi"][:], in0=gi_src, scalar1=-0.5 * math.log(128.0), scalar2=0.0, op0=ALU.add, op1=ALU.add),
               r=["graw"], w=["G_gi"])
        sc.add("act", I("activation", out=G["w"][:], in_=gf_src, func=AF.Exp, scale=-1.0), r=["graw"], w=["G_w"])
        sc.add("act", I("activation", out=G["w"][:], in_=G["w"][:], func=AF.Ln, bias=C("epsc", 1, 1), scale=1.0), r=["G_w", "cst"], w=["G_w"])
        sc.add("dve", I("tensor_scalar", out=G["lf"][:], in0=G["w"][:], scalar1=-1.0, scalar2=0.0, op0=ALU.mult, op1=ALU.add),
               r=["G_w"], w=["G_lf"])
        bk2 = next_bank()
        for d_, Tn in ((0, "Tf"), (1, "Tb")):
            sc.add("pe", I("matmul", pbank[bk2][:, d_ * 64:(d_ + 1) * 64], lhsT=C(Tn), rhs=G["lf"][:, d_].rearrange("p t h -> p (t h)"), start=True, stop=True),
                   r=["G_lf", "cst"], w=[("pb", bk2)])
        sc.add("dve", I("tensor_copy", out=G["bcol"][:].rearrange("p d t h -> p (d t h)"), in_=pbank[bk2][:, 0:128]), r=[("pb", bk2)], w=["G_bcol"])
        sc.add("dve", I("tensor_tensor", out=G["col"][:], in0=G["gi"][:], in1=G["bcol"][:], op=ALU.subtract), r=["G_gi", "G_bcol"], w=["G_col"])
        bk3 = next_bank()
        for d_, Sn in ((0, "sel127"), (1, "sel0")):
            sc.add("pe", I("matmul", pbank[bk3][:, d_ * 64:(d_ + 1) * 64], lhsT=C(Sn), rhs=G["bcol"][:, d_].rearrange("p t h -> p (t h)"), start=True, stop=True),
                   r=["G_bcol", "cst"], w=[("pb", bk3)])
        sc.add("dve", I("tensor_copy", out=G["blast"][:].rearrange("p d t h -> p (d t h)"), in_=pbank[bk3][:, 0:128]), r=[("pb", bk3)], w=["G_blast"])
        sc.add("act", I("activation", out=G["decay"][:], in_=G["blast"][:], func=AF.Exp), r=["G_blast"], w=["G_decay"])
        sc.add("dve", I("tensor_tensor", out=G["w"][:], in0=G["col"][:], in1=G["blast"][:], op=ALU.add), r=["G_col", "G_blast"], w=["G_w"])
        sc.add("act", I("activation", out=G["w"][:], in_=G["w"][:], func=AF.Exp), r=["G_w"], w=["G_w"])

    def mlstm_pair(l, hp):
        cv = Carver(RBASE)
        mqk = cv.bf(4 * S).rearrange("p (n t) -> p n t", n=4)
        Vext = cv.bf(16 * 2 * 129).rearrange("p (t a c) -> p t a c", t=16, a=2)
        ogt = cv.bf(16 * 256).rearrange("p (t c) -> p t c", t=16)
        Cstb = cv.bf(16 * 2 * 129).rearrange("p (t a c) -> p t a c", t=16, a=2)
        tmp_base = cv.o
        pre = [cv.bf(2052), cv.bf(2052)]
        cv.o = tmp_base
        Tlf = [cv.f32(256).rearrange("p (a i) -> p a i", a=2) for _ in range(2)]
        rhs2 = [cv.f32(256).rearrange("p (a i) -> p a i", a=2) for _ in range(2)]
        Dx = [cv.f32(256) for _ in range(2)]
        eb = [cv.bf(256).rearrange("p (a i) -> p a i", a=2) for _ in range(2)]
        qs = [cv.bf(256).rearrange("p (a i) -> p a i", a=2) for _ in range(2)]
        PT = [cv.bf(256).rearrange("p (a i) -> p a i", a=2) for _ in range(2)]
        cv.o = max(cv.o, tmp_base + 2 * 2052)
        wK = cv.bf(256).rearrange("p (a c) -> p a c", a=2)
        Crun = [cv.f32(258).rearrange("p (a c) -> p a c", a=2) for _ in range(2)]
        Cf = cv.bf(258).rearrange("p (a c) -> p a c", a=2)
        hsum = cv.f32(256).rearrange("p (a c) -> p a c", a=2)
        sq = cv.f32(256).rearrange("p (a c) -> p a c", a=2)
        gog = cv.f32(256)
        halfg = cv.f32(256)
        ybf = cv.bf(256).rearrange("p (a c) -> p a c", a=2)
        rr = cv.f32(4)
        ssq = cv.f32(2)
        K = lambda nm: ("m", nm)

        w, wk = load_w(l * NFULL + 2 * hp)
        wv = w[:].rearrange("p (n k c) -> p n k c", n=4, k=8)
        for i in range(2):
            sc.add("pool", I("memset", pre[i][:, 0:2], 0.0), w=[K(("pre", i))])
            sc.add("pool", I("memset", pre[i][:, 2050:2052], 0.0), w=[K(("pre", i))])
        for n in range(4):
            c8 = (2 * hp + n) if n < 2 else (4 + 2 * hp + n - 2)
            pb_ = pre[n % 2]
            dg = diag[n % 2]
            for tau in range(5):
                sc.add("pool", I("tensor_tensor", out=dg[:, tau, :], in0=C("identF"), in1=C("convw", l * 40 + tau * 8 + c8, 1).to_broadcast([P, P]), op=ALU.mult),
                       r=["ident_bf", "cst"], w=[("diag", n % 2)])
            for blk in range(4):
                bk = next_bank()
                proj_B(wv[:, n], wk, blk, bk)
                sc.add("act", I("activation", out=pb_[:, 2 + blk * 512:2 + (blk + 1) * 512], in_=pbank[bk][:], func=AF.Copy),
                       r=[("pb", bk)], w=[K(("pre", n % 2))])
            for blk in range(4):
                bk = next_bank()
                for tau in range(5):
                    sc.add("pe", I("matmul", pbank[bk][:], lhsT=dg[:, tau, :], rhs=pb_[:, blk * 512 + tau:blk * 512 + tau + 512], start=(tau == 0), stop=(tau == 4)),
                           r=[K(("pre", n % 2)), ("diag", n % 2)], w=[("pb", bk)])
                sc.add("act", I("activation", out=mqk[:, n, blk * 512:(blk + 1) * 512], in_=pbank[bk][:], func=AF.Silu),
                       r=[("pb", bk)], w=[K(("mqk", n))])
        w, wk = load_w(l * NFULL + 2 * hp + 1)
        wv = w[:].rearrange("p (k c) -> p k c", k=8)
        sc.add("pool", I("memset", Vext[:, :, :, 128:129], 1.0), w=[K("Vext")])
        for t in range(16):
            bk = next_bank()
            proj_A(wv, wk, 512, lambda kc, t=t: hT[:, kc, t * P:(t + 1) * P], pbank[bk][:], bk)
            sc.add("act", I("activation", out=Vext[:, t, :, 0:128], in_=pbank[bk][:, 0:256].rearrange("p (a c) -> p a c", a=2), func=AF.Copy),
                   r=[("pb", bk)], w=[K("Vext")])
            sc.add("act", I("activation", out=ogt[:, t, :], in_=pbank[bk][:, 256:512], func=AF.Tanh, scale=0.5),
                   r=[("pb", bk)], w=[K("ogt")])
        ho = CST_OFF["hng"][0] + l * 512 + hp * 256
        sc.add("sp", I("dma_start", out=halfg[:], in_=cst_d[:, ho:ho + 256]), w=[K("halfg")], dma_slot="hng")
        sc.add("pool", I("tensor_scalar", out=halfg[:], in0=halfg[:], scalar1=0.5, scalar2=0.0, op0=ALU.mult, op1=ALU.add),
               r=[K("halfg")], w=[K("halfg")])

        def gsl(nm, d_, t):
            return G[nm][:, d_, t, 2 * hp:2 * hp + 2]

        def state_update(d_, t):
            bkT = next_bank()
            for a in range(2):
                sc.add("pe", I("transpose", pbf(bkT)[:, a * P:(a + 1) * P], mqk[:, 2 + a, t * P:(t + 1) * P], ident_bf[:]),
                       r=[K(("mqk", 2 + a)), "ident_bf"], w=[("pb", bkT)])
            sc.add("dve", I("tensor_tensor", out=wK[:], in0=pbf(bkT)[:, 0:256].rearrange("p (a c) -> p a c", a=2),
                                                     in1=gsl("w", d_, t).unsqueeze(2).to_broadcast([P, 2, P]), op=ALU.mult),
                   r=[("pb", bkT), "G_w"], w=[K("wK")])
            bkD = next_bank()
            for a in range(2):
                sc.add("pe", I("matmul", pbank[bkD][:, a * 129:(a + 1) * 129], lhsT=wK[:, a, :], rhs=Vext[:, t, a, :], start=True, stop=True),
                       r=[K("wK"), K("Vext")], w=[("pb", bkD)])
            for a in range(2):
                sc.add("dve", I("scalar_tensor_tensor", out=Crun[d_][:, a, :], in0=Crun[d_][:, a, :], scalar=G["decay"][:, d_, t, 2 * hp + a:2 * hp + a + 1],
                                                                    in1=pbank[bkD][:, a * 129:(a + 1) * 129], op0=ALU.mult, op1=ALU.add),
                       r=[("pb", bkD), "G_decay", K(("Crun", d_))], w=[K(("Crun", d_))])

        sc.barrier()
        for d_ in range(2):
            sc.add("pool", I("memset", Crun[d_][:], 0.0), w=[K(("Crun", d_))])
        wKb = [wK, Tlf[0][:].rearrange("p a i -> p (a i)").bitcast(BF16)[:, 0:256].rearrange("p (a c) -> p a c", a=2)]

        def bprep(t):
            buf = wKb[t % 2]
            bkT = next_bank()
            for a in range(2):
                sc.add("pe", I("transpose", pbf(bkT)[:, a * P:(a + 1) * P], mqk[:, 2 + a, t * P:(t + 1) * P], ident_bf[:]),
                       r=[K(("mqk", 2 + a)), "ident_bf"], w=[("pb", bkT)])
            sc.add("dve", I("tensor_tensor", out=buf[:], in0=pbf(bkT)[:, 0:256].rearrange("p (a c) -> p a c", a=2),
                            in1=gsl("w", 1, t).unsqueeze(2).to_broadcast([P, 2, P]), op=ALU.mult),
                   r=[("pb", bkT), "G_w"], w=[K(("wKb", t % 2))])
            bkD = next_bank()
            for a in range(2):
                sc.add("pe", I("matmul", pbank[bkD][:, a * 129:(a + 1) * 129], lhsT=buf[:, a, :], rhs=Vext[:, t, a, :], start=True, stop=True),
                       r=[K(("wKb", t % 2)), K("Vext")], w=[("pb", bkD)])
            return bkD

        bkD_next = bprep(15)
        for t in range(15, -1, -1):
            sc.add("act", I("activation", out=Cstb[:, t], in_=Crun[1][:], func=AF.Copy), r=[K(("Crun", 1))], w=[K("Cstb")])
            if t > 0:
                bkD_cur = bkD_next
                if t > 1:
                    bkD_next = bprep(t - 1)
                for a in range(2):
                    sc.add("dve", I("scalar_tensor_tensor", out=Crun[1][:, a, :], in0=Crun[1][:, a, :], scalar=G["decay"][:, 1, t, 2 * hp + a:2 * hp + a + 1],
                                    in1=pbank[bkD_cur][:, a * 129:(a + 1) * 129], op0=ALU.mult, op1=ALU.add),
                           r=[("pb", bkD_cur), "G_decay", K(("Crun", 1))], w=[K(("Crun", 1))])
        sc.barrier()

        def state_update_f(t):
            for a in range(2):
                sc.add("pe", I("transpose", pbf(5)[:, a * P:(a + 1) * P], mqk[:, 2 + a, t * P:(t + 1) * P], ident_bf[:]),
                       r=[K(("mqk", 2 + a)), "ident_bf"], w=[("pb", 5)])
            sc.add("dve", I("tensor_tensor", out=wK[:], in0=pbf(5)[:, 0:256].rearrange("p (a c) -> p a c", a=2),
                            in1=gsl("w", 0, t).unsqueeze(2).to_broadcast([P, 2, P]), op=ALU.mult),
                   r=[("pb", 5), "G_w"], w=[K("wK")])
            for a in range(2):
                sc.add("pe", I("matmul", pbank[7][:, a * 129:(a + 1) * 129], lhsT=wK[:, a, :], rhs=Vext[:, t, a, :], start=True, stop=True),
                       r=[K("wK"), K("Vext")], w=[("pb", 7)])
            for a in range(2):
                sc.add("dve", I("scalar_tensor_tensor", out=Crun[0][:, a, :], in0=Crun[0][:, a, :], scalar=G["decay"][:, 0, t, 2 * hp + a:2 * hp + a + 1],
                                in1=pbank[7][:, a * 129:(a + 1) * 129], op0=ALU.mult, op1=ALU.add),
                       r=[("pb", 7), "G_decay", K(("Crun", 0))], w=[K(("Crun", 0))])

        def y_transposes(t):
            for a in range(2):
                sc.add("pe", I("transpose", pbf(6)[:, a * P:(a + 1) * P], ybf[:, a, :], ident_bf[:]),
                       r=[K("ybf"), "ident_bf"], w=[("pb", 6)])
            sc.add("act", I("activation", out=yT[:, 2 * hp:2 * hp + 2, t * P:(t + 1) * P], in_=pbf(6)[:, 0:256].rearrange("p (a c) -> p a c", a=2), func=AF.Copy),
                   r=[("pb", 6)], w=[("yT", 2 * hp), ("yT", 2 * hp + 1)])

        def stA(t):
            tl = slice(t * P, (t + 1) * P)
            for a in range(2):
                sc.add("pe", I("matmul", pbank[0][:, a * P:(a + 1) * P], lhsT=mqk[:, 2 + a, tl], rhs=mqk[:, a, tl], start=True, stop=True),
                       r=[K(("mqk", a)), K(("mqk", 2 + a))], w=[("pb", 0)])
            for d_ in range(2):
                Tn, Nn = ("Tf", "NEGf") if d_ == 0 else ("Tb", "NEGb")
                sc.add("pool", I("tensor_tensor", out=Tlf[d_][:], in0=C(Tn).unsqueeze(1).to_broadcast([P, 2, P]),
                                 in1=gsl("lf", d_, t).unsqueeze(2).to_broadcast([P, 2, P]), op=ALU.mult),
                       r=["cst", "G_lf"], w=[K(("Tlf", d_))])
                sc.add("pool", I("tensor_tensor", out=rhs2[d_][:], in0=C(Nn).unsqueeze(1).to_broadcast([P, 2, P]),
                                 in1=gsl("col", d_, t).unsqueeze(2).to_broadcast([P, 2, P]), op=ALU.add),
                       r=["cst", "G_col"], w=[K(("rhs2", d_))])
            for d_ in range(2):
                bk_ = 3 + d_
                Tl2 = Tlf[d_][:].rearrange("p a i -> p (a i)")
                sc.add("pe", I("matmul", pbank[bk_][:, 0:256], lhsT=C("onesF"), rhs=Tl2, start=True, stop=True),
                       r=["cst", K(("Tlf", d_))], w=[("pb", bk_)])
                sc.add("pe", I("matmul", pbank[bk_][:, 256:512], lhsT=C("onesF"), rhs=Tl2, start=True, stop=False),
                       r=["cst", K(("Tlf", d_))], w=[("pb", bk_)])
                sc.add("pe", I("matmul", pbank[bk_][:, 256:512], lhsT=C("identF"), rhs=rhs2[d_][:].rearrange("p a i -> p (a i)"), start=False, stop=True),
                       r=["cst", K(("rhs2", d_))], w=[("pb", bk_)])

        def stB(t):
            tl = slice(t * P, (t + 1) * P)
            for d_ in range(2):
                bk_ = 3 + d_
                sc.add("act", I("activation", out=Dx[d_][:], in_=pbank[bk_][:, 256:512], func=AF.Exp),
                       r=[("pb", bk_)], w=[K(("Dx", d_))])
                sc.add("act", I("activation", out=eb[d_][:].rearrange("p a i -> p (a i)"), in_=pbank[bk_][:, 0:256], func=AF.Exp),
                       r=[("pb", bk_)], w=[K(("eb", d_))])
                sc.add("dve", I("tensor_tensor", out=PT[d_][:].rearrange("p a i -> p (a i)"), in0=pbank[0][:, 0:256], in1=Dx[d_][:], op=ALU.mult),
                       r=[("pb", 0), K(("Dx", d_))], w=[K(("PT", d_))])
                sc.add("dve", I("tensor_tensor", out=qs[d_][:], in0=mqk[:, 0:2, tl], in1=eb[d_][:], op=ALU.mult),
                       r=[K(("mqk", 0)), K(("mqk", 1)), K(("eb", d_))], w=[K(("qs", d_))])

        def stC(t):
            bkO = 1 + (t % 2)
            for d_ in range(2):
                for a in range(2):
                    cprev = Cf[:, a, :] if d_ == 0 else Cstb[:, t, a, :]
                    ck = K("Cf") if d_ == 0 else K("Cstb")
                    oc = (a * 2 + d_) * P
                    sc.add("pe", I("matmul", pbank[bkO][:, oc:oc + P], lhsT=PT[d_][:, a, :], rhs=Vext[:, t, a, 0:128], start=True, stop=False),
                           r=[K(("PT", d_)), K("Vext")], w=[("pb", bkO)])
                    sc.add("pe", I("matmul", pbank[bkO][:, oc:oc + P], lhsT=qs[d_][:, a, :], rhs=cprev[:, 0:128], start=False, stop=True),
                           r=[K(("qs", d_)), ck], w=[("pb", bkO)])
            for d_ in range(2):
                for a in range(2):
                    cprev = Cf[:, a, :] if d_ == 0 else Cstb[:, t, a, :]
                    ck = K("Cf") if d_ == 0 else K("Cstb")
                    dc = 258 + a * 2 + d_
                    sc.add("pe", I("matmul", pbank[7][:, dc:dc + 1], lhsT=PT[d_][:, a, :], rhs=Vext[:, t, a, 128:129], start=True, stop=False),
                           r=[K(("PT", d_)), K("Vext")], w=[("pb", 7)])
                    sc.add("pe", I("matmul", pbank[7][:, dc:dc + 1], lhsT=qs[d_][:, a, :], rhs=cprev[:, 128:129], start=False, stop=True),
                           r=[K(("qs", d_)), ck], w=[("pb", 7)])

        def stD(t):
            bkO = 1 + (t % 2)
            rk_ = K("rr")
            sc.add("act", I("activation", out=rr[:, 0:4], in_=pbank[7][:, 258:262], func=AF.Abs), r=[("pb", 7)], w=[rk_])
            sc.add("dve", I("tensor_scalar", out=rr[:, 0:4], in0=rr[:, 0:4], scalar1=1.0, scalar2=0.0, op0=ALU.max, op1=ALU.add), r=[rk_], w=[rk_])
            sc.add("dve", I("reciprocal", out=rr[:, 0:4], in_=rr[:, 0:4]), r=[rk_], w=[rk_])
            for a in range(2):
                oc = a * 2 * P
                sc.add("act", I("activation", out=hsum[:, a, :], in_=pbank[bkO][:, oc:oc + P], func=AF.Copy, scale=rr[:, 2 * a:2 * a + 1]),
                       r=[("pb", bkO), rk_], w=[K(("hsum", a))])
            for a in range(2):
                oc = a * 2 * P
                sc.add("dve", I("scalar_tensor_tensor", out=hsum[:, a, :], in0=pbank[bkO][:, oc + P:oc + 2 * P], scalar=rr[:, 2 * a + 1:2 * a + 2], in1=hsum[:, a, :], op0=ALU.mult, op1=ALU.add),
                       r=[("pb", bkO), rk_, K(("hsum", a))], w=[K(("hsum", a))])
            for a in range(2):
                sc.add("act", I("activation", out=sq[:, a, :], in_=hsum[:, a, :], func=AF.Square, accum_out=ssq[:, a:a + 1]),
                       r=[K(("hsum", a))], w=[K(("sq", a)), K(("ssq", a))])
            hk = [K(("hsum", 0)), K(("hsum", 1))]
            sk = [K(("ssq", 0)), K(("ssq", 1))]
            sc.add("act", I("activation", out=ssq[:], in_=ssq[:], func=AF.Ln, bias=C("epsc", 0, 1), scale=1.0 / 128.0), r=sk + ["cst"], w=sk)
            sc.add("act", I("activation", out=ssq[:], in_=ssq[:], func=AF.Exp, scale=-0.5), r=sk, w=sk)
            sc.add("dve", I("scalar_tensor_tensor", out=gog[:], in0=ogt[:, t, :], scalar=1.0, in1=halfg[:], op0=ALU.add, op1=ALU.mult),
                   r=[K("ogt"), K("halfg")], w=[K("gog")])
            sc.add("dve", I("tensor_tensor", out=sq[:], in0=hsum[:], in1=ssq[:].unsqueeze(2).to_broadcast([P, 2, P]), op=ALU.mult),
                   r=hk + sk + [K(("sq", 0)), K(("sq", 1))], w=[K(("sq", 0)), K(("sq", 1))])
            sc.add("pool", I("tensor_tensor", out=ybf[:].rearrange("p a c -> p (a c)"), in0=sq[:].rearrange("p a c -> p (a c)"), in1=gog[:], op=ALU.mult),
                   r=[K(("sq", 0)), K(("sq", 1)), K("gog")], w=[K("ybf")])

        def cf_copy():
            sc.add("act", I("activation", out=Cf[:], in_=Crun[0][:], func=AF.Copy), r=[K(("Crun", 0))], w=[K("Cf")])

        stA(0)
        stB(0)
        cf_copy()
        for t in range(16):
            stC(t)
            if t < 15:
                state_update_f(t)
                cf_copy()
                stA(t + 1)
                stB(t + 1)
            if t > 0:
                y_transposes(t - 1)
            stD(t)
        y_transposes(15)
        sc.barrier()

    def attn_pair(l, hp):
        cv = Carver(RBASE)
        aq = cv.bf(S)
        ak = cv.bf(S)
        Vp = [cv.bf(16 * P).rearrange("p (t c) -> p t c", t=16) for _ in range(3)]
        accn = cv.f32(S)
        accd = cv.f32(S)
        qP = cv.bf(S)
        kP = cv.bf(S)
        PtP = [cv.bf(512).rearrange("p (a c) -> p a c", a=2) for _ in range(3)]
        ropeb = [cv.f32(1024).rearrange("p (k t) -> p k t", k=2) for _ in range(2)]
        t1 = [cv.f32(512)] * 2
        t2 = [cv.f32(512)] * 2
        K = lambda nm: ("a", nm)
        DILS = (1, 4, 16)
        scnt = {"n": 0}
        wsm_, wkv = load_wsmall(l * NSMALL + 1 + hp, 1024)
        wvv = wsm_[:, 0:1024].rearrange("p (k c) -> p k c", k=8)
        w, wk = load_w(l * NFULL + 4 + hp)
        wv = w[:].rearrange("p (n k c) -> p n k c", n=4, k=8)
        for qi, dst in ((0, aq), (1, ak)):
            for blk in range(4):
                lo = blk * 512
                rb = ropeb[blk % 2]
                sc.add("sp", I("dma_start", out=rb[:], in_=rope_d[:, :, lo:lo + 512].rearrange("k p t -> p k t")), w=[K(("rope", blk % 2))], dma_slot=("rope", blk % 2))
                bk0 = next_bank()
                proj_B(wv[:, 2 * qi], wk, blk, bk0)
                bk1 = next_bank()
                proj_B(wv[:, 2 * qi + 1], wk, blk, bk1)
                sc.add("dve", I("tensor_tensor", out=t1[blk % 2][:], in0=pbank[bk0][:], in1=rb[:, 0, :], op=ALU.mult),
                       r=[("pb", bk0), K(("rope", blk % 2))], w=[K(("t1", 0))])
                sc.add("dve", I("tensor_tensor", out=t2[blk % 2][:], in0=pbank[bk1][:], in1=rb[:, 1, :], op=ALU.mult),
                       r=[("pb", bk1), K(("rope", blk % 2))], w=[K(("t2", 0))])
                sc.add("pool", I("tensor_tensor", out=dst[:, lo:lo + 512], in0=t1[blk % 2][:], in1=t2[blk % 2][:], op=ALU.add),
                       r=[K(("t1", 0)), K(("t2", 0))], w=[K(("qk", qi))])
        VT = qP
        if not cfg.get("vt", True):
            for di, d_ in enumerate(DILS):
                nb = 16 // d_
                for g in range(4):
                    bk = next_bank()
                    for tt in range(4):
                        tp = 4 * g + tt
                        r_, lb = tp // nb, tp % nb
                        st = d_ * P * lb + r_

                        def tok(kc, st=st, d_=d_):
                            return hT[:, kc, st:st + d_ * (P - 1) + 1:d_]
                        proj_A(wvv, wkv, P, tok, pbank[bk][:, tt * P:(tt + 1) * P], bk)
                    sc.add("act", I("activation", out=Vp[di][:, 4 * g:4 * g + 4, :], in_=pbank[bk][:].rearrange("p (t c) -> p t c", t=4), func=AF.Copy),
                           r=[("pb", bk)], w=[K(("V", di))])
        for blk in (range(4) if cfg.get("vt", True) else []):
            bk = next_bank()
            lo = blk * 512
            for kc in range(8):
                sc.add("pe", I("matmul", pbank[bk][:], lhsT=wvv[:, kc, :], rhs=hT[:, kc, lo:lo + 512], start=(kc == 0), stop=(kc == 7)),
                       r=[wkv, ("hT", kc)], w=[("pb", bk)])
            sc.add("act", I("activation", out=VT[:, lo:lo + 512], in_=pbank[bk][:], func=AF.Copy), r=[("pb", bk)], w=[K(("qP", blk))])
        for di, d_ in (enumerate(DILS) if cfg.get("vt", True) else []):
            nb = 16 // d_
            if d_ == 1:
                src, skeys = VT, [K(("qP", c_)) for c_ in range(4)]
            else:
                if d_ == 4:
                    sc.add("act", I("activation", out=kP[:].rearrange("p (r l) -> p r l", r=d_), in_=VT[:].rearrange("p (l r) -> p r l", r=d_), func=AF.Copy),
                           r=[K(("qP", c_)) for c_ in range(4)], w=[K(("kP", c_)) for c_ in range(4)])
                else:
                    sc.add("dve", I("tensor_copy", out=kP[:].rearrange("p (r l) -> p r l", r=d_), in_=VT[:].rearrange("p (l r) -> p r l", r=d_)),
                           r=[K(("qP", c_)) for c_ in range(4)], w=[K(("kP", c_)) for c_ in range(4)])
                src, skeys = kP, [K(("kP", c_)) for c_ in range(4)]
            for g in range(4):
                bk = next_bank()
                for tt in range(4):
                    tp = 4 * g + tt
                    sc.add("pe", I("transpose", pbf(bk)[:, tt * P:(tt + 1) * P], src[:, tp * P:(tp + 1) * P], ident_bf[:]),
                           r=skeys + ["ident_bf"], w=[("pb", bk)])
                if g % 2 == 0:
                    sc.add("act", I("activation", out=Vp[di][:, 4 * g:4 * g + 4, :], in_=pbf(bk)[:, 0:512].rearrange("p (t c) -> p t c", t=4), func=AF.Copy),
                           r=[("pb", bk)], w=[K(("V", di))])
                else:
                    sc.add("dve", I("tensor_copy", out=Vp[di][:, 4 * g:4 * g + 4, :], in_=pbf(bk)[:, 0:512].rearrange("p (t c) -> p t c", t=4)),
                           r=[("pb", bk)], w=[K(("V", di))])

        def perm_copy(d_, c):
            if d_ == 4:
                qi_ = aq[:].rearrange("p (l r) -> p r l", r=4)[:, c, :]
                ki_ = ak[:].rearrange("p (l r) -> p r l", r=4)[:, c, :]
                qo_, ko_ = qP[:, c * 512:(c + 1) * 512], kP[:, c * 512:(c + 1) * 512]
            else:
                qi_ = aq[:].rearrange("p (l r) -> p r l", r=16)[:, 4 * c:4 * c + 4, :]
                ki_ = ak[:].rearrange("p (l r) -> p r l", r=16)[:, 4 * c:4 * c + 4, :]
                qo_ = qP[:, c * 512:(c + 1) * 512].rearrange("p (r l) -> p r l", r=4)
                ko_ = kP[:, c * 512:(c + 1) * 512].rearrange("p (r l) -> p r l", r=4)
            sc.add("pool", I("tensor_copy", out=qo_, in_=qi_), r=[K(("qk", 0))], w=[K(("qP", c))])
            sc.add("dve", I("tensor_copy", out=ko_, in_=ki_), r=[K(("qk", 1))], w=[K(("kP", c))])

        kts = []
        blk_last = {}
        for di, d_ in enumerate(DILS):
            nb = 16 // d_
            Ld = nb * P
            for r_ in range(d_):
                for kt in range(nb):
                    lo, hi = max(0, kt * P - 64), min(Ld, kt * P + 192)
                    u = dict(di=di, d=d_, T=r_ * nb + kt, c_lo=r_ * Ld + lo, c_hi=r_ * Ld + hi, m0=lo - (kt * P - 64))
                    for b in range(u["c_lo"] // 512, (u["c_hi"] - 1) // 512 + 1):
                        blk_last[(di, b)] = len(kts)
                    kts.append(u)
        started = set()

        def views(u):
            if u["d"] == 1:
                return aq, ak, [K(("qk", 0))], [K(("qk", 1))]
            cq = range(u["c_lo"] // 512, (u["c_hi"] - 1) // 512 + 1)
            return qP, kP, [K(("qP", c_)) for c_ in cq], [K(("kP", u["T"] // 4))]

        def stageAB(j, u):
            d_ = u["d"]
            if cfg.get("early", True):
                if d_ == 1 and u["T"] in (2, 5, 8, 11):
                    perm_copy(4, (u["T"] - 2) // 3)
                if d_ == 4 and u["T"] % 4 == 0 and u["T"] > 0:
                    perm_copy(16, u["T"] // 4 - 1)
                if d_ == 16 and u["T"] == 0:
                    perm_copy(16, 3)
            elif d_ > 1 and u["T"] == 0:
                for c_ in range(4):
                    perm_copy(d_, c_)
            qv, kv, rq, rk = views(u)
            T, nc_ = u["T"], u["c_hi"] - u["c_lo"]
            pt = PtP[j % 3]
            for a in range(2):
                rows = slice(64 * a, 64 * a + 64)
                bkS = 4 + ((2 * j + a) % 4)
                sc.add("pe", I("matmul", pbank[bkS][:, 0:nc_], lhsT=kv[rows, T * P:(T + 1) * P], rhs=qv[rows, u["c_lo"]:u["c_hi"]], start=True, stop=True),
                       r=rq + rk, w=[("pb", bkS)])
            for a in range(2):
                bkS = 4 + ((2 * j + a) % 4)
                sc.add("act", I("activation", out=pt[:, a, 0:nc_], in_=pbank[bkS][:, 0:nc_], func=AF.Exp, scale=0.125),
                       r=[("pb", bkS)], w=[K(("Pt", j % 3, a))])
            meng = "pool" if (j % 3 == 0 and nc_ == 256 and cfg.get("poolmask", True)) else "dve"
            sc.add(meng, I("tensor_tensor", out=pt[:, :, 0:nc_], in0=pt[:, :, 0:nc_], in1=mask2[:, :, u["m0"]:u["m0"] + nc_], op=ALU.mult),
                   r=[K(("Pt", j % 3, 0)), K(("Pt", j % 3, 1)), "mask2"], w=[K(("Pt", j % 3, 0)), K(("Pt", j % 3, 1))])

        def stageC(j, u):
            di, d_, T = u["di"], u["d"], u["T"]
            pt = PtP[j % 3]
            blks = list(range(u["c_lo"] // 512, (u["c_hi"] - 1) // 512 + 1))
            for b in blks:
                s_lo, s_hi = max(u["c_lo"], 512 * b), min(u["c_hi"], 512 * (b + 1))
                gi_ = di * 4 + b
                bkN, bkD = (0, 1) if gi_ % 2 == 0 else (2, 3)
                for tag in ("N", "D"):
                    for a in range(2):
                        rows = slice(64 * a, 64 * a + 64)
                        mv = pt[:, a, s_lo - u["c_lo"]:s_hi - u["c_lo"]]
                        if tag == "N":
                            bk_, lhs, rkeys = bkN, Vp[di][:, T, rows], [K(("V", di))]
                        else:
                            bk_, lhs, rkeys = bkD, ones1_bf[:, 0:64], ["ones1_bf"]
                        first = (gi_, a, tag) not in started
                        started.add((gi_, a, tag))
                        sc.add("pe", I("matmul", pbank[bk_][rows, s_lo - 512 * b:s_hi - 512 * b], lhsT=lhs, rhs=mv, start=first, stop=True, skip_group_check=True),
                               r=rkeys + [K(("Pt", j % 3, a))], w=[("pb", bk_)])
            for g in blks:
                if blk_last[(di, g)] != j:
                    continue
                gi_ = di * 4 + g
                bkN, bkD = (0, 1) if gi_ % 2 == 0 else (2, 3)
                if d_ == 1:
                    sc.add("act", I("activation", out=accn[:, g * 512:(g + 1) * 512], in_=pbank[bkN][:], func=AF.Copy), r=[("pb", bkN)], w=[K("accn")])
                    sc.add("dve", I("tensor_copy", out=accd[:, g * 512:(g + 1) * 512], in_=pbank[bkD][:]), r=[("pb", bkD)], w=[K("accd")])
                else:
                    if d_ == 4:
                        vn = accn[:].rearrange("p (l r) -> p r l", r=4)[:, g, :]
                        vd = accd[:].rearrange("p (l r) -> p r l", r=4)[:, g, :]
                        pn, pd = pbank[bkN][:], pbank[bkD][:]
                    else:
                        vn = accn[:].rearrange("p (l r) -> p r l", r=16)[:, 4 * g:4 * g + 4, :]
                        vd = accd[:].rearrange("p (l r) -> p r l", r=16)[:, 4 * g:4 * g + 4, :]
                        pn = pbank[bkN][:].rearrange("p (r l) -> p r l", r=4)
                        pd = pbank[bkD][:].rearrange("p (r l) -> p r l", r=4)
                    sc.add("dve", I("tensor_tensor", out=vn, in0=vn, in1=pn, op=ALU.add), r=[("pb", bkN), K("accn")], w=[K("accn")])
                    sc.add("dve", I("tensor_tensor", out=vd, in0=vd, in1=pd, op=ALU.add), r=[("pb", bkD), K("accd")], w=[K("accd")])

        for j in range(len(kts) + 1):
            if j < len(kts):
                stageAB(j, kts[j])
            if j - 1 >= 0:
                stageC(j - 1, kts[j - 1])
        sc.add("dve", I("reciprocal", out=accd[:], in_=accd[:]), r=[K("accd")], w=[K("accd")])
        sc.add("dve", I("tensor_tensor", out=yT[:, hp, :], in0=accn[:], in1=accd[:], op=ALU.mult), r=[K("accn"), K("accd")], w=[("yT", hp)])
        sc.barrier()

    def out_proj(l, hf):
        w, wk = load_w(l * NFULL + 8 + hf)
        wv = w[:].rearrange("p (m k c) -> p m k c", m=8, k=4)
        for m in range(8):
            for blk in range(4):
                lo = blk * 512
                bk = next_bank()
                for kc in range(4):
                    sc.add("pe", I("matmul", pbank[bk][:], lhsT=wv[:, m, kc, :], rhs=yT[:, kc, lo:lo + 512], start=(kc == 0), stop=(kc == 3)),
                           r=[wk, ("yT", kc)], w=[("pb", bk)])
                sc.add("dve", I("tensor_tensor", out=xT[:, m, lo:lo + 512], in0=xT[:, m, lo:lo + 512], in1=pbank[bk][:], op=ALU.add),
                       r=[("pb", bk), ("xT", m)], w=[("xT", m)])

    for l in range(nlay):
        if cfg["mlstm"] or cfg["attn"]:
            rmsnorm("g1", l * 8, 0, S, lambda c, lo, hi: hT[:, c, lo:hi], lambda c: [("hT", c)])
        if cfg["mlstm"]:
            gates(l)
            for hp in range(2):
                mlstm_pair(l, hp)
            out_proj(l, 0)
            sc.barrier()
        if cfg["attn"]:
            for hp in range(4):
                attn_pair(l, hp)
            out_proj(l, 1)
            sc.barrier()
        if cfg["ffn"]:
            ffn(l)
            sc.barrier()

    ov = out_d.rearrange("(c p) t -> p c t", p=P)
    sc.barrier()
    finT = U[:, 0:2048].bitcast(F32).rearrange("p (a t) -> p a t", a=2)
    fcnt = {"n": 0}

    def fin_dst(c, lo, hi):
        return finT[:, c % 2, :]

    for b in range(4):
        lo, hi = b * 512, (b + 1) * 512
        bk = next_bank()
        for c in range(8):
            q = sqb[c % 2]
            sc.add("act", I("activation", out=q[:], in_=xT[:, c, lo:hi], func=AF.Square),
                   r=[("xT", c)], w=[("sqb", c % 2)])
            sc.add("pe", I("matmul", pbank[bk][:], lhsT=ones_bf[:], rhs=q[:], start=(c == 0), stop=(c == 7)),
                   r=[("sqb", c % 2), "ones_bf"], w=[("pb", bk)])
        r_ = rst[b % 2]
        sc.add("act", I("activation", out=r_[:], in_=pbank[bk][:], func=AF.Ln, bias=C("epsc", 0, 1), scale=1.0),
               r=[("pb", bk), "cst"], w=[("rst", 0)])
        sc.add("act", I("activation", out=r_[:], in_=r_[:], func=AF.Exp, scale=-0.5),
               r=[("rst", 0)], w=[("rst", 0)])
        for c in range(8):
            sl = c % 2
            sc.add("dve", I("scalar_tensor_tensor",
                out=finT[:, sl, :], in0=xT[:, c, lo:hi], scalar=C("gf", c, 1), in1=r_[:], op0=ALU.mult, op1=ALU.mult),
                r=[("xT", c), ("rst", 0), "cst"], w=[("fin", sl)])
            sc.add("sp", I("dma_start", out=ov[:, c, lo:hi], in_=finT[:, sl, :]),
                   r=[("fin", sl)], w=[("out", sl)], dma_slot=("out", sl))
    sc.add("sp", None, r=[("out", 0), ("out", 1)])

    sc.emit(nc, es)
    es.close()
    return nc


_PREP_CACHE = {}


def kernel(x, norm1_g, w_in, conv_w, gate_i_b, gate_f_b, head_norm_g, w_out, norm2_g, w_up, w_down, final_g, _cfg=None):
    cfg = dict(CFG)
    if _cfg:
        cfg.update(_cfg)
    inp = dict(x=x, norm1_g=norm1_g, w_in=w_in, conv_w=conv_w, gate_i_b=gate_i_b, gate_f_b=gate_f_b,
               head_norm_g=head_norm_g, w_out=w_out, norm2_g=norm2_g, w_up=w_up, w_down=w_down, final_g=final_g)
    inp = {k: np.asarray(v) for k, v in inp.items()}
    wfull, wsmall = prep_weights(inp)
    cst = prep_consts(inp)
    rope = prep_rope()
    nc = build(cfg)
    xs = np.asarray(inp["x"], np.float32)
    in_maps = []
    for b in range(8):
        in_maps.append({"xT": np.ascontiguousarray(xs[b].T), "wfull": wfull, "wsmall": wsmall, "cst": cst, "rope": rope})
    res = run_bass_kernel_spmd(nc, in_maps, core_ids=list(range(8)))
    out = np.stack([np.ascontiguousarray(r["outT"].T) for r in res.results], axis=0)
    return out.astype(np.float32)
```

```python
import math
from contextlib import ExitStack

import numpy as np
import concourse.bass as bass
import concourse.mybir as mybir
from concourse.bass_utils import run_bass_kernel_spmd

F32 = mybir.dt.float32
BF16 = mybir.dt.bfloat16
ALU = mybir.AluOpType
AF = mybir.ActivationFunctionType
AX = mybir.AxisListType

P = 128
S = 2048
D = 1024
DFF = 4096
NL = 2
EPS = 1e-6
NFULL = 26
NSMALL = 5
TB = 1024

CFG = {"mlstm": True, "attn": True, "ffn": True, "layers": NL, "debug": False}


def I(name, *args, **kw):
    return (name, args, kw)


class Op:
    __slots__ = ("eng", "fn", "deps", "dma", "slot", "tok", "signal", "waits", "idx", "semkey", "semval")


class WK(tuple):
    def __new__(cls, slot):
        return tuple.__new__(cls, (("wb", slot, 0), ("wb", slot, 1)))


def _flat(keys):
    out = []
    for k in keys:
        if isinstance(k, WK):
            out.extend(k)
        else:
            out.append(k)
    return out


class Sched:
    EPOCH = 30000

    def __init__(self):
        self.ops = []
        self.last_w = {}
        self.readers = {}
        self.stream_len = {"pe": 0, "dve": 0, "act": 0, "pool": 0, "sp": 0}
        self.slot_cnt = {}

    def add(self, eng, fn, r=(), w=(), dma_slot=None):
        op = Op()
        op.eng, op.fn, op.dma, op.slot = eng, fn, dma_slot is not None, dma_slot
        op.signal = False
        op.waits = []
        op.idx = len(self.ops)
        r = _flat(r)
        w = _flat(w)
        deps = set()
        for k in r:
            if k in self.last_w:
                deps.add(self.last_w[k])
        for k in w:
            if k in self.last_w:
                deps.add(self.last_w[k])
            for j in self.readers.get(k, ()):
                deps.add(j)
        op.deps = deps
        if op.dma:
            c = self.slot_cnt.get(dma_slot, 0) + 1
            self.slot_cnt[dma_slot] = c
            op.tok = (("dma", dma_slot), c)
            self.stream_len[eng] += 1
        else:
            self.stream_len[eng] += 1
            op.tok = (("eng", eng), self.stream_len[eng])
        for k in r:
            self.readers.setdefault(k, []).append(op.idx)
        for k in w:
            self.last_w[k] = op.idx
            self.readers[k] = []
        self.ops.append(op)
        return op

    def barrier(self):
        last = {}
        for op in self.ops:
            if op.fn is not None:
                last[op.eng] = op.idx
        for e in list(self.stream_len):
            op = Op()
            op.eng, op.fn, op.dma, op.slot = e, None, False, None
            op.signal = False
            op.waits = []
            op.idx = len(self.ops)
            op.deps = set(v for k, v in last.items() if k != e)
            self.stream_len[e] += 1
            op.tok = (("eng", e), self.stream_len[e])
            self.ops.append(op)

    def analyse(self):
        seen = {e: {} for e in self.stream_len}
        for op in self.ops:
            sn = seen[op.eng]
            for j in sorted(op.deps):
                d = self.ops[j]
                if (not d.dma) and d.eng == "pe" and op.eng == "pe" and not op.dma:
                    continue
                fam, v = d.tok
                if sn.get(fam, 0) >= v:
                    continue
                sn[fam] = v
                d.signal = True
                op.waits.append(j)
        rank = {e: 0 for e in self.stream_len}
        self.nepoch = {e: 1 for e in self.stream_len}
        for op in self.ops:
            if op.dma:
                op.semkey = op.tok[0]
                op.semval = 16 * op.tok[1]
            elif op.signal:
                r = rank[op.eng]
                rank[op.eng] = r + 1
                ep = r // self.EPOCH
                self.nepoch[op.eng] = max(self.nepoch[op.eng], ep + 1)
                op.semkey = ("eng", op.eng, ep)
                op.semval = r - ep * self.EPOCH + 1

    def emit(self, nc, es):
        self.analyse()
        sems = {}

        def getsem(key):
            if key not in sems:
                nm = "s_" + "_".join(str(x) for x in key).replace("(", "").replace(")", "").replace(",", "_").replace(" ", "").replace("'", "")
                sems[key] = es.enter_context(nc.semaphore(nm))
            return sems[key]

        for op in self.ops:
            if op.dma or op.signal:
                getsem(op.semkey)
        block = es.enter_context(nc.Block())
        by_eng = {e: [o for o in self.ops if o.eng == e] for e in self.stream_len}

        def body(eng_name):
            def f(eng):
                for op in by_eng[eng_name]:
                    for j in op.waits:
                        d = self.ops[j]
                        eng.wait_ge(sems[d.semkey], d.semval)
                    if op.fn is None:
                        continue
                    name, args, kw = op.fn
                    ins = getattr(eng, name)(*args, **kw)
                    if op.dma:
                        ins.then_inc(sems[op.semkey], 16)
                    elif op.signal:
                        ins.then_inc(sems[op.semkey], 1)
            return f

        block.tensor(body("pe"))
        block.vector(body("dve"))
        block.scalar(body("act"))
        block.gpsimd(body("pool"))
        block.sync(body("sp"))


def _bform(w, cols):
    out = np.empty((P, 4, 8, P), np.float32)
    for n, ci in enumerate(cols):
        blk = w[:, ci]
        out[:, n] = blk.reshape(8, P, P).transpose(1, 0, 2)
    return out.reshape(P, 4096)


def _aform(w, ci):
    blk = w[:, ci]
    C = blk.shape[1]
    return blk.reshape(8, P, C).transpose(1, 0, 2).reshape(P, 8 * C)


def prep_weights(inp):
    wfull = np.zeros((NL * NFULL, P, 4096), np.float32)
    wsmall = np.zeros((NL * NSMALL, P, 1024), np.float32)
    ar = np.arange(P)
    for l in range(NL):
        w_in = np.asarray(inp["w_in"][l], np.float32)
        w_out = np.asarray(inp["w_out"][l], np.float32)
        w_up = np.asarray(inp["w_up"][l], np.float32)
        w_dn = np.asarray(inp["w_down"][l], np.float32)
        fb = l * NFULL
        sb = l * NSMALL
        for hp in range(2):
            h0, h1 = 2 * hp, 2 * hp + 1
            wfull[fb + 2 * hp] = _bform(w_in, [h0 * P + ar, h1 * P + ar, 512 + h0 * P + ar, 512 + h1 * P + ar])
            ci = np.concatenate([1024 + h0 * P + ar, 1024 + h1 * P + ar, 1536 + h0 * P + ar, 1536 + h1 * P + ar])
            wfull[fb + 2 * hp + 1] = _aform(w_in, ci)
        a = ar // 64
        dd = ar % 64
        sw = a * 64 + (dd + 32) % 64
        for hp in range(4):
            qb, kb = 2064 + hp * P, 2576 + hp * P
            wfull[fb + 4 + hp] = _bform(w_in, [qb + ar, qb + sw, kb + ar, kb + sw])
            wsmall[sb + 1 + hp] = _aform(w_in, 3088 + hp * P + ar)
        wsmall[sb + 0, :, :128] = _aform(w_in, 2048 + np.arange(16))
        for hf in range(2):
            blk = w_out[hf * 512:(hf + 1) * 512, :]
            t = blk.reshape(4, P, 8, P).transpose(1, 2, 0, 3)
            wfull[fb + 8 + hf] = t.reshape(P, 4096)
        for g in range(8):
            wfull[fb + 10 + g] = _bform(w_up, [(4 * g + n) * P + ar for n in range(4)])
        for m in range(8):
            blk = w_dn[:, m * P:(m + 1) * P]
            wfull[fb + 18 + m] = blk.reshape(32, P, P).transpose(1, 0, 2).reshape(P, 4096)
    return wfull, wsmall


def _cst_layout():
    off = {}
    o = 0

    def put(name, n):
        nonlocal o
        off[name] = (o, n)
        o += n
    put("g1", NL * 8)
    put("g2", NL * 8)
    put("gf", 8)
    put("convw", NL * 40)
    put("gb", NL * 16)
    for nm in ("Tf", "Tb", "NEGf", "NEGb", "identF", "onesF", "sel127", "sel0"):
        put(nm, P)
    put("epsc", 4)
    off["_NCS"] = (o, 0)
    put("hng", NL * 512)
    put("band", 512)
    return off, o


CST_OFF, NCST = _cst_layout()
NCS = CST_OFF["_NCS"][0]


def prep_consts(inp):
    c = np.zeros((P, NCST), np.float32)

    def setc(name, arr):
        o, n = CST_OFF[name]
        c[:, o:o + n] = arr.reshape(P, n)
    g1 = np.asarray(inp["norm1_g"], np.float32).reshape(NL, 8, P).transpose(2, 0, 1)
    g2 = np.asarray(inp["norm2_g"], np.float32).reshape(NL, 8, P).transpose(2, 0, 1)
    gf = np.asarray(inp["final_g"], np.float32).reshape(8, P).transpose(1, 0)
    setc("g1", np.ascontiguousarray(g1))
    setc("g2", np.ascontiguousarray(g2))
    setc("gf", np.ascontiguousarray(gf))
    cw = np.asarray(inp["conv_w"], np.float32).reshape(NL, 5, 8, P).transpose(3, 0, 1, 2)
    setc("convw", np.ascontiguousarray(cw))
    gb = np.concatenate([np.asarray(inp["gate_i_b"], np.float32), np.asarray(inp["gate_f_b"], np.float32)], axis=1)
    setc("gb", np.ascontiguousarray(np.broadcast_to(gb.reshape(1, NL * 16), (P, NL * 16))))
    hng = np.asarray(inp["head_norm_g"], np.float32).reshape(1, NL * 512)
    setc("hng", np.ascontiguousarray(np.broadcast_to(hng, (P, NL * 512))))
    k = np.arange(P)[:, None]
    i = np.arange(P)[None, :]
    NEG = -30000.0
    setc("Tf", (k <= i).astype(np.float32))
    setc("Tb", (k >= i).astype(np.float32))
    setc("NEGf", np.where(k <= i, 0.0, NEG).astype(np.float32))
    setc("NEGb", np.where(k >= i, 0.0, NEG).astype(np.float32))
    setc("identF", np.eye(P, dtype=np.float32))
    setc("onesF", np.ones((P, P), np.float32))
    s127 = np.zeros((P, P), np.float32)
    s127[127, :] = 1.0
    s0 = np.zeros((P, P), np.float32)
    s0[0, :] = 1.0
    setc("sel127", s127)
    setc("sel0", s0)
    cc = np.arange(256)[None, :]
    m1 = ((cc >= k) & (cc <= k + 128)).astype(np.float32)
    band = np.concatenate([m1, m1], axis=1)
    setc("band", band)
    ec = np.zeros((P, 4), np.float32)
    ec[:, 0] = EPS
    ec[:, 1] = 1.0
    ec[:, 2] = -0.5 * math.log(128.0)
    setc("epsc", ec)
    return c


def prep_rope():
    half = 32
    inv = (10000.0 ** (-np.arange(half, dtype=np.float32) / half)).astype(np.float32)
    ang = np.arange(S, dtype=np.float32)[:, None] * inv[None, :]
    cos = np.cos(ang).astype(np.float32).T
    sin = np.sin(ang).astype(np.float32).T
    cos64 = np.concatenate([cos, cos], 0)
    sin64 = np.concatenate([-sin, sin], 0)
    r = np.zeros((2, P, S), np.float32)
    r[0] = np.concatenate([cos64, cos64], 0)
    r[1] = np.concatenate([sin64, sin64], 0)
    return r


def build(cfg=CFG):
    nc = bass.Bass("TRN2", target_bir_lowering=False)
    nlay = cfg["layers"]
    xT_d = nc.dram_tensor("xT", [D, S], F32, kind="ExternalInput").ap()
    wf_d = nc.dram_tensor("wfull", [NL * NFULL, P, 4096], F32, kind="ExternalInput").ap()
    ws_d = nc.dram_tensor("wsmall", [NL * NSMALL, P, 1024], F32, kind="ExternalInput").ap()
    cst_d = nc.dram_tensor("cst", [P, NCST], F32, kind="ExternalInput").ap()
    rope_d = nc.dram_tensor("rope", [2, P, S], F32, kind="ExternalInput").ap()
    out_d = nc.dram_tensor("outT", [D, S], F32, kind="ExternalOutput").ap()

    es = ExitStack()
    sc = Sched()

    def sb(name, shape, dt=F32):
        return es.enter_context(nc.sbuf_tensor(name, shape, dt))

    def ps(name, shape, dt=F32):
        return es.enter_context(nc.psum_tensor(name, shape, dt))

    xT = sb("xT_sb", [P, 8, S])
    cst = sb("cst_sb", [P, NCS])
    NWB = 2
    wb = [sb(f"wb{i}", [P, 4096], BF16) for i in range(NWB)]
    wsb = sb("wsb", [P, 1024], BF16)
    NU = 54944
    U = sb("U", [P, NU], BF16)
    hT = U[:, 0:16384].rearrange("p (c t) -> p c t", c=8)
    yT = U[:, 16384:24576].rearrange("p (c t) -> p c t", c=4)
    RBASE = 24576

    class Carver:
        def __init__(self, base):
            self.o = base

        def bf(self, n):
            a = U[:, self.o:self.o + n]
            self.o += n + (n % 2)
            assert self.o <= NU
            return a

        def f32(self, n):
            a = U[:, self.o:self.o + 2 * n].bitcast(F32)
            self.o += 2 * n
            assert self.o <= NU
            return a

    sqb = [sb(f"sqb{i}", [P, 512], BF16) for i in range(2)]
    rst = [sb("rst0", [P, 512])] * 2
    ones_bf = sb("ones_bf", [P, P], BF16)
    ones1_bf = sb("ones1_bf", [P, 64], BF16)
    ident_bf = sb("ident_bf", [P, P], BF16)
    mask2 = sb("mask2", [P, 2, 256], BF16)
    diag = [sb(f"diag{i}", [P, 5, P], BF16) for i in range(2)]
    G = {nm: sb("G_" + nm, [P, 2, 16, 4]) for nm in ("gi", "lf", "bcol", "col", "blast", "w", "decay")}
    graw = sb("graw", [P, 16, 16])
    pbank = [ps(f"pb{i}", [P, 512]) for i in range(8)]

    def C(name, a=0, n=None):
        o, ln = CST_OFF[name]
        n = ln - a if n is None else n
        return cst[:, o + a:o + a + n]

    sc.add("sp", I("dma_start", out=cst[:], in_=cst_d[:, 0:NCS]), w=["cst"], dma_slot="cst")
    xv = xT_d.rearrange("(c p) t -> p c t", p=P)
    for c in range(8):
        sc.add("sp", I("dma_start", out=xT[:, c, :], in_=xv[:, c, :]), w=[("xT", c)], dma_slot=("x", c))
    sc.add("pool", I("memset", ones_bf[:], 1.0 / D), w=["ones_bf"])
    sc.add("pool", I("memset", ones1_bf[:], 1.0), w=["ones1_bf"])
    sc.add("dve", I("tensor_copy", out=ident_bf[:], in_=C("identF")), r=["cst"], w=["ident_bf"])
    bo = CST_OFF["band"][0]
    sc.add("pool", I("dma_start", out=mask2[:].rearrange("p a b -> p (a b)"), in_=cst_d[:, bo:bo + 512]), w=["mask2"], dma_slot="band")

    wstate = {"n": 0, "issued": 0}
    plan = []
    for l_ in range(nlay):
        fb_ = l_ * NFULL
        if cfg["mlstm"]:
            plan += [(fb_ + 0, 4096, 0), (fb_ + 1, 4096, 0), (fb_ + 2, 4096, 0), (fb_ + 3, 4096, 0), (fb_ + 8, 4096, 0)]
        if cfg["attn"]:
            plan += [(fb_ + 4 + hp_, 4096, 0) for hp_ in range(4)] + [(fb_ + 9, 4096, 0)]
        if cfg["ffn"]:
            for half_ in range(2):
                plan += [(fb_ + 10 + half_ * 4 + g_, 4096, 0) for g_ in range(4)]
                plan += [(fb_ + 18 + m_, 2048, half_ * 2048) for m_ in range(8)]

    def _issue(idx):
        gidx, ncols, coloff = plan[idx]
        slot = idx % NWB
        buf = wb[slot]
        for hlf in range(ncols // 2048):
            sc.add("pool", I("dma_start", out=buf[:, hlf * 2048:(hlf + 1) * 2048], in_=wf_d[gidx, :, coloff + hlf * 2048:coloff + (hlf + 1) * 2048]),
                   w=[("wb", slot, hlf)], dma_slot=("wb", slot, hlf))

    def load_w(gidx, ncols=4096, coloff=0):
        n = wstate["n"]
        assert plan[n] == (gidx, ncols, coloff), (n, plan[n], gidx, ncols, coloff)
        wstate["n"] = n + 1
        while wstate["issued"] <= min(n + 1, len(plan) - 1):
            _issue(wstate["issued"])
            wstate["issued"] += 1
        slot = n % NWB
        if ncols // 2048 == 1:
            return wb[slot], ("wb", slot, 0)
        return wb[slot], WK(slot)

    def load_wsmall(gidx, ncols=1024):
        sc.add("pool", I("dma_start", out=wsb[:, 0:ncols], in_=ws_d[gidx, :, 0:ncols]), w=["wsb"], dma_slot="wsb")
        return wsb, "wsb"

    pcount = {"n": 0}

    def next_bank():
        i = pcount["n"] % 8
        pcount["n"] += 1
        return i

    def rmsnorm(gname, goff, t0, t1, dst_fn, dst_keys):
        nb = (t1 - t0) // 512
        for b in range(nb):
            lo = t0 + b * 512
            hi = lo + 512
            bk = next_bank()
            for c in range(8):
                q = sqb[c % 2]
                sc.add("act", I("activation", out=q[:], in_=xT[:, c, lo:hi], func=AF.Square),
                       r=[("xT", c)], w=[("sqb", c % 2)])
                sc.add("pe", I("matmul", pbank[bk][:], lhsT=ones_bf[:], rhs=q[:], start=(c == 0), stop=(c == 7)),
                       r=[("sqb", c % 2), "ones_bf"], w=[("pb", bk)])
            r_ = rst[b % 2]
            sc.add("act", I("activation", out=r_[:], in_=pbank[bk][:], func=AF.Ln, bias=C("epsc", 0, 1), scale=1.0),
                   r=[("pb", bk), "cst"], w=[("rst", 0)])
            sc.add("act", I("activation", out=r_[:], in_=r_[:], func=AF.Exp, scale=-0.5),
                   r=[("rst", 0)], w=[("rst", 0)])
            for c in range(8):
                eng = "dve"
                sc.add(eng, I("scalar_tensor_tensor",
                    out=dst_fn(c, lo, hi), in0=xT[:, c, lo:hi], scalar=C(gname, goff + c, 1), in1=r_[:], op0=ALU.mult, op1=ALU.mult),
                    r=[("xT", c), ("rst", 0), "cst"], w=dst_keys(c))

    def ffn(l):
        uTv = U[:, 16384:16384 + 16 * S].rearrange("p (n t) -> p n t", t=S)
        rtmp = [U[:, 49152 + i * 1024:49152 + (i + 1) * 1024].bitcast(F32) for i in range(2)]
        rmsnorm("g2", l * 8, 0, S, lambda c, lo, hi: hT[:, c, lo:hi], lambda c: [("hT", c)])
        cnt = 0
        for half in range(2):
            for g in range(4):
                w, wk = load_w(l * NFULL + 10 + half * 4 + g)
                wv = w[:].rearrange("p (n k c) -> p n k c", n=4, k=8)
                for n in range(4):
                    ch = 4 * g + n
                    for blk in range(4):
                        lo = blk * 512
                        bk = next_bank()
                        for kc in range(8):
                            sc.add("pe", I("matmul", pbank[bk][:], lhsT=wv[:, n, kc, :], rhs=hT[:, kc, lo:lo + 512], start=(kc == 0), stop=(kc == 7)),
                                   r=[wk, ("hT", kc)], w=[("pb", bk)])
                        rt = rtmp[cnt % 2]
                        sc.add("act", I("activation", out=rt[:], in_=pbank[bk][:], func=AF.Relu), r=[("pb", bk)], w=[("rtmp", cnt % 2)])
                        sc.add("pool", I("tensor_tensor", out=uTv[:, ch, lo:lo + 512], in0=rt[:], in1=rt[:], op=ALU.mult),
                               r=[("rtmp", cnt % 2)], w=[("uT", ch, blk)])
                        cnt += 1
            for m in range(8):
                w, wk = load_w(l * NFULL + 18 + m, ncols=2048, coloff=half * 2048)
                wv = w[:, 0:2048].rearrange("p (n c) -> p n c", n=16)
                for blk in range(4):
                    lo = blk * 512
                    bk = next_bank()
                    for n in range(16):
                        sc.add("pe", I("matmul", pbank[bk][:], lhsT=wv[:, n, :], rhs=uTv[:, n, lo:lo + 512], start=(n == 0), stop=(n == 15)),
                               r=[wk, ("uT", n, blk)], w=[("pb", bk)])
                    sc.add("dve", I("tensor_tensor", out=xT[:, m, lo:lo + 512], in0=xT[:, m, lo:lo + 512], in1=pbank[bk][:], op=ALU.add),
                           r=[("pb", bk), ("xT", m)], w=[("xT", m)])

    def proj_B(wv_n, wk, blk, bk):
        lo = blk * 512
        for kc in range(8):
            sc.add("pe", I("matmul", pbank[bk][:], lhsT=wv_n[:, kc, :], rhs=hT[:, kc, lo:lo + 512], start=(kc == 0), stop=(kc == 7)),
                   r=[wk, ("hT", kc)], w=[("pb", bk)])

    def proj_A(wv, wk, ncol, tok_ap_fn, out_ap, bk):
        for kc in range(8):
            sc.add("pe", I("matmul", out_ap, lhsT=tok_ap_fn(kc), rhs=wv[:, kc, 0:ncol], start=(kc == 0), stop=(kc == 7)),
                   r=[wk, ("hT", kc)], w=[("pb", bk)])

    def pbf(bk):
        return pbank[bk][:].bitcast(BF16)

    def gates(l):
        w, wk = load_wsmall(l * NSMALL + 0, 128)
        wv = w[:, 0:128].rearrange("p (k c) -> p k c", k=8)
        bk = next_bank()
        for t in range(16):
            proj_A(wv, wk, 16, lambda kc, t=t: hT[:, kc, t * P:(t + 1) * P], pbank[bk][:, t * 16:(t + 1) * 16], bk)
        gbv = C("gb", l * 16, 16).unsqueeze(1).to_broadcast([P, 16, 16])
        sc.add("dve", I("tensor_tensor", out=graw[:], in0=pbank[bk][:, 0:256].rearrange("p (t c) -> p t c", t=16), in1=gbv, op=ALU.add),
               r=[("pb", bk), "cst"], w=["graw"])
        gi_src = graw[:, :, 0:8].rearrange("p t (d h) -> p d t h", d=2)
        gf_src = graw[:, :, 8:16].rearrange("p t (d h) -> p d t h", d=2)
        sc.add("dve", I("tensor_scalar", out=G["gi"][:], in0=gi_src, scalar1=-0.5 * math.log(128.0), scalar2=0.0, op0=ALU.add, op1=ALU.add),
               r=["graw"], w=["G_gi"])
        sc.add("act", I("activation", out=G["w"][:], in_=gf_src, func=AF.Exp, scale=-1.0), r=["graw"], w=["G_w"])
        sc.add("act", I("activation", out=G["w"][:], in_=G["w"][:], func=AF.Ln, bias=C("epsc", 1, 1), scale=1.0), r=["G_w", "cst"], w=["G_w"])
        sc.add("dve", I("tensor_scalar", out=G["lf"][:], in0=G["w"][:], scalar1=-1.0, scalar2=0.0, op0=ALU.mult, op1=ALU.add),
               r=["G_w"], w=["G_lf"])
        bk2 = next_bank()
        for d_, Tn in ((0, "Tf"), (1, "Tb")):
            sc.add("pe", I("matmul", pbank[bk2][:, d_ * 64:(d_ + 1) * 64], lhsT=C(Tn), rhs=G["lf"][:, d_].rearrange("p t h -> p (t h)"), start=True, stop=True),
                   r=["G_lf", "cst"], w=[("pb", bk2)])
        sc.add("dve", I("tensor_copy", out=G["bcol"][:].rearrange("p d t h -> p (d t h)"), in_=pbank[bk2][:, 0:128]), r=[("pb", bk2)], w=["G_bcol"])
        sc.add("dve", I("tensor_tensor", out=G["col"][:], in0=G["gi"][:], in1=G["bcol"][:], op=ALU.subtract), r=["G_gi", "G_bcol"], w=["G_col"])
        bk3 = next_bank()
        for d_, Sn in ((0, "sel127"), (1, "sel0")):
            sc.add("pe", I("matmul", pbank[bk3][:, d_ * 64:(d_ + 1) * 64], lhsT=C(Sn), rhs=G["bcol"][:, d_].rearrange("p t h -> p (t h)"), start=True, stop=True),
                   r=["G_bcol", "cst"], w=[("pb", bk3)])
        sc.add("dve", I("tensor_copy", out=G["blast"][:].rearrange("p d t h -> p (d t h)"), in_=pbank[bk3][:, 0:128]), r=[("pb", bk3)], w=["G_blast"])
        sc.add("act", I("activation", out=G["decay"][:], in_=G["blast"][:], func=AF.Exp), r=["G_blast"], w=["G_decay"])
        sc.add("dve", I("tensor_tensor", out=G["w"][:], in0=G["col"][:], in1=G["blast"][:], op=ALU.add), r=["G_col", "G_blast"], w=["G_w"])
        sc.add("act", I("activation", out=G["w"][:], in_=G["w"][:], func=AF.Exp), r=["G_w"], w=["G_w"])

    def mlstm_pair(l, hp):
        cv = Carver(RBASE)
        mqk = cv.bf(4 * S).rearrange("p (n t) -> p n t", n=4)
        Vext = cv.bf(16 * 2 * 129).rearrange("p (t a c) -> p t a c", t=16, a=2)
        ogt = cv.bf(16 * 256).rearrange("p (t c) -> p t c", t=16)
        Cstb = cv.bf(16 * 2 * 129).rearrange("p (t a c) -> p t a c", t=16, a=2)
        tmp_base = cv.o
        pre = [cv.bf(2052), cv.bf(2052)]
        cv.o = tmp_base
        Tlf = [cv.f32(256).rearrange("p (a i) -> p a i", a=2) for _ in range(2)]
        rhs2 = [cv.f32(256).rearrange("p (a i) -> p a i", a=2) for _ in range(2)]
        Dx = [cv.f32(256) for _ in range(2)]
        eb = [cv.bf(256).rearrange("p (a i) -> p a i", a=2) for _ in range(2)]
        qs = [cv.bf(256).rearrange("p (a i) -> p a i", a=2) for _ in range(2)]
        PT = [cv.bf(256).rearrange("p (a i) -> p a i", a=2) for _ in range(2)]
        cv.o = max(cv.o, tmp_base + 2 * 2052)
        wK = cv.bf(256).rearrange("p (a c) -> p a c", a=2)
        Crun = [cv.f32(258).rearrange("p (a c) -> p a c", a=2) for _ in range(2)]
        Cf = cv.bf(258).rearrange("p (a c) -> p a c", a=2)
        hsum = cv.f32(256).rearrange("p (a c) -> p a c", a=2)
        sq = cv.f32(256).rearrange("p (a c) -> p a c", a=2)
        gog = cv.f32(256)
        halfg = cv.f32(256)
        ybf = cv.bf(256).rearrange("p (a c) -> p a c", a=2)
        rr = cv.f32(4)
        ssq = cv.f32(2)
        K = lambda nm: ("m", nm)

        w, wk = load_w(l * NFULL + 2 * hp)
        wv = w[:].rearrange("p (n k c) -> p n k c", n=4, k=8)
        for i in range(2):
            sc.add("pool", I("memset", pre[i][:, 0:2], 0.0), w=[K(("pre", i))])
            sc.add("pool", I("memset", pre[i][:, 2050:2052], 0.0), w=[K(("pre", i))])
        for n in range(4):
            c8 = (2 * hp + n) if n < 2 else (4 + 2 * hp + n - 2)
            pb_ = pre[n % 2]
            dg = diag[n % 2]
            for tau in range(5):
                sc.add("pool", I("tensor_tensor", out=dg[:, tau, :], in0=C("identF"), in1=C("convw", l * 40 + tau * 8 + c8, 1).to_broadcast([P, P]), op=ALU.mult),
                       r=["ident_bf", "cst"], w=[("diag", n % 2)])
            for blk in range(4):
                bk = next_bank()
                proj_B(wv[:, n], wk, blk, bk)
                sc.add("act", I("activation", out=pb_[:, 2 + blk * 512:2 + (blk + 1) * 512], in_=pbank[bk][:], func=AF.Copy),
                       r=[("pb", bk)], w=[K(("pre", n % 2))])
            for blk in range(4):
                bk = next_bank()
                for tau in range(5):
                    sc.add("pe", I("matmul", pbank[bk][:], lhsT=dg[:, tau, :], rhs=pb_[:, blk * 512 + tau:blk * 512 + tau + 512], start=(tau == 0), stop=(tau == 4)),
                           r=[K(("pre", n % 2)), ("diag", n % 2)], w=[("pb", bk)])
                sc.add("act", I("activation", out=mqk[:, n, blk * 512:(blk + 1) * 512], in_=pbank[bk][:], func=AF.Silu),
                       r=[("pb", bk)], w=[K(("mqk", n))])
        w, wk = load_w(l * NFULL + 2 * hp + 1)
        wv = w[:].rearrange("p (k c) -> p k c", k=8)
        sc.add("pool", I("memset", Vext[:, :, :, 128:129], 1.0), w=[K("Vext")])
        for t in range(16):
            bk = next_bank()
            proj_A(wv, wk, 512, lambda kc, t=t: hT[:, kc, t * P:(t + 1) * P], pbank[bk][:], bk)
            sc.add("act", I("activation", out=Vext[:, t, :, 0:128], in_=pbank[bk][:, 0:256].rearrange("p (a c) -> p a c", a=2), func=AF.Copy),
                   r=[("pb", bk)], w=[K("Vext")])
            sc.add("act", I("activation", out=ogt[:, t, :], in_=pbank[bk][:, 256:512], func=AF.Tanh, scale=0.5),
                   r=[("pb", bk)], w=[K("ogt")])
        ho = CST_OFF["hng"][0] + l * 512 + hp * 256
        sc.add("sp", I("dma_start", out=halfg[:], in_=cst_d[:, ho:ho + 256]), w=[K("halfg")], dma_slot="hng")
        sc.add("pool", I("tensor_scalar", out=halfg[:], in0=halfg[:], scalar1=0.5, scalar2=0.0, op0=ALU.mult, op1=ALU.add),
               r=[K("halfg")], w=[K("halfg")])

        def gsl(nm, d_, t):
            return G[nm][:, d_, t, 2 * hp:2 * hp + 2]

        def state_update(d_, t):
            bkT = next_bank()
            for a in range(2):
                sc.add("pe", I("transpose", pbf(bkT)[:, a * P:(a + 1) * P], mqk[:, 2 + a, t * P:(t + 1) * P], ident_bf[:]),
                       r=[K(("mqk", 2 + a)), "ident_bf"], w=[("pb", bkT)])
            sc.add("dve", I("tensor_tensor", out=wK[:], in0=pbf(bkT)[:, 0:256].rearrange("p (a c) -> p a c", a=2),
                                                     in1=gsl("w", d_, t).unsqueeze(2).to_broadcast([P, 2, P]), op=ALU.mult),
                   r=[("pb", bkT), "G_w"], w=[K("wK")])
            bkD = next_bank()
            for a in range(2):
                sc.add("pe", I("matmul", pbank[bkD][:, a * 129:(a + 1) * 129], lhsT=wK[:, a, :], rhs=Vext[:, t, a, :], start=True, stop=True),
                       r=[K("wK"), K("Vext")], w=[("pb", bkD)])
            for a in range(2):
                sc.add("dve", I("scalar_tensor_tensor", out=Crun[d_][:, a, :], in0=Crun[d_][:, a, :], scalar=G["decay"][:, d_, t, 2 * hp + a:2 * hp + a + 1],
                                                                    in1=pbank[bkD][:, a * 129:(a + 1) * 129], op0=ALU.mult, op1=ALU.add),
                       r=[("pb", bkD), "G_decay", K(("Crun", d_))], w=[K(("Crun", d_))])

        sc.barrier()
        for d_ in range(2):
            sc.add("pool", I("memset", Crun[d_][:], 0.0), w=[K(("Crun", d_))])
        for t in range(15, -1, -1):
            sc.add("act", I("activation", out=Cstb[:, t], in_=Crun[1][:], func=AF.Copy), r=[K(("Crun", 1))], w=[K("Cstb")])
            if t > 0:
                state_update(1, t)

        def state_update_f(t):
            for a in range(2):
                sc.add("pe", I("transpose", pbf(5)[:, a * P:(a + 1) * P], mqk[:, 2 + a, t * P:(t + 1) * P], ident_bf[:]),
                       r=[K(("mqk", 2 + a)), "ident_bf"], w=[("pb", 5)])
            sc.add("dve", I("tensor_tensor", out=wK[:], in0=pbf(5)[:, 0:256].rearrange("p (a c) -> p a c", a=2),
                            in1=gsl("w", 0, t).unsqueeze(2).to_broadcast([P, 2, P]), op=ALU.mult),
                   r=[("pb", 5), "G_w"], w=[K("wK")])
            for a in range(2):
                sc.add("pe", I("matmul", pbank[7][:, a * 129:(a + 1) * 129], lhsT=wK[:, a, :], rhs=Vext[:, t, a, :], start=True, stop=True),
                       r=[K("wK"), K("Vext")], w=[("pb", 7)])
            for a in range(2):
                sc.add("dve", I("scalar_tensor_tensor", out=Crun[0][:, a, :], in0=Crun[0][:, a, :], scalar=G["decay"][:, 0, t, 2 * hp + a:2 * hp + a + 1],
                                in1=pbank[7][:, a * 129:(a + 1) * 129], op0=ALU.mult, op1=ALU.add),
                       r=[("pb", 7), "G_decay", K(("Crun", 0))], w=[K(("Crun", 0))])

        def y_transposes(t):
            for a in range(2):
                sc.add("pe", I("transpose", pbf(6)[:, a * P:(a + 1) * P], ybf[:, a, :], ident_bf[:]),
                       r=[K("ybf"), "ident_bf"], w=[("pb", 6)])
            sc.add("act", I("activation", out=yT[:, 2 * hp:2 * hp + 2, t * P:(t + 1) * P], in_=pbf(6)[:, 0:256].rearrange("p (a c) -> p a c", a=2), func=AF.Copy),
                   r=[("pb", 6)], w=[("yT", 2 * hp), ("yT", 2 * hp + 1)])

        def stA(t):
            tl = slice(t * P, (t + 1) * P)
            for a in range(2):
                sc.add("pe", I("matmul", pbank[0][:, a * P:(a + 1) * P], lhsT=mqk[:, 2 + a, tl], rhs=mqk[:, a, tl], start=True, stop=True),
                       r=[K(("mqk", a)), K(("mqk", 2 + a))], w=[("pb", 0)])
            for d_ in range(2):
                Tn, Nn = ("Tf", "NEGf") if d_ == 0 else ("Tb", "NEGb")
                sc.add("pool", I("tensor_tensor", out=Tlf[d_][:], in0=C(Tn).unsqueeze(1).to_broadcast([P, 2, P]),
                                 in1=gsl("lf", d_, t).unsqueeze(2).to_broadcast([P, 2, P]), op=ALU.mult),
                       r=["cst", "G_lf"], w=[K(("Tlf", d_))])
                sc.add("pool", I("tensor_tensor", out=rhs2[d_][:], in0=C(Nn).unsqueeze(1).to_broadcast([P, 2, P]),
                                 in1=gsl("col", d_, t).unsqueeze(2).to_broadcast([P, 2, P]), op=ALU.add),
                       r=["cst", "G_col"], w=[K(("rhs2", d_))])
            for d_ in range(2):
                bk_ = 3 + d_
                Tl2 = Tlf[d_][:].rearrange("p a i -> p (a i)")
                sc.add("pe", I("matmul", pbank[bk_][:, 0:256], lhsT=C("onesF"), rhs=Tl2, start=True, stop=True),
                       r=["cst", K(("Tlf", d_))], w=[("pb", bk_)])
                sc.add("pe", I("matmul", pbank[bk_][:, 256:512], lhsT=C("onesF"), rhs=Tl2, start=True, stop=False),
                       r=["cst", K(("Tlf", d_))], w=[("pb", bk_)])
                sc.add("pe", I("matmul", pbank[bk_][:, 256:512], lhsT=C("identF"), rhs=rhs2[d_][:].rearrange("p a i -> p (a i)"), start=False, stop=True),
                       r=["cst", K(("rhs2", d_))], w=[("pb", bk_)])

        def stB(t):
            tl = slice(t * P, (t + 1) * P)
            for d_ in range(2):
                bk_ = 3 + d_
                sc.add("act", I("activation", out=Dx[d_][:], in_=pbank[bk_][:, 256:512], func=AF.Exp),
                       r=[("pb", bk_)], w=[K(("Dx", d_))])
                sc.add("act", I("activation", out=eb[d_][:].rearrange("p a i -> p (a i)"), in_=pbank[bk_][:, 0:256], func=AF.Exp),
                       r=[("pb", bk_)], w=[K(("eb", d_))])
                sc.add("dve", I("tensor_tensor", out=PT[d_][:].rearrange("p a i -> p (a i)"), in0=pbank[0][:, 0:256], in1=Dx[d_][:], op=ALU.mult),
                       r=[("pb", 0), K(("Dx", d_))], w=[K(("PT", d_))])
                sc.add("dve", I("tensor_tensor", out=qs[d_][:], in0=mqk[:, 0:2, tl], in1=eb[d_][:], op=ALU.mult),
                       r=[K(("mqk", 0)), K(("mqk", 1)), K(("eb", d_))], w=[K(("qs", d_))])

        def stC(t):
            bkO = 1 + (t % 2)
            for d_ in range(2):
                for a in range(2):
                    cprev = Cf[:, a, :] if d_ == 0 else Cstb[:, t, a, :]
                    ck = K("Cf") if d_ == 0 else K("Cstb")
                    oc = (a * 2 + d_) * P
                    sc.add("pe", I("matmul", pbank[bkO][:, oc:oc + P], lhsT=PT[d_][:, a, :], rhs=Vext[:, t, a, 0:128], start=True, stop=False),
                           r=[K(("PT", d_)), K("Vext")], w=[("pb", bkO)])
                    sc.add("pe", I("matmul", pbank[bkO][:, oc:oc + P], lhsT=qs[d_][:, a, :], rhs=cprev[:, 0:128], start=False, stop=True),
                           r=[K(("qs", d_)), ck], w=[("pb", bkO)])
            for d_ in range(2):
                for a in range(2):
                    cprev = Cf[:, a, :] if d_ == 0 else Cstb[:, t, a, :]
                    ck = K("Cf") if d_ == 0 else K("Cstb")
                    dc = 258 + a * 2 + d_
                    sc.add("pe", I("matmul", pbank[7][:, dc:dc + 1], lhsT=PT[d_][:, a, :], rhs=Vext[:, t, a, 128:129], start=True, stop=False),
                           r=[K(("PT", d_)), K("Vext")], w=[("pb", 7)])
                    sc.add("pe", I("matmul", pbank[7][:, dc:dc + 1], lhsT=qs[d_][:, a, :], rhs=cprev[:, 128:129], start=False, stop=True),
                           r=[K(("qs", d_)), ck], w=[("pb", 7)])

        def stD(t):
            bkO = 1 + (t % 2)
            rk_ = K("rr")
            sc.add("act", I("activation", out=rr[:, 0:4], in_=pbank[7][:, 258:262], func=AF.Abs), r=[("pb", 7)], w=[rk_])
            sc.add("dve", I("tensor_scalar", out=rr[:, 0:4], in0=rr[:, 0:4], scalar1=1.0, scalar2=0.0, op0=ALU.max, op1=ALU.add), r=[rk_], w=[rk_])
            sc.add("dve", I("reciprocal", out=rr[:, 0:4], in_=rr[:, 0:4]), r=[rk_], w=[rk_])
            for a in range(2):
                oc = a * 2 * P
                sc.add("act", I("activation", out=hsum[:, a, :], in_=pbank[bkO][:, oc:oc + P], func=AF.Copy, scale=rr[:, 2 * a:2 * a + 1]),
                       r=[("pb", bkO), rk_], w=[K(("hsum", a))])
            for a in range(2):
                oc = a * 2 * P
                sc.add("dve", I("scalar_tensor_tensor", out=hsum[:, a, :], in0=pbank[bkO][:, oc + P:oc + 2 * P], scalar=rr[:, 2 * a + 1:2 * a + 2], in1=hsum[:, a, :], op0=ALU.mult, op1=ALU.add),
                       r=[("pb", bkO), rk_, K(("hsum", a))], w=[K(("hsum", a))])
            for a in range(2):
                sc.add("act", I("activation", out=sq[:, a, :], in_=hsum[:, a, :], func=AF.Square, accum_out=ssq[:, a:a + 1]),
                       r=[K(("hsum", a))], w=[K(("sq", a)), K(("ssq", a))])
            hk = [K(("hsum", 0)), K(("hsum", 1))]
            sk = [K(("ssq", 0)), K(("ssq", 1))]
            sc.add("act", I("activation", out=ssq[:], in_=ssq[:], func=AF.Ln, bias=C("epsc", 0, 1), scale=1.0 / 128.0), r=sk + ["cst"], w=sk)
            sc.add("act", I("activation", out=ssq[:], in_=ssq[:], func=AF.Exp, scale=-0.5), r=sk, w=sk)
            sc.add("dve", I("scalar_tensor_tensor", out=gog[:], in0=ogt[:, t, :], scalar=1.0, in1=halfg[:], op0=ALU.add, op1=ALU.mult),
                   r=[K("ogt"), K("halfg")], w=[K("gog")])
            sc.add("dve", I("tensor_tensor", out=sq[:], in0=hsum[:], in1=ssq[:].unsqueeze(2).to_broadcast([P, 2, P]), op=ALU.mult),
                   r=hk + sk + [K(("sq", 0)), K(("sq", 1))], w=[K(("sq", 0)), K(("sq", 1))])
            sc.add("pool", I("tensor_tensor", out=ybf[:].rearrange("p a c -> p (a c)"), in0=sq[:].rearrange("p a c -> p (a c)"), in1=gog[:], op=ALU.mult),
                   r=[K(("sq", 0)), K(("sq", 1)), K("gog")], w=[K("ybf")])

        def cf_copy():
            sc.add("act", I("activation", out=Cf[:], in_=Crun[0][:], func=AF.Copy), r=[K(("Crun", 0))], w=[K("Cf")])

        stA(0)
        stB(0)
        cf_copy()
        for t in range(16):
            stC(t)
            if t < 15:
                state_update_f(t)
                cf_copy()
                stA(t + 1)
                stB(t + 1)
            if t > 0:
                y_transposes(t - 1)
            stD(t)
        y_transposes(15)
        sc.barrier()

    def attn_pair(l, hp):
        cv = Carver(RBASE)
        aq = cv.bf(S)
        ak = cv.bf(S)
        Vp = [cv.bf(16 * P).rearrange("p (t c) -> p t c", t=16) for _ in range(3)]
        accn = cv.f32(S)
        accd = cv.f32(S)
        qP = cv.bf(S)
        kP = cv.bf(S)
        PtP = [cv.bf(512).rearrange("p (a c) -> p a c", a=2) for _ in range(3)]
        ropeb = [cv.f32(1024).rearrange("p (k t) -> p k t", k=2) for _ in range(2)]
        t1 = [cv.f32(512)] * 2
        t2 = [cv.f32(512)] * 2
        K = lambda nm: ("a", nm)
        DILS = (1, 4, 16)
        scnt = {"n": 0}
        wsm_, wkv = load_wsmall(l * NSMALL + 1 + hp, 1024)
        wvv = wsm_[:, 0:1024].rearrange("p (k c) -> p k c", k=8)
        w, wk = load_w(l * NFULL + 4 + hp)
        wv = w[:].rearrange("p (n k c) -> p n k c", n=4, k=8)
        for qi, dst in ((0, aq), (1, ak)):
            for blk in range(4):
                lo = blk * 512
                rb = ropeb[blk % 2]
                sc.add("sp", I("dma_start", out=rb[:], in_=rope_d[:, :, lo:lo + 512].rearrange("k p t -> p k t")), w=[K(("rope", blk % 2))], dma_slot=("rope", blk % 2))
                bk0 = next_bank()
                proj_B(wv[:, 2 * qi], wk, blk, bk0)
                bk1 = next_bank()
                proj_B(wv[:, 2 * qi + 1], wk, blk, bk1)
                sc.add("dve", I("tensor_tensor", out=t1[blk % 2][:], in0=pbank[bk0][:], in1=rb[:, 0, :], op=ALU.mult),
                       r=[("pb", bk0), K(("rope", blk % 2))], w=[K(("t1", 0))])
                sc.add("dve", I("tensor_tensor", out=t2[blk % 2][:], in0=pbank[bk1][:], in1=rb[:, 1, :], op=ALU.mult),
                       r=[("pb", bk1), K(("rope", blk % 2))], w=[K(("t2", 0))])
                sc.add("pool", I("tensor_tensor", out=dst[:, lo:lo + 512], in0=t1[blk % 2][:], in1=t2[blk % 2][:], op=ALU.add),
                       r=[K(("t1", 0)), K(("t2", 0))], w=[K(("qk", qi))])
        VT = qP
        if not cfg.get("vt", True):
            for di, d_ in enumerate(DILS):
                nb = 16 // d_
                for g in range(4):
                    bk = next_bank()
                    for tt in range(4):
                        tp = 4 * g + tt
                        r_, lb = tp // nb, tp % nb
                        st = d_ * P * lb + r_

                        def tok(kc, st=st, d_=d_):
                            return hT[:, kc, st:st + d_ * (P - 1) + 1:d_]
                        proj_A(wvv, wkv, P, tok, pbank[bk][:, tt * P:(tt + 1) * P], bk)
                    sc.add("act", I("activation", out=Vp[di][:, 4 * g:4 * g + 4, :], in_=pbank[bk][:].rearrange("p (t c) -> p t c", t=4), func=AF.Copy),
                           r=[("pb", bk)], w=[K(("V", di))])
        for blk in (range(4) if cfg.get("vt", True) else []):
            bk = next_bank()
            lo = blk * 512
            for kc in range(8):
                sc.add("pe", I("matmul", pbank[bk][:], lhsT=wvv[:, kc, :], rhs=hT[:, kc, lo:lo + 512], start=(kc == 0), stop=(kc == 7)),
                       r=[wkv, ("hT", kc)], w=[("pb", bk)])
            sc.add("act", I("activation", out=VT[:, lo:lo + 512], in_=pbank[bk][:], func=AF.Copy), r=[("pb", bk)], w=[K(("qP", blk))])
        for di, d_ in (enumerate(DILS) if cfg.get("vt", True) else []):
            nb = 16 // d_
            if d_ == 1:
                src, skeys = VT, [K(("qP", c_)) for c_ in range(4)]
            else:
                if d_ == 4:
                    sc.add("act", I("activation", out=kP[:].rearrange("p (r l) -> p r l", r=d_), in_=VT[:].rearrange("p (l r) -> p r l", r=d_), func=AF.Copy),
                           r=[K(("qP", c_)) for c_ in range(4)], w=[K(("kP", c_)) for c_ in range(4)])
                else:
                    sc.add("dve", I("tensor_copy", out=kP[:].rearrange("p (r l) -> p r l", r=d_), in_=VT[:].rearrange("p (l r) -> p r l", r=d_)),
                           r=[K(("qP", c_)) for c_ in range(4)], w=[K(("kP", c_)) for c_ in range(4)])
                src, skeys = kP, [K(("kP", c_)) for c_ in range(4)]
            for g in range(4):
                bk = next_bank()
                for tt in range(4):
                    tp = 4 * g + tt
                    sc.add("pe", I("transpose", pbf(bk)[:, tt * P:(tt + 1) * P], src[:, tp * P:(tp + 1) * P], ident_bf[:]),
                           r=skeys + ["ident_bf"], w=[("pb", bk)])
                if g % 2 == 0:
                    sc.add("act", I("activation", out=Vp[di][:, 4 * g:4 * g + 4, :], in_=pbf(bk)[:, 0:512].rearrange("p (t c) -> p t c", t=4), func=AF.Copy),
                           r=[("pb", bk)], w=[K(("V", di))])
                else:
                    sc.add("dve", I("tensor_copy", out=Vp[di][:, 4 * g:4 * g + 4, :], in_=pbf(bk)[:, 0:512].rearrange("p (t c) -> p t c", t=4)),
                           r=[("pb", bk)], w=[K(("V", di))])

        def perm_copy(d_, c):
            if d_ == 4:
                qi_ = aq[:].rearrange("p (l r) -> p r l", r=4)[:, c, :]
                ki_ = ak[:].rearrange("p (l r) -> p r l", r=4)[:, c, :]
                qo_, ko_ = qP[:, c * 512:(c + 1) * 512], kP[:, c * 512:(c + 1) * 512]
            else:
                qi_ = aq[:].rearrange("p (l r) -> p r l", r=16)[:, 4 * c:4 * c + 4, :]
                ki_ = ak[:].rearrange("p (l r) -> p r l", r=16)[:, 4 * c:4 * c + 4, :]
                qo_ = qP[:, c * 512:(c + 1) * 512].rearrange("p (r l) -> p r l", r=4)
                ko_ = kP[:, c * 512:(c + 1) * 512].rearrange("p (r l) -> p r l", r=4)
            sc.add("pool", I("tensor_copy", out=qo_, in_=qi_), r=[K(("qk", 0))], w=[K(("qP", c))])
            sc.add("dve", I("tensor_copy", out=ko_, in_=ki_), r=[K(("qk", 1))], w=[K(("kP", c))])

        kts = []
        blk_last = {}
        for di, d_ in enumerate(DILS):
            nb = 16 // d_
            Ld = nb * P
            for r_ in range(d_):
                for kt in range(nb):
                    lo, hi = max(0, kt * P - 64), min(Ld, kt * P + 192)
                    u = dict(di=di, d=d_, T=r_ * nb + kt, c_lo=r_ * Ld + lo, c_hi=r_ * Ld + hi, m0=lo - (kt * P - 64))
                    for b in range(u["c_lo"] // 512, (u["c_hi"] - 1) // 512 + 1):
                        blk_last[(di, b)] = len(kts)
                    kts.append(u)
        started = set()

        def views(u):
            if u["d"] == 1:
                return aq, ak, [K(("qk", 0))], [K(("qk", 1))]
            cq = range(u["c_lo"] // 512, (u["c_hi"] - 1) // 512 + 1)
            return qP, kP, [K(("qP", c_)) for c_ in cq], [K(("kP", u["T"] // 4))]

        def stageAB(j, u):
            d_ = u["d"]
            if cfg.get("early", True):
                if d_ == 1 and u["T"] in (2, 5, 8, 11):
                    perm_copy(4, (u["T"] - 2) // 3)
                if d_ == 4 and u["T"] % 4 == 0 and u["T"] > 0:
                    perm_copy(16, u["T"] // 4 - 1)
                if d_ == 16 and u["T"] == 0:
                    perm_copy(16, 3)
            elif d_ > 1 and u["T"] == 0:
                for c_ in range(4):
                    perm_copy(d_, c_)
            qv, kv, rq, rk = views(u)
            T, nc_ = u["T"], u["c_hi"] - u["c_lo"]
            pt = PtP[j % 3]
            for a in range(2):
                rows = slice(64 * a, 64 * a + 64)
                bkS = 4 + ((2 * j + a) % 4)
                sc.add("pe", I("matmul", pbank[bkS][:, 0:nc_], lhsT=kv[rows, T * P:(T + 1) * P], rhs=qv[rows, u["c_lo"]:u["c_hi"]], start=True, stop=True),
                       r=rq + rk, w=[("pb", bkS)])
            for a in range(2):
                bkS = 4 + ((2 * j + a) % 4)
                sc.add("act", I("activation", out=pt[:, a, 0:nc_], in_=pbank[bkS][:, 0:nc_], func=AF.Exp, scale=0.125),
                       r=[("pb", bkS)], w=[K(("Pt", j % 3, a))])
            meng = "pool" if (j % 3 == 0 and nc_ == 256 and cfg.get("poolmask", True)) else "dve"
            sc.add(meng, I("tensor_tensor", out=pt[:, :, 0:nc_], in0=pt[:, :, 0:nc_], in1=mask2[:, :, u["m0"]:u["m0"] + nc_], op=ALU.mult),
                   r=[K(("Pt", j % 3, 0)), K(("Pt", j % 3, 1)), "mask2"], w=[K(("Pt", j % 3, 0)), K(("Pt", j % 3, 1))])

        def stageC(j, u):
            di, d_, T = u["di"], u["d"], u["T"]
            pt = PtP[j % 3]
            blks = list(range(u["c_lo"] // 512, (u["c_hi"] - 1) // 512 + 1))
            for b in blks:
                s_lo, s_hi = max(u["c_lo"], 512 * b), min(u["c_hi"], 512 * (b + 1))
                gi_ = di * 4 + b
                bkN, bkD = (0, 1) if gi_ % 2 == 0 else (2, 3)
                for tag in ("N", "D"):
                    for a in range(2):
                        rows = slice(64 * a, 64 * a + 64)
                        mv = pt[:, a, s_lo - u["c_lo"]:s_hi - u["c_lo"]]
                        if tag == "N":
                            bk_, lhs, rkeys = bkN, Vp[di][:, T, rows], [K(("V", di))]
                        else:
                            bk_, lhs, rkeys = bkD, ones1_bf[:, 0:64], ["ones1_bf"]
                        first = (gi_, a, tag) not in started
                        started.add((gi_, a, tag))
                        sc.add("pe", I("matmul", pbank[bk_][rows, s_lo - 512 * b:s_hi - 512 * b], lhsT=lhs, rhs=mv, start=first, stop=True, skip_group_check=True),
                               r=rkeys + [K(("Pt", j % 3, a))], w=[("pb", bk_)])
            for g in blks:
                if blk_last[(di, g)] != j:
                    continue
                gi_ = di * 4 + g
                bkN, bkD = (0, 1) if gi_ % 2 == 0 else (2, 3)
                if d_ == 1:
                    sc.add("act", I("activation", out=accn[:, g * 512:(g + 1) * 512], in_=pbank[bkN][:], func=AF.Copy), r=[("pb", bkN)], w=[K("accn")])
                    sc.add("dve", I("tensor_copy", out=accd[:, g * 512:(g + 1) * 512], in_=pbank[bkD][:]), r=[("pb", bkD)], w=[K("accd")])
                else:
                    if d_ == 4:
                        vn = accn[:].rearrange("p (l r) -> p r l", r=4)[:, g, :]
                        vd = accd[:].rearrange("p (l r) -> p r l", r=4)[:, g, :]
                        pn, pd = pbank[bkN][:], pbank[bkD][:]
                    else:
                        vn = accn[:].rearrange("p (l r) -> p r l", r=16)[:, 4 * g:4 * g + 4, :]
                        vd = accd[:].rearrange("p (l r) -> p r l", r=16)[:, 4 * g:4 * g + 4, :]
                        pn = pbank[bkN][:].rearrange("p (r l) -> p r l", r=4)
                        pd = pbank[bkD][:].rearrange("p (r l) -> p r l", r=4)
                    sc.add("dve", I("tensor_tensor", out=vn, in0=vn, in1=pn, op=ALU.add), r=[("pb", bkN), K("accn")], w=[K("accn")])
                    sc.add("dve", I("tensor_tensor", out=vd, in0=vd, in1=pd, op=ALU.add), r=[("pb", bkD), K("accd")], w=[K("accd")])

        for j in range(len(kts) + 1):
            if j < len(kts):
                stageAB(j, kts[j])
            if j - 1 >= 0:
                stageC(j - 1, kts[j - 1])
        sc.add("dve", I("reciprocal", out=accd[:], in_=accd[:]), r=[K("accd")], w=[K("accd")])
        sc.add("dve", I("tensor_tensor", out=yT[:, hp, :], in0=accn[:], in1=accd[:], op=ALU.mult), r=[K("accn"), K("accd")], w=[("yT", hp)])
        if hp == 3:
            sc.barrier()

    def out_proj(l, hf):
        w, wk = load_w(l * NFULL + 8 + hf)
        wv = w[:].rearrange("p (m k c) -> p m k c", m=8, k=4)
        for m in range(8):
            for blk in range(4):
                lo = blk * 512
                bk = next_bank()
                for kc in range(4):
                    sc.add("pe", I("matmul", pbank[bk][:], lhsT=wv[:, m, kc, :], rhs=yT[:, kc, lo:lo + 512], start=(kc == 0), stop=(kc == 3)),
                           r=[wk, ("yT", kc)], w=[("pb", bk)])
                sc.add("dve", I("tensor_tensor", out=xT[:, m, lo:lo + 512], in0=xT[:, m, lo:lo + 512], in1=pbank[bk][:], op=ALU.add),
                       r=[("pb", bk), ("xT", m)], w=[("xT", m)])

    for l in range(nlay):
        if cfg["mlstm"] or cfg["attn"]:
            rmsnorm("g1", l * 8, 0, S, lambda c, lo, hi: hT[:, c, lo:hi], lambda c: [("hT", c)])
        if cfg["mlstm"]:
            gates(l)
            for hp in range(2):
                mlstm_pair(l, hp)
            out_proj(l, 0)
        if cfg["attn"]:
            for hp in range(4):
                attn_pair(l, hp)
            out_proj(l, 1)
            sc.barrier()
        if cfg["ffn"]:
            ffn(l)
            sc.barrier()

    ov = out_d.rearrange("(c p) t -> p c t", p=P)
    sc.barrier()
    finT = U[:, 0:2048].bitcast(F32).rearrange("p (a t) -> p a t", a=2)
    fcnt = {"n": 0}

    def fin_dst(c, lo, hi):
        return finT[:, c % 2, :]

    for b in range(4):
        lo, hi = b * 512, (b + 1) * 512
        bk = next_bank()
        for c in range(8):
            q = sqb[c % 2]
            sc.add("act", I("activation", out=q[:], in_=xT[:, c, lo:hi], func=AF.Square),
                   r=[("xT", c)], w=[("sqb", c % 2)])
            sc.add("pe", I("matmul", pbank[bk][:], lhsT=ones_bf[:], rhs=q[:], start=(c == 0), stop=(c == 7)),
                   r=[("sqb", c % 2), "ones_bf"], w=[("pb", bk)])
        r_ = rst[b % 2]
        sc.add("act", I("activation", out=r_[:], in_=pbank[bk][:], func=AF.Ln, bias=C("epsc", 0, 1), scale=1.0),
               r=[("pb", bk), "cst"], w=[("rst", 0)])
        sc.add("act", I("activation", out=r_[:], in_=r_[:], func=AF.Exp, scale=-0.5),
               r=[("rst", 0)], w=[("rst", 0)])
        for c in range(8):
            sl = c % 2
            sc.add("dve", I("scalar_tensor_tensor",
                out=finT[:, sl, :], in0=xT[:, c, lo:hi], scalar=C("gf", c, 1), in1=r_[:], op0=ALU.mult, op1=ALU.mult),
                r=[("xT", c), ("rst", 0), "cst"], w=[("fin", sl)])
            sc.add("sp", I("dma_start", out=ov[:, c, lo:hi], in_=finT[:, sl, :]),
                   r=[("fin", sl)], w=[("out", sl)], dma_slot=("out", sl))
    sc.add("sp", None, r=[("out", 0), ("out", 1)])

    sc.emit(nc, es)
    es.close()
    return nc


_PREP_CACHE = {}


def kernel(x, norm1_g, w_in, conv_w, gate_i_b, gate_f_b, head_norm_g, w_out, norm2_g, w_up, w_down, final_g, _cfg=None):
    cfg = dict(CFG)
    if _cfg:
        cfg.update(_cfg)
    inp = dict(x=x, norm1_g=norm1_g, w_in=w_in, conv_w=conv_w, gate_i_b=gate_i_b, gate_f_b=gate_f_b,
               head_norm_g=head_norm_g, w_out=w_out, norm2_g=norm2_g, w_up=w_up, w_down=w_down, final_g=final_g)
    inp = {k: np.asarray(v) for k, v in inp.items()}
    wfull, wsmall = prep_weights(inp)
    cst = prep_consts(inp)
    rope = prep_rope()
    nc = build(cfg)
    xs = np.asarray(inp["x"], np.float32)
    in_maps = []
    for b in range(8):
        in_maps.append({"xT": np.ascontiguousarray(xs[b].T), "wfull": wfull, "wsmall": wsmall, "cst": cst, "rope": rope})
    res = run_bass_kernel_spmd(nc, in_maps, core_ids=list(range(8)))
    out = np.stack([np.ascontiguousarray(r["outT"].T) for r in res.results], axis=0)
    return out.astype(np.float32)
```

```python
import math
from contextlib import ExitStack

import numpy as np
import concourse.bass as bass
import concourse.mybir as mybir
from concourse.bass_utils import run_bass_kernel_spmd

F32 = mybir.dt.float32
BF16 = mybir.dt.bfloat16
ALU = mybir.AluOpType
AF = mybir.ActivationFunctionType
AX = mybir.AxisListType

P = 128
S = 2048
D = 1024
DFF = 4096
NL = 2
EPS = 1e-6
NFULL = 26
NSMALL = 5
TB = 1024

CFG = {"mlstm": True, "attn": True, "ffn": True, "layers": NL, "debug": False}


def I(name, *args, **kw):
    return (name, args, kw)


class Op:
    __slots__ = ("eng", "fn", "deps", "dma", "slot", "tok", "signal", "waits", "idx", "semkey", "semval")


class WK(tuple):
    def __new__(cls, slot):
        return tuple.__new__(cls, (("wb", slot, 0), ("wb", slot, 1)))


def _flat(keys):
    out = []
    for k in keys:
        if isinstance(k, WK):
            out.extend(k)
        else:
            out.append(k)
    return out


class Sched:
    EPOCH = 30000

    def __init__(self):
        self.ops = []
        self.last_w = {}
        self.readers = {}
        self.stream_len = {"pe": 0, "dve": 0, "act": 0, "pool": 0, "sp": 0}
        self.slot_cnt = {}

    def add(self, eng, fn, r=(), w=(), dma_slot=None):
        op = Op()
        op.eng, op.fn, op.dma, op.slot = eng, fn, dma_slot is not None, dma_slot
        op.signal = False
        op.waits = []
        op.idx = len(self.ops)
        r = _flat(r)
        w = _flat(w)
        deps = set()
        for k in r:
            if k in self.last_w:
                deps.add(self.last_w[k])
        for k in w:
            if k in self.last_w:
                deps.add(self.last_w[k])
            for j in self.readers.get(k, ()):
                deps.add(j)
        op.deps = deps
        if op.dma:
            c = self.slot_cnt.get(dma_slot, 0) + 1
            self.slot_cnt[dma_slot] = c
            op.tok = (("dma", dma_slot), c)
            self.stream_len[eng] += 1
        else:
            self.stream_len[eng] += 1
            op.tok = (("eng", eng), self.stream_len[eng])
        for k in r:
            self.readers.setdefault(k, []).append(op.idx)
        for k in w:
            self.last_w[k] = op.idx
            self.readers[k] = []
        self.ops.append(op)
        return op

    def barrier(self):
        last = {}
        for op in self.ops:
            if op.fn is not None:
                last[op.eng] = op.idx
        for e in list(self.stream_len):
            op = Op()
            op.eng, op.fn, op.dma, op.slot = e, None, False, None
            op.signal = False
            op.waits = []
            op.idx = len(self.ops)
            op.deps = set(v for k, v in last.items() if k != e)
            self.stream_len[e] += 1
            op.tok = (("eng", e), self.stream_len[e])
            self.ops.append(op)

    def analyse(self):
        seen = {e: {} for e in self.stream_len}
        for op in self.ops:
            sn = seen[op.eng]
            for j in sorted(op.deps):
                d = self.ops[j]
                if (not d.dma) and d.eng == "pe" and op.eng == "pe" and not op.dma:
                    continue
                fam, v = d.tok
                if sn.get(fam, 0) >= v:
                    continue
                sn[fam] = v
                d.signal = True
                op.waits.append(j)
        rank = {e: 0 for e in self.stream_len}
        self.nepoch = {e: 1 for e in self.stream_len}
        for op in self.ops:
            if op.dma:
                op.semkey = op.tok[0]
                op.semval = 16 * op.tok[1]
            elif op.signal:
                r = rank[op.eng]
                rank[op.eng] = r + 1
                ep = r // self.EPOCH
                self.nepoch[op.eng] = max(self.nepoch[op.eng], ep + 1)
                op.semkey = ("eng", op.eng, ep)
                op.semval = r - ep * self.EPOCH + 1

    def emit(self, nc, es):
        self.analyse()
        sems = {}

        def getsem(key):
            if key not in sems:
                nm = "s_" + "_".join(str(x) for x in key).replace("(", "").replace(")", "").replace(",", "_").replace(" ", "").replace("'", "")
                sems[key] = es.enter_context(nc.semaphore(nm))
            return sems[key]

        for op in self.ops:
            if op.dma or op.signal:
                getsem(op.semkey)
        block = es.enter_context(nc.Block())
        by_eng = {e: [o for o in self.ops if o.eng == e] for e in self.stream_len}

        def body(eng_name):
            def f(eng):
                for op in by_eng[eng_name]:
                    for j in op.waits:
                        d = self.ops[j]
                        eng.wait_ge(sems[d.semkey], d.semval)
                    if op.fn is None:
                        continue
                    name, args, kw = op.fn
                    ins = getattr(eng, name)(*args, **kw)
                    if op.dma:
                        ins.then_inc(sems[op.semkey], 16)
                    elif op.signal:
                        ins.then_inc(sems[op.semkey], 1)
            return f

        block.tensor(body("pe"))
        block.vector(body("dve"))
        block.scalar(body("act"))
        block.gpsimd(body("pool"))
        block.sync(body("sp"))


def _bform(w, cols):
    out = np.empty((P, 4, 8, P), np.float32)
    for n, ci in enumerate(cols):
        blk = w[:, ci]
        out[:, n] = blk.reshape(8, P, P).transpose(1, 0, 2)
    return out.reshape(P, 4096)


def _aform(w, ci):
    blk = w[:, ci]
    C = blk.shape[1]
    return blk.reshape(8, P, C).transpose(1, 0, 2).reshape(P, 8 * C)


def prep_weights(inp):
    wfull = np.zeros((NL * NFULL, P, 4096), np.float32)
    wsmall = np.zeros((NL * NSMALL, P, 1024), np.float32)
    ar = np.arange(P)
    for l in range(NL):
        w_in = np.asarray(inp["w_in"][l], np.float32)
        w_out = np.asarray(inp["w_out"][l], np.float32)
        w_up = np.asarray(inp["w_up"][l], np.float32)
        w_dn = np.asarray(inp["w_down"][l], np.float32)
        fb = l * NFULL
        sb = l * NSMALL
        for hp in range(2):
            h0, h1 = 2 * hp, 2 * hp + 1
            wfull[fb + 2 * hp] = _bform(w_in, [h0 * P + ar, h1 * P + ar, 512 + h0 * P + ar, 512 + h1 * P + ar])
            ci = np.concatenate([1024 + h0 * P + ar, 1024 + h1 * P + ar, 1536 + h0 * P + ar, 1536 + h1 * P + ar])
            wfull[fb + 2 * hp + 1] = _aform(w_in, ci)
        a = ar // 64
        dd = ar % 64
        sw = a * 64 + (dd + 32) % 64
        for hp in range(4):
            qb, kb = 2064 + hp * P, 2576 + hp * P
            wfull[fb + 4 + hp] = _bform(w_in, [qb + ar, qb + sw, kb + ar, kb + sw])
            wsmall[sb + 1 + hp] = _aform(w_in, 3088 + hp * P + ar)
        wsmall[sb + 0, :, :128] = _aform(w_in, 2048 + np.arange(16))
        for hf in range(2):
            blk = w_out[hf * 512:(hf + 1) * 512, :]
            t = blk.reshape(4, P, 8, P).transpose(1, 2, 0, 3)
            wfull[fb + 8 + hf] = t.reshape(P, 4096)
        for g in range(8):
            wfull[fb + 10 + g] = _bform(w_up, [(4 * g + n) * P + ar for n in range(4)])
        for m in range(8):
            blk = w_dn[:, m * P:(m + 1) * P]
            wfull[fb + 18 + m] = blk.reshape(32, P, P).transpose(1, 0, 2).reshape(P, 4096)
    return wfull, wsmall


def _cst_layout():
    off = {}
    o = 0

    def put(name, n):
        nonlocal o
        off[name] = (o, n)
        o += n
    put("g1", NL * 8)
    put("g2", NL * 8)
    put("gf", 8)
    put("convw", NL * 40)
    put("gb", NL * 16)
    for nm in ("Tf", "Tb", "NEGf", "NEGb", "identF", "onesF", "sel127", "sel0"):
        put(nm, P)
    put("epsc", 4)
    off["_NCS"] = (o, 0)
    put("hng", NL * 512)
    put("band", 512)
    return off, o


CST_OFF, NCST = _cst_layout()
NCS = CST_OFF["_NCS"][0]


def prep_consts(inp):
    c = np.zeros((P, NCST), np.float32)

    def setc(name, arr):
        o, n = CST_OFF[name]
        c[:, o:o + n] = arr.reshape(P, n)
    g1 = np.asarray(inp["norm1_g"], np.float32).reshape(NL, 8, P).transpose(2, 0, 1)
    g2 = np.asarray(inp["norm2_g"], np.float32).reshape(NL, 8, P).transpose(2, 0, 1)
    gf = np.asarray(inp["final_g"], np.float32).reshape(8, P).transpose(1, 0)
    setc("g1", np.ascontiguousarray(g1))
    setc("g2", np.ascontiguousarray(g2))
    setc("gf", np.ascontiguousarray(gf))
    cw = np.asarray(inp["conv_w"], np.float32).reshape(NL, 5, 8, P).transpose(3, 0, 1, 2)
    setc("convw", np.ascontiguousarray(cw))
    gb = np.concatenate([np.asarray(inp["gate_i_b"], np.float32), np.asarray(inp["gate_f_b"], np.float32)], axis=1)
    setc("gb", np.ascontiguousarray(np.broadcast_to(gb.reshape(1, NL * 16), (P, NL * 16))))
    hng = np.asarray(inp["head_norm_g"], np.float32).reshape(1, NL * 512)
    setc("hng", np.ascontiguousarray(np.broadcast_to(hng, (P, NL * 512))))
    k = np.arange(P)[:, None]
    i = np.arange(P)[None, :]
    NEG = -30000.0
    setc("Tf", (k <= i).astype(np.float32))
    setc("Tb", (k >= i).astype(np.float32))
    setc("NEGf", np.where(k <= i, 0.0, NEG).astype(np.float32))
    setc("NEGb", np.where(k >= i, 0.0, NEG).astype(np.float32))
    setc("identF", np.eye(P, dtype=np.float32))
    setc("onesF", np.ones((P, P), np.float32))
    s127 = np.zeros((P, P), np.float32)
    s127[127, :] = 1.0
    s0 = np.zeros((P, P), np.float32)
    s0[0, :] = 1.0
    setc("sel127", s127)
    setc("sel0", s0)
    cc = np.arange(256)[None, :]
    m1 = ((cc >= k) & (cc <= k + 128)).astype(np.float32)
    band = np.concatenate([m1, m1], axis=1)
    setc("band", band)
    ec = np.zeros((P, 4), np.float32)
    ec[:, 0] = EPS
    ec[:, 1] = 1.0
    ec[:, 2] = -0.5 * math.log(128.0)
    setc("epsc", ec)
    return c


def prep_rope():
    half = 32
    inv = (10000.0 ** (-np.arange(half, dtype=np.float32) / half)).astype(np.float32)
    ang = np.arange(S, dtype=np.float32)[:, None] * inv[None, :]
    cos = np.cos(ang).astype(np.float32).T
    sin = np.sin(ang).astype(np.float32).T
    cos64 = np.concatenate([cos, cos], 0)
    sin64 = np.concatenate([-sin, sin], 0)
    r = np.zeros((2, P, S), np.float32)
    r[0] = np.concatenate([cos64, cos64], 0)
    r[1] = np.concatenate([sin64, sin64], 0)
    return r


def build(cfg=CFG):
    nc = bass.Bass("TRN2", target_bir_lowering=False)
    nlay = cfg["layers"]
    xT_d = nc.dram_tensor("xT", [D, S], F32, kind="ExternalInput").ap()
    wf_d = nc.dram_tensor("wfull", [NL * NFULL, P, 4096], F32, kind="ExternalInput").ap()
    ws_d = nc.dram_tensor("wsmall", [NL * NSMALL, P, 1024], F32, kind="ExternalInput").ap()
    cst_d = nc.dram_tensor("cst", [P, NCST], F32, kind="ExternalInput").ap()
    rope_d = nc.dram_tensor("rope", [2, P, S], F32, kind="ExternalInput").ap()
    out_d = nc.dram_tensor("outT", [D, S], F32, kind="ExternalOutput").ap()

    es = ExitStack()
    sc = Sched()

    def sb(name, shape, dt=F32):
        return es.enter_context(nc.sbuf_tensor(name, shape, dt))

    def ps(name, shape, dt=F32):
        return es.enter_context(nc.psum_tensor(name, shape, dt))

    xT = sb("xT_sb", [P, 8, S])
    cst = sb("cst_sb", [P, NCS])
    NWB = 2
    wb = [sb(f"wb{i}", [P, 4096], BF16) for i in range(NWB)]
    wsb = sb("wsb", [P, 1024], BF16)
    NU = 54944
    U = sb("U", [P, NU], BF16)
    hT = U[:, 0:16384].rearrange("p (c t) -> p c t", c=8)
    yT = U[:, 16384:24576].rearrange("p (c t) -> p c t", c=4)
    RBASE = 24576

    class Carver:
        def __init__(self, base):
            self.o = base

        def bf(self, n):
            a = U[:, self.o:self.o + n]
            self.o += n + (n % 2)
            assert self.o <= NU
            return a

        def f32(self, n):
            a = U[:, self.o:self.o + 2 * n].bitcast(F32)
            self.o += 2 * n
            assert self.o <= NU
            return a

    sqb = [sb(f"sqb{i}", [P, 512], BF16) for i in range(2)]
    rst = [sb("rst0", [P, 512])] * 2
    ones_bf = sb("ones_bf", [P, P], BF16)
    ones1_bf = sb("ones1_bf", [P, 64], BF16)
    ident_bf = sb("ident_bf", [P, P], BF16)
    mask2 = sb("mask2", [P, 2, 256], BF16)
    diag = [sb(f"diag{i}", [P, 5, P], BF16) for i in range(2)]
    G = {nm: sb("G_" + nm, [P, 2, 16, 4]) for nm in ("gi", "lf", "bcol", "col", "blast", "w", "decay")}
    graw = sb("graw", [P, 16, 16])
    pbank = [ps(f"pb{i}", [P, 512]) for i in range(8)]

    def C(name, a=0, n=None):
        o, ln = CST_OFF[name]
        n = ln - a if n is None else n
        return cst[:, o + a:o + a + n]

    sc.add("sp", I("dma_start", out=cst[:], in_=cst_d[:, 0:NCS]), w=["cst"], dma_slot="cst")
    xv = xT_d.rearrange("(c p) t -> p c t", p=P)
    for c in range(8):
        sc.add("sp", I("dma_start", out=xT[:, c, :], in_=xv[:, c, :]), w=[("xT", c)], dma_slot=("x", c))
    sc.add("pool", I("memset", ones_bf[:], 1.0 / D), w=["ones_bf"])
    sc.add("pool", I("memset", ones1_bf[:], 1.0), w=["ones1_bf"])
    sc.add("dve", I("tensor_copy", out=ident_bf[:], in_=C("identF")), r=["cst"], w=["ident_bf"])
    bo = CST_OFF["band"][0]
    sc.add("pool", I("dma_start", out=mask2[:].rearrange("p a b -> p (a b)"), in_=cst_d[:, bo:bo + 512]), w=["mask2"], dma_slot="band")

    wstate = {"n": 0, "issued": 0}
    plan = []
    for l_ in range(nlay):
        fb_ = l_ * NFULL
        if cfg["mlstm"]:
            plan += [(fb_ + 0, 4096, 0), (fb_ + 1, 4096, 0), (fb_ + 2, 4096, 0), (fb_ + 3, 4096, 0), (fb_ + 8, 4096, 0)]
        if cfg["attn"]:
            plan += [(fb_ + 4 + hp_, 4096, 0) for hp_ in range(4)] + [(fb_ + 9, 4096, 0)]
        if cfg["ffn"]:
            for half_ in range(2):
                plan += [(fb_ + 10 + half_ * 4 + g_, 4096, 0) for g_ in range(4)]
                plan += [(fb_ + 18 + m_, 2048, half_ * 2048) for m_ in range(8)]

    def _issue(idx):
        gidx, ncols, coloff = plan[idx]
        slot = idx % NWB
        buf = wb[slot]
        for hlf in range(ncols // 2048):
            sc.add("pool", I("dma_start", out=buf[:, hlf * 2048:(hlf + 1) * 2048], in_=wf_d[gidx, :, coloff + hlf * 2048:coloff + (hlf + 1) * 2048]),
                   w=[("wb", slot, hlf)], dma_slot=("wb", slot, hlf))

    def load_w(gidx, ncols=4096, coloff=0):
        n = wstate["n"]
        assert plan[n] == (gidx, ncols, coloff), (n, plan[n], gidx, ncols, coloff)
        wstate["n"] = n + 1
        while wstate["issued"] <= min(n + 1, len(plan) - 1):
            _issue(wstate["issued"])
            wstate["issued"] += 1
        slot = n % NWB
        if ncols // 2048 == 1:
            return wb[slot], ("wb", slot, 0)
        return wb[slot], WK(slot)

    def load_wsmall(gidx, ncols=1024):
        sc.add("pool", I("dma_start", out=wsb[:, 0:ncols], in_=ws_d[gidx, :, 0:ncols]), w=["wsb"], dma_slot="wsb")
        return wsb, "wsb"

    pcount = {"n": 0}

    def next_bank():
        i = pcount["n"] % 8
        pcount["n"] += 1
        return i

    def rmsnorm(gname, goff, t0, t1, dst_fn, dst_keys):
        nb = (t1 - t0) // 512
        for b in range(nb):
            lo = t0 + b * 512
            hi = lo + 512
            bk = next_bank()
            for c in range(8):
                q = sqb[c % 2]
                sc.add("act", I("activation", out=q[:], in_=xT[:, c, lo:hi], func=AF.Square),
                       r=[("xT", c)], w=[("sqb", c % 2)])
                sc.add("pe", I("matmul", pbank[bk][:], lhsT=ones_bf[:], rhs=q[:], start=(c == 0), stop=(c == 7)),
                       r=[("sqb", c % 2), "ones_bf"], w=[("pb", bk)])
            r_ = rst[b % 2]
            sc.add("act", I("activation", out=r_[:], in_=pbank[bk][:], func=AF.Ln, bias=C("epsc", 0, 1), scale=1.0),
                   r=[("pb", bk), "cst"], w=[("rst", 0)])
            sc.add("act", I("activation", out=r_[:], in_=r_[:], func=AF.Exp, scale=-0.5),
                   r=[("rst", 0)], w=[("rst", 0)])
            for c in range(8):
                eng = "dve"
                sc.add(eng, I("scalar_tensor_tensor",
                    out=dst_fn(c, lo, hi), in0=xT[:, c, lo:hi], scalar=C(gname, goff + c, 1), in1=r_[:], op0=ALU.mult, op1=ALU.mult),
                    r=[("xT", c), ("rst", 0), "cst"], w=dst_keys(c))

    def ffn(l):
        uTv = U[:, 16384:16384 + 16 * S].rearrange("p (n t) -> p n t", t=S)
        rtmp = [U[:, 49152 + i * 1024:49152 + (i + 1) * 1024].bitcast(F32) for i in range(2)]
        rmsnorm("g2", l * 8, 0, S, lambda c, lo, hi: hT[:, c, lo:hi], lambda c: [("hT", c)])
        cnt = 0
        for half in range(2):
            for g in range(4):
                w, wk = load_w(l * NFULL + 10 + half * 4 + g)
                wv = w[:].rearrange("p (n k c) -> p n k c", n=4, k=8)
                for n in range(4):
                    ch = 4 * g + n
                    for blk in range(4):
                        lo = blk * 512
                        bk = next_bank()
                        for kc in range(8):
                            sc.add("pe", I("matmul", pbank[bk][:], lhsT=wv[:, n, kc, :], rhs=hT[:, kc, lo:lo + 512], start=(kc == 0), stop=(kc == 7)),
                                   r=[wk, ("hT", kc)], w=[("pb", bk)])
                        rt = rtmp[cnt % 2]
                        sc.add("act", I("activation", out=rt[:], in_=pbank[bk][:], func=AF.Relu), r=[("pb", bk)], w=[("rtmp", cnt % 2)])
                        sc.add("pool", I("tensor_tensor", out=uTv[:, ch, lo:lo + 512], in0=rt[:], in1=rt[:], op=ALU.mult),
                               r=[("rtmp", cnt % 2)], w=[("uT", ch, blk)])
                        cnt += 1
            for m in range(8):
                w, wk = load_w(l * NFULL + 18 + m, ncols=2048, coloff=half * 2048)
                wv = w[:, 0:2048].rearrange("p (n c) -> p n c", n=16)
                for blk in range(4):
                    lo = blk * 512
                    bk = next_bank()
                    for n in range(16):
                        sc.add("pe", I("matmul", pbank[bk][:], lhsT=wv[:, n, :], rhs=uTv[:, n, lo:lo + 512], start=(n == 0), stop=(n == 15)),
                               r=[wk, ("uT", n, blk)], w=[("pb", bk)])
                    sc.add("dve", I("tensor_tensor", out=xT[:, m, lo:lo + 512], in0=xT[:, m, lo:lo + 512], in1=pbank[bk][:], op=ALU.add),
                           r=[("pb", bk), ("xT", m)], w=[("xT", m)])

    def proj_B(wv_n, wk, blk, bk):
        lo = blk * 512
        for kc in range(8):
            sc.add("pe", I("matmul", pbank[bk][:], lhsT=wv_n[:, kc, :], rhs=hT[:, kc, lo:lo + 512], start=(kc == 0), stop=(kc == 7)),
                   r=[wk, ("hT", kc)], w=[("pb", bk)])

    def proj_A(wv, wk, ncol, tok_ap_fn, out_ap, bk):
        for kc in range(8):
            sc.add("pe", I("matmul", out_ap, lhsT=tok_ap_fn(kc), rhs=wv[:, kc, 0:ncol], start=(kc == 0), stop=(kc == 7)),
                   r=[wk, ("hT", kc)], w=[("pb", bk)])

    def pbf(bk):
        return pbank[bk][:].bitcast(BF16)

    def gates(l):
        w, wk = load_wsmall(l * NSMALL + 0, 128)
        wv = w[:, 0:128].rearrange("p (k c) -> p k c", k=8)
        bk = next_bank()
        for t in range(16):
            proj_A(wv, wk, 16, lambda kc, t=t: hT[:, kc, t * P:(t + 1) * P], pbank[bk][:, t * 16:(t + 1) * 16], bk)
        gbv = C("gb", l * 16, 16).unsqueeze(1).to_broadcast([P, 16, 16])
        sc.add("dve", I("tensor_tensor", out=graw[:], in0=pbank[bk][:, 0:256].rearrange("p (t c) -> p t c", t=16), in1=gbv, op=ALU.add),
               r=[("pb", bk), "cst"], w=["graw"])
        gi_src = graw[:, :, 0:8].rearrange("p t (d h) -> p d t h", d=2)
        gf_src = graw[:, :, 8:16].rearrange("p t (d h) -> p d t h", d=2)
        sc.add("dve", I("tensor_scalar", out=G["gi"][:], in0=gi_src, scalar1=-0.5 * math.log(128.0), scalar2=0.0, op0=ALU.add, op1=ALU.add),
               r=["graw"], w=["G_gi"])
        sc.add("act", I("activation", out=G["w"][:], in_=gf_src, func=AF.Exp, scale=-1.0), r=["graw"], w=["G_w"])
        sc.add("act", I("activation", out=G["w"][:], in_=G["w"][:], func=AF.Ln, bias=C("epsc", 1, 1), scale=1.0), r=["G_w", "cst"], w=["G_w"])
        sc.add("dve", I("tensor_scalar", out=G["lf"][:], in0=G["w"][:], scalar1=-1.0, scalar2=0.0, op0=ALU.mult, op1=ALU.add),
               r=["G_w"], w=["G_lf"])
        bk2 = next_bank()
        for d_, Tn in ((0, "Tf"), (1, "Tb")):
            sc.add("pe", I("matmul", pbank[bk2][:, d_ * 64:(d_ + 1) * 64], lhsT=C(Tn), rhs=G["lf"][:, d_].rearrange("p t h -> p (t h)"), start=True, stop=True),
                   r=["G_lf", "cst"], w=[("pb", bk2)])
        sc.add("dve", I("tensor_copy", out=G["bcol"][:].rearrange("p d t h -> p (d t h)"), in_=pbank[bk2][:, 0:128]), r=[("pb", bk2)], w=["G_bcol"])
        sc.add("dve", I("tensor_tensor", out=G["col"][:], in0=G["gi"][:], in1=G["bcol"][:], op=ALU.subtract), r=["G_gi", "G_bcol"], w=["G_col"])
        bk3 = next_bank()
        for d_, Sn in ((0, "sel127"), (1, "sel0")):
            sc.add("pe", I("matmul", pbank[bk3][:, d_ * 64:(d_ + 1) * 64], lhsT=C(Sn), rhs=G["bcol"][:, d_].rearrange("p t h -> p (t h)"), start=True, stop=True),
                   r=["G_bcol", "cst"], w=[("pb", bk3)])
        sc.add("dve", I("tensor_copy", out=G["blast"][:].rearrange("p d t h -> p (d t h)"), in_=pbank[bk3][:, 0:128]), r=[("pb", bk3)], w=["G_blast"])
        sc.add("act", I("activation", out=G["decay"][:], in_=G["blast"][:], func=AF.Exp), r=["G_blast"], w=["G_decay"])
        sc.add("dve", I("tensor_tensor", out=G["w"][:], in0=G["col"][:], in1=G["blast"][:], op=ALU.add), r=["G_col", "G_blast"], w=["G_w"])
        sc.add("act", I("activation", out=G["w"][:], in_=G["w"][:], func=AF.Exp), r=["G_w"], w=["G_w"])

    def mlstm_pair(l, hp):
        cv = Carver(RBASE)
        mqk = cv.bf(4 * S).rearrange("p (n t) -> p n t", n=4)
        Vext = cv.bf(16 * 2 * 129).rearrange("p (t a c) -> p t a c", t=16, a=2)
        ogt = cv.bf(16 * 256).rearrange("p (t c) -> p t c", t=16)
        Cstb = cv.bf(16 * 2 * 129).rearrange("p (t a c) -> p t a c", t=16, a=2)
        tmp_base = cv.o
        pre = [cv.bf(2052), cv.bf(2052)]
        cv.o = tmp_base
        Tlf = [cv.f32(256).rearrange("p (a i) -> p a i", a=2) for _ in range(2)]
        rhs2 = [cv.f32(256).rearrange("p (a i) -> p a i", a=2) for _ in range(2)]
        Dx = [cv.f32(256) for _ in range(2)]
        eb = [cv.bf(256).rearrange("p (a i) -> p a i", a=2) for _ in range(2)]
        qs = [cv.bf(256).rearrange("p (a i) -> p a i", a=2) for _ in range(2)]
        PT = [cv.bf(256).rearrange("p (a i) -> p a i", a=2) for _ in range(2)]
        cv.o = max(cv.o, tmp_base + 2 * 2052)
        wK = cv.bf(256).rearrange("p (a c) -> p a c", a=2)
        Crun = [cv.f32(258).rearrange("p (a c) -> p a c", a=2) for _ in range(2)]
        Cf = cv.bf(258).rearrange("p (a c) -> p a c", a=2)
        hsum = cv.f32(256).rearrange("p (a c) -> p a c", a=2)
        sq = cv.f32(256).rearrange("p (a c) -> p a c", a=2)
        gog = cv.f32(256)
        halfg = cv.f32(256)
        ybf = cv.bf(256).rearrange("p (a c) -> p a c", a=2)
        rr = cv.f32(4)
        ssq = cv.f32(2)
        K = lambda nm: ("m", nm)

        w, wk = load_w(l * NFULL + 2 * hp)
        wv = w[:].rearrange("p (n k c) -> p n k c", n=4, k=8)
        for i in range(2):
            sc.add("pool", I("memset", pre[i][:, 0:2], 0.0), w=[K(("pre", i))])
            sc.add("pool", I("memset", pre[i][:, 2050:2052], 0.0), w=[K(("pre", i))])
        for n in range(4):
            c8 = (2 * hp + n) if n < 2 else (4 + 2 * hp + n - 2)
            pb_ = pre[n % 2]
            dg = diag[n % 2]
            for tau in range(5):
                sc.add("pool", I("tensor_tensor", out=dg[:, tau, :], in0=C("identF"), in1=C("convw", l * 40 + tau * 8 + c8, 1).to_broadcast([P, P]), op=ALU.mult),
                       r=["ident_bf", "cst"], w=[("diag", n % 2)])
            for blk in range(4):
                bk = next_bank()
                proj_B(wv[:, n], wk, blk, bk)
                sc.add("act", I("activation", out=pb_[:, 2 + blk * 512:2 + (blk + 1) * 512], in_=pbank[bk][:], func=AF.Copy),
                       r=[("pb", bk)], w=[K(("pre", n % 2))])
            for blk in range(4):
                bk = next_bank()
                for tau in range(5):
                    sc.add("pe", I("matmul", pbank[bk][:], lhsT=dg[:, tau, :], rhs=pb_[:, blk * 512 + tau:blk * 512 + tau + 512], start=(tau == 0), stop=(tau == 4)),
                           r=[K(("pre", n % 2)), ("diag", n % 2)], w=[("pb", bk)])
                sc.add("act", I("activation", out=mqk[:, n, blk * 512:(blk + 1) * 512], in_=pbank[bk][:], func=AF.Silu),
                       r=[("pb", bk)], w=[K(("mqk", n))])
        w, wk = load_w(l * NFULL + 2 * hp + 1)
        wv = w[:].rearrange("p (k c) -> p k c", k=8)
        sc.add("pool", I("memset", Vext[:, :, :, 128:129], 1.0), w=[K("Vext")])
        for t in range(16):
            bk = next_bank()
            proj_A(wv, wk, 512, lambda kc, t=t: hT[:, kc, t * P:(t + 1) * P], pbank[bk][:], bk)
            sc.add("act", I("activation", out=Vext[:, t, :, 0:128], in_=pbank[bk][:, 0:256].rearrange("p (a c) -> p a c", a=2), func=AF.Copy),
                   r=[("pb", bk)], w=[K("Vext")])
            sc.add("act", I("activation", out=ogt[:, t, :], in_=pbank[bk][:, 256:512], func=AF.Tanh, scale=0.5),
                   r=[("pb", bk)], w=[K("ogt")])
        ho = CST_OFF["hng"][0] + l * 512 + hp * 256
        sc.add("sp", I("dma_start", out=halfg[:], in_=cst_d[:, ho:ho + 256]), w=[K("halfg")], dma_slot="hng")
        sc.add("pool", I("tensor_scalar", out=halfg[:], in0=halfg[:], scalar1=0.5, scalar2=0.0, op0=ALU.mult, op1=ALU.add),
               r=[K("halfg")], w=[K("halfg")])

        def gsl(nm, d_, t):
            return G[nm][:, d_, t, 2 * hp:2 * hp + 2]

        def state_update(d_, t):
            bkT = next_bank()
            for a in range(2):
                sc.add("pe", I("transpose", pbf(bkT)[:, a * P:(a + 1) * P], mqk[:, 2 + a, t * P:(t + 1) * P], ident_bf[:]),
                       r=[K(("mqk", 2 + a)), "ident_bf"], w=[("pb", bkT)])
            sc.add("dve", I("tensor_tensor", out=wK[:], in0=pbf(bkT)[:, 0:256].rearrange("p (a c) -> p a c", a=2),
                                                     in1=gsl("w", d_, t).unsqueeze(2).to_broadcast([P, 2, P]), op=ALU.mult),
                   r=[("pb", bkT), "G_w"], w=[K("wK")])
            bkD = next_bank()
            for a in range(2):
                sc.add("pe", I("matmul", pbank[bkD][:, a * 129:(a + 1) * 129], lhsT=wK[:, a, :], rhs=Vext[:, t, a, :], start=True, stop=True),
                       r=[K("wK"), K("Vext")], w=[("pb", bkD)])
            for a in range(2):
                sc.add("dve", I("scalar_tensor_tensor", out=Crun[d_][:, a, :], in0=Crun[d_][:, a, :], scalar=G["decay"][:, d_, t, 2 * hp + a:2 * hp + a + 1],
                                                                    in1=pbank[bkD][:, a * 129:(a + 1) * 129], op0=ALU.mult, op1=ALU.add),
                       r=[("pb", bkD), "G_decay", K(("Crun", d_))], w=[K(("Crun", d_))])

        sc.barrier()
        for d_ in range(2):
            sc.add("pool", I("memset", Crun[d_][:], 0.0), w=[K(("Crun", d_))])
        for t in range(15, -1, -1):
            sc.add("act", I("activation", out=Cstb[:, t], in_=Crun[1][:], func=AF.Copy), r=[K(("Crun", 1))], w=[K("Cstb")])
            if t > 0:
                state_update(1, t)

        def state_update_f(t):
            for a in range(2):
                sc.add("pe", I("transpose", pbf(5)[:, a * P:(a + 1) * P], mqk[:, 2 + a, t * P:(t + 1) * P], ident_bf[:]),
                       r=[K(("mqk", 2 + a)), "ident_bf"], w=[("pb", 5)])
            sc.add("dve", I("tensor_tensor", out=wK[:], in0=pbf(5)[:, 0:256].rearrange("p (a c) -> p a c", a=2),
                            in1=gsl("w", 0, t).unsqueeze(2).to_broadcast([P, 2, P]), op=ALU.mult),
                   r=[("pb", 5), "G_w"], w=[K("wK")])
            for a in range(2):
                sc.add("pe", I("matmul", pbank[7][:, a * 129:(a + 1) * 129], lhsT=wK[:, a, :], rhs=Vext[:, t, a, :], start=True, stop=True),
                       r=[K("wK"), K("Vext")], w=[("pb", 7)])
            for a in range(2):
                sc.add("dve", I("scalar_tensor_tensor", out=Crun[0][:, a, :], in0=Crun[0][:, a, :], scalar=G["decay"][:, 0, t, 2 * hp + a:2 * hp + a + 1],
                                in1=pbank[7][:, a * 129:(a + 1) * 129], op0=ALU.mult, op1=ALU.add),
                       r=[("pb", 7), "G_decay", K(("Crun", 0))], w=[K(("Crun", 0))])

        def y_transposes(t):
            for a in range(2):
                sc.add("pe", I("transpose", pbf(6)[:, a * P:(a + 1) * P], ybf[:, a, :], ident_bf[:]),
                       r=[K("ybf"), "ident_bf"], w=[("pb", 6)])
            sc.add("act", I("activation", out=yT[:, 2 * hp:2 * hp + 2, t * P:(t + 1) * P], in_=pbf(6)[:, 0:256].rearrange("p (a c) -> p a c", a=2), func=AF.Copy),
                   r=[("pb", 6)], w=[("yT", 2 * hp), ("yT", 2 * hp + 1)])

        def stA(t):
            tl = slice(t * P, (t + 1) * P)
            for a in range(2):
                sc.add("pe", I("matmul", pbank[0][:, a * P:(a + 1) * P], lhsT=mqk[:, 2 + a, tl], rhs=mqk[:, a, tl], start=True, stop=True),
                       r=[K(("mqk", a)), K(("mqk", 2 + a))], w=[("pb", 0)])
            for d_ in range(2):
                Tn, Nn = ("Tf", "NEGf") if d_ == 0 else ("Tb", "NEGb")
                sc.add("pool", I("tensor_tensor", out=Tlf[d_][:], in0=C(Tn).unsqueeze(1).to_broadcast([P, 2, P]),
                                 in1=gsl("lf", d_, t).unsqueeze(2).to_broadcast([P, 2, P]), op=ALU.mult),
                       r=["cst", "G_lf"], w=[K(("Tlf", d_))])
                sc.add("pool", I("tensor_tensor", out=rhs2[d_][:], in0=C(Nn).unsqueeze(1).to_broadcast([P, 2, P]),
                                 in1=gsl("col", d_, t).unsqueeze(2).to_broadcast([P, 2, P]), op=ALU.add),
                       r=["cst", "G_col"], w=[K(("rhs2", d_))])
            for d_ in range(2):
                bk_ = 3 + d_
                Tl2 = Tlf[d_][:].rearrange("p a i -> p (a i)")
                sc.add("pe", I("matmul", pbank[bk_][:, 0:256], lhsT=C("onesF"), rhs=Tl2, start=True, stop=True),
                       r=["cst", K(("Tlf", d_))], w=[("pb", bk_)])
                sc.add("pe", I("matmul", pbank[bk_][:, 256:512], lhsT=C("onesF"), rhs=Tl2, start=True, stop=False),
                       r=["cst", K(("Tlf", d_))], w=[("pb", bk_)])
                sc.add("pe", I("matmul", pbank[bk_][:, 256:512], lhsT=C("identF"), rhs=rhs2[d_][:].rearrange("p a i -> p (a i)"), start=False, stop=True),
                       r=["cst", K(("rhs2", d_))], w=[("pb", bk_)])

        def stB(t):
            tl = slice(t * P, (t + 1) * P)
            for d_ in range(2):
                bk_ = 3 + d_
                sc.add("act", I("activation", out=Dx[d_][:], in_=pbank[bk_][:, 256:512], func=AF.Exp),
                       r=[("pb", bk_)], w=[K(("Dx", d_))])
                sc.add("act", I("activation", out=eb[d_][:].rearrange("p a i -> p (a i)"), in_=pbank[bk_][:, 0:256], func=AF.Exp),
                       r=[("pb", bk_)], w=[K(("eb", d_))])
                sc.add("dve", I("tensor_tensor", out=PT[d_][:].rearrange("p a i -> p (a i)"), in0=pbank[0][:, 0:256], in1=Dx[d_][:], op=ALU.mult),
                       r=[("pb", 0), K(("Dx", d_))], w=[K(("PT", d_))])
                sc.add("dve", I("tensor_tensor", out=qs[d_][:], in0=mqk[:, 0:2, tl], in1=eb[d_][:], op=ALU.mult),
                       r=[K(("mqk", 0)), K(("mqk", 1)), K(("eb", d_))], w=[K(("qs", d_))])

        def stC(t):
            bkO = 1 + (t % 2)
            for d_ in range(2):
                for a in range(2):
                    cprev = Cf[:, a, :] if d_ == 0 else Cstb[:, t, a, :]
                    ck = K("Cf") if d_ == 0 else K("Cstb")
                    oc = (a * 2 + d_) * P
                    sc.add("pe", I("matmul", pbank[bkO][:, oc:oc + P], lhsT=PT[d_][:, a, :], rhs=Vext[:, t, a, 0:128], start=True, stop=False),
                           r=[K(("PT", d_)), K("Vext")], w=[("pb", bkO)])
                    sc.add("pe", I("matmul", pbank[bkO][:, oc:oc + P], lhsT=qs[d_][:, a, :], rhs=cprev[:, 0:128], start=False, stop=True),
                           r=[K(("qs", d_)), ck], w=[("pb", bkO)])
            for d_ in range(2):
                for a in range(2):
                    cprev = Cf[:, a, :] if d_ == 0 else Cstb[:, t, a, :]
                    ck = K("Cf") if d_ == 0 else K("Cstb")
                    dc = 258 + a * 2 + d_
                    sc.add("pe", I("matmul", pbank[7][:, dc:dc + 1], lhsT=PT[d_][:, a, :], rhs=Vext[:, t, a, 128:129], start=True, stop=False),
                           r=[K(("PT", d_)), K("Vext")], w=[("pb", 7)])
                    sc.add("pe", I("matmul", pbank[7][:, dc:dc + 1], lhsT=qs[d_][:, a, :], rhs=cprev[:, 128:129], start=False, stop=True),
                           r=[K(("qs", d_)), ck], w=[("pb", 7)])

        def stD(t):
            bkO = 1 + (t % 2)
            rk_ = K("rr")
            sc.add("act", I("activation", out=rr[:, 0:4], in_=pbank[7][:, 258:262], func=AF.Abs), r=[("pb", 7)], w=[rk_])
            sc.add("dve", I("tensor_scalar", out=rr[:, 0:4], in0=rr[:, 0:4], scalar1=1.0, scalar2=0.0, op0=ALU.max, op1=ALU.add), r=[rk_], w=[rk_])
            sc.add("dve", I("reciprocal", out=rr[:, 0:4], in_=rr[:, 0:4]), r=[rk_], w=[rk_])
            for a in range(2):
                oc = a * 2 * P
                sc.add("act", I("activation", out=hsum[:, a, :], in_=pbank[bkO][:, oc:oc + P], func=AF.Copy, scale=rr[:, 2 * a:2 * a + 1]),
                       r=[("pb", bkO), rk_], w=[K(("hsum", a))])
            for a in range(2):
                oc = a * 2 * P
                sc.add("dve", I("scalar_tensor_tensor", out=hsum[:, a, :], in0=pbank[bkO][:, oc + P:oc + 2 * P], scalar=rr[:, 2 * a + 1:2 * a + 2], in1=hsum[:, a, :], op0=ALU.mult, op1=ALU.add),
                       r=[("pb", bkO), rk_, K(("hsum", a))], w=[K(("hsum", a))])
            for a in range(2):
                sc.add("act", I("activation", out=sq[:, a, :], in_=hsum[:, a, :], func=AF.Square, accum_out=ssq[:, a:a + 1]),
                       r=[K(("hsum", a))], w=[K(("sq", a)), K(("ssq", a))])
            hk = [K(("hsum", 0)), K(("hsum", 1))]
            sk = [K(("ssq", 0)), K(("ssq", 1))]
            sc.add("act", I("activation", out=ssq[:], in_=ssq[:], func=AF.Ln, bias=C("epsc", 0, 1), scale=1.0 / 128.0), r=sk + ["cst"], w=sk)
            sc.add("act", I("activation", out=ssq[:], in_=ssq[:], func=AF.Exp, scale=-0.5), r=sk, w=sk)
            sc.add("dve", I("scalar_tensor_tensor", out=gog[:], in0=ogt[:, t, :], scalar=1.0, in1=halfg[:], op0=ALU.add, op1=ALU.mult),
                   r=[K("ogt"), K("halfg")], w=[K("gog")])
            sc.add("dve", I("tensor_tensor", out=sq[:], in0=hsum[:], in1=ssq[:].unsqueeze(2).to_broadcast([P, 2, P]), op=ALU.mult),
                   r=hk + sk + [K(("sq", 0)), K(("sq", 1))], w=[K(("sq", 0)), K(("sq", 1))])
            sc.add("pool", I("tensor_tensor", out=ybf[:].rearrange("p a c -> p (a c)"), in0=sq[:].rearrange("p a c -> p (a c)"), in1=gog[:], op=ALU.mult),
                   r=[K(("sq", 0)), K(("sq", 1)), K("gog")], w=[K("ybf")])

        def cf_copy():
            sc.add("act", I("activation", out=Cf[:], in_=Crun[0][:], func=AF.Copy), r=[K(("Crun", 0))], w=[K("Cf")])

        stA(0)
        stB(0)
        cf_copy()
        for t in range(16):
            stC(t)
            if t < 15:
                state_update_f(t)
                cf_copy()
                stA(t + 1)
                stB(t + 1)
            if t > 0:
                y_transposes(t - 1)
            stD(t)
        y_transposes(15)
        sc.barrier()

    def attn_pair(l, hp):
        cv = Carver(RBASE)
        aq = cv.bf(S)
        ak = cv.bf(S)
        Vp = [cv.bf(16 * P).rearrange("p (t c) -> p t c", t=16) for _ in range(3)]
        accn = cv.f32(S)
        accd = cv.f32(S)
        qP = cv.bf(S)
        kP = cv.bf(S)
        PtP = [cv.bf(512).rearrange("p (a c) -> p a c", a=2) for _ in range(3)]
        ropeb = [cv.f32(1024).rearrange("p (k t) -> p k t", k=2) for _ in range(2)]
        t1 = [cv.f32(512)] * 2
        t2 = [cv.f32(512)] * 2
        K = lambda nm: ("a", nm)
        DILS = (1, 4, 16)
        scnt = {"n": 0}
        wsm_, wkv = load_wsmall(l * NSMALL + 1 + hp, 1024)
        wvv = wsm_[:, 0:1024].rearrange("p (k c) -> p k c", k=8)
        w, wk = load_w(l * NFULL + 4 + hp)
        wv = w[:].rearrange("p (n k c) -> p n k c", n=4, k=8)
        for qi, dst in ((0, aq), (1, ak)):
            for blk in range(4):
                lo = blk * 512
                rb = ropeb[blk % 2]
                sc.add("sp", I("dma_start", out=rb[:], in_=rope_d[:, :, lo:lo + 512].rearrange("k p t -> p k t")), w=[K(("rope", blk % 2))], dma_slot=("rope", blk % 2))
                bk0 = next_bank()
                proj_B(wv[:, 2 * qi], wk, blk, bk0)
                bk1 = next_bank()
                proj_B(wv[:, 2 * qi + 1], wk, blk, bk1)
                sc.add("dve", I("tensor_tensor", out=t1[blk % 2][:], in0=pbank[bk0][:], in1=rb[:, 0, :], op=ALU.mult),
                       r=[("pb", bk0), K(("rope", blk % 2))], w=[K(("t1", 0))])
                sc.add("dve", I("tensor_tensor", out=t2[blk % 2][:], in0=pbank[bk1][:], in1=rb[:, 1, :], op=ALU.mult),
                       r=[("pb", bk1), K(("rope", blk % 2))], w=[K(("t2", 0))])
                sc.add("pool", I("tensor_tensor", out=dst[:, lo:lo + 512], in0=t1[blk % 2][:], in1=t2[blk % 2][:], op=ALU.add),
                       r=[K(("t1", 0)), K(("t2", 0))], w=[K(("qk", qi))])
        VT = qP
        if not cfg.get("vt", True):
            for di, d_ in enumerate(DILS):
                nb = 16 // d_
                for g in range(4):
                    bk = next_bank()
                    for tt in range(4):
                        tp = 4 * g + tt
                        r_, lb = tp // nb, tp % nb
                        st = d_ * P * lb + r_

                        def tok(kc, st=st, d_=d_):
                            return hT[:, kc, st:st + d_ * (P - 1) + 1:d_]
                        proj_A(wvv, wkv, P, tok, pbank[bk][:, tt * P:(tt + 1) * P], bk)
                    sc.add("act", I("activation", out=Vp[di][:, 4 * g:4 * g + 4, :], in_=pbank[bk][:].rearrange("p (t c) -> p t c", t=4), func=AF.Copy),
                           r=[("pb", bk)], w=[K(("V", di))])
        for blk in (range(4) if cfg.get("vt", True) else []):
            bk = next_bank()
            lo = blk * 512
            for kc in range(8):
                sc.add("pe", I("matmul", pbank[bk][:], lhsT=wvv[:, kc, :], rhs=hT[:, kc, lo:lo + 512], start=(kc == 0), stop=(kc == 7)),
                       r=[wkv, ("hT", kc)], w=[("pb", bk)])
            sc.add("act", I("activation", out=VT[:, lo:lo + 512], in_=pbank[bk][:], func=AF.Copy), r=[("pb", bk)], w=[K(("qP", blk))])
        for di, d_ in (enumerate(DILS) if cfg.get("vt", True) else []):
            nb = 16 // d_
            if d_ == 1:
                src, skeys = VT, [K(("qP", c_)) for c_ in range(4)]
            else:
                if d_ == 4:
                    sc.add("act", I("activation", out=kP[:].rearrange("p (r l) -> p r l", r=d_), in_=VT[:].rearrange("p (l r) -> p r l", r=d_), func=AF.Copy),
                           r=[K(("qP", c_)) for c_ in range(4)], w=[K(("kP", c_)) for c_ in range(4)])
                else:
                    sc.add("dve", I("tensor_copy", out=kP[:].rearrange("p (r l) -> p r l", r=d_), in_=VT[:].rearrange("p (l r) -> p r l", r=d_)),
                           r=[K(("qP", c_)) for c_ in range(4)], w=[K(("kP", c_)) for c_ in range(4)])
                src, skeys = kP, [K(("kP", c_)) for c_ in range(4)]
            for g in range(4):
                bk = next_bank()
                for tt in range(4):
                    tp = 4 * g + tt
                    sc.add("pe", I("transpose", pbf(bk)[:, tt * P:(tt + 1) * P], src[:, tp * P:(tp + 1) * P], ident_bf[:]),
                           r=skeys + ["ident_bf"], w=[("pb", bk)])
                if g % 2 == 0:
                    sc.add("act", I("activation", out=Vp[di][:, 4 * g:4 * g + 4, :], in_=pbf(bk)[:, 0:512].rearrange("p (t c) -> p t c", t=4), func=AF.Copy),
                           r=[("pb", bk)], w=[K(("V", di))])
                else:
                    sc.add("dve", I("tensor_copy", out=Vp[di][:, 4 * g:4 * g + 4, :], in_=pbf(bk)[:, 0:512].rearrange("p (t c) -> p t c", t=4)),
                           r=[("pb", bk)], w=[K(("V", di))])

        def perm_copy(d_, c):
            if d_ == 4:
                qi_ = aq[:].rearrange("p (l r) -> p r l", r=4)[:, c, :]
                ki_ = ak[:].rearrange("p (l r) -> p r l", r=4)[:, c, :]
                qo_, ko_ = qP[:, c * 512:(c + 1) * 512], kP[:, c * 512:(c + 1) * 512]
            else:
                qi_ = aq[:].rearrange("p (l r) -> p r l", r=16)[:, 4 * c:4 * c + 4, :]
                ki_ = ak[:].rearrange("p (l r) -> p r l", r=16)[:, 4 * c:4 * c + 4, :]
                qo_ = qP[:, c * 512:(c + 1) * 512].rearrange("p (r l) -> p r l", r=4)
                ko_ = kP[:, c * 512:(c + 1) * 512].rearrange("p (r l) -> p r l", r=4)
            sc.add("pool", I("tensor_copy", out=qo_, in_=qi_), r=[K(("qk", 0))], w=[K(("qP", c))])
            sc.add("dve", I("tensor_copy", out=ko_, in_=ki_), r=[K(("qk", 1))], w=[K(("kP", c))])

        kts = []
        blk_last = {}
        for di, d_ in enumerate(DILS):
            nb = 16 // d_
            Ld = nb * P
            for r_ in range(d_):
                for kt in range(nb):
                    lo, hi = max(0, kt * P - 64), min(Ld, kt * P + 192)
                    u = dict(di=di, d=d_, T=r_ * nb + kt, c_lo=r_ * Ld + lo, c_hi=r_ * Ld + hi, m0=lo - (kt * P - 64))
                    for b in range(u["c_lo"] // 512, (u["c_hi"] - 1) // 512 + 1):
                        blk_last[(di, b)] = len(kts)
                    kts.append(u)
        started = set()

        def views(u):
            if u["d"] == 1:
                return aq, ak, [K(("qk", 0))], [K(("qk", 1))]
            cq = range(u["c_lo"] // 512, (u["c_hi"] - 1) // 512 + 1)
            return qP, kP, [K(("qP", c_)) for c_ in cq], [K(("kP", u["T"] // 4))]

        def stageAB(j, u):
            d_ = u["d"]
            if cfg.get("early", True):
                if d_ == 1 and u["T"] in (2, 5, 8, 11):
                    perm_copy(4, (u["T"] - 2) // 3)
                if d_ == 4 and u["T"] % 4 == 0 and u["T"] > 0:
                    perm_copy(16, u["T"] // 4 - 1)
                if d_ == 16 and u["T"] == 0:
                    perm_copy(16, 3)
            elif d_ > 1 and u["T"] == 0:
                for c_ in range(4):
                    perm_copy(d_, c_)
            qv, kv, rq, rk = views(u)
            T, nc_ = u["T"], u["c_hi"] - u["c_lo"]
            pt = PtP[j % 3]
            for a in range(2):
                rows = slice(64 * a, 64 * a + 64)
                bkS = 4 + ((2 * j + a) % 4)
                sc.add("pe", I("matmul", pbank[bkS][:, 0:nc_], lhsT=kv[rows, T * P:(T + 1) * P], rhs=qv[rows, u["c_lo"]:u["c_hi"]], start=True, stop=True),
                       r=rq + rk, w=[("pb", bkS)])
            for a in range(2):
                bkS = 4 + ((2 * j + a) % 4)
                sc.add("act", I("activation", out=pt[:, a, 0:nc_], in_=pbank[bkS][:, 0:nc_], func=AF.Exp, scale=0.125),
                       r=[("pb", bkS)], w=[K(("Pt", j % 3, a))])
            meng = "pool" if (j % 3 == 0 and nc_ == 256 and cfg.get("poolmask", True)) else "dve"
            sc.add(meng, I("tensor_tensor", out=pt[:, :, 0:nc_], in0=pt[:, :, 0:nc_], in1=mask2[:, :, u["m0"]:u["m0"] + nc_], op=ALU.mult),
                   r=[K(("Pt", j % 3, 0)), K(("Pt", j % 3, 1)), "mask2"], w=[K(("Pt", j % 3, 0)), K(("Pt", j % 3, 1))])

        def stageC(j, u):
            di, d_, T = u["di"], u["d"], u["T"]
            pt = PtP[j % 3]
            blks = list(range(u["c_lo"] // 512, (u["c_hi"] - 1) // 512 + 1))
            for b in blks:
                s_lo, s_hi = max(u["c_lo"], 512 * b), min(u["c_hi"], 512 * (b + 1))
                gi_ = di * 4 + b
                bkN, bkD = (0, 1) if gi_ % 2 == 0 else (2, 3)
                for tag in ("N", "D"):
                    for a in range(2):
                        rows = slice(64 * a, 64 * a + 64)
                        mv = pt[:, a, s_lo - u["c_lo"]:s_hi - u["c_lo"]]
                        if tag == "N":
                            bk_, lhs, rkeys = bkN, Vp[di][:, T, rows], [K(("V", di))]
                        else:
                            bk_, lhs, rkeys = bkD, ones1_bf[:, 0:64], ["ones1_bf"]
                        first = (gi_, a, tag) not in started
                        started.add((gi_, a, tag))
                        sc.add("pe", I("matmul", pbank[bk_][rows, s_lo - 512 * b:s_hi - 512 * b], lhsT=lhs, rhs=mv, start=first, stop=True, skip_group_check=True),
                               r=rkeys + [K(("Pt", j % 3, a))], w=[("pb", bk_)])
            for g in blks:
                if blk_last[(di, g)] != j:
                    continue
                gi_ = di * 4 + g
                bkN, bkD = (0, 1) if gi_ % 2 == 0 else (2, 3)
                if d_ == 1:
                    sc.add("act", I("activation", out=accn[:, g * 512:(g + 1) * 512], in_=pbank[bkN][:], func=AF.Copy), r=[("pb", bkN)], w=[K("accn")])
                    sc.add("dve", I("tensor_copy", out=accd[:, g * 512:(g + 1) * 512], in_=pbank[bkD][:]), r=[("pb", bkD)], w=[K("accd")])
                else:
                    if d_ == 4:
                        vn = accn[:].rearrange("p (l r) -> p r l", r=4)[:, g, :]
                        vd = accd[:].rearrange("p (l r) -> p r l", r=4)[:, g, :]
                        pn, pd = pbank[bkN][:], pbank[bkD][:]
                    else:
                        vn = accn[:].rearrange("p (l r) -> p r l", r=16)[:, 4 * g:4 * g + 4, :]
                        vd = accd[:].rearrange("p (l r) -> p r l", r=16)[:, 4 * g:4 * g + 4, :]
                        pn = pbank[bkN][:].rearrange("p (r l) -> p r l", r=4)
                        pd = pbank[bkD][:].rearrange("p (r l) -> p r l", r=4)
                    sc.add("dve", I("tensor_tensor", out=vn, in0=vn, in1=pn, op=ALU.add), r=[("pb", bkN), K("accn")], w=[K("accn")])
                    sc.add("dve", I("tensor_tensor", out=vd, in0=vd, in1=pd, op=ALU.add), r=[("pb", bkD), K("accd")], w=[K("accd")])

        for j in range(len(kts) + 1):
            if j < len(kts):
                stageAB(j, kts[j])
            if j - 1 >= 0:
                stageC(j - 1, kts[j - 1])
        sc.add("dve", I("reciprocal", out=accd[:], in_=accd[:]), r=[K("accd")], w=[K("accd")])
        sc.add("dve", I("tensor_tensor", out=yT[:, hp, :], in0=accn[:], in1=accd[:], op=ALU.mult), r=[K("accn"), K("accd")], w=[("yT", hp)])
        if hp == 3:
            sc.barrier()

    def out_proj(l, hf):
        w, wk = load_w(l * NFULL + 8 + hf)
        wv = w[:].rearrange("p (m k c) -> p m k c", m=8, k=4)
        for m in range(8):
            for blk in range(4):
                lo = blk * 512
                bk = next_bank()
                for kc in range(4):
                    sc.add("pe", I("matmul", pbank[bk][:], lhsT=wv[:, m, kc, :], rhs=yT[:, kc, lo:lo + 512], start=(kc == 0), stop=(kc == 3)),
                           r=[wk, ("yT", kc)], w=[("pb", bk)])
                sc.add("dve", I("tensor_tensor", out=xT[:, m, lo:lo + 512], in0=xT[:, m, lo:lo + 512], in1=pbank[bk][:], op=ALU.add),
                       r=[("pb", bk), ("xT", m)], w=[("xT", m)])

    for l in range(nlay):
        if cfg["mlstm"] or cfg["attn"]:
            rmsnorm("g1", l * 8, 0, S, lambda c, lo, hi: hT[:, c, lo:hi], lambda c: [("hT", c)])
        if cfg["mlstm"]:
            gates(l)
            if l > 0:
                sc.barrier()
            for hp in range(2):
                mlstm_pair(l, hp)
            out_proj(l, 0)
        if cfg["attn"]:
            for hp in range(4):
                attn_pair(l, hp)
            out_proj(l, 1)
            sc.barrier()
        if cfg["ffn"]:
            ffn(l)
            if not cfg["mlstm"] or l == nlay - 1:
                sc.barrier()

    ov = out_d.rearrange("(c p) t -> p c t", p=P)
    sc.barrier()
    finT = U[:, 0:2048].bitcast(F32).rearrange("p (a t) -> p a t", a=2)
    fcnt = {"n": 0}

    def fin_dst(c, lo, hi):
        return finT[:, c % 2, :]

    for b in range(4):
        lo, hi = b * 512, (b + 1) * 512
        bk = next_bank()
        for c in range(8):
            q = sqb[c % 2]
            sc.add("act", I("activation", out=q[:], in_=xT[:, c, lo:hi], func=AF.Square),
                   r=[("xT", c)], w=[("sqb", c % 2)])
            sc.add("pe", I("matmul", pbank[bk][:], lhsT=ones_bf[:], rhs=q[:], start=(c == 0), stop=(c == 7)),
                   r=[("sqb", c % 2), "ones_bf"], w=[("pb", bk)])
        r_ = rst[b % 2]
        sc.add("act", I("activation", out=r_[:], in_=pbank[bk][:], func=AF.Ln, bias=C("epsc", 0, 1), scale=1.0),
               r=[("pb", bk), "cst"], w=[("rst", 0)])
        sc.add("act", I("activation", out=r_[:], in_=r_[:], func=AF.Exp, scale=-0.5),
               r=[("rst", 0)], w=[("rst", 0)])
        for c in range(8):
            sl = c % 2
            sc.add("dve", I("scalar_tensor_tensor",
                out=finT[:, sl, :], in0=xT[:, c, lo:hi], scalar=C("gf", c, 1), in1=r_[:], op0=ALU.mult, op1=ALU.mult),
                r=[("xT", c), ("rst", 0), "cst"], w=[("fin", sl)])
            sc.add("sp", I("dma_start", out=ov[:, c, lo:hi], in_=finT[:, sl, :]),
                   r=[("fin", sl)], w=[("out", sl)], dma_slot=("out", sl))
    sc.add("sp", None, r=[("out", 0), ("out", 1)])

    sc.emit(nc, es)
    es.close()
    return nc


_PREP_CACHE = {}


def kernel(x, norm1_g, w_in, conv_w, gate_i_b, gate_f_b, head_norm_g, w_out, norm2_g, w_up, w_down, final_g, _cfg=None):
    cfg = dict(CFG)
    if _cfg:
        cfg.update(_cfg)
    inp = dict(x=x, norm1_g=norm1_g, w_in=w_in, conv_w=conv_w, gate_i_b=gate_i_b, gate_f_b=gate_f_b,
               head_norm_g=head_norm_g, w_out=w_out, norm2_g=norm2_g, w_up=w_up, w_down=w_down, final_g=final_g)
    inp = {k: np.asarray(v) for k, v in inp.items()}
    wfull, wsmall = prep_weights(inp)
    cst = prep_consts(inp)
    rope = prep_rope()
    nc = build(cfg)
    xs = np.asarray(inp["x"], np.float32)
    in_maps = []
    for b in range(8):
        in_maps.append({"xT": np.ascontiguousarray(xs[b].T), "wfull": wfull, "wsmall": wsmall, "cst": cst, "rope": rope})
    res = run_bass_kernel_spmd(nc, in_maps, core_ids=list(range(8)))
    out = np.stack([np.ascontiguousarray(r["outT"].T) for r in res.results], axis=0)
    return out.astype(np.float32)
```

```python
import math
from contextlib import ExitStack

import numpy as np
import concourse.bass as bass
import concourse.mybir as mybir
from concourse.bass_utils import run_bass_kernel_spmd

F32 = mybir.dt.float32
BF16 = mybir.dt.bfloat16
ALU = mybir.AluOpType
AF = mybir.ActivationFunctionType
AX = mybir.AxisListType

P = 128
S = 2048
D = 1024
DFF = 4096
NL = 2
EPS = 1e-6
NFULL = 26
NSMALL = 5
TB = 1024

CFG = {"mlstm": True, "attn": True, "ffn": True, "layers": NL, "debug": False}


def I(name, *args, **kw):
    return (name, args, kw)


class Op:
    __slots__ = ("eng", "fn", "deps", "dma", "slot", "tok", "signal", "waits", "idx", "semkey", "semval")


class WK(tuple):
    def __new__(cls, slot):
        return tuple.__new__(cls, (("wb", slot, 0), ("wb", slot, 1)))


def _flat(keys):
    out = []
    for k in keys:
        if isinstance(k, WK):
            out.extend(k)
        else:
            out.append(k)
    return out


class Sched:
    EPOCH = 30000

    def __init__(self):
        self.ops = []
        self.last_w = {}
        self.readers = {}
        self.stream_len = {"pe": 0, "dve": 0, "act": 0, "pool": 0, "sp": 0}
        self.slot_cnt = {}

    def add(self, eng, fn, r=(), w=(), dma_slot=None):
        op = Op()
        op.eng, op.fn, op.dma, op.slot = eng, fn, dma_slot is not None, dma_slot
        op.signal = False
        op.waits = []
        op.idx = len(self.ops)
        r = _flat(r)
        w = _flat(w)
        deps = set()
        for k in r:
            if k in self.last_w:
                deps.add(self.last_w[k])
        for k in w:
            if k in self.last_w:
                deps.add(self.last_w[k])
            for j in self.readers.get(k, ()):
                deps.add(j)
        op.deps = deps
        if op.dma:
            c = self.slot_cnt.get(dma_slot, 0) + 1
            self.slot_cnt[dma_slot] = c
            op.tok = (("dma", dma_slot), c)
            self.stream_len[eng] += 1
        else:
            self.stream_len[eng] += 1
            op.tok = (("eng", eng), self.stream_len[eng])
        for k in r:
            self.readers.setdefault(k, []).append(op.idx)
        for k in w:
            self.last_w[k] = op.idx
            self.readers[k] = []
        self.ops.append(op)
        return op

    def barrier(self):
        last = {}
        for op in self.ops:
            if op.fn is not None:
                last[op.eng] = op.idx
        for e in list(self.stream_len):
            op = Op()
            op.eng, op.fn, op.dma, op.slot = e, None, False, None
            op.signal = False
            op.waits = []
            op.idx = len(self.ops)
            op.deps = set(v for k, v in last.items() if k != e)
            self.stream_len[e] += 1
            op.tok = (("eng", e), self.stream_len[e])
            self.ops.append(op)

    def analyse(self):
        seen = {e: {} for e in self.stream_len}
        for op in self.ops:
            sn = seen[op.eng]
            for j in sorted(op.deps):
                d = self.ops[j]
                if (not d.dma) and d.eng == "pe" and op.eng == "pe" and not op.dma:
                    continue
                fam, v = d.tok
                if sn.get(fam, 0) >= v:
                    continue
                sn[fam] = v
                d.signal = True
                op.waits.append(j)
        rank = {e: 0 for e in self.stream_len}
        self.nepoch = {e: 1 for e in self.stream_len}
        for op in self.ops:
            if op.dma:
                op.semkey = op.tok[0]
                op.semval = 16 * op.tok[1]
            elif op.signal:
                r = rank[op.eng]
                rank[op.eng] = r + 1
                ep = r // self.EPOCH
                self.nepoch[op.eng] = max(self.nepoch[op.eng], ep + 1)
                op.semkey = ("eng", op.eng, ep)
                op.semval = r - ep * self.EPOCH + 1

    def emit(self, nc, es):
        self.analyse()
        sems = {}

        def getsem(key):
            if key not in sems:
                nm = "s_" + "_".join(str(x) for x in key).replace("(", "").replace(")", "").replace(",", "_").replace(" ", "").replace("'", "")
                sems[key] = es.enter_context(nc.semaphore(nm))
            return sems[key]

        for op in self.ops:
            if op.dma or op.signal:
                getsem(op.semkey)
        block = es.enter_context(nc.Block())
        by_eng = {e: [o for o in self.ops if o.eng == e] for e in self.stream_len}

        def body(eng_name):
            def f(eng):
                for op in by_eng[eng_name]:
                    for j in op.waits:
                        d = self.ops[j]
                        eng.wait_ge(sems[d.semkey], d.semval)
                    if op.fn is None:
                        continue
                    name, args, kw = op.fn
                    ins = getattr(eng, name)(*args, **kw)
                    if op.dma:
                        ins.then_inc(sems[op.semkey], 16)
                    elif op.signal:
                        ins.then_inc(sems[op.semkey], 1)
            return f

        block.tensor(body("pe"))
        block.vector(body("dve"))
        block.scalar(body("act"))
        block.gpsimd(body("pool"))
        block.sync(body("sp"))


def _bform(w, cols):
    out = np.empty((P, 4, 8, P), np.float32)
    for n, ci in enumerate(cols):
        blk = w[:, ci]
        out[:, n] = blk.reshape(8, P, P).transpose(1, 0, 2)
    return out.reshape(P, 4096)


def _aform(w, ci):
    blk = w[:, ci]
    C = blk.shape[1]
    return blk.reshape(8, P, C).transpose(1, 0, 2).reshape(P, 8 * C)


def prep_weights(inp):
    wfull = np.zeros((NL * NFULL, P, 4096), np.float32)
    wsmall = np.zeros((NL * NSMALL, P, 1024), np.float32)
    ar = np.arange(P)
    for l in range(NL):
        w_in = np.asarray(inp["w_in"][l], np.float32)
        w_out = np.asarray(inp["w_out"][l], np.float32)
        w_up = np.asarray(inp["w_up"][l], np.float32)
        w_dn = np.asarray(inp["w_down"][l], np.float32)
        fb = l * NFULL
        sb = l * NSMALL
        for hp in range(2):
            h0, h1 = 2 * hp, 2 * hp + 1
            wfull[fb + 2 * hp] = _bform(w_in, [h0 * P + ar, h1 * P + ar, 512 + h0 * P + ar, 512 + h1 * P + ar])
            ci = np.concatenate([1024 + h0 * P + ar, 1024 + h1 * P + ar, 1536 + h0 * P + ar, 1536 + h1 * P + ar])
            wfull[fb + 2 * hp + 1] = _aform(w_in, ci)
        a = ar // 64
        dd = ar % 64
        sw = a * 64 + (dd + 32) % 64
        for hp in range(4):
            qb, kb = 2064 + hp * P, 2576 + hp * P
            wfull[fb + 4 + hp] = _bform(w_in, [qb + ar, qb + sw, kb + ar, kb + sw])
            wsmall[sb + 1 + hp] = _aform(w_in, 3088 + hp * P + ar)
        wsmall[sb + 0, :, :128] = _aform(w_in, 2048 + np.arange(16))
        for hf in range(2):
            blk = w_out[hf * 512:(hf + 1) * 512, :]
            t = blk.reshape(4, P, 8, P).transpose(1, 2, 0, 3)
            wfull[fb + 8 + hf] = t.reshape(P, 4096)
        for g in range(8):
            wfull[fb + 10 + g] = _bform(w_up, [(4 * g + n) * P + ar for n in range(4)])
        for m in range(8):
            blk = w_dn[:, m * P:(m + 1) * P]
            wfull[fb + 18 + m] = blk.reshape(32, P, P).transpose(1, 0, 2).reshape(P, 4096)
    return wfull, wsmall


def _cst_layout():
    off = {}
    o = 0

    def put(name, n):
        nonlocal o
        off[name] = (o, n)
        o += n
    put("g1", NL * 8)
    put("g2", NL * 8)
    put("gf", 8)
    put("convw", NL * 40)
    put("gb", NL * 16)
    for nm in ("Tf", "Tb", "NEGf", "NEGb", "identF", "onesF", "sel127", "sel0"):
        put(nm, P)
    put("epsc", 4)
    off["_NCS"] = (o, 0)
    put("hng", NL * 512)
    put("band", 512)
    return off, o


CST_OFF, NCST = _cst_layout()
NCS = CST_OFF["_NCS"][0]


def prep_consts(inp):
    c = np.zeros((P, NCST), np.float32)

    def setc(name, arr):
        o, n = CST_OFF[name]
        c[:, o:o + n] = arr.reshape(P, n)
    g1 = np.asarray(inp["norm1_g"], np.float32).reshape(NL, 8, P).transpose(2, 0, 1)
    g2 = np.asarray(inp["norm2_g"], np.float32).reshape(NL, 8, P).transpose(2, 0, 1)
    gf = np.asarray(inp["final_g"], np.float32).reshape(8, P).transpose(1, 0)
    setc("g1", np.ascontiguousarray(g1))
    setc("g2", np.ascontiguousarray(g2))
    setc("gf", np.ascontiguousarray(gf))
    cw = np.asarray(inp["conv_w"], np.float32).reshape(NL, 5, 8, P).transpose(3, 0, 1, 2)
    setc("convw", np.ascontiguousarray(cw))
    gb = np.concatenate([np.asarray(inp["gate_i_b"], np.float32), np.asarray(inp["gate_f_b"], np.float32)], axis=1)
    setc("gb", np.ascontiguousarray(np.broadcast_to(gb.reshape(1, NL * 16), (P, NL * 16))))
    hng = np.asarray(inp["head_norm_g"], np.float32).reshape(1, NL * 512)
    setc("hng", np.ascontiguousarray(np.broadcast_to(hng, (P, NL * 512))))
    k = np.arange(P)[:, None]
    i = np.arange(P)[None, :]
    NEG = -30000.0
    setc("Tf", (k <= i).astype(np.float32))
    setc("Tb", (k >= i).astype(np.float32))
    setc("NEGf", np.where(k <= i, 0.0, NEG).astype(np.float32))
    setc("NEGb", np.where(k >= i, 0.0, NEG).astype(np.float32))
    setc("identF", np.eye(P, dtype=np.float32))
    setc("onesF", np.ones((P, P), np.float32))
    s127 = np.zeros((P, P), np.float32)
    s127[127, :] = 1.0
    s0 = np.zeros((P, P), np.float32)
    s0[0, :] = 1.0
    setc("sel127", s127)
    setc("sel0", s0)
    cc = np.arange(256)[None, :]
    m1 = ((cc >= k) & (cc <= k + 128)).astype(np.float32)
    band = np.concatenate([m1, m1], axis=1)
    setc("band", band)
    ec = np.zeros((P, 4), np.float32)
    ec[:, 0] = EPS
    ec[:, 1] = 1.0
    ec[:, 2] = -0.5 * math.log(128.0)
    setc("epsc", ec)
    return c


def prep_rope():
    half = 32
    inv = (10000.0 ** (-np.arange(half, dtype=np.float32) / half)).astype(np.float32)
    ang = np.arange(S, dtype=np.float32)[:, None] * inv[None, :]
    cos = np.cos(ang).astype(np.float32).T
    sin = np.sin(ang).astype(np.float32).T
    cos64 = np.concatenate([cos, cos], 0)
    sin64 = np.concatenate([-sin, sin], 0)
    r = np.zeros((2, P, S), np.float32)
    r[0] = np.concatenate([cos64, cos64], 0)
    r[1] = np.concatenate([sin64, sin64], 0)
    return r


def build(cfg=CFG):
    nc = bass.Bass("TRN2", target_bir_lowering=False)
    nlay = cfg["layers"]
    xT_d = nc.dram_tensor("xT", [D, S], F32, kind="ExternalInput").ap()
    wf_d = nc.dram_tensor("wfull", [NL * NFULL, P, 4096], F32, kind="ExternalInput").ap()
    ws_d = nc.dram_tensor("wsmall", [NL * NSMALL, P, 1024], F32, kind="ExternalInput").ap()
    cst_d = nc.dram_tensor("cst", [P, NCST], F32, kind="ExternalInput").ap()
    rope_d = nc.dram_tensor("rope", [2, P, S], F32, kind="ExternalInput").ap()
    out_d = nc.dram_tensor("outT", [D, S], F32, kind="ExternalOutput").ap()

    es = ExitStack()
    sc = Sched()

    def sb(name, shape, dt=F32):
        return es.enter_context(nc.sbuf_tensor(name, shape, dt))

    def ps(name, shape, dt=F32):
        return es.enter_context(nc.psum_tensor(name, shape, dt))

    xT = sb("xT_sb", [P, 8, S])
    cst = sb("cst_sb", [P, NCS])
    NWB = 2
    wb = [sb(f"wb{i}", [P, 4096], BF16) for i in range(NWB)]
    wsb = sb("wsb", [P, 1024], BF16)
    NU = 54944
    U = sb("U", [P, NU], BF16)
    hT = U[:, 0:16384].rearrange("p (c t) -> p c t", c=8)
    yT = U[:, 16384:24576].rearrange("p (c t) -> p c t", c=4)
    RBASE = 24576

    class Carver:
        def __init__(self, base):
            self.o = base

        def bf(self, n):
            a = U[:, self.o:self.o + n]
            self.o += n + (n % 2)
            assert self.o <= NU
            return a

        def f32(self, n):
            a = U[:, self.o:self.o + 2 * n].bitcast(F32)
            self.o += 2 * n
            assert self.o <= NU
            return a

    sqb = [sb(f"sqb{i}", [P, 512], BF16) for i in range(2)]
    rst = [sb("rst0", [P, 512])] * 2
    ones_bf = sb("ones_bf", [P, P], BF16)
    ones1_bf = sb("ones1_bf", [P, 64], BF16)
    ident_bf = sb("ident_bf", [P, P], BF16)
    mask2 = sb("mask2", [P, 2, 256], BF16)
    diag = [sb(f"diag{i}", [P, 5, P], BF16) for i in range(2)]
    G = {nm: sb("G_" + nm, [P, 2, 16, 4]) for nm in ("gi", "lf", "bcol", "col", "blast", "w", "decay")}
    graw = sb("graw", [P, 16, 16])
    pbank = [ps(f"pb{i}", [P, 512]) for i in range(8)]

    def C(name, a=0, n=None):
        o, ln = CST_OFF[name]
        n = ln - a if n is None else n
        return cst[:, o + a:o + a + n]

    sc.add("sp", I("dma_start", out=cst[:], in_=cst_d[:, 0:NCS]), w=["cst"], dma_slot="cst")
    xv = xT_d.rearrange("(c p) t -> p c t", p=P)
    for c in range(8):
        sc.add("sp", I("dma_start", out=xT[:, c, :], in_=xv[:, c, :]), w=[("xT", c)], dma_slot=("x", c))
    sc.add("pool", I("memset", ones_bf[:], 1.0 / D), w=["ones_bf"])
    sc.add("pool", I("memset", ones1_bf[:], 1.0), w=["ones1_bf"])
    sc.add("dve", I("tensor_copy", out=ident_bf[:], in_=C("identF")), r=["cst"], w=["ident_bf"])
    bo = CST_OFF["band"][0]
    sc.add("pool", I("dma_start", out=mask2[:].rearrange("p a b -> p (a b)"), in_=cst_d[:, bo:bo + 512]), w=["mask2"], dma_slot="band")

    wstate = {"n": 0, "issued": 0}
    plan = []
    for l_ in range(nlay):
        fb_ = l_ * NFULL
        if cfg["mlstm"]:
            plan += [(fb_ + 0, 4096, 0), (fb_ + 1, 4096, 0), (fb_ + 2, 4096, 0), (fb_ + 3, 4096, 0), (fb_ + 8, 4096, 0)]
        if cfg["attn"]:
            plan += [(fb_ + 4 + hp_, 4096, 0) for hp_ in range(4)] + [(fb_ + 9, 4096, 0)]
        if cfg["ffn"]:
            for half_ in range(2):
                plan += [(fb_ + 10 + half_ * 4 + g_, 4096, 0) for g_ in range(4)]
                plan += [(fb_ + 18 + m_, 2048, half_ * 2048) for m_ in range(8)]

    def _issue(idx):
        gidx, ncols, coloff = plan[idx]
        slot = idx % NWB
        buf = wb[slot]
        for hlf in range(ncols // 2048):
            sc.add("pool", I("dma_start", out=buf[:, hlf * 2048:(hlf + 1) * 2048], in_=wf_d[gidx, :, coloff + hlf * 2048:coloff + (hlf + 1) * 2048]),
                   w=[("wb", slot, hlf)], dma_slot=("wb", slot, hlf))

    def load_w(gidx, ncols=4096, coloff=0):
        n = wstate["n"]
        assert plan[n] == (gidx, ncols, coloff), (n, plan[n], gidx, ncols, coloff)
        wstate["n"] = n + 1
        while wstate["issued"] <= min(n + 1, len(plan) - 1):
            _issue(wstate["issued"])
            wstate["issued"] += 1
        slot = n % NWB
        if ncols // 2048 == 1:
            return wb[slot], ("wb", slot, 0)
        return wb[slot], WK(slot)

    def load_wsmall(gidx, ncols=1024):
        sc.add("pool", I("dma_start", out=wsb[:, 0:ncols], in_=ws_d[gidx, :, 0:ncols]), w=["wsb"], dma_slot="wsb")
        return wsb, "wsb"

    pcount = {"n": 0}

    def next_bank():
        i = pcount["n"] % 8
        pcount["n"] += 1
        return i

    def rmsnorm(gname, goff, t0, t1, dst_fn, dst_keys):
        nb = (t1 - t0) // 512
        for b in range(nb):
            lo = t0 + b * 512
            hi = lo + 512
            bk = next_bank()
            for c in range(8):
                q = sqb[c % 2]
                sc.add("act", I("activation", out=q[:], in_=xT[:, c, lo:hi], func=AF.Square),
                       r=[("xT", c)], w=[("sqb", c % 2)])
                sc.add("pe", I("matmul", pbank[bk][:], lhsT=ones_bf[:], rhs=q[:], start=(c == 0), stop=(c == 7)),
                       r=[("sqb", c % 2), "ones_bf"], w=[("pb", bk)])
            r_ = rst[b % 2]
            sc.add("act", I("activation", out=r_[:], in_=pbank[bk][:], func=AF.Ln, bias=C("epsc", 0, 1), scale=1.0),
                   r=[("pb", bk), "cst"], w=[("rst", 0)])
            sc.add("act", I("activation", out=r_[:], in_=r_[:], func=AF.Exp, scale=-0.5),
                   r=[("rst", 0)], w=[("rst", 0)])
            for c in range(8):
                eng = "dve"
                sc.add(eng, I("scalar_tensor_tensor",
                    out=dst_fn(c, lo, hi), in0=xT[:, c, lo:hi], scalar=C(gname, goff + c, 1), in1=r_[:], op0=ALU.mult, op1=ALU.mult),
                    r=[("xT", c), ("rst", 0), "cst"], w=dst_keys(c))

    def ffn(l):
        uTv = U[:, 16384:16384 + 16 * S].rearrange("p (n t) -> p n t", t=S)
        rtmp = [U[:, 49152 + i * 1024:49152 + (i + 1) * 1024].bitcast(F32) for i in range(2)]
        rmsnorm("g2", l * 8, 0, S, lambda c, lo, hi: hT[:, c, lo:hi], lambda c: [("hT", c)])
        cnt = 0
        for half in range(2):
            for g in range(4):
                w, wk = load_w(l * NFULL + 10 + half * 4 + g)
                wv = w[:].rearrange("p (n k c) -> p n k c", n=4, k=8)
                for n in range(4):
                    ch = 4 * g + n
                    for blk in range(4):
                        lo = blk * 512
                        bk = next_bank()
                        for kc in range(8):
                            sc.add("pe", I("matmul", pbank[bk][:], lhsT=wv[:, n, kc, :], rhs=hT[:, kc, lo:lo + 512], start=(kc == 0), stop=(kc == 7)),
                                   r=[wk, ("hT", kc)], w=[("pb", bk)])
                        rt = rtmp[cnt % 2]
                        sc.add("act", I("activation", out=rt[:], in_=pbank[bk][:], func=AF.Relu), r=[("pb", bk)], w=[("rtmp", cnt % 2)])
                        sc.add("pool", I("tensor_tensor", out=uTv[:, ch, lo:lo + 512], in0=rt[:], in1=rt[:], op=ALU.mult),
                               r=[("rtmp", cnt % 2)], w=[("uT", ch, blk)])
                        cnt += 1
            for m in range(8):
                w, wk = load_w(l * NFULL + 18 + m, ncols=2048, coloff=half * 2048)
                wv = w[:, 0:2048].rearrange("p (n c) -> p n c", n=16)
                for blk in range(4):
                    lo = blk * 512
                    bk = next_bank()
                    for n in range(16):
                        sc.add("pe", I("matmul", pbank[bk][:], lhsT=wv[:, n, :], rhs=uTv[:, n, lo:lo + 512], start=(n == 0), stop=(n == 15)),
                               r=[wk, ("uT", n, blk)], w=[("pb", bk)])
                    sc.add("dve", I("tensor_tensor", out=xT[:, m, lo:lo + 512], in0=xT[:, m, lo:lo + 512], in1=pbank[bk][:], op=ALU.add),
                           r=[("pb", bk), ("xT", m)], w=[("xT", m)])

    def proj_B(wv_n, wk, blk, bk):
        lo = blk * 512
        for kc in range(8):
            sc.add("pe", I("matmul", pbank[bk][:], lhsT=wv_n[:, kc, :], rhs=hT[:, kc, lo:lo + 512], start=(kc == 0), stop=(kc == 7)),
                   r=[wk, ("hT", kc)], w=[("pb", bk)])

    def proj_A(wv, wk, ncol, tok_ap_fn, out_ap, bk):
        for kc in range(8):
            sc.add("pe", I("matmul", out_ap, lhsT=tok_ap_fn(kc), rhs=wv[:, kc, 0:ncol], start=(kc == 0), stop=(kc == 7)),
                   r=[wk, ("hT", kc)], w=[("pb", bk)])

    def pbf(bk):
        return pbank[bk][:].bitcast(BF16)

    def gates(l):
        w, wk = load_wsmall(l * NSMALL + 0, 128)
        wv = w[:, 0:128].rearrange("p (k c) -> p k c", k=8)
        bk = next_bank()
        for t in range(16):
            proj_A(wv, wk, 16, lambda kc, t=t: hT[:, kc, t * P:(t + 1) * P], pbank[bk][:, t * 16:(t + 1) * 16], bk)
        gbv = C("gb", l * 16, 16).unsqueeze(1).to_broadcast([P, 16, 16])
        sc.add("dve", I("tensor_tensor", out=graw[:], in0=pbank[bk][:, 0:256].rearrange("p (t c) -> p t c", t=16), in1=gbv, op=ALU.add),
               r=[("pb", bk), "cst"], w=["graw"])
        gi_src = graw[:, :, 0:8].rearrange("p t (d h) -> p d t h", d=2)
        gf_src = graw[:, :, 8:16].rearrange("p t (d h) -> p d t h", d=2)
        sc.add("dve", I("tensor_scalar", out=G["gi"][:], in0=gi_src, scalar1=-0.5 * math.log(128.0), scalar2=0.0, op0=ALU.add, op1=ALU.add),
               r=["graw"], w=["G_gi"])
        sc.add("act", I("activation", out=G["w"][:], in_=gf_src, func=AF.Exp, scale=-1.0), r=["graw"], w=["G_w"])
        sc.add("act", I("activation", out=G["w"][:], in_=G["w"][:], func=AF.Ln, bias=C("epsc", 1, 1), scale=1.0), r=["G_w", "cst"], w=["G_w"])
        sc.add("dve", I("tensor_scalar", out=G["lf"][:], in0=G["w"][:], scalar1=-1.0, scalar2=0.0, op0=ALU.mult, op1=ALU.add),
               r=["G_w"], w=["G_lf"])
        bk2 = next_bank()
        for d_, Tn in ((0, "Tf"), (1, "Tb")):
            sc.add("pe", I("matmul", pbank[bk2][:, d_ * 64:(d_ + 1) * 64], lhsT=C(Tn), rhs=G["lf"][:, d_].rearrange("p t h -> p (t h)"), start=True, stop=True),
                   r=["G_lf", "cst"], w=[("pb", bk2)])
        sc.add("dve", I("tensor_copy", out=G["bcol"][:].rearrange("p d t h -> p (d t h)"), in_=pbank[bk2][:, 0:128]), r=[("pb", bk2)], w=["G_bcol"])
        sc.add("dve", I("tensor_tensor", out=G["col"][:], in0=G["gi"][:], in1=G["bcol"][:], op=ALU.subtract), r=["G_gi", "G_bcol"], w=["G_col"])
        bk3 = next_bank()
        for d_, Sn in ((0, "sel127"), (1, "sel0")):
            sc.add("pe", I("matmul", pbank[bk3][:, d_ * 64:(d_ + 1) * 64], lhsT=C(Sn), rhs=G["bcol"][:, d_].rearrange("p t h -> p (t h)"), start=True, stop=True),
                   r=["G_bcol", "cst"], w=[("pb", bk3)])
        sc.add("dve", I("tensor_copy", out=G["blast"][:].rearrange("p d t h -> p (d t h)"), in_=pbank[bk3][:, 0:128]), r=[("pb", bk3)], w=["G_blast"])
        sc.add("act", I("activation", out=G["decay"][:], in_=G["blast"][:], func=AF.Exp), r=["G_blast"], w=["G_decay"])
        sc.add("dve", I("tensor_tensor", out=G["w"][:], in0=G["col"][:], in1=G["blast"][:], op=ALU.add), r=["G_col", "G_blast"], w=["G_w"])
        sc.add("act", I("activation", out=G["w"][:], in_=G["w"][:], func=AF.Exp), r=["G_w"], w=["G_w"])

    def mlstm_pair(l, hp):
        cv = Carver(RBASE)
        mqk = cv.bf(4 * S).rearrange("p (n t) -> p n t", n=4)
        Vext = cv.bf(16 * 2 * 129).rearrange("p (t a c) -> p t a c", t=16, a=2)
        ogt = cv.bf(16 * 256).rearrange("p (t c) -> p t c", t=16)
        Cstb = cv.bf(16 * 2 * 129).rearrange("p (t a c) -> p t a c", t=16, a=2)
        tmp_base = cv.o
        pre = [cv.bf(2052), cv.bf(2052)]
        cv.o = tmp_base
        Tlf = [cv.f32(256).rearrange("p (a i) -> p a i", a=2) for _ in range(2)]
        rhs2 = [cv.f32(256).rearrange("p (a i) -> p a i", a=2) for _ in range(2)]
        Dx = [cv.f32(256) for _ in range(2)]
        eb = [cv.bf(256).rearrange("p (a i) -> p a i", a=2) for _ in range(2)]
        qs = [cv.bf(256).rearrange("p (a i) -> p a i", a=2) for _ in range(2)]
        PT = [cv.bf(256).rearrange("p (a i) -> p a i", a=2) for _ in range(2)]
        cv.o = max(cv.o, tmp_base + 2 * 2052)
        wK = cv.bf(256).rearrange("p (a c) -> p a c", a=2)
        Crun = [cv.f32(258).rearrange("p (a c) -> p a c", a=2) for _ in range(2)]
        Cf = cv.bf(258).rearrange("p (a c) -> p a c", a=2)
        hsum = cv.f32(256).rearrange("p (a c) -> p a c", a=2)
        sq = cv.f32(256).rearrange("p (a c) -> p a c", a=2)
        gog = cv.f32(256)
        halfg = cv.f32(256)
        ybf = cv.bf(256).rearrange("p (a c) -> p a c", a=2)
        rr = cv.f32(4)
        ssq = cv.f32(2)
        K = lambda nm: ("m", nm)

        w, wk = load_w(l * NFULL + 2 * hp)
        wv = w[:].rearrange("p (n k c) -> p n k c", n=4, k=8)
        for i in range(2):
            sc.add("pool", I("memset", pre[i][:, 0:2], 0.0), w=[K(("pre", i))])
            sc.add("pool", I("memset", pre[i][:, 2050:2052], 0.0), w=[K(("pre", i))])
        for n in range(4):
            c8 = (2 * hp + n) if n < 2 else (4 + 2 * hp + n - 2)
            pb_ = pre[n % 2]
            dg = diag[n % 2]
            for tau in range(5):
                sc.add("pool", I("tensor_tensor", out=dg[:, tau, :], in0=C("identF"), in1=C("convw", l * 40 + tau * 8 + c8, 1).to_broadcast([P, P]), op=ALU.mult),
                       r=["ident_bf", "cst"], w=[("diag", n % 2)])
            for blk in range(4):
                bk = next_bank()
                proj_B(wv[:, n], wk, blk, bk)
                sc.add("act", I("activation", out=pb_[:, 2 + blk * 512:2 + (blk + 1) * 512], in_=pbank[bk][:], func=AF.Copy),
                       r=[("pb", bk)], w=[K(("pre", n % 2))])
            for blk in range(4):
                bk = next_bank()
                for tau in range(5):
                    sc.add("pe", I("matmul", pbank[bk][:], lhsT=dg[:, tau, :], rhs=pb_[:, blk * 512 + tau:blk * 512 + tau + 512], start=(tau == 0), stop=(tau == 4)),
                           r=[K(("pre", n % 2)), ("diag", n % 2)], w=[("pb", bk)])
                sc.add("act", I("activation", out=mqk[:, n, blk * 512:(blk + 1) * 512], in_=pbank[bk][:], func=AF.Silu),
                       r=[("pb", bk)], w=[K(("mqk", n))])
        w, wk = load_w(l * NFULL + 2 * hp + 1)
        wv = w[:].rearrange("p (k c) -> p k c", k=8)
        sc.add("pool", I("memset", Vext[:, :, :, 128:129], 1.0), w=[K("Vext")])
        for t in range(16):
            bk = next_bank()
            proj_A(wv, wk, 512, lambda kc, t=t: hT[:, kc, t * P:(t + 1) * P], pbank[bk][:], bk)
            sc.add("act", I("activation", out=Vext[:, t, :, 0:128], in_=pbank[bk][:, 0:256].rearrange("p (a c) -> p a c", a=2), func=AF.Copy),
                   r=[("pb", bk)], w=[K("Vext")])
            sc.add("act", I("activation", out=ogt[:, t, :], in_=pbank[bk][:, 256:512], func=AF.Tanh, scale=0.5),
                   r=[("pb", bk)], w=[K("ogt")])
        ho = CST_OFF["hng"][0] + l * 512 + hp * 256
        sc.add("sp", I("dma_start", out=halfg[:], in_=cst_d[:, ho:ho + 256]), w=[K("halfg")], dma_slot="hng")
        sc.add("pool", I("tensor_scalar", out=halfg[:], in0=halfg[:], scalar1=0.5, scalar2=0.0, op0=ALU.mult, op1=ALU.add),
               r=[K("halfg")], w=[K("halfg")])

        def gsl(nm, d_, t):
            return G[nm][:, d_, t, 2 * hp:2 * hp + 2]

        def state_update(d_, t):
            bkT = next_bank()
            for a in range(2):
                sc.add("pe", I("transpose", pbf(bkT)[:, a * P:(a + 1) * P], mqk[:, 2 + a, t * P:(t + 1) * P], ident_bf[:]),
                       r=[K(("mqk", 2 + a)), "ident_bf"], w=[("pb", bkT)])
            sc.add("dve", I("tensor_tensor", out=wK[:], in0=pbf(bkT)[:, 0:256].rearrange("p (a c) -> p a c", a=2),
                                                     in1=gsl("w", d_, t).unsqueeze(2).to_broadcast([P, 2, P]), op=ALU.mult),
                   r=[("pb", bkT), "G_w"], w=[K("wK")])
            bkD = next_bank()
            for a in range(2):
                sc.add("pe", I("matmul", pbank[bkD][:, a * 129:(a + 1) * 129], lhsT=wK[:, a, :], rhs=Vext[:, t, a, :], start=True, stop=True),
                       r=[K("wK"), K("Vext")], w=[("pb", bkD)])
            for a in range(2):
                sc.add("dve", I("scalar_tensor_tensor", out=Crun[d_][:, a, :], in0=Crun[d_][:, a, :], scalar=G["decay"][:, d_, t, 2 * hp + a:2 * hp + a + 1],
                                                                    in1=pbank[bkD][:, a * 129:(a + 1) * 129], op0=ALU.mult, op1=ALU.add),
                       r=[("pb", bkD), "G_decay", K(("Crun", d_))], w=[K(("Crun", d_))])

        sc.barrier()
        for d_ in range(2):
            sc.add("pool", I("memset", Crun[d_][:], 0.0), w=[K(("Crun", d_))])
        for t in range(15, -1, -1):
            sc.add("act", I("activation", out=Cstb[:, t], in_=Crun[1][:], func=AF.Copy), r=[K(("Crun", 1))], w=[K("Cstb")])
            if t > 0:
                state_update(1, t)

        def state_update_f(t):
            for a in range(2):
                sc.add("pe", I("transpose", pbf(5)[:, a * P:(a + 1) * P], mqk[:, 2 + a, t * P:(t + 1) * P], ident_bf[:]),
                       r=[K(("mqk", 2 + a)), "ident_bf"], w=[("pb", 5)])
            sc.add("dve", I("tensor_tensor", out=wK[:], in0=pbf(5)[:, 0:256].rearrange("p (a c) -> p a c", a=2),
                            in1=gsl("w", 0, t).unsqueeze(2).to_broadcast([P, 2, P]), op=ALU.mult),
                   r=[("pb", 5), "G_w"], w=[K("wK")])
            for a in range(2):
                sc.add("pe", I("matmul", pbank[7][:, a * 129:(a + 1) * 129], lhsT=wK[:, a, :], rhs=Vext[:, t, a, :], start=True, stop=True),
                       r=[K("wK"), K("Vext")], w=[("pb", 7)])
            for a in range(2):
                sc.add("dve", I("scalar_tensor_tensor", out=Crun[0][:, a, :], in0=Crun[0][:, a, :], scalar=G["decay"][:, 0, t, 2 * hp + a:2 * hp + a + 1],
                                in1=pbank[7][:, a * 129:(a + 1) * 129], op0=ALU.mult, op1=ALU.add),
                       r=[("pb", 7), "G_decay", K(("Crun", 0))], w=[K(("Crun", 0))])

        def y_transposes(t):
            for a in range(2):
                sc.add("pe", I("transpose", pbf(6)[:, a * P:(a + 1) * P], ybf[:, a, :], ident_bf[:]),
                       r=[K("ybf"), "ident_bf"], w=[("pb", 6)])
            sc.add("act", I("activation", out=yT[:, 2 * hp:2 * hp + 2, t * P:(t + 1) * P], in_=pbf(6)[:, 0:256].rearrange("p (a c) -> p a c", a=2), func=AF.Copy),
                   r=[("pb", 6)], w=[("yT", 2 * hp), ("yT", 2 * hp + 1)])

        def stA(t):
            tl = slice(t * P, (t + 1) * P)
            for a in range(2):
                sc.add("pe", I("matmul", pbank[0][:, a * P:(a + 1) * P], lhsT=mqk[:, 2 + a, tl], rhs=mqk[:, a, tl], start=True, stop=True),
                       r=[K(("mqk", a)), K(("mqk", 2 + a))], w=[("pb", 0)])
            for d_ in range(2):
                Tn, Nn = ("Tf", "NEGf") if d_ == 0 else ("Tb", "NEGb")
                sc.add("pool", I("tensor_tensor", out=Tlf[d_][:], in0=C(Tn).unsqueeze(1).to_broadcast([P, 2, P]),
                                 in1=gsl("lf", d_, t).unsqueeze(2).to_broadcast([P, 2, P]), op=ALU.mult),
                       r=["cst", "G_lf"], w=[K(("Tlf", d_))])
                sc.add("pool", I("tensor_tensor", out=rhs2[d_][:], in0=C(Nn).unsqueeze(1).to_broadcast([P, 2, P]),
                                 in1=gsl("col", d_, t).unsqueeze(2).to_broadcast([P, 2, P]), op=ALU.add),
                       r=["cst", "G_col"], w=[K(("rhs2", d_))])
            for d_ in range(2):
                bk_ = 3 + d_
                Tl2 = Tlf[d_][:].rearrange("p a i -> p (a i)")
                sc.add("pe", I("matmul", pbank[bk_][:, 0:256], lhsT=C("onesF"), rhs=Tl2, start=True, stop=True),
                       r=["cst", K(("Tlf", d_))], w=[("pb", bk_)])
                sc.add("pe", I("matmul", pbank[bk_][:, 256:512], lhsT=C("onesF"), rhs=Tl2, start=True, stop=False),
                       r=["cst", K(("Tlf", d_))], w=[("pb", bk_)])
                sc.add("pe", I("matmul", pbank[bk_][:, 256:512], lhsT=C("identF"), rhs=rhs2[d_][:].rearrange("p a i -> p (a i)"), start=False, stop=True),
                       r=["cst", K(("rhs2", d_))], w=[("pb", bk_)])

        def stB(t):
            tl = slice(t * P, (t + 1) * P)
            for d_ in range(2):
                bk_ = 3 + d_
                sc.add("act", I("activation", out=Dx[d_][:], in_=pbank[bk_][:, 256:512], func=AF.Exp),
                       r=[("pb", bk_)], w=[K(("Dx", d_))])
                sc.add("act", I("activation", out=eb[d_][:].rearrange("p a i -> p (a i)"), in_=pbank[bk_][:, 0:256], func=AF.Exp),
                       r=[("pb", bk_)], w=[K(("eb", d_))])
                sc.add("dve", I("tensor_tensor", out=PT[d_][:].rearrange("p a i -> p (a i)"), in0=pbank[0][:, 0:256], in1=Dx[d_][:], op=ALU.mult),
                       r=[("pb", 0), K(("Dx", d_))], w=[K(("PT", d_))])
                sc.add("dve", I("tensor_tensor", out=qs[d_][:], in0=mqk[:, 0:2, tl], in1=eb[d_][:], op=ALU.mult),
                       r=[K(("mqk", 0)), K(("mqk", 1)), K(("eb", d_))], w=[K(("qs", d_))])

        def stC(t):
            bkO = 1 + (t % 2)
            for d_ in range(2):
                for a in range(2):
                    cprev = Cf[:, a, :] if d_ == 0 else Cstb[:, t, a, :]
                    ck = K("Cf") if d_ == 0 else K("Cstb")
                    oc = (a * 2 + d_) * P
                    sc.add("pe", I("matmul", pbank[bkO][:, oc:oc + P], lhsT=PT[d_][:, a, :], rhs=Vext[:, t, a, 0:128], start=True, stop=False),
                           r=[K(("PT", d_)), K("Vext")], w=[("pb", bkO)])
                    sc.add("pe", I("matmul", pbank[bkO][:, oc:oc + P], lhsT=qs[d_][:, a, :], rhs=cprev[:, 0:128], start=False, stop=True),
                           r=[K(("qs", d_)), ck], w=[("pb", bkO)])
            for d_ in range(2):
                for a in range(2):
                    cprev = Cf[:, a, :] if d_ == 0 else Cstb[:, t, a, :]
                    ck = K("Cf") if d_ == 0 else K("Cstb")
                    dc = 258 + a * 2 + d_
                    sc.add("pe", I("matmul", pbank[7][:, dc:dc + 1], lhsT=PT[d_][:, a, :], rhs=Vext[:, t, a, 128:129], start=True, stop=False),
                           r=[K(("PT", d_)), K("Vext")], w=[("pb", 7)])
                    sc.add("pe", I("matmul", pbank[7][:, dc:dc + 1], lhsT=qs[d_][:, a, :], rhs=cprev[:, 128:129], start=False, stop=True),
                           r=[K(("qs", d_)), ck], w=[("pb", 7)])

        def stD(t):
            bkO = 1 + (t % 2)
            rk_ = K("rr")
            sc.add("act", I("activation", out=rr[:, 0:4], in_=pbank[7][:, 258:262], func=AF.Abs), r=[("pb", 7)], w=[rk_])
            sc.add("dve", I("tensor_scalar", out=rr[:, 0:4], in0=rr[:, 0:4], scalar1=1.0, scalar2=0.0, op0=ALU.max, op1=ALU.add), r=[rk_], w=[rk_])
            sc.add("dve", I("reciprocal", out=rr[:, 0:4], in_=rr[:, 0:4]), r=[rk_], w=[rk_])
            for a in range(2):
                oc = a * 2 * P
                sc.add("act", I("activation", out=hsum[:, a, :], in_=pbank[bkO][:, oc:oc + P], func=AF.Copy, scale=rr[:, 2 * a:2 * a + 1]),
                       r=[("pb", bkO), rk_], w=[K(("hsum", a))])
            for a in range(2):
                oc = a * 2 * P
                sc.add("dve", I("scalar_tensor_tensor", out=hsum[:, a, :], in0=pbank[bkO][:, oc + P:oc + 2 * P], scalar=rr[:, 2 * a + 1:2 * a + 2], in1=hsum[:, a, :], op0=ALU.mult, op1=ALU.add),
                       r=[("pb", bkO), rk_, K(("hsum", a))], w=[K(("hsum", a))])
            for a in range(2):
                sc.add("act", I("activation", out=sq[:, a, :], in_=hsum[:, a, :], func=AF.Square, accum_out=ssq[:, a:a + 1]),
                       r=[K(("hsum", a))], w=[K(("sq", a)), K(("ssq", a))])
            hk = [K(("hsum", 0)), K(("hsum", 1))]
            sk = [K(("ssq", 0)), K(("ssq", 1))]
            sc.add("act", I("activation", out=ssq[:], in_=ssq[:], func=AF.Ln, bias=C("epsc", 0, 1), scale=1.0 / 128.0), r=sk + ["cst"], w=sk)
            sc.add("act", I("activation", out=ssq[:], in_=ssq[:], func=AF.Exp, scale=-0.5), r=sk, w=sk)
            sc.add("dve", I("scalar_tensor_tensor", out=gog[:], in0=ogt[:, t, :], scalar=1.0, in1=halfg[:], op0=ALU.add, op1=ALU.mult),
                   r=[K("ogt"), K("halfg")], w=[K("gog")])
            sc.add("dve", I("tensor_tensor", out=sq[:], in0=hsum[:], in1=ssq[:].unsqueeze(2).to_broadcast([P, 2, P]), op=ALU.mult),
                   r=hk + sk + [K(("sq", 0)), K(("sq", 1))], w=[K(("sq", 0)), K(("sq", 1))])

        def ybf_op():
            sc.add("pool", I("tensor_tensor", out=ybf[:].rearrange("p a c -> p (a c)"), in0=sq[:].rearrange("p a c -> p (a c)"), in1=gog[:], op=ALU.mult),
                   r=[K(("sq", 0)), K(("sq", 1)), K("gog")], w=[K("ybf")])

        def cf_copy():
            sc.add("act", I("activation", out=Cf[:], in_=Crun[0][:], func=AF.Copy), r=[K(("Crun", 0))], w=[K("Cf")])

        stA(0)
        stB(0)
        cf_copy()
        for t in range(16):
            stC(t)
            if t < 15:
                state_update_f(t)
                cf_copy()
                stA(t + 1)
                stB(t + 1)
            if t > 0:
                ybf_op()
                y_transposes(t - 1)
            stD(t)
        ybf_op()
        y_transposes(15)
        sc.barrier()

    def attn_pair(l, hp):
        cv = Carver(RBASE)
        aq = cv.bf(S)
        ak = cv.bf(S)
        Vp = [cv.bf(16 * P).rearrange("p (t c) -> p t c", t=16) for _ in range(3)]
        accn = cv.f32(S)
        accd = cv.f32(S)
        qP = cv.bf(S)
        kP = cv.bf(S)
        PtP = [cv.bf(512).rearrange("p (a c) -> p a c", a=2) for _ in range(3)]
        ropeb = [cv.f32(1024).rearrange("p (k t) -> p k t", k=2) for _ in range(2)]
        t1 = [cv.f32(512)] * 2
        t2 = [cv.f32(512)] * 2
        K = lambda nm: ("a", nm)
        DILS = (1, 4, 16)
        scnt = {"n": 0}
        wsm_, wkv = load_wsmall(l * NSMALL + 1 + hp, 1024)
        wvv = wsm_[:, 0:1024].rearrange("p (k c) -> p k c", k=8)
        w, wk = load_w(l * NFULL + 4 + hp)
        wv = w[:].rearrange("p (n k c) -> p n k c", n=4, k=8)
        for qi, dst in ((0, aq), (1, ak)):
            for blk in range(4):
                lo = blk * 512
                rb = ropeb[blk % 2]
                sc.add("sp", I("dma_start", out=rb[:], in_=rope_d[:, :, lo:lo + 512].rearrange("k p t -> p k t")), w=[K(("rope", blk % 2))], dma_slot=("rope", blk % 2))
                bk0 = next_bank()
                proj_B(wv[:, 2 * qi], wk, blk, bk0)
                bk1 = next_bank()
                proj_B(wv[:, 2 * qi + 1], wk, blk, bk1)
                sc.add("dve", I("tensor_tensor", out=t1[blk % 2][:], in0=pbank[bk0][:], in1=rb[:, 0, :], op=ALU.mult),
                       r=[("pb", bk0), K(("rope", blk % 2))], w=[K(("t1", 0))])
                sc.add("dve", I("tensor_tensor", out=t2[blk % 2][:], in0=pbank[bk1][:], in1=rb[:, 1, :], op=ALU.mult),
                       r=[("pb", bk1), K(("rope", blk % 2))], w=[K(("t2", 0))])
                sc.add("pool", I("tensor_tensor", out=dst[:, lo:lo + 512], in0=t1[blk % 2][:], in1=t2[blk % 2][:], op=ALU.add),
                       r=[K(("t1", 0)), K(("t2", 0))], w=[K(("qk", qi))])
        VT = qP
        if not cfg.get("vt", True):
            for di, d_ in enumerate(DILS):
                nb = 16 // d_
                for g in range(4):
                    bk = next_bank()
                    for tt in range(4):
                        tp = 4 * g + tt
                        r_, lb = tp // nb, tp % nb
                        st = d_ * P * lb + r_

                        def tok(kc, st=st, d_=d_):
                            return hT[:, kc, st:st + d_ * (P - 1) + 1:d_]
                        proj_A(wvv, wkv, P, tok, pbank[bk][:, tt * P:(tt + 1) * P], bk)
                    sc.add("act", I("activation", out=Vp[di][:, 4 * g:4 * g + 4, :], in_=pbank[bk][:].rearrange("p (t c) -> p t c", t=4), func=AF.Copy),
                           r=[("pb", bk)], w=[K(("V", di))])
        for blk in (range(4) if cfg.get("vt", True) else []):
            bk = next_bank()
            lo = blk * 512
            for kc in range(8):
                sc.add("pe", I("matmul", pbank[bk][:], lhsT=wvv[:, kc, :], rhs=hT[:, kc, lo:lo + 512], start=(kc == 0), stop=(kc == 7)),
                       r=[wkv, ("hT", kc)], w=[("pb", bk)])
            sc.add("act", I("activation", out=VT[:, lo:lo + 512], in_=pbank[bk][:], func=AF.Copy), r=[("pb", bk)], w=[K(("qP", blk))])
        for di, d_ in (enumerate(DILS) if cfg.get("vt", True) else []):
            nb = 16 // d_
            if d_ == 1:
                src, skeys = VT, [K(("qP", c_)) for c_ in range(4)]
            else:
                if d_ == 4:
                    sc.add("act", I("activation", out=kP[:].rearrange("p (r l) -> p r l", r=d_), in_=VT[:].rearrange("p (l r) -> p r l", r=d_), func=AF.Copy),
                           r=[K(("qP", c_)) for c_ in range(4)], w=[K(("kP", c_)) for c_ in range(4)])
                else:
                    sc.add("dve", I("tensor_copy", out=kP[:].rearrange("p (r l) -> p r l", r=d_), in_=VT[:].rearrange("p (l r) -> p r l", r=d_)),
                           r=[K(("qP", c_)) for c_ in range(4)], w=[K(("kP", c_)) for c_ in range(4)])
                src, skeys = kP, [K(("kP", c_)) for c_ in range(4)]
            for g in range(4):
                bk = next_bank()
                for tt in range(4):
                    tp = 4 * g + tt
                    sc.add("pe", I("transpose", pbf(bk)[:, tt * P:(tt + 1) * P], src[:, tp * P:(tp + 1) * P], ident_bf[:]),
                           r=skeys + ["ident_bf"], w=[("pb", bk)])
                if g % 2 == 0:
                    sc.add("act", I("activation", out=Vp[di][:, 4 * g:4 * g + 4, :], in_=pbf(bk)[:, 0:512].rearrange("p (t c) -> p t c", t=4), func=AF.Copy),
                           r=[("pb", bk)], w=[K(("V", di))])
                else:
                    sc.add("dve", I("tensor_copy", out=Vp[di][:, 4 * g:4 * g + 4, :], in_=pbf(bk)[:, 0:512].rearrange("p (t c) -> p t c", t=4)),
                           r=[("pb", bk)], w=[K(("V", di))])

        def perm_copy(d_, c):
            if d_ == 4:
                qi_ = aq[:].rearrange("p (l r) -> p r l", r=4)[:, c, :]
                ki_ = ak[:].rearrange("p (l r) -> p r l", r=4)[:, c, :]
                qo_, ko_ = qP[:, c * 512:(c + 1) * 512], kP[:, c * 512:(c + 1) * 512]
            else:
                qi_ = aq[:].rearrange("p (l r) -> p r l", r=16)[:, 4 * c:4 * c + 4, :]
                ki_ = ak[:].rearrange("p (l r) -> p r l", r=16)[:, 4 * c:4 * c + 4, :]
                qo_ = qP[:, c * 512:(c + 1) * 512].rearrange("p (r l) -> p r l", r=4)
                ko_ = kP[:, c * 512:(c + 1) * 512].rearrange("p (r l) -> p r l", r=4)
            sc.add("pool", I("tensor_copy", out=qo_, in_=qi_), r=[K(("qk", 0))], w=[K(("qP", c))])
            sc.add("dve", I("tensor_copy", out=ko_, in_=ki_), r=[K(("qk", 1))], w=[K(("kP", c))])

        kts = []
        blk_last = {}
        for di, d_ in enumerate(DILS):
            nb = 16 // d_
            Ld = nb * P
            for r_ in range(d_):
                for kt in range(nb):
                    lo, hi = max(0, kt * P - 64), min(Ld, kt * P + 192)
                    u = dict(di=di, d=d_, T=r_ * nb + kt, c_lo=r_ * Ld + lo, c_hi=r_ * Ld + hi, m0=lo - (kt * P - 64))
                    for b in range(u["c_lo"] // 512, (u["c_hi"] - 1) // 512 + 1):
                        blk_last[(di, b)] = len(kts)
                    kts.append(u)
        started = set()

        def views(u):
            if u["d"] == 1:
                return aq, ak, [K(("qk", 0))], [K(("qk", 1))]
            cq = range(u["c_lo"] // 512, (u["c_hi"] - 1) // 512 + 1)
            return qP, kP, [K(("qP", c_)) for c_ in cq], [K(("kP", u["T"] // 4))]

        def stageAB(j, u):
            d_ = u["d"]
            if cfg.get("early", True):
                if d_ == 1 and u["T"] in (2, 5, 8, 11):
                    perm_copy(4, (u["T"] - 2) // 3)
                if d_ == 4 and u["T"] % 4 == 0 and u["T"] > 0:
                    perm_copy(16, u["T"] // 4 - 1)
                if d_ == 16 and u["T"] == 0:
                    perm_copy(16, 3)
            elif d_ > 1 and u["T"] == 0:
                for c_ in range(4):
                    perm_copy(d_, c_)
            qv, kv, rq, rk = views(u)
            T, nc_ = u["T"], u["c_hi"] - u["c_lo"]
            pt = PtP[j % 3]
            for a in range(2):
                rows = slice(64 * a, 64 * a + 64)
                bkS = 4 + ((2 * j + a) % 4)
                sc.add("pe", I("matmul", pbank[bkS][:, 0:nc_], lhsT=kv[rows, T * P:(T + 1) * P], rhs=qv[rows, u["c_lo"]:u["c_hi"]], start=True, stop=True),
                       r=rq + rk, w=[("pb", bkS)])
            for a in range(2):
                bkS = 4 + ((2 * j + a) % 4)
                sc.add("act", I("activation", out=pt[:, a, 0:nc_], in_=pbank[bkS][:, 0:nc_], func=AF.Exp, scale=0.125),
                       r=[("pb", bkS)], w=[K(("Pt", j % 3, a))])
            meng = "pool" if (j % 3 == 0 and nc_ == 256 and cfg.get("poolmask", True)) else "dve"
            sc.add(meng, I("tensor_tensor", out=pt[:, :, 0:nc_], in0=pt[:, :, 0:nc_], in1=mask2[:, :, u["m0"]:u["m0"] + nc_], op=ALU.mult),
                   r=[K(("Pt", j % 3, 0)), K(("Pt", j % 3, 1)), "mask2"], w=[K(("Pt", j % 3, 0)), K(("Pt", j % 3, 1))])

        def stageC(j, u):
            di, d_, T = u["di"], u["d"], u["T"]
            pt = PtP[j % 3]
            blks = list(range(u["c_lo"] // 512, (u["c_hi"] - 1) // 512 + 1))
            for b in blks:
                s_lo, s_hi = max(u["c_lo"], 512 * b), min(u["c_hi"], 512 * (b + 1))
                gi_ = di * 4 + b
                bkN, bkD = (0, 1) if gi_ % 2 == 0 else (2, 3)
                for tag in ("N", "D"):
                    for a in range(2):
                        rows = slice(64 * a, 64 * a + 64)
                        mv = pt[:, a, s_lo - u["c_lo"]:s_hi - u["c_lo"]]
                        if tag == "N":
                            bk_, lhs, rkeys = bkN, Vp[di][:, T, rows], [K(("V", di))]
                        else:
                            bk_, lhs, rkeys = bkD, ones1_bf[:, 0:64], ["ones1_bf"]
                        first = (gi_, a, tag) not in started
                        started.add((gi_, a, tag))
                        sc.add("pe", I("matmul", pbank[bk_][rows, s_lo - 512 * b:s_hi - 512 * b], lhsT=lhs, rhs=mv, start=first, stop=True, skip_group_check=True),
                               r=rkeys + [K(("Pt", j % 3, a))], w=[("pb", bk_)])
            for g in blks:
                if blk_last[(di, g)] != j:
                    continue
                gi_ = di * 4 + g
                bkN, bkD = (0, 1) if gi_ % 2 == 0 else (2, 3)
                if d_ == 1:
                    sc.add("act", I("activation", out=accn[:, g * 512:(g + 1) * 512], in_=pbank[bkN][:], func=AF.Copy), r=[("pb", bkN)], w=[K("accn")])
                    sc.add("dve", I("tensor_copy", out=accd[:, g * 512:(g + 1) * 512], in_=pbank[bkD][:]), r=[("pb", bkD)], w=[K("accd")])
                else:
                    if d_ == 4:
                        vn = accn[:].rearrange("p (l r) -> p r l", r=4)[:, g, :]
                        vd = accd[:].rearrange("p (l r) -> p r l", r=4)[:, g, :]
                        pn, pd = pbank[bkN][:], pbank[bkD][:]
                    else:
                        vn = accn[:].rearrange("p (l r) -> p r l", r=16)[:, 4 * g:4 * g + 4, :]
                        vd = accd[:].rearrange("p (l r) -> p r l", r=16)[:, 4 * g:4 * g + 4, :]
                        pn = pbank[bkN][:].rearrange("p (r l) -> p r l", r=4)
                        pd = pbank[bkD][:].rearrange("p (r l) -> p r l", r=4)
                    sc.add("dve", I("tensor_tensor", out=vn, in0=vn, in1=pn, op=ALU.add), r=[("pb", bkN), K("accn")], w=[K("accn")])
                    sc.add("dve", I("tensor_tensor", out=vd, in0=vd, in1=pd, op=ALU.add), r=[("pb", bkD), K("accd")], w=[K("accd")])

        for j in range(len(kts) + 1):
            if j < len(kts):
                stageAB(j, kts[j])
            if j - 1 >= 0:
                stageC(j - 1, kts[j - 1])
        sc.add("dve", I("reciprocal", out=accd[:], in_=accd[:]), r=[K("accd")], w=[K("accd")])
        sc.add("dve", I("tensor_tensor", out=yT[:, hp, :], in0=accn[:], in1=accd[:], op=ALU.mult), r=[K("accn"), K("accd")], w=[("yT", hp)])
        if hp == 3:
            sc.barrier()

    def out_proj(l, hf):
        w, wk = load_w(l * NFULL + 8 + hf)
        wv = w[:].rearrange("p (m k c) -> p m k c", m=8, k=4)
        for m in range(8):
            for blk in range(4):
                lo = blk * 512
                bk = next_bank()
                for kc in range(4):
                    sc.add("pe", I("matmul", pbank[bk][:], lhsT=wv[:, m, kc, :], rhs=yT[:, kc, lo:lo + 512], start=(kc == 0), stop=(kc == 3)),
                           r=[wk, ("yT", kc)], w=[("pb", bk)])
                sc.add("dve", I("tensor_tensor", out=xT[:, m, lo:lo + 512], in0=xT[:, m, lo:lo + 512], in1=pbank[bk][:], op=ALU.add),
                       r=[("pb", bk), ("xT", m)], w=[("xT", m)])

    for l in range(nlay):
        if cfg["mlstm"] or cfg["attn"]:
            rmsnorm("g1", l * 8, 0, S, lambda c, lo, hi: hT[:, c, lo:hi], lambda c: [("hT", c)])
        if cfg["mlstm"]:
            gates(l)
            for hp in range(2):
                mlstm_pair(l, hp)
            out_proj(l, 0)
        if cfg["attn"]:
            for hp in range(4):
                attn_pair(l, hp)
            out_proj(l, 1)
            sc.barrier()
        if cfg["ffn"]:
            ffn(l)
            sc.barrier()

    ov = out_d.rearrange("(c p) t -> p c t", p=P)
    sc.barrier()
    finT = U[:, 0:2048].bitcast(F32).rearrange("p (a t) -> p a t", a=2)
    fcnt = {"n": 0}

    def fin_dst(c, lo, hi):
        return finT[:, c % 2, :]

    for b in range(4):
        lo, hi = b * 512, (b + 1) * 512
        bk = next_bank()
        for c in range(8):
            q = sqb[c % 2]
            sc.add("act", I("activation", out=q[:], in_=xT[:, c, lo:hi], func=AF.Square),
                   r=[("xT", c)], w=[("sqb", c % 2)])
            sc.add("pe", I("matmul", pbank[bk][:], lhsT=ones_bf[:], rhs=q[:], start=(c == 0), stop=(c == 7)),
                   r=[("sqb", c % 2), "ones_bf"], w=[("pb", bk)])
        r_ = rst[b % 2]
        sc.add("act", I("activation", out=r_[:], in_=pbank[bk][:], func=AF.Ln, bias=C("epsc", 0, 1), scale=1.0),
               r=[("pb", bk), "cst"], w=[("rst", 0)])
        sc.add("act", I("activation", out=r_[:], in_=r_[:], func=AF.Exp, scale=-0.5),
               r=[("rst", 0)], w=[("rst", 0)])
        for c in range(8):
            sl = c % 2
            sc.add("dve", I("scalar_tensor_tensor",
                out=finT[:, sl, :], in0=xT[:, c, lo:hi], scalar=C("gf", c, 1), in1=r_[:], op0=ALU.mult, op1=ALU.mult),
                r=[("xT", c), ("rst", 0), "cst"], w=[("fin", sl)])
            sc.add("sp", I("dma_start", out=ov[:, c, lo:hi], in_=finT[:, sl, :]),
                   r=[("fin", sl)], w=[("out", sl)], dma_slot=("out", sl))
    sc.add("sp", None, r=[("out", 0), ("out", 1)])

    sc.emit(nc, es)
    es.close()
    return nc


_PREP_CACHE = {}


def kernel(x, norm1_g, w_in, conv_w, gate_i_b, gate_f_b, head_norm_g, w_out, norm2_g, w_up, w_down, final_g, _cfg=None):
    cfg = dict(CFG)
    if _cfg:
        cfg.update(_cfg)
    inp = dict(x=x, norm1_g=norm1_g, w_in=w_in, conv_w=conv_w, gate_i_b=gate_i_b, gate_f_b=gate_f_b,
               head_norm_g=head_norm_g, w_out=w_out, norm2_g=norm2_g, w_up=w_up, w_down=w_down, final_g=final_g)
    inp = {k: np.asarray(v) for k, v in inp.items()}
    wfull, wsmall = prep_weights(inp)
    cst = prep_consts(inp)
    rope = prep_rope()
    nc = build(cfg)
    xs = np.asarray(inp["x"], np.float32)
    in_maps = []
    for b in range(8):
        in_maps.append({"xT": np.ascontiguousarray(xs[b].T), "wfull": wfull, "wsmall": wsmall, "cst": cst, "rope": rope})
    res = run_bass_kernel_spmd(nc, in_maps, core_ids=list(range(8)))
    out = np.stack([np.ascontiguousarray(r["outT"].T) for r in res.results], axis=0)
    return out.astype(np.float32)
```

```python
import math
from contextlib import ExitStack

import numpy as np
import concourse.bass as bass
import concourse.mybir as mybir
from concourse.bass_utils import run_bass_kernel_spmd

F32 = mybir.dt.float32
BF16 = mybir.dt.bfloat16
ALU = mybir.AluOpType
AF = mybir.ActivationFunctionType
AX = mybir.AxisListType

P = 128
S = 2048
D = 1024
DFF = 4096
NL = 2
EPS = 1e-6
NFULL = 26
NSMALL = 5
TB = 1024

CFG = {"mlstm": True, "attn": True, "ffn": True, "layers": NL, "debug": False}


def I(name, *args, **kw):
    return (name, args, kw)


class Op:
    __slots__ = ("eng", "fn", "deps", "dma", "slot", "tok", "signal", "waits", "idx", "semkey", "semval")


class WK(tuple):
    def __new__(cls, slot):
        return tuple.__new__(cls, (("wb", slot, 0), ("wb", slot, 1)))


def _flat(keys):
    out = []
    for k in keys:
        if isinstance(k, WK):
            out.extend(k)
        else:
            out.append(k)
    return out


class Sched:
    EPOCH = 30000

    def __init__(self):
        self.ops = []
        self.last_w = {}
        self.readers = {}
        self.stream_len = {"pe": 0, "dve": 0, "act": 0, "pool": 0, "sp": 0}
        self.slot_cnt = {}

    def add(self, eng, fn, r=(), w=(), dma_slot=None):
        op = Op()
        op.eng, op.fn, op.dma, op.slot = eng, fn, dma_slot is not None, dma_slot
        op.signal = False
        op.waits = []
        op.idx = len(self.ops)
        r = _flat(r)
        w = _flat(w)
        deps = set()
        for k in r:
            if k in self.last_w:
                deps.add(self.last_w[k])
        for k in w:
            if k in self.last_w:
                deps.add(self.last_w[k])
            for j in self.readers.get(k, ()):
                deps.add(j)
        op.deps = deps
        if op.dma:
            c = self.slot_cnt.get(dma_slot, 0) + 1
            self.slot_cnt[dma_slot] = c
            op.tok = (("dma", dma_slot), c)
            self.stream_len[eng] += 1
        else:
            self.stream_len[eng] += 1
            op.tok = (("eng", eng), self.stream_len[eng])
        for k in r:
            self.readers.setdefault(k, []).append(op.idx)
        for k in w:
            self.last_w[k] = op.idx
            self.readers[k] = []
        self.ops.append(op)
        return op

    def barrier(self):
        last = {}
        for op in self.ops:
            if op.fn is not None:
                last[op.eng] = op.idx
        for e in list(self.stream_len):
            op = Op()
            op.eng, op.fn, op.dma, op.slot = e, None, False, None
            op.signal = False
            op.waits = []
            op.idx = len(self.ops)
            op.deps = set(v for k, v in last.items() if k != e)
            self.stream_len[e] += 1
            op.tok = (("eng", e), self.stream_len[e])
            self.ops.append(op)

    def analyse(self):
        seen = {e: {} for e in self.stream_len}
        for op in self.ops:
            sn = seen[op.eng]
            for j in sorted(op.deps):
                d = self.ops[j]
                if (not d.dma) and d.eng == "pe" and op.eng == "pe" and not op.dma:
                    continue
                fam, v = d.tok
                if sn.get(fam, 0) >= v:
                    continue
                sn[fam] = v
                d.signal = True
                op.waits.append(j)
        rank = {e: 0 for e in self.stream_len}
        self.nepoch = {e: 1 for e in self.stream_len}
        for op in self.ops:
            if op.dma:
                op.semkey = op.tok[0]
                op.semval = 16 * op.tok[1]
            elif op.signal:
                r = rank[op.eng]
                rank[op.eng] = r + 1
                ep = r // self.EPOCH
                self.nepoch[op.eng] = max(self.nepoch[op.eng], ep + 1)
                op.semkey = ("eng", op.eng, ep)
                op.semval = r - ep * self.EPOCH + 1

    def emit(self, nc, es):
        self.analyse()
        sems = {}

        def getsem(key):
            if key not in sems:
                nm = "s_" + "_".join(str(x) for x in key).replace("(", "").replace(")", "").replace(",", "_").replace(" ", "").replace("'", "")
                sems[key] = es.enter_context(nc.semaphore(nm))
            return sems[key]

        for op in self.ops:
            if op.dma or op.signal:
                getsem(op.semkey)
        block = es.enter_context(nc.Block())
        by_eng = {e: [o for o in self.ops if o.eng == e] for e in self.stream_len}

        def body(eng_name):
            def f(eng):
                for op in by_eng[eng_name]:
                    for j in op.waits:
                        d = self.ops[j]
                        eng.wait_ge(sems[d.semkey], d.semval)
                    if op.fn is None:
                        continue
                    name, args, kw = op.fn
                    ins = getattr(eng, name)(*args, **kw)
                    if op.dma:
                        ins.then_inc(sems[op.semkey], 16)
                    elif op.signal:
                        ins.then_inc(sems[op.semkey], 1)
            return f

        block.tensor(body("pe"))
        block.vector(body("dve"))
        block.scalar(body("act"))
        block.gpsimd(body("pool"))
        block.sync(body("sp"))


def _bform(w, cols):
    out = np.empty((P, 4, 8, P), np.float32)
    for n, ci in enumerate(cols):
        blk = w[:, ci]
        out[:, n] = blk.reshape(8, P, P).transpose(1, 0, 2)
    return out.reshape(P, 4096)


def _aform(w, ci):
    blk = w[:, ci]
    C = blk.shape[1]
    return blk.reshape(8, P, C).transpose(1, 0, 2).reshape(P, 8 * C)


def prep_weights(inp):
    wfull = np.zeros((NL * NFULL, P, 4096), np.float32)
    wsmall = np.zeros((NL * NSMALL, P, 1024), np.float32)
    ar = np.arange(P)
    for l in range(NL):
        w_in = np.asarray(inp["w_in"][l], np.float32)
        w_out = np.asarray(inp["w_out"][l], np.float32)
        w_up = np.asarray(inp["w_up"][l], np.float32)
        w_dn = np.asarray(inp["w_down"][l], np.float32)
        fb = l * NFULL
        sb = l * NSMALL
        for hp in range(2):
            h0, h1 = 2 * hp, 2 * hp + 1
            wfull[fb + 2 * hp] = _bform(w_in, [h0 * P + ar, h1 * P + ar, 512 + h0 * P + ar, 512 + h1 * P + ar])
            ci = np.concatenate([1024 + h0 * P + ar, 1024 + h1 * P + ar, 1536 + h0 * P + ar, 1536 + h1 * P + ar])
            wfull[fb + 2 * hp + 1] = _aform(w_in, ci)
        a = ar // 64
        dd = ar % 64
        sw = a * 64 + (dd + 32) % 64
        for hp in range(4):
            qb, kb = 2064 + hp * P, 2576 + hp * P
            wfull[fb + 4 + hp] = _bform(w_in, [qb + ar, qb + sw, kb + ar, kb + sw])
            wsmall[sb + 1 + hp] = _aform(w_in, 3088 + hp * P + ar)
        wsmall[sb + 0, :, :128] = _aform(w_in, 2048 + np.arange(16))
        for hf in range(2):
            blk = w_out[hf * 512:(hf + 1) * 512, :]
            t = blk.reshape(4, P, 8, P).transpose(1, 2, 0, 3)
            wfull[fb + 8 + hf] = t.reshape(P, 4096)
        for g in range(8):
            wfull[fb + 10 + g] = _bform(w_up, [(4 * g + n) * P + ar for n in range(4)])
        for m in range(8):
            blk = w_dn[:, m * P:(m + 1) * P]
            wfull[fb + 18 + m] = blk.reshape(32, P, P).transpose(1, 0, 2).reshape(P, 4096)
    return wfull, wsmall


def _cst_layout():
    off = {}
    o = 0

    def put(name, n):
        nonlocal o
        off[name] = (o, n)
        o += n
    put("g1", NL * 8)
    put("g2", NL * 8)
    put("gf", 8)
    put("convw", NL * 40)
    put("gb", NL * 16)
    for nm in ("Tf", "Tb", "NEGf", "NEGb", "identF", "onesF", "sel127", "sel0"):
        put(nm, P)
    put("epsc", 4)
    off["_NCS"] = (o, 0)
    put("hng", NL * 512)
    put("band", 512)
    return off, o


CST_OFF, NCST = _cst_layout()
NCS = CST_OFF["_NCS"][0]


def prep_consts(inp):
    c = np.zeros((P, NCST), np.float32)

    def setc(name, arr):
        o, n = CST_OFF[name]
        c[:, o:o + n] = arr.reshape(P, n)
    g1 = np.asarray(inp["norm1_g"], np.float32).reshape(NL, 8, P).transpose(2, 0, 1)
    g2 = np.asarray(inp["norm2_g"], np.float32).reshape(NL, 8, P).transpose(2, 0, 1)
    gf = np.asarray(inp["final_g"], np.float32).reshape(8, P).transpose(1, 0)
    setc("g1", np.ascontiguousarray(g1))
    setc("g2", np.ascontiguousarray(g2))
    setc("gf", np.ascontiguousarray(gf))
    cw = np.asarray(inp["conv_w"], np.float32).reshape(NL, 5, 8, P).transpose(3, 0, 1, 2)
    setc("convw", np.ascontiguousarray(cw))
    gb = np.concatenate([np.asarray(inp["gate_i_b"], np.float32), np.asarray(inp["gate_f_b"], np.float32)], axis=1)
    setc("gb", np.ascontiguousarray(np.broadcast_to(gb.reshape(1, NL * 16), (P, NL * 16))))
    hng = np.asarray(inp["head_norm_g"], np.float32).reshape(1, NL * 512)
    setc("hng", np.ascontiguousarray(np.broadcast_to(hng, (P, NL * 512))))
    k = np.arange(P)[:, None]
    i = np.arange(P)[None, :]
    NEG = -30000.0
    setc("Tf", (k <= i).astype(np.float32))
    setc("Tb", (k >= i).astype(np.float32))
    setc("NEGf", np.where(k <= i, 0.0, NEG).astype(np.float32))
    setc("NEGb", np.where(k >= i, 0.0, NEG).astype(np.float32))
    setc("identF", np.eye(P, dtype=np.float32))
    setc("onesF", np.ones((P, P), np.float32))
    s127 = np.zeros((P, P), np.float32)
    s127[127, :] = 1.0
    s0 = np.zeros((P, P), np.float32)
    s0[0, :] = 1.0
    setc("sel127", s127)
    setc("sel0", s0)
    cc = np.arange(256)[None, :]
    m1 = ((cc >= k) & (cc <= k + 128)).astype(np.float32)
    band = np.concatenate([m1, m1], axis=1)
    setc("band", band)
    ec = np.zeros((P, 4), np.float32)
    ec[:, 0] = EPS
    ec[:, 1] = 1.0
    ec[:, 2] = -0.5 * math.log(128.0)
    setc("epsc", ec)
    return c


def prep_rope():
    half = 32
    inv = (10000.0 ** (-np.arange(half, dtype=np.float32) / half)).astype(np.float32)
    ang = np.arange(S, dtype=np.float32)[:, None] * inv[None, :]
    cos = np.cos(ang).astype(np.float32).T
    sin = np.sin(ang).astype(np.float32).T
    cos64 = np.concatenate([cos, cos], 0)
    sin64 = np.concatenate([-sin, sin], 0)
    r = np.zeros((2, P, S), np.float32)
    r[0] = np.concatenate([cos64, cos64], 0)
    r[1] = np.concatenate([sin64, sin64], 0)
    return r


def build(cfg=CFG):
    nc = bass.Bass("TRN2", target_bir_lowering=False)
    nlay = cfg["layers"]
    xT_d = nc.dram_tensor("xT", [D, S], F32, kind="ExternalInput").ap()
    wf_d = nc.dram_tensor("wfull", [NL * NFULL, P, 4096], F32, kind="ExternalInput").ap()
    ws_d = nc.dram_tensor("wsmall", [NL * NSMALL, P, 1024], F32, kind="ExternalInput").ap()
    cst_d = nc.dram_tensor("cst", [P, NCST], F32, kind="ExternalInput").ap()
    rope_d = nc.dram_tensor("rope", [2, P, S], F32, kind="ExternalInput").ap()
    out_d = nc.dram_tensor("outT", [D, S], F32, kind="ExternalOutput").ap()

    es = ExitStack()
    sc = Sched()

    def sb(name, shape, dt=F32):
        return es.enter_context(nc.sbuf_tensor(name, shape, dt))

    def ps(name, shape, dt=F32):
        return es.enter_context(nc.psum_tensor(name, shape, dt))

    xT = sb("xT_sb", [P, 8, S])
    cst = sb("cst_sb", [P, NCS])
    NWB = 2
    wb = [sb(f"wb{i}", [P, 4096], BF16) for i in range(NWB)]
    wsb = sb("wsb", [P, 1024], BF16)
    NU = 54944
    U = sb("U", [P, NU], BF16)
    hT = U[:, 0:16384].rearrange("p (c t) -> p c t", c=8)
    yT = U[:, 16384:24576].rearrange("p (c t) -> p c t", c=4)
    RBASE = 24576

    class Carver:
        def __init__(self, base):
            self.o = base

        def bf(self, n):
            a = U[:, self.o:self.o + n]
            self.o += n + (n % 2)
            assert self.o <= NU
            return a

        def f32(self, n):
            a = U[:, self.o:self.o + 2 * n].bitcast(F32)
            self.o += 2 * n
            assert self.o <= NU
            return a

    sqb = [sb(f"sqb{i}", [P, 512], BF16) for i in range(2)]
    rst = [sb("rst0", [P, 512])] * 2
    ones_bf = sb("ones_bf", [P, P], BF16)
    ones1_bf = sb("ones1_bf", [P, 64], BF16)
    ident_bf = sb("ident_bf", [P, P], BF16)
    mask2 = sb("mask2", [P, 2, 256], BF16)
    diag = [sb(f"diag{i}", [P, 5, P], BF16) for i in range(2)]
    G = {nm: sb("G_" + nm, [P, 2, 16, 4]) for nm in ("gi", "lf", "bcol", "col", "blast", "w", "decay")}
    graw = sb("graw", [P, 16, 16])
    pbank = [ps(f"pb{i}", [P, 512]) for i in range(8)]

    def C(name, a=0, n=None):
        o, ln = CST_OFF[name]
        n = ln - a if n is None else n
        return cst[:, o + a:o + a + n]

    sc.add("sp", I("dma_start", out=cst[:], in_=cst_d[:, 0:NCS]), w=["cst"], dma_slot="cst")
    xv = xT_d.rearrange("(c p) t -> p c t", p=P)
    for c in range(8):
        sc.add("sp", I("dma_start", out=xT[:, c, :], in_=xv[:, c, :]), w=[("xT", c)], dma_slot=("x", c))
    sc.add("pool", I("memset", ones_bf[:], 1.0 / D), w=["ones_bf"])
    sc.add("pool", I("memset", ones1_bf[:], 1.0), w=["ones1_bf"])
    sc.add("dve", I("tensor_copy", out=ident_bf[:], in_=C("identF")), r=["cst"], w=["ident_bf"])
    bo = CST_OFF["band"][0]
    sc.add("pool", I("dma_start", out=mask2[:].rearrange("p a b -> p (a b)"), in_=cst_d[:, bo:bo + 512]), w=["mask2"], dma_slot="band")

    wstate = {"n": 0, "issued": 0}
    plan = []
    for l_ in range(nlay):
        fb_ = l_ * NFULL
        if cfg["mlstm"]:
            plan += [(fb_ + 0, 4096, 0), (fb_ + 1, 4096, 0), (fb_ + 2, 4096, 0), (fb_ + 3, 4096, 0), (fb_ + 8, 4096, 0)]
        if cfg["attn"]:
            plan += [(fb_ + 4 + hp_, 4096, 0) for hp_ in range(4)] + [(fb_ + 9, 4096, 0)]
        if cfg["ffn"]:
            for half_ in range(2):
                plan += [(fb_ + 10 + half_ * 4 + g_, 4096, 0) for g_ in range(4)]
                plan += [(fb_ + 18 + m_, 2048, half_ * 2048) for m_ in range(8)]

    def _issue(idx):
        gidx, ncols, coloff = plan[idx]
        slot = idx % NWB
        buf = wb[slot]
        for hlf in range(ncols // 2048):
            sc.add("pool", I("dma_start", out=buf[:, hlf * 2048:(hlf + 1) * 2048], in_=wf_d[gidx, :, coloff + hlf * 2048:coloff + (hlf + 1) * 2048]),
                   w=[("wb", slot, hlf)], dma_slot=("wb", slot, hlf))

    def load_w(gidx, ncols=4096, coloff=0):
        n = wstate["n"]
        assert plan[n] == (gidx, ncols, coloff), (n, plan[n], gidx, ncols, coloff)
        wstate["n"] = n + 1
        while wstate["issued"] <= min(n + 1, len(plan) - 1):
            _issue(wstate["issued"])
            wstate["issued"] += 1
        slot = n % NWB
        if ncols // 2048 == 1:
            return wb[slot], ("wb", slot, 0)
        return wb[slot], WK(slot)

    def load_wsmall(gidx, ncols=1024):
        sc.add("pool", I("dma_start", out=wsb[:, 0:ncols], in_=ws_d[gidx, :, 0:ncols]), w=["wsb"], dma_slot="wsb")
        return wsb, "wsb"

    pcount = {"n": 0}

    def next_bank():
        i = pcount["n"] % 8
        pcount["n"] += 1
        return i

    def rmsnorm(gname, goff, t0, t1, dst_fn, dst_keys):
        nb = (t1 - t0) // 512
        for b in range(nb):
            lo = t0 + b * 512
            hi = lo + 512
            bk = next_bank()
            for c in range(8):
                q = sqb[c % 2]
                sc.add("act", I("activation", out=q[:], in_=xT[:, c, lo:hi], func=AF.Square),
                       r=[("xT", c)], w=[("sqb", c % 2)])
                sc.add("pe", I("matmul", pbank[bk][:], lhsT=ones_bf[:], rhs=q[:], start=(c == 0), stop=(c == 7)),
                       r=[("sqb", c % 2), "ones_bf"], w=[("pb", bk)])
            r_ = rst[b % 2]
            sc.add("act", I("activation", out=r_[:], in_=pbank[bk][:], func=AF.Ln, bias=C("epsc", 0, 1), scale=1.0),
                   r=[("pb", bk), "cst"], w=[("rst", 0)])
            sc.add("act", I("activation", out=r_[:], in_=r_[:], func=AF.Exp, scale=-0.5),
                   r=[("rst", 0)], w=[("rst", 0)])
            for c in range(8):
                eng = "dve"
                sc.add(eng, I("scalar_tensor_tensor",
                    out=dst_fn(c, lo, hi), in0=xT[:, c, lo:hi], scalar=C(gname, goff + c, 1), in1=r_[:], op0=ALU.mult, op1=ALU.mult),
                    r=[("xT", c), ("rst", 0), "cst"], w=dst_keys(c))

    def ffn(l):
        uTv = U[:, 16384:16384 + 16 * S].rearrange("p (n t) -> p n t", t=S)
        rtmp = [U[:, 49152 + i * 1024:49152 + (i + 1) * 1024].bitcast(F32) for i in range(2)]
        rmsnorm("g2", l * 8, 0, S, lambda c, lo, hi: hT[:, c, lo:hi], lambda c: [("hT", c)])
        cnt = 0
        for half in range(2):
            for g in range(4):
                w, wk = load_w(l * NFULL + 10 + half * 4 + g)
                wv = w[:].rearrange("p (n k c) -> p n k c", n=4, k=8)
                for n in range(4):
                    ch = 4 * g + n
                    for blk in range(4):
                        lo = blk * 512
                        bk = next_bank()
                        for kc in range(8):
                            sc.add("pe", I("matmul", pbank[bk][:], lhsT=wv[:, n, kc, :], rhs=hT[:, kc, lo:lo + 512], start=(kc == 0), stop=(kc == 7)),
                                   r=[wk, ("hT", kc)], w=[("pb", bk)])
                        rt = rtmp[cnt % 2]
                        sc.add("act", I("activation", out=rt[:], in_=pbank[bk][:], func=AF.Relu), r=[("pb", bk)], w=[("rtmp", cnt % 2)])
                        sc.add("pool", I("tensor_tensor", out=uTv[:, ch, lo:lo + 512], in0=rt[:], in1=rt[:], op=ALU.mult),
                               r=[("rtmp", cnt % 2)], w=[("uT", ch, blk)])
                        cnt += 1
            for m in range(8):
                w, wk = load_w(l * NFULL + 18 + m, ncols=2048, coloff=half * 2048)
                wv = w[:, 0:2048].rearrange("p (n c) -> p n c", n=16)
                for blk in range(4):
                    lo = blk * 512
                    bk = next_bank()
                    for n in range(16):
                        sc.add("pe", I("matmul", pbank[bk][:], lhsT=wv[:, n, :], rhs=uTv[:, n, lo:lo + 512], start=(n == 0), stop=(n == 15)),
                               r=[wk, ("uT", n, blk)], w=[("pb", bk)])
                    sc.add("dve", I("tensor_tensor", out=xT[:, m, lo:lo + 512], in0=xT[:, m, lo:lo + 512], in1=pbank[bk][:], op=ALU.add),
                           r=[("pb", bk), ("xT", m)], w=[("xT", m)])

    def proj_B(wv_n, wk, blk, bk):
        lo = blk * 512
        for kc in range(8):
            sc.add("pe", I("matmul", pbank[bk][:], lhsT=wv_n[:, kc, :], rhs=hT[:, kc, lo:lo + 512], start=(kc == 0), stop=(kc == 7)),
                   r=[wk, ("hT", kc)], w=[("pb", bk)])

    def proj_A(wv, wk, ncol, tok_ap_fn, out_ap, bk):
        for kc in range(8):
            sc.add("pe", I("matmul", out_ap, lhsT=tok_ap_fn(kc), rhs=wv[:, kc, 0:ncol], start=(kc == 0), stop=(kc == 7)),
                   r=[wk, ("hT", kc)], w=[("pb", bk)])

    def pbf(bk):
        return pbank[bk][:].bitcast(BF16)

    def gates(l):
        w, wk = load_wsmall(l * NSMALL + 0, 128)
        wv = w[:, 0:128].rearrange("p (k c) -> p k c", k=8)
        bk = next_bank()
        for t in range(16):
            proj_A(wv, wk, 16, lambda kc, t=t: hT[:, kc, t * P:(t + 1) * P], pbank[bk][:, t * 16:(t + 1) * 16], bk)
        gbv = C("gb", l * 16, 16).unsqueeze(1).to_broadcast([P, 16, 16])
        sc.add("dve", I("tensor_tensor", out=graw[:], in0=pbank[bk][:, 0:256].rearrange("p (t c) -> p t c", t=16), in1=gbv, op=ALU.add),
               r=[("pb", bk), "cst"], w=["graw"])
        gi_src = graw[:, :, 0:8].rearrange("p t (d h) -> p d t h", d=2)
        gf_src = graw[:, :, 8:16].rearrange("p t (d h) -> p d t h", d=2)
        sc.add("dve", I("tensor_scalar", out=G["gi"][:], in0=gi_src, scalar1=-0.5 * math.log(128.0), scalar2=0.0, op0=ALU.add, op1=ALU.add),
               r=["graw"], w=["G_gi"])
        sc.add("act", I("activation", out=G["w"][:], in_=gf_src, func=AF.Exp, scale=-1.0), r=["graw"], w=["G_w"])
        sc.add("act", I("activation", out=G["w"][:], in_=G["w"][:], func=AF.Ln, bias=C("epsc", 1, 1), scale=1.0), r=["G_w", "cst"], w=["G_w"])
        sc.add("dve", I("tensor_scalar", out=G["lf"][:], in0=G["w"][:], scalar1=-1.0, scalar2=0.0, op0=ALU.mult, op1=ALU.add),
               r=["G_w"], w=["G_lf"])
        bk2 = next_bank()
        for d_, Tn in ((0, "Tf"), (1, "Tb")):
            sc.add("pe", I("matmul", pbank[bk2][:, d_ * 64:(d_ + 1) * 64], lhsT=C(Tn), rhs=G["lf"][:, d_].rearrange("p t h -> p (t h)"), start=True, stop=True),
                   r=["G_lf", "cst"], w=[("pb", bk2)])
        sc.add("dve", I("tensor_copy", out=G["bcol"][:].rearrange("p d t h -> p (d t h)"), in_=pbank[bk2][:, 0:128]), r=[("pb", bk2)], w=["G_bcol"])
        sc.add("dve", I("tensor_tensor", out=G["col"][:], in0=G["gi"][:], in1=G["bcol"][:], op=ALU.subtract), r=["G_gi", "G_bcol"], w=["G_col"])
        bk3 = next_bank()
        for d_, Sn in ((0, "sel127"), (1, "sel0")):
            sc.add("pe", I("matmul", pbank[bk3][:, d_ * 64:(d_ + 1) * 64], lhsT=C(Sn), rhs=G["bcol"][:, d_].rearrange("p t h -> p (t h)"), start=True, stop=True),
                   r=["G_bcol", "cst"], w=[("pb", bk3)])
        sc.add("dve", I("tensor_copy", out=G["blast"][:].rearrange("p d t h -> p (d t h)"), in_=pbank[bk3][:, 0:128]), r=[("pb", bk3)], w=["G_blast"])
        sc.add("act", I("activation", out=G["decay"][:], in_=G["blast"][:], func=AF.Exp), r=["G_blast"], w=["G_decay"])
        sc.add("dve", I("tensor_tensor", out=G["w"][:], in0=G["col"][:], in1=G["blast"][:], op=ALU.add), r=["G_col", "G_blast"], w=["G_w"])
        sc.add("act", I("activation", out=G["w"][:], in_=G["w"][:], func=AF.Exp), r=["G_w"], w=["G_w"])

    def mlstm_pair(l, hp):
        cv = Carver(RBASE)
        mqk = cv.bf(4 * S).rearrange("p (n t) -> p n t", n=4)
        Vext = cv.bf(16 * 2 * 129).rearrange("p (t a c) -> p t a c", t=16, a=2)
        ogt = cv.bf(16 * 256).rearrange("p (t c) -> p t c", t=16)
        Cstb = cv.bf(16 * 2 * 129).rearrange("p (t a c) -> p t a c", t=16, a=2)
        tmp_base = cv.o
        pre = [cv.bf(2052), cv.bf(2052)]
        cv.o = tmp_base
        Tlf = [cv.f32(256).rearrange("p (a i) -> p a i", a=2) for _ in range(2)]
        rhs2 = [cv.f32(256).rearrange("p (a i) -> p a i", a=2) for _ in range(2)]
        Dx = [cv.f32(256) for _ in range(2)]
        eb = [cv.bf(256).rearrange("p (a i) -> p a i", a=2) for _ in range(2)]
        qs = [cv.bf(256).rearrange("p (a i) -> p a i", a=2) for _ in range(2)]
        PT = [cv.bf(256).rearrange("p (a i) -> p a i", a=2) for _ in range(2)]
        cv.o = max(cv.o, tmp_base + 2 * 2052)
        wK = cv.bf(256).rearrange("p (a c) -> p a c", a=2)
        Crun = [cv.f32(258).rearrange("p (a c) -> p a c", a=2) for _ in range(2)]
        Cf = cv.bf(258).rearrange("p (a c) -> p a c", a=2)
        hsum = cv.f32(256).rearrange("p (a c) -> p a c", a=2)
        sq = cv.f32(256).rearrange("p (a c) -> p a c", a=2)
        gog = cv.f32(256)
        halfg = cv.f32(256)
        ybf = cv.bf(256).rearrange("p (a c) -> p a c", a=2)
        rr = cv.f32(4)
        ssq = cv.f32(2)
        K = lambda nm: ("m", nm)

        w, wk = load_w(l * NFULL + 2 * hp)
        wv = w[:].rearrange("p (n k c) -> p n k c", n=4, k=8)
        for i in range(2):
            sc.add("pool", I("memset", pre[i][:, 0:2], 0.0), w=[K(("pre", i))])
            sc.add("pool", I("memset", pre[i][:, 2050:2052], 0.0), w=[K(("pre", i))])
        for n in range(4):
            c8 = (2 * hp + n) if n < 2 else (4 + 2 * hp + n - 2)
            pb_ = pre[n % 2]
            dg = diag[n % 2]
            for tau in range(5):
                sc.add("pool", I("tensor_tensor", out=dg[:, tau, :], in0=C("identF"), in1=C("convw", l * 40 + tau * 8 + c8, 1).to_broadcast([P, P]), op=ALU.mult),
                       r=["ident_bf", "cst"], w=[("diag", n % 2)])
            for blk in range(4):
                bk = next_bank()
                proj_B(wv[:, n], wk, blk, bk)
                sc.add("act", I("activation", out=pb_[:, 2 + blk * 512:2 + (blk + 1) * 512], in_=pbank[bk][:], func=AF.Copy),
                       r=[("pb", bk)], w=[K(("pre", n % 2))])
            for blk in range(4):
                bk = next_bank()
                for tau in range(5):
                    sc.add("pe", I("matmul", pbank[bk][:], lhsT=dg[:, tau, :], rhs=pb_[:, blk * 512 + tau:blk * 512 + tau + 512], start=(tau == 0), stop=(tau == 4)),
                           r=[K(("pre", n % 2)), ("diag", n % 2)], w=[("pb", bk)])
                sc.add("act", I("activation", out=mqk[:, n, blk * 512:(blk + 1) * 512], in_=pbank[bk][:], func=AF.Silu),
                       r=[("pb", bk)], w=[K(("mqk", n))])
        w, wk = load_w(l * NFULL + 2 * hp + 1)
        wv = w[:].rearrange("p (k c) -> p k c", k=8)
        sc.add("pool", I("memset", Vext[:, :, :, 128:129], 1.0), w=[K("Vext")])
        for t in range(16):
            bk = next_bank()
            proj_A(wv, wk, 512, lambda kc, t=t: hT[:, kc, t * P:(t + 1) * P], pbank[bk][:], bk)
            sc.add("act", I("activation", out=Vext[:, t, :, 0:128], in_=pbank[bk][:, 0:256].rearrange("p (a c) -> p a c", a=2), func=AF.Copy),
                   r=[("pb", bk)], w=[K("Vext")])
            sc.add("act", I("activation", out=ogt[:, t, :], in_=pbank[bk][:, 256:512], func=AF.Tanh, scale=0.5),
                   r=[("pb", bk)], w=[K("ogt")])
        ho = CST_OFF["hng"][0] + l * 512 + hp * 256
        sc.add("sp", I("dma_start", out=halfg[:], in_=cst_d[:, ho:ho + 256]), w=[K("halfg")], dma_slot="hng")
        sc.add("pool", I("tensor_scalar", out=halfg[:], in0=halfg[:], scalar1=0.5, scalar2=0.0, op0=ALU.mult, op1=ALU.add),
               r=[K("halfg")], w=[K("halfg")])

        def gsl(nm, d_, t):
            return G[nm][:, d_, t, 2 * hp:2 * hp + 2]

        def state_update(d_, t):
            bkT = next_bank()
            for a in range(2):
                sc.add("pe", I("transpose", pbf(bkT)[:, a * P:(a + 1) * P], mqk[:, 2 + a, t * P:(t + 1) * P], ident_bf[:]),
                       r=[K(("mqk", 2 + a)), "ident_bf"], w=[("pb", bkT)])
            sc.add("dve", I("tensor_tensor", out=wK[:], in0=pbf(bkT)[:, 0:256].rearrange("p (a c) -> p a c", a=2),
                                                     in1=gsl("w", d_, t).unsqueeze(2).to_broadcast([P, 2, P]), op=ALU.mult),
                   r=[("pb", bkT), "G_w"], w=[K("wK")])
            bkD = next_bank()
            for a in range(2):
                sc.add("pe", I("matmul", pbank[bkD][:, a * 129:(a + 1) * 129], lhsT=wK[:, a, :], rhs=Vext[:, t, a, :], start=True, stop=True),
                       r=[K("wK"), K("Vext")], w=[("pb", bkD)])
            for a in range(2):
                sc.add("dve", I("scalar_tensor_tensor", out=Crun[d_][:, a, :], in0=Crun[d_][:, a, :], scalar=G["decay"][:, d_, t, 2 * hp + a:2 * hp + a + 1],
                                                                    in1=pbank[bkD][:, a * 129:(a + 1) * 129], op0=ALU.mult, op1=ALU.add),
                       r=[("pb", bkD), "G_decay", K(("Crun", d_))], w=[K(("Crun", d_))])

        sc.barrier()
        for d_ in range(2):
            sc.add("pool", I("memset", Crun[d_][:], 0.0), w=[K(("Crun", d_))])
        for t in range(15, -1, -1):
            sc.add("act", I("activation", out=Cstb[:, t], in_=Crun[1][:], func=AF.Copy), r=[K(("Crun", 1))], w=[K("Cstb")])
            if t > 0:
                state_update(1, t)

        def state_update_f1(t):
            for a in range(2):
                sc.add("pe", I("transpose", pbf(5)[:, a * P:(a + 1) * P], mqk[:, 2 + a, t * P:(t + 1) * P], ident_bf[:]),
                       r=[K(("mqk", 2 + a)), "ident_bf"], w=[("pb", 5)])
            sc.add("dve", I("tensor_tensor", out=wK[:], in0=pbf(5)[:, 0:256].rearrange("p (a c) -> p a c", a=2),
                            in1=gsl("w", 0, t).unsqueeze(2).to_broadcast([P, 2, P]), op=ALU.mult),
                   r=[("pb", 5), "G_w"], w=[K("wK")])

        def state_update_f2(t):
            for a in range(2):
                sc.add("pe", I("matmul", pbank[7][:, a * 129:(a + 1) * 129], lhsT=wK[:, a, :], rhs=Vext[:, t, a, :], start=True, stop=True),
                       r=[K("wK"), K("Vext")], w=[("pb", 7)])
            for a in range(2):
                sc.add("dve", I("scalar_tensor_tensor", out=Crun[0][:, a, :], in0=Crun[0][:, a, :], scalar=G["decay"][:, 0, t, 2 * hp + a:2 * hp + a + 1],
                                in1=pbank[7][:, a * 129:(a + 1) * 129], op0=ALU.mult, op1=ALU.add),
                       r=[("pb", 7), "G_decay", K(("Crun", 0))], w=[K(("Crun", 0))])

        def y_transposes(t):
            for a in range(2):
                sc.add("pe", I("transpose", pbf(6)[:, a * P:(a + 1) * P], ybf[:, a, :], ident_bf[:]),
                       r=[K("ybf"), "ident_bf"], w=[("pb", 6)])
            sc.add("act", I("activation", out=yT[:, 2 * hp:2 * hp + 2, t * P:(t + 1) * P], in_=pbf(6)[:, 0:256].rearrange("p (a c) -> p a c", a=2), func=AF.Copy),
                   r=[("pb", 6)], w=[("yT", 2 * hp), ("yT", 2 * hp + 1)])

        def stA(t):
            tl = slice(t * P, (t + 1) * P)
            for a in range(2):
                sc.add("pe", I("matmul", pbank[0][:, a * P:(a + 1) * P], lhsT=mqk[:, 2 + a, tl], rhs=mqk[:, a, tl], start=True, stop=True),
                       r=[K(("mqk", a)), K(("mqk", 2 + a))], w=[("pb", 0)])
            for d_ in range(2):
                Tn, Nn = ("Tf", "NEGf") if d_ == 0 else ("Tb", "NEGb")
                sc.add("pool", I("tensor_tensor", out=Tlf[d_][:], in0=C(Tn).unsqueeze(1).to_broadcast([P, 2, P]),
                                 in1=gsl("lf", d_, t).unsqueeze(2).to_broadcast([P, 2, P]), op=ALU.mult),
                       r=["cst", "G_lf"], w=[K(("Tlf", d_))])
                sc.add("pool", I("tensor_tensor", out=rhs2[d_][:], in0=C(Nn).unsqueeze(1).to_broadcast([P, 2, P]),
                                 in1=gsl("col", d_, t).unsqueeze(2).to_broadcast([P, 2, P]), op=ALU.add),
                       r=["cst", "G_col"], w=[K(("rhs2", d_))])
            for d_ in range(2):
                bk_ = 3 + d_
                Tl2 = Tlf[d_][:].rearrange("p a i -> p (a i)")
                sc.add("pe", I("matmul", pbank[bk_][:, 0:256], lhsT=C("onesF"), rhs=Tl2, start=True, stop=True),
                       r=["cst", K(("Tlf", d_))], w=[("pb", bk_)])
                sc.add("pe", I("matmul", pbank[bk_][:, 256:512], lhsT=C("onesF"), rhs=Tl2, start=True, stop=False),
                       r=["cst", K(("Tlf", d_))], w=[("pb", bk_)])
                sc.add("pe", I("matmul", pbank[bk_][:, 256:512], lhsT=C("identF"), rhs=rhs2[d_][:].rearrange("p a i -> p (a i)"), start=False, stop=True),
                       r=["cst", K(("rhs2", d_))], w=[("pb", bk_)])

        def stB(t):
            tl = slice(t * P, (t + 1) * P)
            for d_ in range(2):
                bk_ = 3 + d_
                sc.add("act", I("activation", out=Dx[d_][:], in_=pbank[bk_][:, 256:512], func=AF.Exp),
                       r=[("pb", bk_)], w=[K(("Dx", d_))])
                sc.add("act", I("activation", out=eb[d_][:].rearrange("p a i -> p (a i)"), in_=pbank[bk_][:, 0:256], func=AF.Exp),
                       r=[("pb", bk_)], w=[K(("eb", d_))])
                sc.add("dve", I("tensor_tensor", out=PT[d_][:].rearrange("p a i -> p (a i)"), in0=pbank[0][:, 0:256], in1=Dx[d_][:], op=ALU.mult),
                       r=[("pb", 0), K(("Dx", d_))], w=[K(("PT", d_))])
                sc.add("dve", I("tensor_tensor", out=qs[d_][:], in0=mqk[:, 0:2, tl], in1=eb[d_][:], op=ALU.mult),
                       r=[K(("mqk", 0)), K(("mqk", 1)), K(("eb", d_))], w=[K(("qs", d_))])

        def stC(t):
            bkO = 1 + (t % 2)
            for d_ in range(2):
                for a in range(2):
                    cprev = Cf[:, a, :] if d_ == 0 else Cstb[:, t, a, :]
                    ck = K("Cf") if d_ == 0 else K("Cstb")
                    oc = (a * 2 + d_) * P
                    sc.add("pe", I("matmul", pbank[bkO][:, oc:oc + P], lhsT=PT[d_][:, a, :], rhs=Vext[:, t, a, 0:128], start=True, stop=False),
                           r=[K(("PT", d_)), K("Vext")], w=[("pb", bkO)])
                    sc.add("pe", I("matmul", pbank[bkO][:, oc:oc + P], lhsT=qs[d_][:, a, :], rhs=cprev[:, 0:128], start=False, stop=True),
                           r=[K(("qs", d_)), ck], w=[("pb", bkO)])
            for d_ in range(2):
                for a in range(2):
                    cprev = Cf[:, a, :] if d_ == 0 else Cstb[:, t, a, :]
                    ck = K("Cf") if d_ == 0 else K("Cstb")
                    dc = 258 + a * 2 + d_
                    sc.add("pe", I("matmul", pbank[7][:, dc:dc + 1], lhsT=PT[d_][:, a, :], rhs=Vext[:, t, a, 128:129], start=True, stop=False),
                           r=[K(("PT", d_)), K("Vext")], w=[("pb", 7)])
                    sc.add("pe", I("matmul", pbank[7][:, dc:dc + 1], lhsT=qs[d_][:, a, :], rhs=cprev[:, 128:129], start=False, stop=True),
                           r=[K(("qs", d_)), ck], w=[("pb", 7)])

        def stD(t):
            bkO = 1 + (t % 2)
            rk_ = K("rr")
            sc.add("act", I("activation", out=rr[:, 0:4], in_=pbank[7][:, 258:262], func=AF.Abs), r=[("pb", 7)], w=[rk_])
            sc.add("dve", I("tensor_scalar", out=rr[:, 0:4], in0=rr[:, 0:4], scalar1=1.0, scalar2=0.0, op0=ALU.max, op1=ALU.add), r=[rk_], w=[rk_])
            sc.add("dve", I("reciprocal", out=rr[:, 0:4], in_=rr[:, 0:4]), r=[rk_], w=[rk_])
            for a in range(2):
                oc = a * 2 * P
                sc.add("act", I("activation", out=hsum[:, a, :], in_=pbank[bkO][:, oc:oc + P], func=AF.Copy, scale=rr[:, 2 * a:2 * a + 1]),
                       r=[("pb", bkO), rk_], w=[K(("hsum", a))])
            for a in range(2):
                oc = a * 2 * P
                sc.add("dve", I("scalar_tensor_tensor", out=hsum[:, a, :], in0=pbank[bkO][:, oc + P:oc + 2 * P], scalar=rr[:, 2 * a + 1:2 * a + 2], in1=hsum[:, a, :], op0=ALU.mult, op1=ALU.add),
                       r=[("pb", bkO), rk_, K(("hsum", a))], w=[K(("hsum", a))])
            for a in range(2):
                sc.add("act", I("activation", out=sq[:, a, :], in_=hsum[:, a, :], func=AF.Square, accum_out=ssq[:, a:a + 1]),
                       r=[K(("hsum", a))], w=[K(("sq", a)), K(("ssq", a))])
            hk = [K(("hsum", 0)), K(("hsum", 1))]
            sk = [K(("ssq", 0)), K(("ssq", 1))]
            sc.add("act", I("activation", out=ssq[:], in_=ssq[:], func=AF.Ln, bias=C("epsc", 0, 1), scale=1.0 / 128.0), r=sk + ["cst"], w=sk)
            sc.add("act", I("activation", out=ssq[:], in_=ssq[:], func=AF.Exp, scale=-0.5), r=sk, w=sk)
            sc.add("dve", I("scalar_tensor_tensor", out=gog[:], in0=ogt[:, t, :], scalar=1.0, in1=halfg[:], op0=ALU.add, op1=ALU.mult),
                   r=[K("ogt"), K("halfg")], w=[K("gog")])
            sc.add("dve", I("tensor_tensor", out=sq[:], in0=hsum[:], in1=ssq[:].unsqueeze(2).to_broadcast([P, 2, P]), op=ALU.mult),
                   r=hk + sk + [K(("sq", 0)), K(("sq", 1))], w=[K(("sq", 0)), K(("sq", 1))])

        def ybf_op():
            sc.add("pool", I("tensor_tensor", out=ybf[:].rearrange("p a c -> p (a c)"), in0=sq[:].rearrange("p a c -> p (a c)"), in1=gog[:], op=ALU.mult),
                   r=[K(("sq", 0)), K(("sq", 1)), K("gog")], w=[K("ybf")])

        def cf_copy():
            sc.add("act", I("activation", out=Cf[:], in_=Crun[0][:], func=AF.Copy), r=[K(("Crun", 0))], w=[K("Cf")])

        stA(0)
        stB(0)
        cf_copy()
        for t in range(16):
            stC(t)
            if t < 15:
                state_update_f1(t)
                stA(t + 1)
                state_update_f2(t)
                cf_copy()
                stB(t + 1)
            if t > 0:
                ybf_op()
                y_transposes(t - 1)
            stD(t)
        ybf_op()
        y_transposes(15)
        sc.barrier()

    def attn_pair(l, hp):
        cv = Carver(RBASE)
        aq = cv.bf(S)
        ak = cv.bf(S)
        Vp = [cv.bf(16 * P).rearrange("p (t c) -> p t c", t=16) for _ in range(3)]
        accn = cv.f32(S)
        accd = cv.f32(S)
        qP = cv.bf(S)
        kP = cv.bf(S)
        PtP = [cv.bf(512).rearrange("p (a c) -> p a c", a=2) for _ in range(3)]
        ropeb = [cv.f32(1024).rearrange("p (k t) -> p k t", k=2) for _ in range(2)]
        t1 = [cv.f32(512)] * 2
        t2 = [cv.f32(512)] * 2
        K = lambda nm: ("a", nm)
        DILS = (1, 4, 16)
        scnt = {"n": 0}
        wsm_, wkv = load_wsmall(l * NSMALL + 1 + hp, 1024)
        wvv = wsm_[:, 0:1024].rearrange("p (k c) -> p k c", k=8)
        w, wk = load_w(l * NFULL + 4 + hp)
        wv = w[:].rearrange("p (n k c) -> p n k c", n=4, k=8)
        for qi, dst in ((0, aq), (1, ak)):
            for blk in range(4):
                lo = blk * 512
                rb = ropeb[blk % 2]
                sc.add("sp", I("dma_start", out=rb[:], in_=rope_d[:, :, lo:lo + 512].rearrange("k p t -> p k t")), w=[K(("rope", blk % 2))], dma_slot=("rope", blk % 2))
                bk0 = next_bank()
                proj_B(wv[:, 2 * qi], wk, blk, bk0)
                bk1 = next_bank()
                proj_B(wv[:, 2 * qi + 1], wk, blk, bk1)
                sc.add("dve", I("tensor_tensor", out=t1[blk % 2][:], in0=pbank[bk0][:], in1=rb[:, 0, :], op=ALU.mult),
                       r=[("pb", bk0), K(("rope", blk % 2))], w=[K(("t1", 0))])
                sc.add("dve", I("tensor_tensor", out=t2[blk % 2][:], in0=pbank[bk1][:], in1=rb[:, 1, :], op=ALU.mult),
                       r=[("pb", bk1), K(("rope", blk % 2))], w=[K(("t2", 0))])
                sc.add("pool", I("tensor_tensor", out=dst[:, lo:lo + 512], in0=t1[blk % 2][:], in1=t2[blk % 2][:], op=ALU.add),
                       r=[K(("t1", 0)), K(("t2", 0))], w=[K(("qk", qi))])
        VT = qP
        if not cfg.get("vt", True):
            for di, d_ in enumerate(DILS):
                nb = 16 // d_
                for g in range(4):
                    bk = next_bank()
                    for tt in range(4):
                        tp = 4 * g + tt
                        r_, lb = tp // nb, tp % nb
                        st = d_ * P * lb + r_

                        def tok(kc, st=st, d_=d_):
                            return hT[:, kc, st:st + d_ * (P - 1) + 1:d_]
                        proj_A(wvv, wkv, P, tok, pbank[bk][:, tt * P:(tt + 1) * P], bk)
                    sc.add("act", I("activation", out=Vp[di][:, 4 * g:4 * g + 4, :], in_=pbank[bk][:].rearrange("p (t c) -> p t c", t=4), func=AF.Copy),
                           r=[("pb", bk)], w=[K(("V", di))])
        for blk in (range(4) if cfg.get("vt", True) else []):
            bk = next_bank()
            lo = blk * 512
            for kc in range(8):
                sc.add("pe", I("matmul", pbank[bk][:], lhsT=wvv[:, kc, :], rhs=hT[:, kc, lo:lo + 512], start=(kc == 0), stop=(kc == 7)),
                       r=[wkv, ("hT", kc)], w=[("pb", bk)])
            sc.add("act", I("activation", out=VT[:, lo:lo + 512], in_=pbank[bk][:], func=AF.Copy), r=[("pb", bk)], w=[K(("qP", blk))])
        for di, d_ in (enumerate(DILS) if cfg.get("vt", True) else []):
            nb = 16 // d_
            if d_ == 1:
                src, skeys = VT, [K(("qP", c_)) for c_ in range(4)]
            else:
                if d_ == 4:
                    sc.add("act", I("activation", out=kP[:].rearrange("p (r l) -> p r l", r=d_), in_=VT[:].rearrange("p (l r) -> p r l", r=d_), func=AF.Copy),
                           r=[K(("qP", c_)) for c_ in range(4)], w=[K(("kP", c_)) for c_ in range(4)])
                else:
                    sc.add("dve", I("tensor_copy", out=kP[:].rearrange("p (r l) -> p r l", r=d_), in_=VT[:].rearrange("p (l r) -> p r l", r=d_)),
                           r=[K(("qP", c_)) for c_ in range(4)], w=[K(("kP", c_)) for c_ in range(4)])
                src, skeys = kP, [K(("kP", c_)) for c_ in range(4)]
            for g in range(4):
                bk = next_bank()
                for tt in range(4):
                    tp = 4 * g + tt
                    sc.add("pe", I("transpose", pbf(bk)[:, tt * P:(tt + 1) * P], src[:, tp * P:(tp + 1) * P], ident_bf[:]),
                           r=skeys + ["ident_bf"], w=[("pb", bk)])
                if g % 2 == 0:
                    sc.add("act", I("activation", out=Vp[di][:, 4 * g:4 * g + 4, :], in_=pbf(bk)[:, 0:512].rearrange("p (t c) -> p t c", t=4), func=AF.Copy),
                           r=[("pb", bk)], w=[K(("V", di))])
                else:
                    sc.add("dve", I("tensor_copy", out=Vp[di][:, 4 * g:4 * g + 4, :], in_=pbf(bk)[:, 0:512].rearrange("p (t c) -> p t c", t=4)),
                           r=[("pb", bk)], w=[K(("V", di))])

        def perm_copy(d_, c):
            if d_ == 4:
                qi_ = aq[:].rearrange("p (l r) -> p r l", r=4)[:, c, :]
                ki_ = ak[:].rearrange("p (l r) -> p r l", r=4)[:, c, :]
                qo_, ko_ = qP[:, c * 512:(c + 1) * 512], kP[:, c * 512:(c + 1) * 512]
            else:
                qi_ = aq[:].rearrange("p (l r) -> p r l", r=16)[:, 4 * c:4 * c + 4, :]
                ki_ = ak[:].rearrange("p (l r) -> p r l", r=16)[:, 4 * c:4 * c + 4, :]
                qo_ = qP[:, c * 512:(c + 1) * 512].rearrange("p (r l) -> p r l", r=4)
                ko_ = kP[:, c * 512:(c + 1) * 512].rearrange("p (r l) -> p r l", r=4)
            sc.add("pool", I("tensor_copy", out=qo_, in_=qi_), r=[K(("qk", 0))], w=[K(("qP", c))])
            sc.add("dve", I("tensor_copy", out=ko_, in_=ki_), r=[K(("qk", 1))], w=[K(("kP", c))])

        kts = []
        blk_last = {}
        for di, d_ in enumerate(DILS):
            nb = 16 // d_
            Ld = nb * P
            for r_ in range(d_):
                for kt in range(nb):
                    lo, hi = max(0, kt * P - 64), min(Ld, kt * P + 192)
                    u = dict(di=di, d=d_, T=r_ * nb + kt, c_lo=r_ * Ld + lo, c_hi=r_ * Ld + hi, m0=lo - (kt * P - 64))
                    for b in range(u["c_lo"] // 512, (u["c_hi"] - 1) // 512 + 1):
                        blk_last[(di, b)] = len(kts)
                    kts.append(u)
        started = set()

        def views(u):
            if u["d"] == 1:
                return aq, ak, [K(("qk", 0))], [K(("qk", 1))]
            cq = range(u["c_lo"] // 512, (u["c_hi"] - 1) // 512 + 1)
            return qP, kP, [K(("qP", c_)) for c_ in cq], [K(("kP", u["T"] // 4))]

        def stageAB(j, u):
            d_ = u["d"]
            if cfg.get("early", True):
                if d_ == 1 and u["T"] in (2, 5, 8, 11):
                    perm_copy(4, (u["T"] - 2) // 3)
                if d_ == 4 and u["T"] % 4 == 0 and u["T"] > 0:
                    perm_copy(16, u["T"] // 4 - 1)
                if d_ == 16 and u["T"] == 0:
                    perm_copy(16, 3)
            elif d_ > 1 and u["T"] == 0:
                for c_ in range(4):
                    perm_copy(d_, c_)
            qv, kv, rq, rk = views(u)
            T, nc_ = u["T"], u["c_hi"] - u["c_lo"]
            pt = PtP[j % 3]
            for a in range(2):
                rows = slice(64 * a, 64 * a + 64)
                bkS = 4 + ((2 * j + a) % 4)
                sc.add("pe", I("matmul", pbank[bkS][:, 0:nc_], lhsT=kv[rows, T * P:(T + 1) * P], rhs=qv[rows, u["c_lo"]:u["c_hi"]], start=True, stop=True),
                       r=rq + rk, w=[("pb", bkS)])
            for a in range(2):
                bkS = 4 + ((2 * j + a) % 4)
                sc.add("act", I("activation", out=pt[:, a, 0:nc_], in_=pbank[bkS][:, 0:nc_], func=AF.Exp, scale=0.125),
                       r=[("pb", bkS)], w=[K(("Pt", j % 3, a))])
            meng = "pool" if (j % 3 == 0 and nc_ == 256 and cfg.get("poolmask", True)) else "dve"
            sc.add(meng, I("tensor_tensor", out=pt[:, :, 0:nc_], in0=pt[:, :, 0:nc_], in1=mask2[:, :, u["m0"]:u["m0"] + nc_], op=ALU.mult),
                   r=[K(("Pt", j % 3, 0)), K(("Pt", j % 3, 1)), "mask2"], w=[K(("Pt", j % 3, 0)), K(("Pt", j % 3, 1))])

        def stageC(j, u):
            di, d_, T = u["di"], u["d"], u["T"]
            pt = PtP[j % 3]
            blks = list(range(u["c_lo"] // 512, (u["c_hi"] - 1) // 512 + 1))
            for b in blks:
                s_lo, s_hi = max(u["c_lo"], 512 * b), min(u["c_hi"], 512 * (b + 1))
                gi_ = di * 4 + b
                bkN, bkD = (0, 1) if gi_ % 2 == 0 else (2, 3)
                for tag in ("N", "D"):
                    for a in range(2):
                        rows = slice(64 * a, 64 * a + 64)
                        mv = pt[:, a, s_lo - u["c_lo"]:s_hi - u["c_lo"]]
                        if tag == "N":
                            bk_, lhs, rkeys = bkN, Vp[di][:, T, rows], [K(("V", di))]
                        else:
                            bk_, lhs, rkeys = bkD, ones1_bf[:, 0:64], ["ones1_bf"]
                        first = (gi_, a, tag) not in started
                        started.add((gi_, a, tag))
                        sc.add("pe", I("matmul", pbank[bk_][rows, s_lo - 512 * b:s_hi - 512 * b], lhsT=lhs, rhs=mv, start=first, stop=True, skip_group_check=True),
                               r=rkeys + [K(("Pt", j % 3, a))], w=[("pb", bk_)])
            for g in blks:
                if blk_last[(di, g)] != j:
                    continue
                gi_ = di * 4 + g
                bkN, bkD = (0, 1) if gi_ % 2 == 0 else (2, 3)
                if d_ == 1:
                    sc.add("act", I("activation", out=accn[:, g * 512:(g + 1) * 512], in_=pbank[bkN][:], func=AF.Copy), r=[("pb", bkN)], w=[K("accn")])
                    sc.add("dve", I("tensor_copy", out=accd[:, g * 512:(g + 1) * 512], in_=pbank[bkD][:]), r=[("pb", bkD)], w=[K("accd")])
                else:
                    if d_ == 4:
                        vn = accn[:].rearrange("p (l r) -> p r l", r=4)[:, g, :]
                        vd = accd[:].rearrange("p (l r) -> p r l", r=4)[:, g, :]
                        pn, pd = pbank[bkN][:], pbank[bkD][:]
                    else:
                        vn = accn[:].rearrange("p (l r) -> p r l", r=16)[:, 4 * g:4 * g + 4, :]
                        vd = accd[:].rearrange("p (l r) -> p r l", r=16)[:, 4 * g:4 * g + 4, :]
                        pn = pbank[bkN][:].rearrange("p (r l) -> p r l", r=4)
                        pd = pbank[bkD][:].rearrange("p (r l) -> p r l", r=4)
                    sc.add("dve", I("tensor_tensor", out=vn, in0=vn, in1=pn, op=ALU.add), r=[("pb", bkN), K("accn")], w=[K("accn")])
                    sc.add("dve", I("tensor_tensor", out=vd, in0=vd, in1=pd, op=ALU.add), r=[("pb", bkD), K("accd")], w=[K("accd")])

        for j in range(len(kts) + 1):
            if j < len(kts):
                stageAB(j, kts[j])
            if j - 1 >= 0:
                stageC(j - 1, kts[j - 1])
        sc.add("dve", I("reciprocal", out=accd[:], in_=accd[:]), r=[K("accd")], w=[K("accd")])
        sc.add("dve", I("tensor_tensor", out=yT[:, hp, :], in0=accn[:], in1=accd[:], op=ALU.mult), r=[K("accn"), K("accd")], w=[("yT", hp)])
        if hp == 3:
            sc.barrier()

    def out_proj(l, hf):
        w, wk = load_w(l * NFULL + 8 + hf)
        wv = w[:].rearrange("p (m k c) -> p m k c", m=8, k=4)
        for m in range(8):
            for blk in range(4):
                lo = blk * 512
                bk = next_bank()
                for kc in range(4):
                    sc.add("pe", I("matmul", pbank[bk][:], lhsT=wv[:, m, kc, :], rhs=yT[:, kc, lo:lo + 512], start=(kc == 0), stop=(kc == 3)),
                           r=[wk, ("yT", kc)], w=[("pb", bk)])
                sc.add("dve", I("tensor_tensor", out=xT[:, m, lo:lo + 512], in0=xT[:, m, lo:lo + 512], in1=pbank[bk][:], op=ALU.add),
                       r=[("pb", bk), ("xT", m)], w=[("xT", m)])

    for l in range(nlay):
        if cfg["mlstm"] or cfg["attn"]:
            rmsnorm("g1", l * 8, 0, S, lambda c, lo, hi: hT[:, c, lo:hi], lambda c: [("hT", c)])
        if cfg["mlstm"]:
            gates(l)
            for hp in range(2):
                mlstm_pair(l, hp)
            out_proj(l, 0)
        if cfg["attn"]:
            for hp in range(4):
                attn_pair(l, hp)
            out_proj(l, 1)
            sc.barrier()
        if cfg["ffn"]:
            ffn(l)
            sc.barrier()

    ov = out_d.rearrange("(c p) t -> p c t", p=P)
    sc.barrier()
    finT = U[:, 0:2048].bitcast(F32).rearrange("p (a t) -> p a t", a=2)
    fcnt = {"n": 0}

    def fin_dst(c, lo, hi):
        return finT[:, c % 2, :]

    for b in range(4):
        lo, hi = b * 512, (b + 1) * 512
        bk = next_bank()
        for c in range(8):
            q = sqb[c % 2]
            sc.add("act", I("activation", out=q[:], in_=xT[:, c, lo:hi], func=AF.Square),
                   r=[("xT", c)], w=[("sqb", c % 2)])
            sc.add("pe", I("matmul", pbank[bk][:], lhsT=ones_bf[:], rhs=q[:], start=(c == 0), stop=(c == 7)),
                   r=[("sqb", c % 2), "ones_bf"], w=[("pb", bk)])
        r_ = rst[b % 2]
        sc.add("act", I("activation", out=r_[:], in_=pbank[bk][:], func=AF.Ln, bias=C("epsc", 0, 1), scale=1.0),
               r=[("pb", bk), "cst"], w=[("rst", 0)])
        sc.add("act", I("activation", out=r_[:], in_=r_[:], func=AF.Exp, scale=-0.5),
               r=[("rst", 0)], w=[("rst", 0)])
        for c in range(8):
            sl = c % 2
            sc.add("dve", I("scalar_tensor_tensor",
                out=finT[:, sl, :], in0=xT[:, c, lo:hi], scalar=C("gf", c, 1), in1=r_[:], op0=ALU.mult, op1=ALU.mult),
                r=[("xT", c), ("rst", 0), "cst"], w=[("fin", sl)])
            sc.add("sp", I("dma_start", out=ov[:, c, lo:hi], in_=finT[:, sl, :]),
                   r=[("fin", sl)], w=[("out", sl)], dma_slot=("out", sl))
    sc.add("sp", None, r=[("out", 0), ("out", 1)])

    sc.emit(nc, es)
    es.close()
    return nc


_PREP_CACHE = {}


def kernel(x, norm1_g, w_in, conv_w, gate_i_b, gate_f_b, head_norm_g, w_out, norm2_g, w_up, w_down, final_g, _cfg=None):
    cfg = dict(CFG)
    if _cfg:
        cfg.update(_cfg)
    inp = dict(x=x, norm1_g=norm1_g, w_in=w_in, conv_w=conv_w, gate_i_b=gate_i_b, gate_f_b=gate_f_b,
               head_norm_g=head_norm_g, w_out=w_out, norm2_g=norm2_g, w_up=w_up, w_down=w_down, final_g=final_g)
    inp = {k: np.asarray(v) for k, v in inp.items()}
    wfull, wsmall = prep_weights(inp)
    cst = prep_consts(inp)
    rope = prep_rope()
    nc = build(cfg)
    xs = np.asarray(inp["x"], np.float32)
    in_maps = []
    for b in range(8):
        in_maps.append({"xT": np.ascontiguousarray(xs[b].T), "wfull": wfull, "wsmall": wsmall, "cst": cst, "rope": rope})
    res = run_bass_kernel_spmd(nc, in_maps, core_ids=list(range(8)))
    out = np.stack([np.ascontiguousarray(r["outT"].T) for r in res.results], axis=0)
    return out.astype(np.float32)
```

```python
import math
from contextlib import ExitStack

import numpy as np
import concourse.bass as bass
import concourse.mybir as mybir
from concourse.bass_utils import run_bass_kernel_spmd

F32 = mybir.dt.float32
BF16 = mybir.dt.bfloat16
ALU = mybir.AluOpType
AF = mybir.ActivationFunctionType
AX = mybir.AxisListType

P = 128
S = 2048
D = 1024
DFF = 4096
NL = 2
EPS = 1e-6
NFULL = 26
NSMALL = 5
TB = 1024

CFG = {"mlstm": True, "attn": True, "ffn": True, "layers": NL, "debug": False}


def I(name, *args, **kw):
    return (name, args, kw)


class Op:
    __slots__ = ("eng", "fn", "deps", "dma", "slot", "tok", "signal", "waits", "idx", "semkey", "semval")


class WK(tuple):
    def __new__(cls, slot):
        return tuple.__new__(cls, (("wb", slot, 0), ("wb", slot, 1)))


def _flat(keys):
    out = []
    for k in keys:
        if isinstance(k, WK):
            out.extend(k)
        else:
            out.append(k)
    return out


class Sched:
    EPOCH = 30000

    def __init__(self):
        self.ops = []
        self.last_w = {}
        self.readers = {}
        self.stream_len = {"pe": 0, "dve": 0, "act": 0, "pool": 0, "sp": 0}
        self.slot_cnt = {}

    def add(self, eng, fn, r=(), w=(), dma_slot=None):
        op = Op()
        op.eng, op.fn, op.dma, op.slot = eng, fn, dma_slot is not None, dma_slot
        op.signal = False
        op.waits = []
        op.idx = len(self.ops)
        r = _flat(r)
        w = _flat(w)
        deps = set()
        for k in r:
            if k in self.last_w:
                deps.add(self.last_w[k])
        for k in w:
            if k in self.last_w:
                deps.add(self.last_w[k])
            for j in self.readers.get(k, ()):
                deps.add(j)
        op.deps = deps
        if op.dma:
            c = self.slot_cnt.get(dma_slot, 0) + 1
            self.slot_cnt[dma_slot] = c
            op.tok = (("dma", dma_slot), c)
            self.stream_len[eng] += 1
        else:
            self.stream_len[eng] += 1
            op.tok = (("eng", eng), self.stream_len[eng])
        for k in r:
            self.readers.setdefault(k, []).append(op.idx)
        for k in w:
            self.last_w[k] = op.idx
            self.readers[k] = []
        self.ops.append(op)
        return op

    def barrier(self):
        last = {}
        for op in self.ops:
            if op.fn is not None:
                last[op.eng] = op.idx
        for e in list(self.stream_len):
            op = Op()
            op.eng, op.fn, op.dma, op.slot = e, None, False, None
            op.signal = False
            op.waits = []
            op.idx = len(self.ops)
            op.deps = set(v for k, v in last.items() if k != e)
            self.stream_len[e] += 1
            op.tok = (("eng", e), self.stream_len[e])
            self.ops.append(op)

    def analyse(self):
        seen = {e: {} for e in self.stream_len}
        for op in self.ops:
            sn = seen[op.eng]
            for j in sorted(op.deps):
                d = self.ops[j]
                if (not d.dma) and d.eng == "pe" and op.eng == "pe" and not op.dma:
                    continue
                fam, v = d.tok
                if sn.get(fam, 0) >= v:
                    continue
                sn[fam] = v
                d.signal = True
                op.waits.append(j)
        rank = {e: 0 for e in self.stream_len}
        self.nepoch = {e: 1 for e in self.stream_len}
        for op in self.ops:
            if op.dma:
                op.semkey = op.tok[0]
                op.semval = 16 * op.tok[1]
            elif op.signal:
                r = rank[op.eng]
                rank[op.eng] = r + 1
                ep = r // self.EPOCH
                self.nepoch[op.eng] = max(self.nepoch[op.eng], ep + 1)
                op.semkey = ("eng", op.eng, ep)
                op.semval = r - ep * self.EPOCH + 1

    def emit(self, nc, es):
        self.analyse()
        sems = {}

        def getsem(key):
            if key not in sems:
                nm = "s_" + "_".join(str(x) for x in key).replace("(", "").replace(")", "").replace(",", "_").replace(" ", "").replace("'", "")
                sems[key] = es.enter_context(nc.semaphore(nm))
            return sems[key]

        for op in self.ops:
            if op.dma or op.signal:
                getsem(op.semkey)
        block = es.enter_context(nc.Block())
        by_eng = {e: [o for o in self.ops if o.eng == e] for e in self.stream_len}

        def body(eng_name):
            def f(eng):
                for op in by_eng[eng_name]:
                    for j in op.waits:
                        d = self.ops[j]
                        eng.wait_ge(sems[d.semkey], d.semval)
                    if op.fn is None:
                        continue
                    name, args, kw = op.fn
                    ins = getattr(eng, name)(*args, **kw)
                    if op.dma:
                        ins.then_inc(sems[op.semkey], 16)
                    elif op.signal:
                        ins.then_inc(sems[op.semkey], 1)
            return f

        block.tensor(body("pe"))
        block.vector(body("dve"))
        block.scalar(body("act"))
        block.gpsimd(body("pool"))
        block.sync(body("sp"))


def _bform(w, cols):
    out = np.empty((P, 4, 8, P), np.float32)
    for n, ci in enumerate(cols):
        blk = w[:, ci]
        out[:, n] = blk.reshape(8, P, P).transpose(1, 0, 2)
    return out.reshape(P, 4096)


def _aform(w, ci):
    blk = w[:, ci]
    C = blk.shape[1]
    return blk.reshape(8, P, C).transpose(1, 0, 2).reshape(P, 8 * C)


def prep_weights(inp):
    wfull = np.zeros((NL * NFULL, P, 4096), np.float32)
    wsmall = np.zeros((NL * NSMALL, P, 1024), np.float32)
    ar = np.arange(P)
    for l in range(NL):
        w_in = np.asarray(inp["w_in"][l], np.float32)
        w_out = np.asarray(inp["w_out"][l], np.float32)
        w_up = np.asarray(inp["w_up"][l], np.float32)
        w_dn = np.asarray(inp["w_down"][l], np.float32)
        fb = l * NFULL
        sb = l * NSMALL
        for hp in range(2):
            h0, h1 = 2 * hp, 2 * hp + 1
            wfull[fb + 2 * hp] = _bform(w_in, [h0 * P + ar, h1 * P + ar, 512 + h0 * P + ar, 512 + h1 * P + ar])
            ci = np.concatenate([1024 + h0 * P + ar, 1024 + h1 * P + ar, 1536 + h0 * P + ar, 1536 + h1 * P + ar])
            wfull[fb + 2 * hp + 1] = _aform(w_in, ci)
        a = ar // 64
        dd = ar % 64
        sw = a * 64 + (dd + 32) % 64
        for hp in range(4):
            qb, kb = 2064 + hp * P, 2576 + hp * P
            wfull[fb + 4 + hp] = _bform(w_in, [qb + ar, qb + sw, kb + ar, kb + sw])
            wsmall[sb + 1 + hp] = _aform(w_in, 3088 + hp * P + ar)
        wsmall[sb + 0, :, :128] = _aform(w_in, 2048 + np.arange(16))
        for hf in range(2):
            blk = w_out[hf * 512:(hf + 1) * 512, :]
            t = blk.reshape(4, P, 8, P).transpose(1, 2, 0, 3)
            wfull[fb + 8 + hf] = t.reshape(P, 4096)
        for g in range(8):
            wfull[fb + 10 + g] = _bform(w_up, [(4 * g + n) * P + ar for n in range(4)])
        for m in range(8):
            blk = w_dn[:, m * P:(m + 1) * P]
            wfull[fb + 18 + m] = blk.reshape(32, P, P).transpose(1, 0, 2).reshape(P, 4096)
    return wfull, wsmall


def _cst_layout():
    off = {}
    o = 0

    def put(name, n):
        nonlocal o
        off[name] = (o, n)
        o += n
    put("g1", NL * 8)
    put("g2", NL * 8)
    put("gf", 8)
    put("convw", NL * 40)
    put("gb", NL * 16)
    for nm in ("Tf", "Tb", "NEGf", "NEGb", "identF", "onesF", "sel127", "sel0"):
        put(nm, P)
    put("epsc", 4)
    off["_NCS"] = (o, 0)
    put("hng", NL * 512)
    put("band", 512)
    return off, o


CST_OFF, NCST = _cst_layout()
NCS = CST_OFF["_NCS"][0]


def prep_consts(inp):
    c = np.zeros((P, NCST), np.float32)

    def setc(name, arr):
        o, n = CST_OFF[name]
        c[:, o:o + n] = arr.reshape(P, n)
    g1 = np.asarray(inp["norm1_g"], np.float32).reshape(NL, 8, P).transpose(2, 0, 1)
    g2 = np.asarray(inp["norm2_g"], np.float32).reshape(NL, 8, P).transpose(2, 0, 1)
    gf = np.asarray(inp["final_g"], np.float32).reshape(8, P).transpose(1, 0)
    setc("g1", np.ascontiguousarray(g1))
    setc("g2", np.ascontiguousarray(g2))
    setc("gf", np.ascontiguousarray(gf))
    cw = np.asarray(inp["conv_w"], np.float32).reshape(NL, 5, 8, P).transpose(3, 0, 1, 2)
    setc("convw", np.ascontiguousarray(cw))
    gb = np.concatenate([np.asarray(inp["gate_i_b"], np.float32), np.asarray(inp["gate_f_b"], np.float32)], axis=1)
    setc("gb", np.ascontiguousarray(np.broadcast_to(gb.reshape(1, NL * 16), (P, NL * 16))))
    hng = np.asarray(inp["head_norm_g"], np.float32).reshape(1, NL * 512)
    setc("hng", np.ascontiguousarray(np.broadcast_to(hng, (P, NL * 512))))
    k = np.arange(P)[:, None]
    i = np.arange(P)[None, :]
    NEG = -30000.0
    setc("Tf", (k <= i).astype(np.float32))
    setc("Tb", (k >= i).astype(np.float32))
    setc("NEGf", np.where(k <= i, 0.0, NEG).astype(np.float32))
    setc("NEGb", np.where(k >= i, 0.0, NEG).astype(np.float32))
    setc("identF", np.eye(P, dtype=np.float32))
    setc("onesF", np.ones((P, P), np.float32))
    s127 = np.zeros((P, P), np.float32)
    s127[127, :] = 1.0
    s0 = np.zeros((P, P), np.float32)
    s0[0, :] = 1.0
    setc("sel127", s127)
    setc("sel0", s0)
    cc = np.arange(256)[None, :]
    m1 = ((cc >= k) & (cc <= k + 128)).astype(np.float32)
    band = np.concatenate([m1, m1], axis=1)
    setc("band", band)
    ec = np.zeros((P, 4), np.float32)
    ec[:, 0] = EPS
    ec[:, 1] = 1.0
    ec[:, 2] = -0.5 * math.log(128.0)
    setc("epsc", ec)
    return c


def prep_rope():
    half = 32
    inv = (10000.0 ** (-np.arange(half, dtype=np.float32) / half)).astype(np.float32)
    ang = np.arange(S, dtype=np.float32)[:, None] * inv[None, :]
    cos = np.cos(ang).astype(np.float32).T
    sin = np.sin(ang).astype(np.float32).T
    cos64 = np.concatenate([cos, cos], 0)
    sin64 = np.concatenate([-sin, sin], 0)
    r = np.zeros((2, P, S), np.float32)
    r[0] = np.concatenate([cos64, cos64], 0)
    r[1] = np.concatenate([sin64, sin64], 0)
    return r


def build(cfg=CFG):
    nc = bass.Bass("TRN2", target_bir_lowering=False)
    nlay = cfg["layers"]
    xT_d = nc.dram_tensor("xT", [D, S], F32, kind="ExternalInput").ap()
    wf_d = nc.dram_tensor("wfull", [NL * NFULL, P, 4096], F32, kind="ExternalInput").ap()
    ws_d = nc.dram_tensor("wsmall", [NL * NSMALL, P, 1024], F32, kind="ExternalInput").ap()
    cst_d = nc.dram_tensor("cst", [P, NCST], F32, kind="ExternalInput").ap()
    rope_d = nc.dram_tensor("rope", [2, P, S], F32, kind="ExternalInput").ap()
    out_d = nc.dram_tensor("outT", [D, S], F32, kind="ExternalOutput").ap()

    es = ExitStack()
    sc = Sched()

    def sb(name, shape, dt=F32):
        return es.enter_context(nc.sbuf_tensor(name, shape, dt))

    def ps(name, shape, dt=F32):
        return es.enter_context(nc.psum_tensor(name, shape, dt))

    xT = sb("xT_sb", [P, 8, S])
    cst = sb("cst_sb", [P, NCS])
    NWB = 2
    wb = [sb(f"wb{i}", [P, 4096], BF16) for i in range(NWB)]
    wsb = sb("wsb", [P, 1024], BF16)
    NU = 54944
    U = sb("U", [P, NU], BF16)
    hT = U[:, 0:16384].rearrange("p (c t) -> p c t", c=8)
    yT = U[:, 16384:24576].rearrange("p (c t) -> p c t", c=4)
    RBASE = 24576

    class Carver:
        def __init__(self, base):
            self.o = base

        def bf(self, n):
            a = U[:, self.o:self.o + n]
            self.o += n + (n % 2)
            assert self.o <= NU
            return a

        def f32(self, n):
            a = U[:, self.o:self.o + 2 * n].bitcast(F32)
            self.o += 2 * n
            assert self.o <= NU
            return a

    sqb = [sb(f"sqb{i}", [P, 512], BF16) for i in range(2)]
    rst = [sb("rst0", [P, 512])] * 2
    ones_bf = sb("ones_bf", [P, P], BF16)
    ones1_bf = sb("ones1_bf", [P, 64], BF16)
    ident_bf = sb("ident_bf", [P, P], BF16)
    mask2 = sb("mask2", [P, 2, 256], BF16)
    diag = [sb(f"diag{i}", [P, 5, P], BF16) for i in range(2)]
    G = {nm: sb("G_" + nm, [P, 2, 16, 4]) for nm in ("gi", "lf", "bcol", "col", "blast", "w", "decay")}
    graw = sb("graw", [P, 16, 16])
    pbank = [ps(f"pb{i}", [P, 512]) for i in range(8)]

    def C(name, a=0, n=None):
        o, ln = CST_OFF[name]
        n = ln - a if n is None else n
        return cst[:, o + a:o + a + n]

    sc.add("sp", I("dma_start", out=cst[:], in_=cst_d[:, 0:NCS]), w=["cst"], dma_slot="cst")
    xv = xT_d.rearrange("(c p) t -> p c t", p=P)
    for c in range(8):
        sc.add("sp", I("dma_start", out=xT[:, c, :], in_=xv[:, c, :]), w=[("xT", c)], dma_slot=("x", c))
    sc.add("pool", I("memset", ones_bf[:], 1.0 / D), w=["ones_bf"])
    sc.add("pool", I("memset", ones1_bf[:], 1.0), w=["ones1_bf"])
    sc.add("dve", I("tensor_copy", out=ident_bf[:], in_=C("identF")), r=["cst"], w=["ident_bf"])
    bo = CST_OFF["band"][0]
    sc.add("pool", I("dma_start", out=mask2[:].rearrange("p a b -> p (a b)"), in_=cst_d[:, bo:bo + 512]), w=["mask2"], dma_slot="band")

    wstate = {"n": 0, "issued": 0}
    plan = []
    for l_ in range(nlay):
        fb_ = l_ * NFULL
        if cfg["mlstm"]:
            plan += [(fb_ + 0, 4096, 0), (fb_ + 1, 4096, 0), (fb_ + 2, 4096, 0), (fb_ + 3, 4096, 0), (fb_ + 8, 4096, 0)]
        if cfg["attn"]:
            plan += [(fb_ + 4 + hp_, 4096, 0) for hp_ in range(4)] + [(fb_ + 9, 4096, 0)]
        if cfg["ffn"]:
            for half_ in range(2):
                plan += [(fb_ + 10 + half_ * 4 + g_, 4096, 0) for g_ in range(4)]
                plan += [(fb_ + 18 + m_, 2048, half_ * 2048) for m_ in range(8)]

    def _issue(idx):
        gidx, ncols, coloff = plan[idx]
        slot = idx % NWB
        buf = wb[slot]
        for hlf in range(ncols // 2048):
            sc.add("pool", I("dma_start", out=buf[:, hlf * 2048:(hlf + 1) * 2048], in_=wf_d[gidx, :, coloff + hlf * 2048:coloff + (hlf + 1) * 2048]),
                   w=[("wb", slot, hlf)], dma_slot=("wb", slot, hlf))

    def load_w(gidx, ncols=4096, coloff=0):
        n = wstate["n"]
        assert plan[n] == (gidx, ncols, coloff), (n, plan[n], gidx, ncols, coloff)
        wstate["n"] = n + 1
        while wstate["issued"] <= min(n + 1, len(plan) - 1):
            _issue(wstate["issued"])
            wstate["issued"] += 1
        slot = n % NWB
        if ncols // 2048 == 1:
            return wb[slot], ("wb", slot, 0)
        return wb[slot], WK(slot)

    def load_wsmall(gidx, ncols=1024):
        sc.add("pool", I("dma_start", out=wsb[:, 0:ncols], in_=ws_d[gidx, :, 0:ncols]), w=["wsb"], dma_slot="wsb")
        return wsb, "wsb"

    pcount = {"n": 0}

    def next_bank():
        i = pcount["n"] % 8
        pcount["n"] += 1
        return i

    def rmsnorm(gname, goff, t0, t1, dst_fn, dst_keys):
        nb = (t1 - t0) // 512
        for b in range(nb):
            lo = t0 + b * 512
            hi = lo + 512
            bk = next_bank()
            for c in range(8):
                q = sqb[c % 2]
                sc.add("act", I("activation", out=q[:], in_=xT[:, c, lo:hi], func=AF.Square),
                       r=[("xT", c)], w=[("sqb", c % 2)])
                sc.add("pe", I("matmul", pbank[bk][:], lhsT=ones_bf[:], rhs=q[:], start=(c == 0), stop=(c == 7)),
                       r=[("sqb", c % 2), "ones_bf"], w=[("pb", bk)])
            r_ = rst[b % 2]
            sc.add("act", I("activation", out=r_[:], in_=pbank[bk][:], func=AF.Ln, bias=C("epsc", 0, 1), scale=1.0),
                   r=[("pb", bk), "cst"], w=[("rst", 0)])
            sc.add("act", I("activation", out=r_[:], in_=r_[:], func=AF.Exp, scale=-0.5),
                   r=[("rst", 0)], w=[("rst", 0)])
            for c in range(8):
                eng = "dve"
                sc.add(eng, I("scalar_tensor_tensor",
                    out=dst_fn(c, lo, hi), in0=xT[:, c, lo:hi], scalar=C(gname, goff + c, 1), in1=r_[:], op0=ALU.mult, op1=ALU.mult),
                    r=[("xT", c), ("rst", 0), "cst"], w=dst_keys(c))

    def ffn(l):
        uTv = U[:, 16384:16384 + 16 * S].rearrange("p (n t) -> p n t", t=S)
        rtmp = [U[:, 49152 + i * 1024:49152 + (i + 1) * 1024].bitcast(F32) for i in range(2)]
        rmsnorm("g2", l * 8, 0, S, lambda c, lo, hi: hT[:, c, lo:hi], lambda c: [("hT", c)])
        cnt = 0
        for half in range(2):
            for g in range(4):
                w, wk = load_w(l * NFULL + 10 + half * 4 + g)
                wv = w[:].rearrange("p (n k c) -> p n k c", n=4, k=8)
                for n in range(4):
                    ch = 4 * g + n
                    for blk in range(4):
                        lo = blk * 512
                        bk = next_bank()
                        for kc in range(8):
                            sc.add("pe", I("matmul", pbank[bk][:], lhsT=wv[:, n, kc, :], rhs=hT[:, kc, lo:lo + 512], start=(kc == 0), stop=(kc == 7)),
                                   r=[wk, ("hT", kc)], w=[("pb", bk)])
                        rt = rtmp[cnt % 2]
                        sc.add("act", I("activation", out=rt[:], in_=pbank[bk][:], func=AF.Relu), r=[("pb", bk)], w=[("rtmp", cnt % 2)])
                        sc.add("pool", I("tensor_tensor", out=uTv[:, ch, lo:lo + 512], in0=rt[:], in1=rt[:], op=ALU.mult),
                               r=[("rtmp", cnt % 2)], w=[("uT", ch, blk)])
                        cnt += 1
            for m in range(8):
                w, wk = load_w(l * NFULL + 18 + m, ncols=2048, coloff=half * 2048)
                wv = w[:, 0:2048].rearrange("p (n c) -> p n c", n=16)
                for blk in range(4):
                    lo = blk * 512
                    bk = next_bank()
                    for n in range(16):
                        sc.add("pe", I("matmul", pbank[bk][:], lhsT=wv[:, n, :], rhs=uTv[:, n, lo:lo + 512], start=(n == 0), stop=(n == 15)),
                               r=[wk, ("uT", n, blk)], w=[("pb", bk)])
                    sc.add("dve", I("tensor_tensor", out=xT[:, m, lo:lo + 512], in0=xT[:, m, lo:lo + 512], in1=pbank[bk][:], op=ALU.add),
                           r=[("pb", bk), ("xT", m)], w=[("xT", m)])

    def proj_B(wv_n, wk, blk, bk):
        lo = blk * 512
        for kc in range(8):
            sc.add("pe", I("matmul", pbank[bk][:], lhsT=wv_n[:, kc, :], rhs=hT[:, kc, lo:lo + 512], start=(kc == 0), stop=(kc == 7)),
                   r=[wk, ("hT", kc)], w=[("pb", bk)])

    def proj_A(wv, wk, ncol, tok_ap_fn, out_ap, bk):
        for kc in range(8):
            sc.add("pe", I("matmul", out_ap, lhsT=tok_ap_fn(kc), rhs=wv[:, kc, 0:ncol], start=(kc == 0), stop=(kc == 7)),
                   r=[wk, ("hT", kc)], w=[("pb", bk)])

    def pbf(bk):
        return pbank[bk][:].bitcast(BF16)

    def gates(l):
        w, wk = load_wsmall(l * NSMALL + 0, 128)
        wv = w[:, 0:128].rearrange("p (k c) -> p k c", k=8)
        bk = next_bank()
        for t in range(16):
            proj_A(wv, wk, 16, lambda kc, t=t: hT[:, kc, t * P:(t + 1) * P], pbank[bk][:, t * 16:(t + 1) * 16], bk)
        gbv = C("gb", l * 16, 16).unsqueeze(1).to_broadcast([P, 16, 16])
        sc.add("dve", I("tensor_tensor", out=graw[:], in0=pbank[bk][:, 0:256].rearrange("p (t c) -> p t c", t=16), in1=gbv, op=ALU.add),
               r=[("pb", bk), "cst"], w=["graw"])
        gi_src = graw[:, :, 0:8].rearrange("p t (d h) -> p d t h", d=2)
        gf_src = graw[:, :, 8:16].rearrange("p t (d h) -> p d t h", d=2)
        sc.add("dve", I("tensor_scalar", out=G["gi"][:], in0=gi_src, scalar1=-0.5 * math.log(128.0), scalar2=0.0, op0=ALU.add, op1=ALU.add),
               r=["graw"], w=["G_gi"])
        sc.add("act", I("activation", out=G["w"][:], in_=gf_src, func=AF.Exp, scale=-1.0), r=["graw"], w=["G_w"])
        sc.add("act", I("activation", out=G["w"][:], in_=G["w"][:], func=AF.Ln, bias=C("epsc", 1, 1), scale=1.0), r=["G_w", "cst"], w=["G_w"])
        sc.add("dve", I("tensor_scalar", out=G["lf"][:], in0=G["w"][:], scalar1=-1.0, scalar2=0.0, op0=ALU.mult, op1=ALU.add),
               r=["G_w"], w=["G_lf"])
        bk2 = next_bank()
        for d_, Tn in ((0, "Tf"), (1, "Tb")):
            sc.add("pe", I("matmul", pbank[bk2][:, d_ * 64:(d_ + 1) * 64], lhsT=C(Tn), rhs=G["lf"][:, d_].rearrange("p t h -> p (t h)"), start=True, stop=True),
                   r=["G_lf", "cst"], w=[("pb", bk2)])
        sc.add("dve", I("tensor_copy", out=G["bcol"][:].rearrange("p d t h -> p (d t h)"), in_=pbank[bk2][:, 0:128]), r=[("pb", bk2)], w=["G_bcol"])
        sc.add("dve", I("tensor_tensor", out=G["col"][:], in0=G["gi"][:], in1=G["bcol"][:], op=ALU.subtract), r=["G_gi", "G_bcol"], w=["G_col"])
        bk3 = next_bank()
        for d_, Sn in ((0, "sel127"), (1, "sel0")):
            sc.add("pe", I("matmul", pbank[bk3][:, d_ * 64:(d_ + 1) * 64], lhsT=C(Sn), rhs=G["bcol"][:, d_].rearrange("p t h -> p (t h)"), start=True, stop=True),
                   r=["G_bcol", "cst"], w=[("pb", bk3)])
        sc.add("dve", I("tensor_copy", out=G["blast"][:].rearrange("p d t h -> p (d t h)"), in_=pbank[bk3][:, 0:128]), r=[("pb", bk3)], w=["G_blast"])
        sc.add("act", I("activation", out=G["decay"][:], in_=G["blast"][:], func=AF.Exp), r=["G_blast"], w=["G_decay"])
        sc.add("dve", I("tensor_tensor", out=G["w"][:], in0=G["col"][:], in1=G["blast"][:], op=ALU.add), r=["G_col", "G_blast"], w=["G_w"])
        sc.add("act", I("activation", out=G["w"][:], in_=G["w"][:], func=AF.Exp), r=["G_w"], w=["G_w"])

    def mlstm_pair(l, hp):
        cv = Carver(RBASE)
        mqk = cv.bf(4 * S).rearrange("p (n t) -> p n t", n=4)
        Vext = cv.bf(16 * 2 * 129).rearrange("p (t a c) -> p t a c", t=16, a=2)
        ogt = cv.bf(16 * 256).rearrange("p (t c) -> p t c", t=16)
        Cstb = cv.bf(16 * 2 * 129).rearrange("p (t a c) -> p t a c", t=16, a=2)
        tmp_base = cv.o
        pre = [cv.bf(2052), cv.bf(2052)]
        cv.o = tmp_base
        Tlf = [cv.f32(256).rearrange("p (a i) -> p a i", a=2) for _ in range(2)]
        rhs2 = [cv.f32(256).rearrange("p (a i) -> p a i", a=2) for _ in range(2)]
        Dx = [cv.f32(256) for _ in range(2)]
        eb = [cv.bf(256).rearrange("p (a i) -> p a i", a=2) for _ in range(2)]
        qs = [cv.bf(256).rearrange("p (a i) -> p a i", a=2) for _ in range(2)]
        PT = [cv.bf(256).rearrange("p (a i) -> p a i", a=2) for _ in range(2)]
        cv.o = max(cv.o, tmp_base + 2 * 2052)
        wK = cv.bf(256).rearrange("p (a c) -> p a c", a=2)
        Crun = [cv.f32(258).rearrange("p (a c) -> p a c", a=2) for _ in range(2)]
        Cf = cv.bf(258).rearrange("p (a c) -> p a c", a=2)
        hsum = cv.f32(256).rearrange("p (a c) -> p a c", a=2)
        sq = cv.f32(256).rearrange("p (a c) -> p a c", a=2)
        gog = cv.f32(256)
        halfg = cv.f32(256)
        ybf = cv.bf(256).rearrange("p (a c) -> p a c", a=2)
        rr = cv.f32(4)
        ssq = cv.f32(2)
        K = lambda nm: ("m", nm)

        w, wk = load_w(l * NFULL + 2 * hp)
        wv = w[:].rearrange("p (n k c) -> p n k c", n=4, k=8)
        for i in range(2):
            sc.add("pool", I("memset", pre[i][:, 0:2], 0.0), w=[K(("pre", i))])
            sc.add("pool", I("memset", pre[i][:, 2050:2052], 0.0), w=[K(("pre", i))])
        for n in range(4):
            c8 = (2 * hp + n) if n < 2 else (4 + 2 * hp + n - 2)
            pb_ = pre[n % 2]
            dg = diag[n % 2]
            for tau in range(5):
                sc.add("pool", I("tensor_tensor", out=dg[:, tau, :], in0=C("identF"), in1=C("convw", l * 40 + tau * 8 + c8, 1).to_broadcast([P, P]), op=ALU.mult),
                       r=["ident_bf", "cst"], w=[("diag", n % 2)])
            for blk in range(4):
                bk = next_bank()
                proj_B(wv[:, n], wk, blk, bk)
                sc.add("act", I("activation", out=pb_[:, 2 + blk * 512:2 + (blk + 1) * 512], in_=pbank[bk][:], func=AF.Copy),
                       r=[("pb", bk)], w=[K(("pre", n % 2))])
            for blk in range(4):
                bk = next_bank()
                for tau in range(5):
                    sc.add("pe", I("matmul", pbank[bk][:], lhsT=dg[:, tau, :], rhs=pb_[:, blk * 512 + tau:blk * 512 + tau + 512], start=(tau == 0), stop=(tau == 4)),
                           r=[K(("pre", n % 2)), ("diag", n % 2)], w=[("pb", bk)])
                sc.add("act", I("activation", out=mqk[:, n, blk * 512:(blk + 1) * 512], in_=pbank[bk][:], func=AF.Silu),
                       r=[("pb", bk)], w=[K(("mqk", n))])
        w, wk = load_w(l * NFULL + 2 * hp + 1)
        wv = w[:].rearrange("p (k c) -> p k c", k=8)
        sc.add("pool", I("memset", Vext[:, :, :, 128:129], 1.0), w=[K("Vext")])
        for t in range(16):
            bk = next_bank()
            proj_A(wv, wk, 512, lambda kc, t=t: hT[:, kc, t * P:(t + 1) * P], pbank[bk][:], bk)
            sc.add("act", I("activation", out=Vext[:, t, :, 0:128], in_=pbank[bk][:, 0:256].rearrange("p (a c) -> p a c", a=2), func=AF.Copy),
                   r=[("pb", bk)], w=[K("Vext")])
            sc.add("act", I("activation", out=ogt[:, t, :], in_=pbank[bk][:, 256:512], func=AF.Tanh, scale=0.5),
                   r=[("pb", bk)], w=[K("ogt")])
        ho = CST_OFF["hng"][0] + l * 512 + hp * 256
        sc.add("sp", I("dma_start", out=halfg[:], in_=cst_d[:, ho:ho + 256]), w=[K("halfg")], dma_slot="hng")
        sc.add("pool", I("tensor_scalar", out=halfg[:], in0=halfg[:], scalar1=0.5, scalar2=0.0, op0=ALU.mult, op1=ALU.add),
               r=[K("halfg")], w=[K("halfg")])

        def gsl(nm, d_, t):
            return G[nm][:, d_, t, 2 * hp:2 * hp + 2]

        def state_update(d_, t):
            bkT = next_bank()
            for a in range(2):
                sc.add("pe", I("transpose", pbf(bkT)[:, a * P:(a + 1) * P], mqk[:, 2 + a, t * P:(t + 1) * P], ident_bf[:]),
                       r=[K(("mqk", 2 + a)), "ident_bf"], w=[("pb", bkT)])
            sc.add("dve", I("tensor_tensor", out=wK[:], in0=pbf(bkT)[:, 0:256].rearrange("p (a c) -> p a c", a=2),
                                                     in1=gsl("w", d_, t).unsqueeze(2).to_broadcast([P, 2, P]), op=ALU.mult),
                   r=[("pb", bkT), "G_w"], w=[K("wK")])
            bkD = next_bank()
            for a in range(2):
                sc.add("pe", I("matmul", pbank[bkD][:, a * 129:(a + 1) * 129], lhsT=wK[:, a, :], rhs=Vext[:, t, a, :], start=True, stop=True),
                       r=[K("wK"), K("Vext")], w=[("pb", bkD)])
            for a in range(2):
                sc.add("dve", I("scalar_tensor_tensor", out=Crun[d_][:, a, :], in0=Crun[d_][:, a, :], scalar=G["decay"][:, d_, t, 2 * hp + a:2 * hp + a + 1],
                                                                    in1=pbank[bkD][:, a * 129:(a + 1) * 129], op0=ALU.mult, op1=ALU.add),
                       r=[("pb", bkD), "G_decay", K(("Crun", d_))], w=[K(("Crun", d_))])

        sc.barrier()
        for d_ in range(2):
            sc.add("pool", I("memset", Crun[d_][:], 0.0), w=[K(("Crun", d_))])
        for t in range(15, -1, -1):
            sc.add("act", I("activation", out=Cstb[:, t], in_=Crun[1][:], func=AF.Copy), r=[K(("Crun", 1))], w=[K("Cstb")])
            if t > 0:
                state_update(1, t)

        def state_update_f1(t):
            for a in range(2):
                sc.add("pe", I("transpose", pbf(5)[:, a * P:(a + 1) * P], mqk[:, 2 + a, t * P:(t + 1) * P], ident_bf[:]),
                       r=[K(("mqk", 2 + a)), "ident_bf"], w=[("pb", 5)])
            sc.add("dve", I("tensor_tensor", out=wK[:], in0=pbf(5)[:, 0:256].rearrange("p (a c) -> p a c", a=2),
                            in1=gsl("w", 0, t).unsqueeze(2).to_broadcast([P, 2, P]), op=ALU.mult),
                   r=[("pb", 5), "G_w"], w=[K("wK")])

        def state_update_f2(t):
            for a in range(2):
                sc.add("pe", I("matmul", pbank[7][:, a * 129:(a + 1) * 129], lhsT=wK[:, a, :], rhs=Vext[:, t, a, :], start=True, stop=True),
                       r=[K("wK"), K("Vext")], w=[("pb", 7)])

        def state_update_f3(t):
            for a in range(2):
                sc.add("dve", I("scalar_tensor_tensor", out=Crun[0][:, a, :], in0=Crun[0][:, a, :], scalar=G["decay"][:, 0, t, 2 * hp + a:2 * hp + a + 1],
                                in1=pbank[7][:, a * 129:(a + 1) * 129], op0=ALU.mult, op1=ALU.add),
                       r=[("pb", 7), "G_decay", K(("Crun", 0))], w=[K(("Crun", 0))])

        def y_transposes(t):
            for a in range(2):
                sc.add("pe", I("transpose", pbf(6)[:, a * P:(a + 1) * P], ybf[:, a, :], ident_bf[:]),
                       r=[K("ybf"), "ident_bf"], w=[("pb", 6)])
            sc.add("act", I("activation", out=yT[:, 2 * hp:2 * hp + 2, t * P:(t + 1) * P], in_=pbf(6)[:, 0:256].rearrange("p (a c) -> p a c", a=2), func=AF.Copy),
                   r=[("pb", 6)], w=[("yT", 2 * hp), ("yT", 2 * hp + 1)])

        def stA(t):
            tl = slice(t * P, (t + 1) * P)
            for a in range(2):
                sc.add("pe", I("matmul", pbank[0][:, a * P:(a + 1) * P], lhsT=mqk[:, 2 + a, tl], rhs=mqk[:, a, tl], start=True, stop=True),
                       r=[K(("mqk", a)), K(("mqk", 2 + a))], w=[("pb", 0)])
            for d_ in range(2):
                Tn, Nn = ("Tf", "NEGf") if d_ == 0 else ("Tb", "NEGb")
                sc.add("pool", I("tensor_tensor", out=Tlf[d_][:], in0=C(Tn).unsqueeze(1).to_broadcast([P, 2, P]),
                                 in1=gsl("lf", d_, t).unsqueeze(2).to_broadcast([P, 2, P]), op=ALU.mult),
                       r=["cst", "G_lf"], w=[K(("Tlf", d_))])
                sc.add("pool", I("tensor_tensor", out=rhs2[d_][:], in0=C(Nn).unsqueeze(1).to_broadcast([P, 2, P]),
                                 in1=gsl("col", d_, t).unsqueeze(2).to_broadcast([P, 2, P]), op=ALU.add),
                       r=["cst", "G_col"], w=[K(("rhs2", d_))])
            for d_ in range(2):
                bk_ = 3 + d_
                Tl2 = Tlf[d_][:].rearrange("p a i -> p (a i)")
                sc.add("pe", I("matmul", pbank[bk_][:, 0:256], lhsT=C("onesF"), rhs=Tl2, start=True, stop=True),
                       r=["cst", K(("Tlf", d_))], w=[("pb", bk_)])
                sc.add("pe", I("matmul", pbank[bk_][:, 256:512], lhsT=C("onesF"), rhs=Tl2, start=True, stop=False),
                       r=["cst", K(("Tlf", d_))], w=[("pb", bk_)])
                sc.add("pe", I("matmul", pbank[bk_][:, 256:512], lhsT=C("identF"), rhs=rhs2[d_][:].rearrange("p a i -> p (a i)"), start=False, stop=True),
                       r=["cst", K(("rhs2", d_))], w=[("pb", bk_)])

        def stB(t):
            tl = slice(t * P, (t + 1) * P)
            for d_ in range(2):
                bk_ = 3 + d_
                sc.add("act", I("activation", out=Dx[d_][:], in_=pbank[bk_][:, 256:512], func=AF.Exp),
                       r=[("pb", bk_)], w=[K(("Dx", d_))])
                sc.add("act", I("activation", out=eb[d_][:].rearrange("p a i -> p (a i)"), in_=pbank[bk_][:, 0:256], func=AF.Exp),
                       r=[("pb", bk_)], w=[K(("eb", d_))])
                sc.add("dve", I("tensor_tensor", out=PT[d_][:].rearrange("p a i -> p (a i)"), in0=pbank[0][:, 0:256], in1=Dx[d_][:], op=ALU.mult),
                       r=[("pb", 0), K(("Dx", d_))], w=[K(("PT", d_))])
                sc.add("dve", I("tensor_tensor", out=qs[d_][:], in0=mqk[:, 0:2, tl], in1=eb[d_][:], op=ALU.mult),
                       r=[K(("mqk", 0)), K(("mqk", 1)), K(("eb", d_))], w=[K(("qs", d_))])

        def stC(t):
            bkO = 1 + (t % 2)
            for d_ in range(2):
                for a in range(2):
                    cprev = Cf[:, a, :] if d_ == 0 else Cstb[:, t, a, :]
                    ck = K("Cf") if d_ == 0 else K("Cstb")
                    oc = (a * 2 + d_) * P
                    sc.add("pe", I("matmul", pbank[bkO][:, oc:oc + P], lhsT=PT[d_][:, a, :], rhs=Vext[:, t, a, 0:128], start=True, stop=False),
                           r=[K(("PT", d_)), K("Vext")], w=[("pb", bkO)])
                    sc.add("pe", I("matmul", pbank[bkO][:, oc:oc + P], lhsT=qs[d_][:, a, :], rhs=cprev[:, 0:128], start=False, stop=True),
                           r=[K(("qs", d_)), ck], w=[("pb", bkO)])
            for d_ in range(2):
                for a in range(2):
                    cprev = Cf[:, a, :] if d_ == 0 else Cstb[:, t, a, :]
                    ck = K("Cf") if d_ == 0 else K("Cstb")
                    dc = 258 + a * 2 + d_
                    sc.add("pe", I("matmul", pbank[7][:, dc:dc + 1], lhsT=PT[d_][:, a, :], rhs=Vext[:, t, a, 128:129], start=True, stop=False),
                           r=[K(("PT", d_)), K("Vext")], w=[("pb", 7)])
                    sc.add("pe", I("matmul", pbank[7][:, dc:dc + 1], lhsT=qs[d_][:, a, :], rhs=cprev[:, 128:129], start=False, stop=True),
                           r=[K(("qs", d_)), ck], w=[("pb", 7)])

        def stD(t):
            bkO = 1 + (t % 2)
            rk_ = K("rr")
            sc.add("act", I("activation", out=rr[:, 0:4], in_=pbank[7][:, 258:262], func=AF.Abs), r=[("pb", 7)], w=[rk_])
            sc.add("dve", I("tensor_scalar", out=rr[:, 0:4], in0=rr[:, 0:4], scalar1=1.0, scalar2=0.0, op0=ALU.max, op1=ALU.add), r=[rk_], w=[rk_])
            sc.add("dve", I("reciprocal", out=rr[:, 0:4], in_=rr[:, 0:4]), r=[rk_], w=[rk_])
            for a in range(2):
                oc = a * 2 * P
                sc.add("act", I("activation", out=hsum[:, a, :], in_=pbank[bkO][:, oc:oc + P], func=AF.Copy, scale=rr[:, 2 * a:2 * a + 1]),
                       r=[("pb", bkO), rk_], w=[K(("hsum", a))])
            for a in range(2):
                oc = a * 2 * P
                sc.add("dve", I("scalar_tensor_tensor", out=hsum[:, a, :], in0=pbank[bkO][:, oc + P:oc + 2 * P], scalar=rr[:, 2 * a + 1:2 * a + 2], in1=hsum[:, a, :], op0=ALU.mult, op1=ALU.add),
                       r=[("pb", bkO), rk_, K(("hsum", a))], w=[K(("hsum", a))])
            for a in range(2):
                sc.add("act", I("activation", out=sq[:, a, :], in_=hsum[:, a, :], func=AF.Square, accum_out=ssq[:, a:a + 1]),
                       r=[K(("hsum", a))], w=[K(("sq", a)), K(("ssq", a))])
            hk = [K(("hsum", 0)), K(("hsum", 1))]
            sk = [K(("ssq", 0)), K(("ssq", 1))]
            sc.add("act", I("activation", out=ssq[:], in_=ssq[:], func=AF.Ln, bias=C("epsc", 0, 1), scale=1.0 / 128.0), r=sk + ["cst"], w=sk)
            sc.add("act", I("activation", out=ssq[:], in_=ssq[:], func=AF.Exp, scale=-0.5), r=sk, w=sk)
            sc.add("dve", I("scalar_tensor_tensor", out=gog[:], in0=ogt[:, t, :], scalar=1.0, in1=halfg[:], op0=ALU.add, op1=ALU.mult),
                   r=[K("ogt"), K("halfg")], w=[K("gog")])
            sc.add("dve", I("tensor_tensor", out=sq[:], in0=hsum[:], in1=ssq[:].unsqueeze(2).to_broadcast([P, 2, P]), op=ALU.mult),
                   r=hk + sk + [K(("sq", 0)), K(("sq", 1))], w=[K(("sq", 0)), K(("sq", 1))])

        def ybf_op():
            sc.add("pool", I("tensor_tensor", out=ybf[:].rearrange("p a c -> p (a c)"), in0=sq[:].rearrange("p a c -> p (a c)"), in1=gog[:], op=ALU.mult),
                   r=[K(("sq", 0)), K(("sq", 1)), K("gog")], w=[K("ybf")])

        def cf_copy():
            sc.add("act", I("activation", out=Cf[:], in_=Crun[0][:], func=AF.Copy), r=[K(("Crun", 0))], w=[K("Cf")])

        stA(0)
        stB(0)
        cf_copy()
        for t in range(16):
            stC(t)
            if t < 15:
                state_update_f1(t)
                stA(t + 1)
                state_update_f2(t)
                stB(t + 1)
                state_update_f3(t)
                cf_copy()
            if t > 0:
                ybf_op()
                y_transposes(t - 1)
            stD(t)
        ybf_op()
        y_transposes(15)
        sc.barrier()

    def attn_pair(l, hp):
        cv = Carver(RBASE)
        aq = cv.bf(S)
        ak = cv.bf(S)
        Vp = [cv.bf(16 * P).rearrange("p (t c) -> p t c", t=16) for _ in range(3)]
        accn = cv.f32(S)
        accd = cv.f32(S)
        qP = cv.bf(S)
        kP = cv.bf(S)
        PtP = [cv.bf(512).rearrange("p (a c) -> p a c", a=2) for _ in range(3)]
        ropeb = [cv.f32(1024).rearrange("p (k t) -> p k t", k=2) for _ in range(2)]
        t1 = [cv.f32(512)] * 2
        t2 = [cv.f32(512)] * 2
        K = lambda nm: ("a", nm)
        DILS = (1, 4, 16)
        scnt = {"n": 0}
        wsm_, wkv = load_wsmall(l * NSMALL + 1 + hp, 1024)
        wvv = wsm_[:, 0:1024].rearrange("p (k c) -> p k c", k=8)
        w, wk = load_w(l * NFULL + 4 + hp)
        wv = w[:].rearrange("p (n k c) -> p n k c", n=4, k=8)
        for qi, dst in ((0, aq), (1, ak)):
            for blk in range(4):
                lo = blk * 512
                rb = ropeb[blk % 2]
                sc.add("sp", I("dma_start", out=rb[:], in_=rope_d[:, :, lo:lo + 512].rearrange("k p t -> p k t")), w=[K(("rope", blk % 2))], dma_slot=("rope", blk % 2))
                bk0 = next_bank()
                proj_B(wv[:, 2 * qi], wk, blk, bk0)
                bk1 = next_bank()
                proj_B(wv[:, 2 * qi + 1], wk, blk, bk1)
                sc.add("dve", I("tensor_tensor", out=t1[blk % 2][:], in0=pbank[bk0][:], in1=rb[:, 0, :], op=ALU.mult),
                       r=[("pb", bk0), K(("rope", blk % 2))], w=[K(("t1", 0))])
                sc.add("dve", I("tensor_tensor", out=t2[blk % 2][:], in0=pbank[bk1][:], in1=rb[:, 1, :], op=ALU.mult),
                       r=[("pb", bk1), K(("rope", blk % 2))], w=[K(("t2", 0))])
                sc.add("pool", I("tensor_tensor", out=dst[:, lo:lo + 512], in0=t1[blk % 2][:], in1=t2[blk % 2][:], op=ALU.add),
                       r=[K(("t1", 0)), K(("t2", 0))], w=[K(("qk", qi))])
        VT = qP
        if not cfg.get("vt", True):
            for di, d_ in enumerate(DILS):
                nb = 16 // d_
                for g in range(4):
                    bk = next_bank()
                    for tt in range(4):
                        tp = 4 * g + tt
                        r_, lb = tp // nb, tp % nb
                        st = d_ * P * lb + r_

                        def tok(kc, st=st, d_=d_):
                            return hT[:, kc, st:st + d_ * (P - 1) + 1:d_]
                        proj_A(wvv, wkv, P, tok, pbank[bk][:, tt * P:(tt + 1) * P], bk)
                    sc.add("act", I("activation", out=Vp[di][:, 4 * g:4 * g + 4, :], in_=pbank[bk][:].rearrange("p (t c) -> p t c", t=4), func=AF.Copy),
                           r=[("pb", bk)], w=[K(("V", di))])
        for blk in (range(4) if cfg.get("vt", True) else []):
            bk = next_bank()
            lo = blk * 512
            for kc in range(8):
                sc.add("pe", I("matmul", pbank[bk][:], lhsT=wvv[:, kc, :], rhs=hT[:, kc, lo:lo + 512], start=(kc == 0), stop=(kc == 7)),
                       r=[wkv, ("hT", kc)], w=[("pb", bk)])
            sc.add("act", I("activation", out=VT[:, lo:lo + 512], in_=pbank[bk][:], func=AF.Copy), r=[("pb", bk)], w=[K(("qP", blk))])
        for di, d_ in (enumerate(DILS) if cfg.get("vt", True) else []):
            nb = 16 // d_
            if d_ == 1:
                src, skeys = VT, [K(("qP", c_)) for c_ in range(4)]
            else:
                if d_ == 4:
                    sc.add("act", I("activation", out=kP[:].rearrange("p (r l) -> p r l", r=d_), in_=VT[:].rearrange("p (l r) -> p r l", r=d_), func=AF.Copy),
                           r=[K(("qP", c_)) for c_ in range(4)], w=[K(("kP", c_)) for c_ in range(4)])
                else:
                    sc.add("dve", I("tensor_copy", out=kP[:].rearrange("p (r l) -> p r l", r=d_), in_=VT[:].rearrange("p (l r) -> p r l", r=d_)),
                           r=[K(("qP", c_)) for c_ in range(4)], w=[K(("kP", c_)) for c_ in range(4)])
                src, skeys = kP, [K(("kP", c_)) for c_ in range(4)]
            for g in range(4):
                bk = next_bank()
                for tt in range(4):
                    tp = 4 * g + tt
                    sc.add("pe", I("transpose", pbf(bk)[:, tt * P:(tt + 1) * P], src[:, tp * P:(tp + 1) * P], ident_bf[:]),
                           r=skeys + ["ident_bf"], w=[("pb", bk)])
                if g % 2 == 0:
                    sc.add("act", I("activation", out=Vp[di][:, 4 * g:4 * g + 4, :], in_=pbf(bk)[:, 0:512].rearrange("p (t c) -> p t c", t=4), func=AF.Copy),
                           r=[("pb", bk)], w=[K(("V", di))])
                else:
                    sc.add("dve", I("tensor_copy", out=Vp[di][:, 4 * g:4 * g + 4, :], in_=pbf(bk)[:, 0:512].rearrange("p (t c) -> p t c", t=4)),
                           r=[("pb", bk)], w=[K(("V", di))])

        def perm_copy(d_, c):
            if d_ == 4:
                qi_ = aq[:].rearrange("p (l r) -> p r l", r=4)[:, c, :]
                ki_ = ak[:].rearrange("p (l r) -> p r l", r=4)[:, c, :]
                qo_, ko_ = qP[:, c * 512:(c + 1) * 512], kP[:, c * 512:(c + 1) * 512]
            else:
                qi_ = aq[:].rearrange("p (l r) -> p r l", r=16)[:, 4 * c:4 * c + 4, :]
                ki_ = ak[:].rearrange("p (l r) -> p r l", r=16)[:, 4 * c:4 * c + 4, :]
                qo_ = qP[:, c * 512:(c + 1) * 512].rearrange("p (r l) -> p r l", r=4)
                ko_ = kP[:, c * 512:(c + 1) * 512].rearrange("p (r l) -> p r l", r=4)
            sc.add("pool", I("tensor_copy", out=qo_, in_=qi_), r=[K(("qk", 0))], w=[K(("qP", c))])
            sc.add("dve", I("tensor_copy", out=ko_, in_=ki_), r=[K(("qk", 1))], w=[K(("kP", c))])

        kts = []
        blk_last = {}
        for di, d_ in enumerate(DILS):
            nb = 16 // d_
            Ld = nb * P
            for r_ in range(d_):
                for kt in range(nb):
                    lo, hi = max(0, kt * P - 64), min(Ld, kt * P + 192)
                    u = dict(di=di, d=d_, T=r_ * nb + kt, c_lo=r_ * Ld + lo, c_hi=r_ * Ld + hi, m0=lo - (kt * P - 64))
                    for b in range(u["c_lo"] // 512, (u["c_hi"] - 1) // 512 + 1):
                        blk_last[(di, b)] = len(kts)
                    kts.append(u)
        started = set()

        def views(u):
            if u["d"] == 1:
                return aq, ak, [K(("qk", 0))], [K(("qk", 1))]
            cq = range(u["c_lo"] // 512, (u["c_hi"] - 1) // 512 + 1)
            return qP, kP, [K(("qP", c_)) for c_ in cq], [K(("kP", u["T"] // 4))]

        def stageAB(j, u):
            d_ = u["d"]
            if cfg.get("early", True):
                if d_ == 1 and u["T"] in (2, 5, 8, 11):
                    perm_copy(4, (u["T"] - 2) // 3)
                if d_ == 4 and u["T"] % 4 == 0 and u["T"] > 0:
                    perm_copy(16, u["T"] // 4 - 1)
                if d_ == 16 and u["T"] == 0:
                    perm_copy(16, 3)
            elif d_ > 1 and u["T"] == 0:
                for c_ in range(4):
                    perm_copy(d_, c_)
            qv, kv, rq, rk = views(u)
            T, nc_ = u["T"], u["c_hi"] - u["c_lo"]
            pt = PtP[j % 3]
            for a in range(2):
                rows = slice(64 * a, 64 * a + 64)
                bkS = 4 + ((2 * j + a) % 4)
                sc.add("pe", I("matmul", pbank[bkS][:, 0:nc_], lhsT=kv[rows, T * P:(T + 1) * P], rhs=qv[rows, u["c_lo"]:u["c_hi"]], start=True, stop=True),
                       r=rq + rk, w=[("pb", bkS)])
            for a in range(2):
                bkS = 4 + ((2 * j + a) % 4)
                sc.add("act", I("activation", out=pt[:, a, 0:nc_], in_=pbank[bkS][:, 0:nc_], func=AF.Exp, scale=0.125),
                       r=[("pb", bkS)], w=[K(("Pt", j % 3, a))])
            meng = "pool" if (j % 3 == 0 and nc_ == 256 and cfg.get("poolmask", True)) else "dve"
            sc.add(meng, I("tensor_tensor", out=pt[:, :, 0:nc_], in0=pt[:, :, 0:nc_], in1=mask2[:, :, u["m0"]:u["m0"] + nc_], op=ALU.mult),
                   r=[K(("Pt", j % 3, 0)), K(("Pt", j % 3, 1)), "mask2"], w=[K(("Pt", j % 3, 0)), K(("Pt", j % 3, 1))])

        def stageC(j, u):
            di, d_, T = u["di"], u["d"], u["T"]
            pt = PtP[j % 3]
            blks = list(range(u["c_lo"] // 512, (u["c_hi"] - 1) // 512 + 1))
            for b in blks:
                s_lo, s_hi = max(u["c_lo"], 512 * b), min(u["c_hi"], 512 * (b + 1))
                gi_ = di * 4 + b
                bkN, bkD = (0, 1) if gi_ % 2 == 0 else (2, 3)
                for tag in ("N", "D"):
                    for a in range(2):
                        rows = slice(64 * a, 64 * a + 64)
                        mv = pt[:, a, s_lo - u["c_lo"]:s_hi - u["c_lo"]]
                        if tag == "N":
                            bk_, lhs, rkeys = bkN, Vp[di][:, T, rows], [K(("V", di))]
                        else:
                            bk_, lhs, rkeys = bkD, ones1_bf[:, 0:64], ["ones1_bf"]
                        first = (gi_, a, tag) not in started
                        started.add((gi_, a, tag))
                        sc.add("pe", I("matmul", pbank[bk_][rows, s_lo - 512 * b:s_hi - 512 * b], lhsT=lhs, rhs=mv, start=first, stop=True, skip_group_check=True),
                               r=rkeys + [K(("Pt", j % 3, a))], w=[("pb", bk_)])
            for g in blks:
                if blk_last[(di, g)] != j:
                    continue
                gi_ = di * 4 + g
                bkN, bkD = (0, 1) if gi_ % 2 == 0 else (2, 3)
                if d_ == 1:
                    sc.add("act", I("activation", out=accn[:, g * 512:(g + 1) * 512], in_=pbank[bkN][:], func=AF.Copy), r=[("pb", bkN)], w=[K("accn")])
                    sc.add("dve", I("tensor_copy", out=accd[:, g * 512:(g + 1) * 512], in_=pbank[bkD][:]), r=[("pb", bkD)], w=[K("accd")])
                else:
                    if d_ == 4:
                        vn = accn[:].rearrange("p (l r) -> p r l", r=4)[:, g, :]
                        vd = accd[:].rearrange("p (l r) -> p r l", r=4)[:, g, :]
                        pn, pd = pbank[bkN][:], pbank[bkD][:]
                    else:
                        vn = accn[:].rearrange("p (l r) -> p r l", r=16)[:, 4 * g:4 * g + 4, :]
                        vd = accd[:].rearrange("p (l r) -> p r l", r=16)[:, 4 * g:4 * g + 4, :]
                        pn = pbank[bkN][:].rearrange("p (r l) -> p r l", r=4)
                        pd = pbank[bkD][:].rearrange("p (r l) -> p r l", r=4)
                    sc.add("dve", I("tensor_tensor", out=vn, in0=vn, in1=pn, op=ALU.add), r=[("pb", bkN), K("accn")], w=[K("accn")])
                    sc.add("dve", I("tensor_tensor", out=vd, in0=vd, in1=pd, op=ALU.add), r=[("pb", bkD), K("accd")], w=[K("accd")])

        for j in range(len(kts) + 1):
            if j < len(kts):
                stageAB(j, kts[j])
            if j - 1 >= 0:
                stageC(j - 1, kts[j - 1])
        sc.add("dve", I("reciprocal", out=accd[:], in_=accd[:]), r=[K("accd")], w=[K("accd")])
        sc.add("dve", I("tensor_tensor", out=yT[:, hp, :], in0=accn[:], in1=accd[:], op=ALU.mult), r=[K("accn"), K("accd")], w=[("yT", hp)])
        if hp == 3:
            sc.barrier()

    def out_proj(l, hf):
        w, wk = load_w(l * NFULL + 8 + hf)
        wv = w[:].rearrange("p (m k c) -> p m k c", m=8, k=4)
        for m in range(8):
            for blk in range(4):
                lo = blk * 512
                bk = next_bank()
                for kc in range(4):
                    sc.add("pe", I("matmul", pbank[bk][:], lhsT=wv[:, m, kc, :], rhs=yT[:, kc, lo:lo + 512], start=(kc == 0), stop=(kc == 3)),
                           r=[wk, ("yT", kc)], w=[("pb", bk)])
                sc.add("dve", I("tensor_tensor", out=xT[:, m, lo:lo + 512], in0=xT[:, m, lo:lo + 512], in1=pbank[bk][:], op=ALU.add),
                       r=[("pb", bk), ("xT", m)], w=[("xT", m)])

    for l in range(nlay):
        if cfg["mlstm"] or cfg["attn"]:
            rmsnorm("g1", l * 8, 0, S, lambda c, lo, hi: hT[:, c, lo:hi], lambda c: [("hT", c)])
        if cfg["mlstm"]:
            gates(l)
            for hp in range(2):
                mlstm_pair(l, hp)
            out_proj(l, 0)
        if cfg["attn"]:
            for hp in range(4):
                attn_pair(l, hp)
            out_proj(l, 1)
            sc.barrier()
        if cfg["ffn"]:
            ffn(l)
            sc.barrier()

    ov = out_d.rearrange("(c p) t -> p c t", p=P)
    sc.barrier()
    finT = U[:, 0:2048].bitcast(F32).rearrange("p (a t) -> p a t", a=2)
    fcnt = {"n": 0}

    def fin_dst(c, lo, hi):
        return finT[:, c % 2, :]

    for b in range(4):
        lo, hi = b * 512, (b + 1) * 512
        bk = next_bank()
        for c in range(8):
            q = sqb[c % 2]
            sc.add("act", I("activation", out=q[:], in_=xT[:, c, lo:hi], func=AF.Square),
                   r=[("xT", c)], w=[("sqb", c % 2)])
            sc.add("pe", I("matmul", pbank[bk][:], lhsT=ones_bf[:], rhs=q[:], start=(c == 0), stop=(c == 7)),
                   r=[("sqb", c % 2), "ones_bf"], w=[("pb", bk)])
        r_ = rst[b % 2]
        sc.add("act", I("activation", out=r_[:], in_=pbank[bk][:], func=AF.Ln, bias=C("epsc", 0, 1), scale=1.0),
               r=[("pb", bk), "cst"], w=[("rst", 0)])
        sc.add("act", I("activation", out=r_[:], in_=r_[:], func=AF.Exp, scale=-0.5),
               r=[("rst", 0)], w=[("rst", 0)])
        for c in range(8):
            sl = c % 2
            sc.add("dve", I("scalar_tensor_tensor",
                out=finT[:, sl, :], in0=xT[:, c, lo:hi], scalar=C("gf", c, 1), in1=r_[:], op0=ALU.mult, op1=ALU.mult),
                r=[("xT", c), ("rst", 0), "cst"], w=[("fin", sl)])
            sc.add("sp", I("dma_start", out=ov[:, c, lo:hi], in_=finT[:, sl, :]),
                   r=[("fin", sl)], w=[("out", sl)], dma_slot=("out", sl))
    sc.add("sp", None, r=[("out", 0), ("out", 1)])

    sc.emit(nc, es)
    es.close()
    return nc


_PREP_CACHE = {}


def kernel(x, norm1_g, w_in, conv_w, gate_i_b, gate_f_b, head_norm_g, w_out, norm2_g, w_up, w_down, final_g, _cfg=None):
    cfg = dict(CFG)
    if _cfg:
        cfg.update(_cfg)
    inp = dict(x=x, norm1_g=norm1_g, w_in=w_in, conv_w=conv_w, gate_i_b=gate_i_b, gate_f_b=gate_f_b,
               head_norm_g=head_norm_g, w_out=w_out, norm2_g=norm2_g, w_up=w_up, w_down=w_down, final_g=final_g)
    inp = {k: np.asarray(v) for k, v in inp.items()}
    wfull, wsmall = prep_weights(inp)
    cst = prep_consts(inp)
    rope = prep_rope()
    nc = build(cfg)
    xs = np.asarray(inp["x"], np.float32)
    in_maps = []
    for b in range(8):
        in_maps.append({"xT": np.ascontiguousarray(xs[b].T), "wfull": wfull, "wsmall": wsmall, "cst": cst, "rope": rope})
    res = run_bass_kernel_spmd(nc, in_maps, core_ids=list(range(8)))
    out = np.stack([np.ascontiguousarray(r["outT"].T) for r in res.results], axis=0)
    return out.astype(np.float32)
```

```python
import math
from contextlib import ExitStack

import numpy as np
import concourse.bass as bass
import concourse.mybir as mybir
from concourse.bass_utils import run_bass_kernel_spmd

F32 = mybir.dt.float32
BF16 = mybir.dt.bfloat16
ALU = mybir.AluOpType
AF = mybir.ActivationFunctionType
AX = mybir.AxisListType

P = 128
S = 2048
D = 1024
DFF = 4096
NL = 2
EPS = 1e-6
NFULL = 26
NSMALL = 5
TB = 1024

CFG = {"mlstm": True, "attn": True, "ffn": True, "layers": NL, "debug": False}


def I(name, *args, **kw):
    return (name, args, kw)


class Op:
    __slots__ = ("eng", "fn", "deps", "dma", "slot", "tok", "signal", "waits", "idx", "semkey", "semval")


class WK(tuple):
    def __new__(cls, slot):
        return tuple.__new__(cls, (("wb", slot, 0), ("wb", slot, 1)))


def _flat(keys):
    out = []
    for k in keys:
        if isinstance(k, WK):
            out.extend(k)
        else:
            out.append(k)
    return out


class Sched:
    EPOCH = 30000

    def __init__(self):
        self.ops = []
        self.last_w = {}
        self.readers = {}
        self.stream_len = {"pe": 0, "dve": 0, "act": 0, "pool": 0, "sp": 0}
        self.slot_cnt = {}

    def add(self, eng, fn, r=(), w=(), dma_slot=None):
        op = Op()
        op.eng, op.fn, op.dma, op.slot = eng, fn, dma_slot is not None, dma_slot
        op.signal = False
        op.waits = []
        op.idx = len(self.ops)
        r = _flat(r)
        w = _flat(w)
        deps = set()
        for k in r:
            if k in self.last_w:
                deps.add(self.last_w[k])
        for k in w:
            if k in self.last_w:
                deps.add(self.last_w[k])
            for j in self.readers.get(k, ()):
                deps.add(j)
        op.deps = deps
        if op.dma:
            c = self.slot_cnt.get(dma_slot, 0) + 1
            self.slot_cnt[dma_slot] = c
            op.tok = (("dma", dma_slot), c)
            self.stream_len[eng] += 1
        else:
            self.stream_len[eng] += 1
            op.tok = (("eng", eng), self.stream_len[eng])
        for k in r:
            self.readers.setdefault(k, []).append(op.idx)
        for k in w:
            self.last_w[k] = op.idx
            self.readers[k] = []
        self.ops.append(op)
        return op

    def barrier(self):
        last = {}
        for op in self.ops:
            if op.fn is not None:
                last[op.eng] = op.idx
        for e in list(self.stream_len):
            op = Op()
            op.eng, op.fn, op.dma, op.slot = e, None, False, None
            op.signal = False
            op.waits = []
            op.idx = len(self.ops)
            op.deps = set(v for k, v in last.items() if k != e)
            self.stream_len[e] += 1
            op.tok = (("eng", e), self.stream_len[e])
            self.ops.append(op)

    def analyse(self):
        seen = {e: {} for e in self.stream_len}
        for op in self.ops:
            sn = seen[op.eng]
            for j in sorted(op.deps):
                d = self.ops[j]
                if (not d.dma) and d.eng == "pe" and op.eng == "pe" and not op.dma:
                    continue
                fam, v = d.tok
                if sn.get(fam, 0) >= v:
                    continue
                sn[fam] = v
                d.signal = True
                op.waits.append(j)
        rank = {e: 0 for e in self.stream_len}
        self.nepoch = {e: 1 for e in self.stream_len}
        for op in self.ops:
            if op.dma:
                op.semkey = op.tok[0]
                op.semval = 16 * op.tok[1]
            elif op.signal:
                r = rank[op.eng]
                rank[op.eng] = r + 1
                ep = r // self.EPOCH
                self.nepoch[op.eng] = max(self.nepoch[op.eng], ep + 1)
                op.semkey = ("eng", op.eng, ep)
                op.semval = r - ep * self.EPOCH + 1

    def emit(self, nc, es):
        self.analyse()
        sems = {}

        def getsem(key):
            if key not in sems:
                nm = "s_" + "_".join(str(x) for x in key).replace("(", "").replace(")", "").replace(",", "_").replace(" ", "").replace("'", "")
                sems[key] = es.enter_context(nc.semaphore(nm))
            return sems[key]

        for op in self.ops:
            if op.dma or op.signal:
                getsem(op.semkey)
        block = es.enter_context(nc.Block())
        by_eng = {e: [o for o in self.ops if o.eng == e] for e in self.stream_len}

        def body(eng_name):
            def f(eng):
                for op in by_eng[eng_name]:
                    for j in op.waits:
                        d = self.ops[j]
                        eng.wait_ge(sems[d.semkey], d.semval)
                    if op.fn is None:
                        continue
                    name, args, kw = op.fn
                    ins = getattr(eng, name)(*args, **kw)
                    if op.dma:
                        ins.then_inc(sems[op.semkey], 16)
                    elif op.signal:
                        ins.then_inc(sems[op.semkey], 1)
            return f

        block.tensor(body("pe"))
        block.vector(body("dve"))
        block.scalar(body("act"))
        block.gpsimd(body("pool"))
        block.sync(body("sp"))


def _bform(w, cols):
    out = np.empty((P, 4, 8, P), np.float32)
    for n, ci in enumerate(cols):
        blk = w[:, ci]
        out[:, n] = blk.reshape(8, P, P).transpose(1, 0, 2)
    return out.reshape(P, 4096)


def _aform(w, ci):
    blk = w[:, ci]
    C = blk.shape[1]
    return blk.reshape(8, P, C).transpose(1, 0, 2).reshape(P, 8 * C)


def prep_weights(inp):
    wfull = np.zeros((NL * NFULL, P, 4096), np.float32)
    wsmall = np.zeros((NL * NSMALL, P, 1024), np.float32)
    ar = np.arange(P)
    for l in range(NL):
        w_in = np.asarray(inp["w_in"][l], np.float32)
        w_out = np.asarray(inp["w_out"][l], np.float32)
        w_up = np.asarray(inp["w_up"][l], np.float32)
        w_dn = np.asarray(inp["w_down"][l], np.float32)
        fb = l * NFULL
        sb = l * NSMALL
        for hp in range(2):
            h0, h1 = 2 * hp, 2 * hp + 1
            wfull[fb + 2 * hp] = _bform(w_in, [h0 * P + ar, h1 * P + ar, 512 + h0 * P + ar, 512 + h1 * P + ar])
            ci = np.concatenate([1024 + h0 * P + ar, 1024 + h1 * P + ar, 1536 + h0 * P + ar, 1536 + h1 * P + ar])
            wfull[fb + 2 * hp + 1] = _aform(w_in, ci)
        a = ar // 64
        dd = ar % 64
        sw = a * 64 + (dd + 32) % 64
        for hp in range(4):
            qb, kb = 2064 + hp * P, 2576 + hp * P
            wfull[fb + 4 + hp] = _bform(w_in, [qb + ar, qb + sw, kb + ar, kb + sw])
            wsmall[sb + 1 + hp] = _aform(w_in, 3088 + hp * P + ar)
        wsmall[sb + 0, :, :128] = _aform(w_in, 2048 + np.arange(16))
        for hf in range(2):
            blk = w_out[hf * 512:(hf + 1) * 512, :]
            t = blk.reshape(4, P, 8, P).transpose(1, 2, 0, 3)
            wfull[fb + 8 + hf] = t.reshape(P, 4096)
        for g in range(8):
            wfull[fb + 10 + g] = _bform(w_up, [(4 * g + n) * P + ar for n in range(4)])
        for m in range(8):
            blk = w_dn[:, m * P:(m + 1) * P]
            wfull[fb + 18 + m] = blk.reshape(32, P, P).transpose(1, 0, 2).reshape(P, 4096)
    return wfull, wsmall


def _cst_layout():
    off = {}
    o = 0

    def put(name, n):
        nonlocal o
        off[name] = (o, n)
        o += n
    put("g1", NL * 8)
    put("g2", NL * 8)
    put("gf", 8)
    put("convw", NL * 40)
    put("gb", NL * 16)
    for nm in ("Tf", "Tb", "NEGf", "NEGb", "identF", "onesF", "sel127", "sel0"):
        put(nm, P)
    put("epsc", 4)
    off["_NCS"] = (o, 0)
    put("hng", NL * 512)
    put("band", 512)
    return off, o


CST_OFF, NCST = _cst_layout()
NCS = CST_OFF["_NCS"][0]


def prep_consts(inp):
    c = np.zeros((P, NCST), np.float32)

    def setc(name, arr):
        o, n = CST_OFF[name]
        c[:, o:o + n] = arr.reshape(P, n)
    g1 = np.asarray(inp["norm1_g"], np.float32).reshape(NL, 8, P).transpose(2, 0, 1)
    g2 = np.asarray(inp["norm2_g"], np.float32).reshape(NL, 8, P).transpose(2, 0, 1)
    gf = np.asarray(inp["final_g"], np.float32).reshape(8, P).transpose(1, 0)
    setc("g1", np.ascontiguousarray(g1))
    setc("g2", np.ascontiguousarray(g2))
    setc("gf", np.ascontiguousarray(gf))
    cw = np.asarray(inp["conv_w"], np.float32).reshape(NL, 5, 8, P).transpose(3, 0, 1, 2)
    setc("convw", np.ascontiguousarray(cw))
    gb = np.concatenate([np.asarray(inp["gate_i_b"], np.float32), np.asarray(inp["gate_f_b"], np.float32)], axis=1)
    setc("gb", np.ascontiguousarray(np.broadcast_to(gb.reshape(1, NL * 16), (P, NL * 16))))
    hng = np.asarray(inp["head_norm_g"], np.float32).reshape(1, NL * 512)
    setc("hng", np.ascontiguousarray(np.broadcast_to(hng, (P, NL * 512))))
    k = np.arange(P)[:, None]
    i = np.arange(P)[None, :]
    NEG = -30000.0
    setc("Tf", (k <= i).astype(np.float32))
    setc("Tb", (k >= i).astype(np.float32))
    setc("NEGf", np.where(k <= i, 0.0, NEG).astype(np.float32))
    setc("NEGb", np.where(k >= i, 0.0, NEG).astype(np.float32))
    setc("identF", np.eye(P, dtype=np.float32))
    setc("onesF", np.ones((P, P), np.float32))
    s127 = np.zeros((P, P), np.float32)
    s127[127, :] = 1.0
    s0 = np.zeros((P, P), np.float32)
    s0[0, :] = 1.0
    setc("sel127", s127)
    setc("sel0", s0)
    cc = np.arange(256)[None, :]
    m1 = ((cc >= k) & (cc <= k + 128)).astype(np.float32)
    band = np.concatenate([m1, m1], axis=1)
    setc("band", band)
    ec = np.zeros((P, 4), np.float32)
    ec[:, 0] = EPS
    ec[:, 1] = 1.0
    ec[:, 2] = -0.5 * math.log(128.0)
    setc("epsc", ec)
    return c


def prep_rope():
    half = 32
    inv = (10000.0 ** (-np.arange(half, dtype=np.float32) / half)).astype(np.float32)
    ang = np.arange(S, dtype=np.float32)[:, None] * inv[None, :]
    cos = np.cos(ang).astype(np.float32).T
    sin = np.sin(ang).astype(np.float32).T
    cos64 = np.concatenate([cos, cos], 0)
    sin64 = np.concatenate([-sin, sin], 0)
    r = np.zeros((2, P, S), np.float32)
    r[0] = np.concatenate([cos64, cos64], 0)
    r[1] = np.concatenate([sin64, sin64], 0)
    return r


def build(cfg=CFG):
    nc = bass.Bass("TRN2", target_bir_lowering=False)
    nlay = cfg["layers"]
    xT_d = nc.dram_tensor("xT", [D, S], F32, kind="ExternalInput").ap()
    wf_d = nc.dram_tensor("wfull", [NL * NFULL, P, 4096], F32, kind="ExternalInput").ap()
    ws_d = nc.dram_tensor("wsmall", [NL * NSMALL, P, 1024], F32, kind="ExternalInput").ap()
    cst_d = nc.dram_tensor("cst", [P, NCST], F32, kind="ExternalInput").ap()
    rope_d = nc.dram_tensor("rope", [2, P, S], F32, kind="ExternalInput").ap()
    out_d = nc.dram_tensor("outT", [D, S], F32, kind="ExternalOutput").ap()

    es = ExitStack()
    sc = Sched()

    def sb(name, shape, dt=F32):
        return es.enter_context(nc.sbuf_tensor(name, shape, dt))

    def ps(name, shape, dt=F32):
        return es.enter_context(nc.psum_tensor(name, shape, dt))

    xT = sb("xT_sb", [P, 8, S])
    cst = sb("cst_sb", [P, NCS])
    NWB = 2
    wb = [sb(f"wb{i}", [P, 4096], BF16) for i in range(NWB)]
    wsb = sb("wsb", [P, 1024], BF16)
    NU = 54944
    U = sb("U", [P, NU], BF16)
    hT = U[:, 0:16384].rearrange("p (c t) -> p c t", c=8)
    yT = U[:, 16384:24576].rearrange("p (c t) -> p c t", c=4)
    RBASE = 24576

    class Carver:
        def __init__(self, base):
            self.o = base

        def bf(self, n):
            a = U[:, self.o:self.o + n]
            self.o += n + (n % 2)
            assert self.o <= NU
            return a

        def f32(self, n):
            a = U[:, self.o:self.o + 2 * n].bitcast(F32)
            self.o += 2 * n
            assert self.o <= NU
            return a

    sqb = [sb(f"sqb{i}", [P, 512], BF16) for i in range(2)]
    rst = [sb("rst0", [P, 512])] * 2
    ones_bf = sb("ones_bf", [P, P], BF16)
    ones1_bf = sb("ones1_bf", [P, 64], BF16)
    ident_bf = sb("ident_bf", [P, P], BF16)
    mask2 = sb("mask2", [P, 2, 256], BF16)
    diag = [sb(f"diag{i}", [P, 5, P], BF16) for i in range(2)]
    G = {nm: sb("G_" + nm, [P, 2, 16, 4]) for nm in ("gi", "lf", "bcol", "col", "blast", "w", "decay")}
    graw = sb("graw", [P, 16, 16])
    pbank = [ps(f"pb{i}", [P, 512]) for i in range(8)]

    def C(name, a=0, n=None):
        o, ln = CST_OFF[name]
        n = ln - a if n is None else n
        return cst[:, o + a:o + a + n]

    sc.add("sp", I("dma_start", out=cst[:], in_=cst_d[:, 0:NCS]), w=["cst"], dma_slot="cst")
    xv = xT_d.rearrange("(c p) t -> p c t", p=P)
    for c in range(8):
        sc.add("sp", I("dma_start", out=xT[:, c, :], in_=xv[:, c, :]), w=[("xT", c)], dma_slot=("x", c))
    sc.add("pool", I("memset", ones_bf[:], 1.0 / D), w=["ones_bf"])
    sc.add("pool", I("memset", ones1_bf[:], 1.0), w=["ones1_bf"])
    sc.add("dve", I("tensor_copy", out=ident_bf[:], in_=C("identF")), r=["cst"], w=["ident_bf"])
    bo = CST_OFF["band"][0]
    sc.add("pool", I("dma_start", out=mask2[:].rearrange("p a b -> p (a b)"), in_=cst_d[:, bo:bo + 512]), w=["mask2"], dma_slot="band")

    wstate = {"n": 0, "issued": 0}
    plan = []
    for l_ in range(nlay):
        fb_ = l_ * NFULL
        if cfg["mlstm"]:
            plan += [(fb_ + 0, 4096, 0), (fb_ + 1, 4096, 0), (fb_ + 2, 4096, 0), (fb_ + 3, 4096, 0), (fb_ + 8, 4096, 0)]
        if cfg["attn"]:
            plan += [(fb_ + 4 + hp_, 4096, 0) for hp_ in range(4)] + [(fb_ + 9, 4096, 0)]
        if cfg["ffn"]:
            for half_ in range(2):
                plan += [(fb_ + 10 + half_ * 4 + g_, 4096, 0) for g_ in range(4)]
                plan += [(fb_ + 18 + m_, 2048, half_ * 2048) for m_ in range(8)]

    def _issue(idx):
        gidx, ncols, coloff = plan[idx]
        slot = idx % NWB
        buf = wb[slot]
        for hlf in range(ncols // 2048):
            sc.add("pool", I("dma_start", out=buf[:, hlf * 2048:(hlf + 1) * 2048], in_=wf_d[gidx, :, coloff + hlf * 2048:coloff + (hlf + 1) * 2048]),
                   w=[("wb", slot, hlf)], dma_slot=("wb", slot, hlf))

    def load_w(gidx, ncols=4096, coloff=0):
        n = wstate["n"]
        assert plan[n] == (gidx, ncols, coloff), (n, plan[n], gidx, ncols, coloff)
        wstate["n"] = n + 1
        while wstate["issued"] <= min(n + 1, len(plan) - 1):
            _issue(wstate["issued"])
            wstate["issued"] += 1
        slot = n % NWB
        if ncols // 2048 == 1:
            return wb[slot], ("wb", slot, 0)
        return wb[slot], WK(slot)

    def load_wsmall(gidx, ncols=1024):
        sc.add("pool", I("dma_start", out=wsb[:, 0:ncols], in_=ws_d[gidx, :, 0:ncols]), w=["wsb"], dma_slot="wsb")
        return wsb, "wsb"

    pcount = {"n": 0}

    def next_bank():
        i = pcount["n"] % 8
        pcount["n"] += 1
        return i

    def rmsnorm(gname, goff, t0, t1, dst_fn, dst_keys):
        nb = (t1 - t0) // 512
        for b in range(nb):
            lo = t0 + b * 512
            hi = lo + 512
            bk = next_bank()
            for c in range(8):
                q = sqb[c % 2]
                sc.add("act", I("activation", out=q[:], in_=xT[:, c, lo:hi], func=AF.Square),
                       r=[("xT", c)], w=[("sqb", c % 2)])
                sc.add("pe", I("matmul", pbank[bk][:], lhsT=ones_bf[:], rhs=q[:], start=(c == 0), stop=(c == 7)),
                       r=[("sqb", c % 2), "ones_bf"], w=[("pb", bk)])
            r_ = rst[b % 2]
            sc.add("act", I("activation", out=r_[:], in_=pbank[bk][:], func=AF.Ln, bias=C("epsc", 0, 1), scale=1.0),
                   r=[("pb", bk), "cst"], w=[("rst", 0)])
            sc.add("act", I("activation", out=r_[:], in_=r_[:], func=AF.Exp, scale=-0.5),
                   r=[("rst", 0)], w=[("rst", 0)])
            for c in range(8):
                eng = "dve"
                sc.add(eng, I("scalar_tensor_tensor",
                    out=dst_fn(c, lo, hi), in0=xT[:, c, lo:hi], scalar=C(gname, goff + c, 1), in1=r_[:], op0=ALU.mult, op1=ALU.mult),
                    r=[("xT", c), ("rst", 0), "cst"], w=dst_keys(c))

    def ffn(l):
        uTv = U[:, 16384:16384 + 16 * S].rearrange("p (n t) -> p n t", t=S)
        rtmp = [U[:, 49152 + i * 1024:49152 + (i + 1) * 1024].bitcast(F32) for i in range(2)]
        rmsnorm("g2", l * 8, 0, S, lambda c, lo, hi: hT[:, c, lo:hi], lambda c: [("hT", c)])
        cnt = 0
        for half in range(2):
            for g in range(4):
                w, wk = load_w(l * NFULL + 10 + half * 4 + g)
                wv = w[:].rearrange("p (n k c) -> p n k c", n=4, k=8)
                for n in range(4):
                    ch = 4 * g + n
                    for blk in range(4):
                        lo = blk * 512
                        bk = next_bank()
                        for kc in range(8):
                            sc.add("pe", I("matmul", pbank[bk][:], lhsT=wv[:, n, kc, :], rhs=hT[:, kc, lo:lo + 512], start=(kc == 0), stop=(kc == 7)),
                                   r=[wk, ("hT", kc)], w=[("pb", bk)])
                        rt = rtmp[cnt % 2]
                        sc.add("act", I("activation", out=rt[:], in_=pbank[bk][:], func=AF.Relu), r=[("pb", bk)], w=[("rtmp", cnt % 2)])
                        sc.add("pool", I("tensor_tensor", out=uTv[:, ch, lo:lo + 512], in0=rt[:], in1=rt[:], op=ALU.mult),
                               r=[("rtmp", cnt % 2)], w=[("uT", ch, blk)])
                        cnt += 1
            for m in range(8):
                w, wk = load_w(l * NFULL + 18 + m, ncols=2048, coloff=half * 2048)
                wv = w[:, 0:2048].rearrange("p (n c) -> p n c", n=16)
                for blk in range(4):
                    lo = blk * 512
                    bk = next_bank()
                    for n in range(16):
                        sc.add("pe", I("matmul", pbank[bk][:], lhsT=wv[:, n, :], rhs=uTv[:, n, lo:lo + 512], start=(n == 0), stop=(n == 15)),
                               r=[wk, ("uT", n, blk)], w=[("pb", bk)])
                    sc.add("dve", I("tensor_tensor", out=xT[:, m, lo:lo + 512], in0=xT[:, m, lo:lo + 512], in1=pbank[bk][:], op=ALU.add),
                           r=[("pb", bk), ("xT", m)], w=[("xT", m)])

    def proj_B(wv_n, wk, blk, bk):
        lo = blk * 512
        for kc in range(8):
            sc.add("pe", I("matmul", pbank[bk][:], lhsT=wv_n[:, kc, :], rhs=hT[:, kc, lo:lo + 512], start=(kc == 0), stop=(kc == 7)),
                   r=[wk, ("hT", kc)], w=[("pb", bk)])

    def proj_A(wv, wk, ncol, tok_ap_fn, out_ap, bk):
        for kc in range(8):
            sc.add("pe", I("matmul", out_ap, lhsT=tok_ap_fn(kc), rhs=wv[:, kc, 0:ncol], start=(kc == 0), stop=(kc == 7)),
                   r=[wk, ("hT", kc)], w=[("pb", bk)])

    def pbf(bk):
        return pbank[bk][:].bitcast(BF16)

    def gates(l):
        w, wk = load_wsmall(l * NSMALL + 0, 128)
        wv = w[:, 0:128].rearrange("p (k c) -> p k c", k=8)
        bk = next_bank()
        for t in range(16):
            proj_A(wv, wk, 16, lambda kc, t=t: hT[:, kc, t * P:(t + 1) * P], pbank[bk][:, t * 16:(t + 1) * 16], bk)
        gbv = C("gb", l * 16, 16).unsqueeze(1).to_broadcast([P, 16, 16])
        sc.add("dve", I("tensor_tensor", out=graw[:], in0=pbank[bk][:, 0:256].rearrange("p (t c) -> p t c", t=16), in1=gbv, op=ALU.add),
               r=[("pb", bk), "cst"], w=["graw"])
        gi_src = graw[:, :, 0:8].rearrange("p t (d h) -> p d t h", d=2)
        gf_src = graw[:, :, 8:16].rearrange("p t (d h) -> p d t h", d=2)
        sc.add("dve", I("tensor_scalar", out=G["gi"][:], in0=gi_src, scalar1=-0.5 * math.log(128.0), scalar2=0.0, op0=ALU.add, op1=ALU.add),
               r=["graw"], w=["G_gi"])
        sc.add("act", I("activation", out=G["w"][:], in_=gf_src, func=AF.Exp, scale=-1.0), r=["graw"], w=["G_w"])
        sc.add("act", I("activation", out=G["w"][:], in_=G["w"][:], func=AF.Ln, bias=C("epsc", 1, 1), scale=1.0), r=["G_w", "cst"], w=["G_w"])
        sc.add("dve", I("tensor_scalar", out=G["lf"][:], in0=G["w"][:], scalar1=-1.0, scalar2=0.0, op0=ALU.mult, op1=ALU.add),
               r=["G_w"], w=["G_lf"])
        bk2 = next_bank()
        for d_, Tn in ((0, "Tf"), (1, "Tb")):
            sc.add("pe", I("matmul", pbank[bk2][:, d_ * 64:(d_ + 1) * 64], lhsT=C(Tn), rhs=G["lf"][:, d_].rearrange("p t h -> p (t h)"), start=True, stop=True),
                   r=["G_lf", "cst"], w=[("pb", bk2)])
        sc.add("dve", I("tensor_copy", out=G["bcol"][:].rearrange("p d t h -> p (d t h)"), in_=pbank[bk2][:, 0:128]), r=[("pb", bk2)], w=["G_bcol"])
        sc.add("dve", I("tensor_tensor", out=G["col"][:], in0=G["gi"][:], in1=G["bcol"][:], op=ALU.subtract), r=["G_gi", "G_bcol"], w=["G_col"])
        bk3 = next_bank()
        for d_, Sn in ((0, "sel127"), (1, "sel0")):
            sc.add("pe", I("matmul", pbank[bk3][:, d_ * 64:(d_ + 1) * 64], lhsT=C(Sn), rhs=G["bcol"][:, d_].rearrange("p t h -> p (t h)"), start=True, stop=True),
                   r=["G_bcol", "cst"], w=[("pb", bk3)])
        sc.add("dve", I("tensor_copy", out=G["blast"][:].rearrange("p d t h -> p (d t h)"), in_=pbank[bk3][:, 0:128]), r=[("pb", bk3)], w=["G_blast"])
        sc.add("act", I("activation", out=G["decay"][:], in_=G["blast"][:], func=AF.Exp), r=["G_blast"], w=["G_decay"])
        sc.add("dve", I("tensor_tensor", out=G["w"][:], in0=G["col"][:], in1=G["blast"][:], op=ALU.add), r=["G_col", "G_blast"], w=["G_w"])
        sc.add("act", I("activation", out=G["w"][:], in_=G["w"][:], func=AF.Exp), r=["G_w"], w=["G_w"])

    def mlstm_pair(l, hp):
        cv = Carver(RBASE)
        mqk = cv.bf(4 * S).rearrange("p (n t) -> p n t", n=4)
        Vext = cv.bf(16 * 2 * 129).rearrange("p (t a c) -> p t a c", t=16, a=2)
        ogt = cv.bf(16 * 256).rearrange("p (t c) -> p t c", t=16)
        Cstb = cv.bf(16 * 2 * 129).rearrange("p (t a c) -> p t a c", t=16, a=2)
        tmp_base = cv.o
        pre = [cv.bf(2052), cv.bf(2052)]
        cv.o = tmp_base
        Tlf = [cv.f32(256).rearrange("p (a i) -> p a i", a=2) for _ in range(2)]
        rhs2 = [cv.f32(256).rearrange("p (a i) -> p a i", a=2) for _ in range(2)]
        Dx = [cv.f32(256) for _ in range(2)]
        eb = [cv.bf(256).rearrange("p (a i) -> p a i", a=2) for _ in range(2)]
        qs = [cv.bf(256).rearrange("p (a i) -> p a i", a=2) for _ in range(2)]
        PT = [cv.bf(256).rearrange("p (a i) -> p a i", a=2) for _ in range(2)]
        cv.o = max(cv.o, tmp_base + 2 * 2052)
        wK = cv.bf(256).rearrange("p (a c) -> p a c", a=2)
        Crun = [cv.f32(258).rearrange("p (a c) -> p a c", a=2) for _ in range(2)]
        Cf = cv.bf(258).rearrange("p (a c) -> p a c", a=2)
        hsum = cv.f32(256).rearrange("p (a c) -> p a c", a=2)
        sq = cv.f32(256).rearrange("p (a c) -> p a c", a=2)
        gog = cv.f32(256)
        halfg = cv.f32(256)
        ybf = cv.bf(256).rearrange("p (a c) -> p a c", a=2)
        rr = cv.f32(4)
        ssq = cv.f32(2)
        K = lambda nm: ("m", nm)

        w, wk = load_w(l * NFULL + 2 * hp)
        wv = w[:].rearrange("p (n k c) -> p n k c", n=4, k=8)
        for i in range(2):
            sc.add("pool", I("memset", pre[i][:, 0:2], 0.0), w=[K(("pre", i))])
            sc.add("pool", I("memset", pre[i][:, 2050:2052], 0.0), w=[K(("pre", i))])
        for n in range(4):
            c8 = (2 * hp + n) if n < 2 else (4 + 2 * hp + n - 2)
            pb_ = pre[n % 2]
            dg = diag[n % 2]
            for tau in range(5):
                sc.add("pool", I("tensor_tensor", out=dg[:, tau, :], in0=C("identF"), in1=C("convw", l * 40 + tau * 8 + c8, 1).to_broadcast([P, P]), op=ALU.mult),
                       r=["ident_bf", "cst"], w=[("diag", n % 2)])
            for blk in range(4):
                bk = next_bank()
                proj_B(wv[:, n], wk, blk, bk)
                sc.add("act", I("activation", out=pb_[:, 2 + blk * 512:2 + (blk + 1) * 512], in_=pbank[bk][:], func=AF.Copy),
                       r=[("pb", bk)], w=[K(("pre", n % 2))])
            for blk in range(4):
                bk = next_bank()
                for tau in range(5):
                    sc.add("pe", I("matmul", pbank[bk][:], lhsT=dg[:, tau, :], rhs=pb_[:, blk * 512 + tau:blk * 512 + tau + 512], start=(tau == 0), stop=(tau == 4)),
                           r=[K(("pre", n % 2)), ("diag", n % 2)], w=[("pb", bk)])
                sc.add("act", I("activation", out=mqk[:, n, blk * 512:(blk + 1) * 512], in_=pbank[bk][:], func=AF.Silu),
                       r=[("pb", bk)], w=[K(("mqk", n))])
        w, wk = load_w(l * NFULL + 2 * hp + 1)
        wv = w[:].rearrange("p (k c) -> p k c", k=8)
        sc.add("pool", I("memset", Vext[:, :, :, 128:129], 1.0), w=[K("Vext")])
        for t in range(16):
            bk = next_bank()
            proj_A(wv, wk, 512, lambda kc, t=t: hT[:, kc, t * P:(t + 1) * P], pbank[bk][:], bk)
            sc.add("act", I("activation", out=Vext[:, t, :, 0:128], in_=pbank[bk][:, 0:256].rearrange("p (a c) -> p a c", a=2), func=AF.Copy),
                   r=[("pb", bk)], w=[K("Vext")])
            sc.add("act", I("activation", out=ogt[:, t, :], in_=pbank[bk][:, 256:512], func=AF.Tanh, scale=0.5),
                   r=[("pb", bk)], w=[K("ogt")])
        ho = CST_OFF["hng"][0] + l * 512 + hp * 256
        sc.add("sp", I("dma_start", out=halfg[:], in_=cst_d[:, ho:ho + 256]), w=[K("halfg")], dma_slot="hng")
        sc.add("pool", I("tensor_scalar", out=halfg[:], in0=halfg[:], scalar1=0.5, scalar2=0.0, op0=ALU.mult, op1=ALU.add),
               r=[K("halfg")], w=[K("halfg")])

        def gsl(nm, d_, t):
            return G[nm][:, d_, t, 2 * hp:2 * hp + 2]

        def state_update(d_, t):
            bkT = next_bank()
            for a in range(2):
                sc.add("pe", I("transpose", pbf(bkT)[:, a * P:(a + 1) * P], mqk[:, 2 + a, t * P:(t + 1) * P], ident_bf[:]),
                       r=[K(("mqk", 2 + a)), "ident_bf"], w=[("pb", bkT)])
            sc.add("dve", I("tensor_tensor", out=wK[:], in0=pbf(bkT)[:, 0:256].rearrange("p (a c) -> p a c", a=2),
                                                     in1=gsl("w", d_, t).unsqueeze(2).to_broadcast([P, 2, P]), op=ALU.mult),
                   r=[("pb", bkT), "G_w"], w=[K("wK")])
            bkD = next_bank()
            for a in range(2):
                sc.add("pe", I("matmul", pbank[bkD][:, a * 129:(a + 1) * 129], lhsT=wK[:, a, :], rhs=Vext[:, t, a, :], start=True, stop=True),
                       r=[K("wK"), K("Vext")], w=[("pb", bkD)])
            for a in range(2):
                sc.add("dve", I("scalar_tensor_tensor", out=Crun[d_][:, a, :], in0=Crun[d_][:, a, :], scalar=G["decay"][:, d_, t, 2 * hp + a:2 * hp + a + 1],
                                                                    in1=pbank[bkD][:, a * 129:(a + 1) * 129], op0=ALU.mult, op1=ALU.add),
                       r=[("pb", bkD), "G_decay", K(("Crun", d_))], w=[K(("Crun", d_))])

        sc.barrier()
        for d_ in range(2):
            sc.add("pool", I("memset", Crun[d_][:], 0.0), w=[K(("Crun", d_))])
        for t in range(15, -1, -1):
            sc.add("act", I("activation", out=Cstb[:, t], in_=Crun[1][:], func=AF.Copy), r=[K(("Crun", 1))], w=[K("Cstb")])
            if t > 0:
                state_update(1, t)

        def state_update_f1(t):
            for a in range(2):
                sc.add("pe", I("transpose", pbf(5)[:, a * P:(a + 1) * P], mqk[:, 2 + a, t * P:(t + 1) * P], ident_bf[:]),
                       r=[K(("mqk", 2 + a)), "ident_bf"], w=[("pb", 5)])
            sc.add("dve", I("tensor_tensor", out=wK[:], in0=pbf(5)[:, 0:256].rearrange("p (a c) -> p a c", a=2),
                            in1=gsl("w", 0, t).unsqueeze(2).to_broadcast([P, 2, P]), op=ALU.mult),
                   r=[("pb", 5), "G_w"], w=[K("wK")])

        def state_update_f2(t):
            for a in range(2):
                sc.add("pe", I("matmul", pbank[7][:, a * 129:(a + 1) * 129], lhsT=wK[:, a, :], rhs=Vext[:, t, a, :], start=True, stop=True),
                       r=[K("wK"), K("Vext")], w=[("pb", 7)])

        def state_update_f3(t):
            for a in range(2):
                sc.add("dve", I("scalar_tensor_tensor", out=Crun[0][:, a, :], in0=Crun[0][:, a, :], scalar=G["decay"][:, 0, t, 2 * hp + a:2 * hp + a + 1],
                                in1=pbank[7][:, a * 129:(a + 1) * 129], op0=ALU.mult, op1=ALU.add),
                       r=[("pb", 7), "G_decay", K(("Crun", 0))], w=[K(("Crun", 0))])

        def y_transposes(t):
            for a in range(2):
                sc.add("pe", I("transpose", pbf(6)[:, a * P:(a + 1) * P], ybf[:, a, :], ident_bf[:]),
                       r=[K("ybf"), "ident_bf"], w=[("pb", 6)])
            sc.add("act", I("activation", out=yT[:, 2 * hp:2 * hp + 2, t * P:(t + 1) * P], in_=pbf(6)[:, 0:256].rearrange("p (a c) -> p a c", a=2), func=AF.Copy),
                   r=[("pb", 6)], w=[("yT", 2 * hp), ("yT", 2 * hp + 1)])

        def stA(t):
            tl = slice(t * P, (t + 1) * P)
            for a in range(2):
                sc.add("pe", I("matmul", pbank[0][:, a * P:(a + 1) * P], lhsT=mqk[:, 2 + a, tl], rhs=mqk[:, a, tl], start=True, stop=True),
                       r=[K(("mqk", a)), K(("mqk", 2 + a))], w=[("pb", 0)])
            for d_ in range(2):
                Tn, Nn = ("Tf", "NEGf") if d_ == 0 else ("Tb", "NEGb")
                sc.add("pool", I("tensor_tensor", out=Tlf[d_][:], in0=C(Tn).unsqueeze(1).to_broadcast([P, 2, P]),
                                 in1=gsl("lf", d_, t).unsqueeze(2).to_broadcast([P, 2, P]), op=ALU.mult),
                       r=["cst", "G_lf"], w=[K(("Tlf", d_))])
                sc.add("pool", I("tensor_tensor", out=rhs2[d_][:], in0=C(Nn).unsqueeze(1).to_broadcast([P, 2, P]),
                                 in1=gsl("col", d_, t).unsqueeze(2).to_broadcast([P, 2, P]), op=ALU.add),
                       r=["cst", "G_col"], w=[K(("rhs2", d_))])
            for d_ in range(2):
                bk_ = 3 + d_
                Tl2 = Tlf[d_][:].rearrange("p a i -> p (a i)")
                sc.add("pe", I("matmul", pbank[bk_][:, 0:256], lhsT=C("onesF"), rhs=Tl2, start=True, stop=True),
                       r=["cst", K(("Tlf", d_))], w=[("pb", bk_)])
                sc.add("pe", I("matmul", pbank[bk_][:, 256:512], lhsT=C("onesF"), rhs=Tl2, start=True, stop=False),
                       r=["cst", K(("Tlf", d_))], w=[("pb", bk_)])
                sc.add("pe", I("matmul", pbank[bk_][:, 256:512], lhsT=C("identF"), rhs=rhs2[d_][:].rearrange("p a i -> p (a i)"), start=False, stop=True),
                       r=["cst", K(("rhs2", d_))], w=[("pb", bk_)])

        def stB(t):
            tl = slice(t * P, (t + 1) * P)
            for d_ in range(2):
                bk_ = 3 + d_
                sc.add("act", I("activation", out=Dx[d_][:], in_=pbank[bk_][:, 256:512], func=AF.Exp),
                       r=[("pb", bk_)], w=[K(("Dx", d_))])
                sc.add("act", I("activation", out=eb[d_][:].rearrange("p a i -> p (a i)"), in_=pbank[bk_][:, 0:256], func=AF.Exp),
                       r=[("pb", bk_)], w=[K(("eb", d_))])
                sc.add("dve", I("tensor_tensor", out=PT[d_][:].rearrange("p a i -> p (a i)"), in0=pbank[0][:, 0:256], in1=Dx[d_][:], op=ALU.mult),
                       r=[("pb", 0), K(("Dx", d_))], w=[K(("PT", d_))])
                sc.add("dve", I("tensor_tensor", out=qs[d_][:], in0=mqk[:, 0:2, tl], in1=eb[d_][:], op=ALU.mult),
                       r=[K(("mqk", 0)), K(("mqk", 1)), K(("eb", d_))], w=[K(("qs", d_))])

        def stC(t):
            bkO = 1 + (t % 2)
            for d_ in range(2):
                for a in range(2):
                    cprev = Cf[:, a, :] if d_ == 0 else Cstb[:, t, a, :]
                    ck = K("Cf") if d_ == 0 else K("Cstb")
                    oc = (a * 2 + d_) * P
                    sc.add("pe", I("matmul", pbank[bkO][:, oc:oc + P], lhsT=PT[d_][:, a, :], rhs=Vext[:, t, a, 0:128], start=True, stop=False),
                           r=[K(("PT", d_)), K("Vext")], w=[("pb", bkO)])
                    sc.add("pe", I("matmul", pbank[bkO][:, oc:oc + P], lhsT=qs[d_][:, a, :], rhs=cprev[:, 0:128], start=False, stop=True),
                           r=[K(("qs", d_)), ck], w=[("pb", bkO)])
            for d_ in range(2):
                for a in range(2):
                    cprev = Cf[:, a, :] if d_ == 0 else Cstb[:, t, a, :]
                    ck = K("Cf") if d_ == 0 else K("Cstb")
                    dc = 258 + a * 2 + d_
                    sc.add("pe", I("matmul", pbank[7][:, dc:dc + 1], lhsT=PT[d_][:, a, :], rhs=Vext[:, t, a, 128:129], start=True, stop=False),
                           r=[K(("PT", d_)), K("Vext")], w=[("pb", 7)])
                    sc.add("pe", I("matmul", pbank[7][:, dc:dc + 1], lhsT=qs[d_][:, a, :], rhs=cprev[:, 128:129], start=False, stop=True),
                           r=[K(("qs", d_)), ck], w=[("pb", 7)])

        def stD(t):
            bkO = 1 + (t % 2)
            rk_ = K("rr")
            sc.add("act", I("activation", out=rr[:, 0:4], in_=pbank[7][:, 258:262], func=AF.Abs), r=[("pb", 7)], w=[rk_])
            sc.add("dve", I("tensor_scalar", out=rr[:, 0:4], in0=rr[:, 0:4], scalar1=1.0, scalar2=0.0, op0=ALU.max, op1=ALU.add), r=[rk_], w=[rk_])
            sc.add("dve", I("reciprocal", out=rr[:, 0:4], in_=rr[:, 0:4]), r=[rk_], w=[rk_])
            for a in range(2):
                oc = a * 2 * P
                sc.add("act", I("activation", out=hsum[:, a, :], in_=pbank[bkO][:, oc:oc + P], func=AF.Copy, scale=rr[:, 2 * a:2 * a + 1]),
                       r=[("pb", bkO), rk_], w=[K(("hsum", a))])
            for a in range(2):
                oc = a * 2 * P
                sc.add("dve", I("scalar_tensor_tensor", out=hsum[:, a, :], in0=pbank[bkO][:, oc + P:oc + 2 * P], scalar=rr[:, 2 * a + 1:2 * a + 2], in1=hsum[:, a, :], op0=ALU.mult, op1=ALU.add),
                       r=[("pb", bkO), rk_, K(("hsum", a))], w=[K(("hsum", a))])
            for a in range(2):
                sc.add("act", I("activation", out=sq[:, a, :], in_=hsum[:, a, :], func=AF.Square, accum_out=ssq[:, a:a + 1]),
                       r=[K(("hsum", a))], w=[K(("sq", a)), K(("ssq", a))])
            hk = [K(("hsum", 0)), K(("hsum", 1))]
            sk = [K(("ssq", 0)), K(("ssq", 1))]
            sc.add("act", I("activation", out=ssq[:], in_=ssq[:], func=AF.Ln, bias=C("epsc", 0, 1), scale=1.0 / 128.0), r=sk + ["cst"], w=sk)
            sc.add("act", I("activation", out=ssq[:], in_=ssq[:], func=AF.Exp, scale=-0.5), r=sk, w=sk)
            sc.add("dve", I("scalar_tensor_tensor", out=gog[:], in0=ogt[:, t, :], scalar=1.0, in1=halfg[:], op0=ALU.add, op1=ALU.mult),
                   r=[K("ogt"), K("halfg")], w=[K("gog")])
            sc.add("dve", I("tensor_tensor", out=sq[:], in0=hsum[:], in1=ssq[:].unsqueeze(2).to_broadcast([P, 2, P]), op=ALU.mult),
                   r=hk + sk + [K(("sq", 0)), K(("sq", 1))], w=[K(("sq", 0)), K(("sq", 1))])

        def ybf_op():
            sc.add("pool", I("tensor_tensor", out=ybf[:].rearrange("p a c -> p (a c)"), in0=sq[:].rearrange("p a c -> p (a c)"), in1=gog[:], op=ALU.mult),
                   r=[K(("sq", 0)), K(("sq", 1)), K("gog")], w=[K("ybf")])

        def cf_copy():
            sc.add("act", I("activation", out=Cf[:], in_=Crun[0][:], func=AF.Copy), r=[K(("Crun", 0))], w=[K("Cf")])

        stA(0)
        stB(0)
        cf_copy()
        for t in range(16):
            stC(t)
            if t < 15:
                if t == 0:
                    state_update_f1(0)
                stA(t + 1)
                state_update_f2(t)
                stB(t + 1)
                state_update_f3(t)
                cf_copy()
            if t > 0:
                ybf_op()
                y_transposes(t - 1)
            if t + 1 < 15:
                state_update_f1(t + 1)
            stD(t)
        ybf_op()
        y_transposes(15)
        sc.barrier()

    def attn_pair(l, hp):
        cv = Carver(RBASE)
        aq = cv.bf(S)
        ak = cv.bf(S)
        Vp = [cv.bf(16 * P).rearrange("p (t c) -> p t c", t=16) for _ in range(3)]
        accn = cv.f32(S)
        accd = cv.f32(S)
        qP = cv.bf(S)
        kP = cv.bf(S)
        PtP = [cv.bf(512).rearrange("p (a c) -> p a c", a=2) for _ in range(3)]
        ropeb = [cv.f32(1024).rearrange("p (k t) -> p k t", k=2) for _ in range(2)]
        t1 = [cv.f32(512)] * 2
        t2 = [cv.f32(512)] * 2
        K = lambda nm: ("a", nm)
        DILS = (1, 4, 16)
        scnt = {"n": 0}
        wsm_, wkv = load_wsmall(l * NSMALL + 1 + hp, 1024)
        wvv = wsm_[:, 0:1024].rearrange("p (k c) -> p k c", k=8)
        w, wk = load_w(l * NFULL + 4 + hp)
        wv = w[:].rearrange("p (n k c) -> p n k c", n=4, k=8)
        for qi, dst in ((0, aq), (1, ak)):
            for blk in range(4):
                lo = blk * 512
                rb = ropeb[blk % 2]
                sc.add("sp", I("dma_start", out=rb[:], in_=rope_d[:, :, lo:lo + 512].rearrange("k p t -> p k t")), w=[K(("rope", blk % 2))], dma_slot=("rope", blk % 2))
                bk0 = next_bank()
                proj_B(wv[:, 2 * qi], wk, blk, bk0)
                bk1 = next_bank()
                proj_B(wv[:, 2 * qi + 1], wk, blk, bk1)
                sc.add("dve", I("tensor_tensor", out=t1[blk % 2][:], in0=pbank[bk0][:], in1=rb[:, 0, :], op=ALU.mult),
                       r=[("pb", bk0), K(("rope", blk % 2))], w=[K(("t1", 0))])
                sc.add("dve", I("tensor_tensor", out=t2[blk % 2][:], in0=pbank[bk1][:], in1=rb[:, 1, :], op=ALU.mult),
                       r=[("pb", bk1), K(("rope", blk % 2))], w=[K(("t2", 0))])
                sc.add("pool", I("tensor_tensor", out=dst[:, lo:lo + 512], in0=t1[blk % 2][:], in1=t2[blk % 2][:], op=ALU.add),
                       r=[K(("t1", 0)), K(("t2", 0))], w=[K(("qk", qi))])
        VT = qP
        if not cfg.get("vt", True):
            for di, d_ in enumerate(DILS):
                nb = 16 // d_
                for g in range(4):
                    bk = next_bank()
                    for tt in range(4):
                        tp = 4 * g + tt
                        r_, lb = tp // nb, tp % nb
                        st = d_ * P * lb + r_

                        def tok(kc, st=st, d_=d_):
                            return hT[:, kc, st:st + d_ * (P - 1) + 1:d_]
                        proj_A(wvv, wkv, P, tok, pbank[bk][:, tt * P:(tt + 1) * P], bk)
                    sc.add("act", I("activation", out=Vp[di][:, 4 * g:4 * g + 4, :], in_=pbank[bk][:].rearrange("p (t c) -> p t c", t=4), func=AF.Copy),
                           r=[("pb", bk)], w=[K(("V", di))])
        for blk in (range(4) if cfg.get("vt", True) else []):
            bk = next_bank()
            lo = blk * 512
            for kc in range(8):
                sc.add("pe", I("matmul", pbank[bk][:], lhsT=wvv[:, kc, :], rhs=hT[:, kc, lo:lo + 512], start=(kc == 0), stop=(kc == 7)),
                       r=[wkv, ("hT", kc)], w=[("pb", bk)])
            sc.add("act", I("activation", out=VT[:, lo:lo + 512], in_=pbank[bk][:], func=AF.Copy), r=[("pb", bk)], w=[K(("qP", blk))])
        for di, d_ in (enumerate(DILS) if cfg.get("vt", True) else []):
            nb = 16 // d_
            if d_ == 1:
                src, skeys = VT, [K(("qP", c_)) for c_ in range(4)]
            else:
                if d_ == 4:
                    sc.add("act", I("activation", out=kP[:].rearrange("p (r l) -> p r l", r=d_), in_=VT[:].rearrange("p (l r) -> p r l", r=d_), func=AF.Copy),
                           r=[K(("qP", c_)) for c_ in range(4)], w=[K(("kP", c_)) for c_ in range(4)])
                else:
                    sc.add("dve", I("tensor_copy", out=kP[:].rearrange("p (r l) -> p r l", r=d_), in_=VT[:].rearrange("p (l r) -> p r l", r=d_)),
                           r=[K(("qP", c_)) for c_ in range(4)], w=[K(("kP", c_)) for c_ in range(4)])
                src, skeys = kP, [K(("kP", c_)) for c_ in range(4)]
            for g in range(4):
                bk = next_bank()
                for tt in range(4):
                    tp = 4 * g + tt
                    sc.add("pe", I("transpose", pbf(bk)[:, tt * P:(tt + 1) * P], src[:, tp * P:(tp + 1) * P], ident_bf[:]),
                           r=skeys + ["ident_bf"], w=[("pb", bk)])
                if g % 2 == 0:
                    sc.add("act", I("activation", out=Vp[di][:, 4 * g:4 * g + 4, :], in_=pbf(bk)[:, 0:512].rearrange("p (t c) -> p t c", t=4), func=AF.Copy),
                           r=[("pb", bk)], w=[K(("V", di))])
                else:
                    sc.add("dve", I("tensor_copy", out=Vp[di][:, 4 * g:4 * g + 4, :], in_=pbf(bk)[:, 0:512].rearrange("p (t c) -> p t c", t=4)),
                           r=[("pb", bk)], w=[K(("V", di))])

        def perm_copy(d_, c):
            if d_ == 4:
                qi_ = aq[:].rearrange("p (l r) -> p r l", r=4)[:, c, :]
                ki_ = ak[:].rearrange("p (l r) -> p r l", r=4)[:, c, :]
                qo_, ko_ = qP[:, c * 512:(c + 1) * 512], kP[:, c * 512:(c + 1) * 512]
            else:
                qi_ = aq[:].rearrange("p (l r) -> p r l", r=16)[:, 4 * c:4 * c + 4, :]
                ki_ = ak[:].rearrange("p (l r) -> p r l", r=16)[:, 4 * c:4 * c + 4, :]
                qo_ = qP[:, c * 512:(c + 1) * 512].rearrange("p (r l) -> p r l", r=4)
                ko_ = kP[:, c * 512:(c + 1) * 512].rearrange("p (r l) -> p r l", r=4)
            sc.add("pool", I("tensor_copy", out=qo_, in_=qi_), r=[K(("qk", 0))], w=[K(("qP", c))])
            sc.add("dve", I("tensor_copy", out=ko_, in_=ki_), r=[K(("qk", 1))], w=[K(("kP", c))])

        kts = []
        blk_last = {}
        for di, d_ in enumerate(DILS):
            nb = 16 // d_
            Ld = nb * P
            for r_ in range(d_):
                for kt in range(nb):
                    lo, hi = max(0, kt * P - 64), min(Ld, kt * P + 192)
                    u = dict(di=di, d=d_, T=r_ * nb + kt, c_lo=r_ * Ld + lo, c_hi=r_ * Ld + hi, m0=lo - (kt * P - 64))
                    for b in range(u["c_lo"] // 512, (u["c_hi"] - 1) // 512 + 1):
                        blk_last[(di, b)] = len(kts)
                    kts.append(u)
        started = set()

        def views(u):
            if u["d"] == 1:
                return aq, ak, [K(("qk", 0))], [K(("qk", 1))]
            cq = range(u["c_lo"] // 512, (u["c_hi"] - 1) // 512 + 1)
            return qP, kP, [K(("qP", c_)) for c_ in cq], [K(("kP", u["T"] // 4))]

        def stageAB(j, u):
            d_ = u["d"]
            if cfg.get("early", True):
                if d_ == 1 and u["T"] in (2, 5, 8, 11):
                    perm_copy(4, (u["T"] - 2) // 3)
                if d_ == 4 and u["T"] % 4 == 0 and u["T"] > 0:
                    perm_copy(16, u["T"] // 4 - 1)
                if d_ == 16 and u["T"] == 0:
                    perm_copy(16, 3)
            elif d_ > 1 and u["T"] == 0:
                for c_ in range(4):
                    perm_copy(d_, c_)
            qv, kv, rq, rk = views(u)
            T, nc_ = u["T"], u["c_hi"] - u["c_lo"]
            pt = PtP[j % 3]
            for a in range(2):
                rows = slice(64 * a, 64 * a + 64)
                bkS = 4 + ((2 * j + a) % 4)
                sc.add("pe", I("matmul", pbank[bkS][:, 0:nc_], lhsT=kv[rows, T * P:(T + 1) * P], rhs=qv[rows, u["c_lo"]:u["c_hi"]], start=True, stop=True),
                       r=rq + rk, w=[("pb", bkS)])
            for a in range(2):
                bkS = 4 + ((2 * j + a) % 4)
                sc.add("act", I("activation", out=pt[:, a, 0:nc_], in_=pbank[bkS][:, 0:nc_], func=AF.Exp, scale=0.125),
                       r=[("pb", bkS)], w=[K(("Pt", j % 3, a))])
            meng = "pool" if (j % 3 == 0 and nc_ == 256 and cfg.get("poolmask", True)) else "dve"
            sc.add(meng, I("tensor_tensor", out=pt[:, :, 0:nc_], in0=pt[:, :, 0:nc_], in1=mask2[:, :, u["m0"]:u["m0"] + nc_], op=ALU.mult),
                   r=[K(("Pt", j % 3, 0)), K(("Pt", j % 3, 1)), "mask2"], w=[K(("Pt", j % 3, 0)), K(("Pt", j % 3, 1))])

        def stageC(j, u):
            di, d_, T = u["di"], u["d"], u["T"]
            pt = PtP[j % 3]
            blks = list(range(u["c_lo"] // 512, (u["c_hi"] - 1) // 512 + 1))
            for b in blks:
                s_lo, s_hi = max(u["c_lo"], 512 * b), min(u["c_hi"], 512 * (b + 1))
                gi_ = di * 4 + b
                bkN, bkD = (0, 1) if gi_ % 2 == 0 else (2, 3)
                for tag in ("N", "D"):
                    for a in range(2):
                        rows = slice(64 * a, 64 * a + 64)
                        mv = pt[:, a, s_lo - u["c_lo"]:s_hi - u["c_lo"]]
                        if tag == "N":
                            bk_, lhs, rkeys = bkN, Vp[di][:, T, rows], [K(("V", di))]
                        else:
                            bk_, lhs, rkeys = bkD, ones1_bf[:, 0:64], ["ones1_bf"]
                        first = (gi_, a, tag) not in started
                        started.add((gi_, a, tag))
                        sc.add("pe", I("matmul", pbank[bk_][rows, s_lo - 512 * b:s_hi - 512 * b], lhsT=lhs, rhs=mv, start=first, stop=True, skip_group_check=True),
                               r=rkeys + [K(("Pt", j % 3, a))], w=[("pb", bk_)])
            for g in blks:
                if blk_last[(di, g)] != j:
                    continue
                gi_ = di * 4 + g
                bkN, bkD = (0, 1) if gi_ % 2 == 0 else (2, 3)
                if d_ == 1:
                    sc.add("act", I("activation", out=accn[:, g * 512:(g + 1) * 512], in_=pbank[bkN][:], func=AF.Copy), r=[("pb", bkN)], w=[K("accn")])
                    sc.add("dve", I("tensor_copy", out=accd[:, g * 512:(g + 1) * 512], in_=pbank[bkD][:]), r=[("pb", bkD)], w=[K("accd")])
                else:
                    if d_ == 4:
                        vn = accn[:].rearrange("p (l r) -> p r l", r=4)[:, g, :]
                        vd = accd[:].rearrange("p (l r) -> p r l", r=4)[:, g, :]
                        pn, pd = pbank[bkN][:], pbank[bkD][:]
                    else:
                        vn = accn[:].rearrange("p (l r) -> p r l", r=16)[:, 4 * g:4 * g + 4, :]
                        vd = accd[:].rearrange("p (l r) -> p r l", r=16)[:, 4 * g:4 * g + 4, :]
                        pn = pbank[bkN][:].rearrange("p (r l) -> p r l", r=4)
                        pd = pbank[bkD][:].rearrange("p (r l) -> p r l", r=4)
                    sc.add("dve", I("tensor_tensor", out=vn, in0=vn, in1=pn, op=ALU.add), r=[("pb", bkN), K("accn")], w=[K("accn")])
                    sc.add("dve", I("tensor_tensor", out=vd, in0=vd, in1=pd, op=ALU.add), r=[("pb", bkD), K("accd")], w=[K("accd")])

        for j in range(len(kts) + 1):
            if j < len(kts):
                stageAB(j, kts[j])
            if j - 1 >= 0:
                stageC(j - 1, kts[j - 1])
        sc.add("dve", I("reciprocal", out=accd[:], in_=accd[:]), r=[K("accd")], w=[K("accd")])
        sc.add("dve", I("tensor_tensor", out=yT[:, hp, :], in0=accn[:], in1=accd[:], op=ALU.mult), r=[K("accn"), K("accd")], w=[("yT", hp)])
        if hp == 3:
            sc.barrier()

    def out_proj(l, hf):
        w, wk = load_w(l * NFULL + 8 + hf)
        wv = w[:].rearrange("p (m k c) -> p m k c", m=8, k=4)
        for m in range(8):
            for blk in range(4):
                lo = blk * 512
                bk = next_bank()
                for kc in range(4):
                    sc.add("pe", I("matmul", pbank[bk][:], lhsT=wv[:, m, kc, :], rhs=yT[:, kc, lo:lo + 512], start=(kc == 0), stop=(kc == 3)),
                           r=[wk, ("yT", kc)], w=[("pb", bk)])
                sc.add("dve", I("tensor_tensor", out=xT[:, m, lo:lo + 512], in0=xT[:, m, lo:lo + 512], in1=pbank[bk][:], op=ALU.add),
                       r=[("pb", bk), ("xT", m)], w=[("xT", m)])

    for l in range(nlay):
        if cfg["mlstm"] or cfg["attn"]:
            rmsnorm("g1", l * 8, 0, S, lambda c, lo, hi: hT[:, c, lo:hi], lambda c: [("hT", c)])
        if cfg["mlstm"]:
            gates(l)
            for hp in range(2):
                mlstm_pair(l, hp)
            out_proj(l, 0)
        if cfg["attn"]:
            for hp in range(4):
                attn_pair(l, hp)
            out_proj(l, 1)
            sc.barrier()
        if cfg["ffn"]:
            ffn(l)
            sc.barrier()

    ov = out_d.rearrange("(c p) t -> p c t", p=P)
    sc.barrier()
    finT = U[:, 0:2048].bitcast(F32).rearrange("p (a t) -> p a t", a=2)
    fcnt = {"n": 0}

    def fin_dst(c, lo, hi):
        return finT[:, c % 2, :]

    for b in range(4):
        lo, hi = b * 512, (b + 1) * 512
        bk = next_bank()
        for c in range(8):
            q = sqb[c % 2]
            sc.add("act", I("activation", out=q[:], in_=xT[:, c, lo:hi], func=AF.Square),
                   r=[("xT", c)], w=[("sqb", c % 2)])
            sc.add("pe", I("matmul", pbank[bk][:], lhsT=ones_bf[:], rhs=q[:], start=(c == 0), stop=(c == 7)),
                   r=[("sqb", c % 2), "ones_bf"], w=[("pb", bk)])
        r_ = rst[b % 2]
        sc.add("act", I("activation", out=r_[:], in_=pbank[bk][:], func=AF.Ln, bias=C("epsc", 0, 1), scale=1.0),
               r=[("pb", bk), "cst"], w=[("rst", 0)])
        sc.add("act", I("activation", out=r_[:], in_=r_[:], func=AF.Exp, scale=-0.5),
               r=[("rst", 0)], w=[("rst", 0)])
        for c in range(8):
            sl = c % 2
            sc.add("dve", I("scalar_tensor_tensor",
                out=finT[:, sl, :], in0=xT[:, c, lo:hi], scalar=C("gf", c, 1), in1=r_[:], op0=ALU.mult, op1=ALU.mult),
                r=[("xT", c), ("rst", 0), "cst"], w=[("fin", sl)])
            sc.add("sp", I("dma_start", out=ov[:, c, lo:hi], in_=finT[:, sl, :]),
                   r=[("fin", sl)], w=[("out", sl)], dma_slot=("out", sl))
    sc.add("sp", None, r=[("out", 0), ("out", 1)])

    sc.emit(nc, es)
    es.close()
    return nc


_PREP_CACHE = {}


def kernel(x, norm1_g, w_in, conv_w, gate_i_b, gate_f_b, head_norm_g, w_out, norm2_g, w_up, w_down, final_g, _cfg=None):
    cfg = dict(CFG)
    if _cfg:
        cfg.update(_cfg)
    inp = dict(x=x, norm1_g=norm1_g, w_in=w_in, conv_w=conv_w, gate_i_b=gate_i_b, gate_f_b=gate_f_b,
               head_norm_g=head_norm_g, w_out=w_out, norm2_g=norm2_g, w_up=w_up, w_down=w_down, final_g=final_g)
    inp = {k: np.asarray(v) for k, v in inp.items()}
    wfull, wsmall = prep_weights(inp)
    cst = prep_consts(inp)
    rope = prep_rope()
    nc = build(cfg)
    xs = np.asarray(inp["x"], np.float32)
    in_maps = []
    for b in range(8):
        in_maps.append({"xT": np.ascontiguousarray(xs[b].T), "wfull": wfull, "wsmall": wsmall, "cst": cst, "rope": rope})
    res = run_bass_kernel_spmd(nc, in_maps, core_ids=list(range(8)))
    out = np.stack([np.ascontiguousarray(r["outT"].T) for r in res.results], axis=0)
    return out.astype(np.float32)
```
